# Optimizing a Trainium2 kernel written in Bass

```python
import math
import jax
import jax.numpy as jnp
from jax import lax
import numpy as np

D_MODEL = 1024
BATCH = 8
SEQ = 4096
DEPTH = 2

GRID_W = 64
CTX_LEN = 256
EPS = 1e-6
W_LRU = 256
LRU_BLOCKS = 4
LRU_BLK = W_LRU // LRU_BLOCKS
CONV_W = 4
LRU_C = 8.0
W_S5 = 256
S5_H = 16
S5_G = W_S5 // S5_H
S5_P = 64
DA_HEADS = 4
DA_DH = 64
DA_DV = 2 * DA_DH
W_DA = DA_HEADS * DA_DV
MIX_W = W_LRU + W_S5 + W_DA
SIDE_W = W_LRU + W_S5 + 2 * W_DA
IN_W = 2 * SIDE_W
SPLIT4 = (W_LRU, W_LRU + W_S5, W_LRU + W_S5 + W_DA)
Q_BLOCK = 128
ROPE_BASE = 10000.0
ROPE_F = DA_DH // 4

kernel_name = 'hybrid_lru_s5_diffattn_prefix'

F32 = jnp.float32


def rms_norm(x, g):
    xf = x.astype(F32)
    y = xf * lax.rsqrt(jnp.mean(xf * xf, axis=-1, keepdims=True) + EPS)
    return (y * g.astype(F32)).astype(x.dtype)


def centred_dwconv(u, w, b):
    left = CONV_W // 2
    y = lax.conv_general_dilated(u, w.astype(u.dtype)[:, None, :], window_strides=(1,),
                                 padding=[(left, CONV_W - 1 - left)],
                                 dimension_numbers=('NWC', 'WIO', 'NWC'),
                                 feature_group_count=u.shape[-1])
    return y + b.astype(u.dtype)


def real_linear_op(e1, e2):
    a1, b1 = e1
    a2, b2 = e2
    return a1 * a2, a2 * b1 + b2


def complex_linear_op(e1, e2):
    ar1, ai1, br1, bi1 = e1
    ar2, ai2, br2, bi2 = e2
    return (ar1 * ar2 - ai1 * ai2, ar1 * ai2 + ai1 * ar2,
            ar2 * br1 - ai2 * bi1 + br2, ar2 * bi1 + ai2 * br1 + bi2)


def real_scan(a, b, h0, reverse):
    if h0 is not None:
        idx = a.shape[1] - 1 if reverse else 0
        b = b.at[:, idx].add(a[:, idx] * h0)
    _, h = lax.associative_scan(real_linear_op, (a, b), reverse=reverse, axis=1)
    return h


def block_diag(xb, w):
    y = jnp.einsum('btnc,ncd->btnd', xb, w.astype(F32))
    return y.reshape(xb.shape[0], xb.shape[1], -1)


def rglru_scans(u, conv_w, conv_b, wa, ba, wx, bx, lam, h0):
    bsz, t, _ = u.shape
    xc = centred_dwconv(u, conv_w, conv_b).astype(F32)
    xb = xc.reshape(bsz, t, LRU_BLOCKS, LRU_BLK)
    hs = []
    for d in range(2):
        gate_r = jax.nn.sigmoid(block_diag(xb, wa[d]) + ba[d].astype(F32))
        gate_i = jax.nn.sigmoid(block_diag(xb, wx[d]) + bx[d].astype(F32))
        log_a = -LRU_C * gate_r * jax.nn.softplus(-lam[d].astype(F32))
        a = jnp.exp(log_a)
        b = jnp.sqrt(-jnp.expm1(2.0 * log_a)) * (gate_i * xc)
        hs.append(real_scan(a, b, None if h0 is None else h0[d], reverse=(d == 1)))
    return hs


def s5_discretise(lam_re, lam_im, log_dt, b_re, b_im):
    lam_re = lam_re.astype(F32)
    lam_im = lam_im.astype(F32)
    dt = jnp.exp(log_dt.astype(F32))[:, None]
    mag = jnp.exp(lam_re * dt)
    ang = lam_im * dt
    ab_re = mag * jnp.cos(ang)
    ab_im = mag * jnp.sin(ang)
    nr = ab_re - 1.0
    ni = ab_im
    den = lam_re * lam_re + lam_im * lam_im
    cf_re = ((nr * lam_re + ni * lam_im) / den)[..., None]
    cf_im = ((ni * lam_re - nr * lam_im) / den)[..., None]
    b_re = b_re.astype(F32)
    b_im = b_im.astype(F32)
    return ab_re, ab_im, cf_re * b_re - cf_im * b_im, cf_re * b_im + cf_im * b_re


def s5_scan(u, disc, h0, reverse):
    ab_re, ab_im, bb_re, bb_im = disc
    bsz, t, _ = u.shape
    ug = u.astype(F32).reshape(bsz, t, S5_G, S5_H)
    bu_re = jnp.einsum('btgh,gph->btgp', ug, bb_re)
    bu_im = jnp.einsum('btgh,gph->btgp', ug, bb_im)
    if h0 is not None:
        idx = t - 1 if reverse else 0
        h_re, h_im = h0
        bu_re = bu_re.at[:, idx].add(ab_re * h_re - ab_im * h_im)
        bu_im = bu_im.at[:, idx].add(ab_re * h_im + ab_im * h_re)
    a_re = jnp.broadcast_to(ab_re, bu_re.shape)
    a_im = jnp.broadcast_to(ab_im, bu_im.shape)
    _, _, s_re, s_im = lax.associative_scan(complex_linear_op, (a_re, a_im, bu_re, bu_im),
                                            reverse=reverse, axis=1)
    return s_re, s_im


def final_state(states, reverse):
    idx = 0 if reverse else -1
    return states[0][:, idx], states[1][:, idx]


def s5_output(u, states, c_re, c_im, d_skip, w_glu, b_glu):
    uf = u.astype(F32)
    bsz, t, _ = u.shape
    y = d_skip.astype(F32) * uf
    for d in range(2):
        s_re, s_im = states[d]
        r = (jnp.einsum('btgp,ghp->btgh', s_re, c_re[d].astype(F32))
             - jnp.einsum('btgp,ghp->btgh', s_im, c_im[d].astype(F32)))
        y = y + r.reshape(bsz, t, W_S5)
    z = jax.nn.gelu(y)
    return z * jax.nn.sigmoid(z @ w_glu.astype(F32) + b_glu.astype(F32))


def rot_half(x, ang):
    f = ang.shape[-1]
    cos = jnp.cos(ang).astype(x.dtype)[:, None, None, :]
    sin = jnp.sin(ang).astype(x.dtype)[:, None, None, :]
    x1, x2 = x[..., :f], x[..., f:]
    return jnp.concatenate([x1 * cos - x2 * sin, x2 * cos + x1 * sin], axis=-1)


def rope_axial(x, ang_row, ang_col):
    half = x.shape[-1] // 2
    return jnp.concatenate([rot_half(x[..., :half], ang_row), rot_half(x[..., half:], ang_col)], axis=-1)


def diff_softmax_attend(q, k, v, lam):
    s = jnp.einsum('bqhmd,bkhmd->bhmqk', q, k).astype(F32) * (DA_DH ** -0.5)
    p = jax.nn.softmax(s, axis=-1)
    w = p[:, :, 0] - lam * p[:, :, 1]
    return jnp.einsum('bhqk,bkhe->bqhe', w.astype(v.dtype), v)


def latent_diff_attention(q, k_all, v_all, lam):
    bsz, t = q.shape[:2]
    nb = t // Q_BLOCK
    qb = jnp.moveaxis(q.reshape(bsz, nb, Q_BLOCK, DA_HEADS, 2, DA_DH), 1, 0)
    ob = lax.map(lambda qq: diff_softmax_attend(qq, k_all, v_all, lam), qb)
    return jnp.moveaxis(ob, 0, 1).reshape(bsz, t, DA_HEADS, DA_DV)


def combine_branches(h_lru, s5_st, u_s5, o_att, g_lru, g_s5, g_att,
                     c_re, c_im, d_skip, w_glu, b_glu, da_g, lam_init, w_out):
    dtype = g_lru.dtype
    bsz, t, _ = g_lru.shape
    y_a = (h_lru[0] + h_lru[1]) * jax.nn.silu(g_lru.astype(F32))
    y_s = s5_output(u_s5, s5_st, c_re, c_im, d_skip, w_glu, b_glu) * jax.nn.silu(g_s5.astype(F32))
    o = rms_norm(o_att, da_g).astype(F32) * (1.0 - lam_init)
    y_d = o.reshape(bsz, t, W_DA) * jax.nn.silu(g_att.astype(F32))
    y = jnp.concatenate([y_a.astype(dtype), y_s.astype(dtype), y_d.astype(dtype)], axis=-1)
    return y @ w_out


def setup_inputs(seed: int = 0) -> dict:
    key = jax.random.key(seed)
    ks = iter(jax.random.split(key, 32))
    L, D = DEPTH, D_MODEL

    def nrm(shape, scale):
        return scale * jax.random.normal(next(ks), shape, F32)

    x = nrm((BATCH, SEQ, D), 1.0)
    c = nrm((BATCH, D), 1.0)
    ctx = nrm((BATCH, CTX_LEN, D), 1.0)
    c_ctx = nrm((D,), 1.0)
    norm_g = 1.0 + nrm((L, D), 0.02)
    w_mod = nrm((L, D, 3 * D), 0.5 * D ** -0.5)
    b_mod = nrm((L, 3 * D), 0.02)
    w_in = nrm((L, D, IN_W), D ** -0.5)
    w_out = nrm((L, MIX_W, D), MIX_W ** -0.5)
    lru_conv_w = nrm((L, CONV_W, W_LRU), CONV_W ** -0.5)
    lru_conv_b = nrm((L, W_LRU), 0.02)
    lru_wa = nrm((L, 2, LRU_BLOCKS, LRU_BLK, LRU_BLK), LRU_BLK ** -0.5)
    lru_ba = nrm((L, 2, W_LRU), 0.02)
    lru_wx = nrm((L, 2, LRU_BLOCKS, LRU_BLK, LRU_BLK), LRU_BLK ** -0.5)
    lru_bx = nrm((L, 2, W_LRU), 0.02)
    a_pow = jax.random.uniform(next(ks), (L, 2, W_LRU), F32, 0.9, 0.999)
    a0 = a_pow ** (1.0 / LRU_C)
    lru_lam = jnp.log(a0) - jnp.log1p(-a0)
    s5_lam_re = -0.5 + nrm((L, 2, S5_G, S5_P), 0.01)
    s5_lam_im = jnp.pi * jnp.arange(S5_P, dtype=F32) + nrm((L, 2, S5_G, S5_P), 0.01)
    s5_log_dt = jax.random.uniform(next(ks), (L, 2, S5_G), F32, math.log(1e-3), math.log(1e-1))
    s5_b_re = nrm((L, 2, S5_G, S5_P, S5_H), (2 * S5_H) ** -0.5)
    s5_b_im = nrm((L, 2, S5_G, S5_P, S5_H), (2 * S5_H) ** -0.5)
    s5_c_re = nrm((L, 2, S5_G, S5_H, S5_P), 0.5)
    s5_c_im = nrm((L, 2, S5_G, S5_H, S5_P), 0.5)
    s5_d = nrm((L, W_S5), 1.0)
    s5_w_glu = nrm((L, W_S5, W_S5), W_S5 ** -0.5)
    s5_b_glu = nrm((L, W_S5), 0.02)
    da_lam = nrm((L, 2, 2, DA_DH), 0.1)
    da_norm_g = 1.0 + nrm((L, DA_DV), 0.02)
    final_g = 1.0 + nrm((D,), 0.02)
    return {'x': x, 'c': c, 'ctx': ctx, 'c_ctx': c_ctx, 'norm_g': norm_g,
            'w_mod': w_mod, 'b_mod': b_mod, 'w_in': w_in, 'w_out': w_out,
            'lru_conv_w': lru_conv_w, 'lru_conv_b': lru_conv_b, 'lru_wa': lru_wa, 'lru_ba': lru_ba,
            'lru_wx': lru_wx, 'lru_bx': lru_bx, 'lru_lam': lru_lam,
            's5_lam_re': s5_lam_re, 's5_lam_im': s5_lam_im, 's5_log_dt': s5_log_dt,
            's5_b_re': s5_b_re, 's5_b_im': s5_b_im, 's5_c_re': s5_c_re, 's5_c_im': s5_c_im,
            's5_d': s5_d, 's5_w_glu': s5_w_glu, 's5_b_glu': s5_b_glu,
            'da_lam': da_lam, 'da_norm_g': da_norm_g, 'final_g': final_g}


def reference(x, c, ctx, c_ctx, norm_g, w_mod, b_mod, w_in, w_out,
              lru_conv_w, lru_conv_b, lru_wa, lru_ba, lru_wx, lru_bx, lru_lam,
              s5_lam_re, s5_lam_im, s5_log_dt, s5_b_re, s5_b_im, s5_c_re, s5_c_im,
              s5_d, s5_w_glu, s5_b_glu, da_lam, da_norm_g, final_g):
    bsz, t, _ = x.shape
    tc = ctx.shape[1]
    rows = t // GRID_W
    row_pos = jnp.broadcast_to(jnp.arange(rows, dtype=F32)[:, None], (rows, GRID_W)).reshape(t)
    col_pos = jnp.broadcast_to(jnp.arange(GRID_W, dtype=F32)[None, :], (rows, GRID_W)).reshape(t)
    inv_freq = jnp.exp(-math.log(ROPE_BASE) * jnp.arange(ROPE_F, dtype=F32) / ROPE_F)
    ang_row = row_pos[:, None] * inv_freq[None, :]
    ang_col = col_pos[:, None] * inv_freq[None, :]

    silu_c = jax.nn.silu(c)
    silu_cc = jax.nn.silu(c_ctx)
    h = x
    hc = ctx
    for l in range(DEPTH):
        last = l == DEPTH - 1
        shift, scale, gate = jnp.split(silu_c @ w_mod[l] + b_mod[l], 3, axis=-1)
        shift_c, scale_c, gate_c = jnp.split(silu_cc @ w_mod[l] + b_mod[l], 3, axis=-1)
        n_lat = rms_norm(h, norm_g[l]) * (1.0 + scale[:, None]) + shift[:, None]
        n_ctx = rms_norm(hc, norm_g[l]) * (1.0 + scale_c) + shift_c
        p_lat = n_lat @ w_in[l]
        p_ctx = n_ctx @ (w_in[l][:, :SIDE_W] if last else w_in[l])
        ua_l, us_l, k_l, v_l = jnp.split(p_lat[..., :SIDE_W], SPLIT4, axis=-1)
        ga_l, gs_l, q_l, gd_l = jnp.split(p_lat[..., SIDE_W:], SPLIT4, axis=-1)
        ua_c, us_c, k_c, v_c = jnp.split(p_ctx[..., :SIDE_W], SPLIT4, axis=-1)

        lru_p = (lru_conv_w[l], lru_conv_b[l], lru_wa[l], lru_ba[l], lru_wx[l], lru_bx[l], lru_lam[l])
        h_lru_c = rglru_scans(ua_c, *lru_p, None)
        h_lru_l = rglru_scans(ua_l, *lru_p, (h_lru_c[0][:, -1], h_lru_c[1][:, 0]))

        disc = [s5_discretise(s5_lam_re[l, d], s5_lam_im[l, d], s5_log_dt[l, d],
                              s5_b_re[l, d], s5_b_im[l, d]) for d in range(2)]
        st_c = [s5_scan(us_c, disc[d], None, d == 1) for d in range(2)]
        st_l = [s5_scan(us_l, disc[d], final_state(st_c[d], d == 1), d == 1) for d in range(2)]

        lam_init = 0.8 - 0.6 * math.exp(-0.3 * l)
        lam = (jnp.exp(jnp.sum(da_lam[l, 0, 0].astype(F32) * da_lam[l, 0, 1].astype(F32)))
               - jnp.exp(jnp.sum(da_lam[l, 1, 0].astype(F32) * da_lam[l, 1, 1].astype(F32))) + lam_init)
        q_l4 = rope_axial(q_l.reshape(bsz, t, DA_HEADS, 2, DA_DH), ang_row, ang_col)
        k_l4 = rope_axial(k_l.reshape(bsz, t, DA_HEADS, 2, DA_DH), ang_row, ang_col)
        k_c4 = k_c.reshape(bsz, tc, DA_HEADS, 2, DA_DH)
        v_c4 = v_c.reshape(bsz, tc, DA_HEADS, DA_DV)
        k_all = jnp.concatenate([k_c4, k_l4], axis=1)
        v_all = jnp.concatenate([v_c4, v_l.reshape(bsz, t, DA_HEADS, DA_DV)], axis=1)
        o_l = latent_diff_attention(q_l4, k_all, v_all, lam)

        s5p = (s5_c_re[l], s5_c_im[l], s5_d[l], s5_w_glu[l], s5_b_glu[l])
        y_l = combine_branches(h_lru_l, st_l, us_l, o_l, ga_l, gs_l, gd_l, *s5p,
                               da_norm_g[l], lam_init, w_out[l])
        h_next = h + gate[:, None] * y_l
        if not last:
            ga_c, gs_c, q_c, gd_c = jnp.split(p_ctx[..., SIDE_W:], SPLIT4, axis=-1)
            o_c = diff_softmax_attend(q_c.reshape(bsz, tc, DA_HEADS, 2, DA_DH), k_c4, v_c4, lam)
            y_c = combine_branches(h_lru_c, st_c, us_c, o_c, ga_c, gs_c, gd_c, *s5p,
                                   da_norm_g[l], lam_init, w_out[l])
            hc = hc + gate_c * y_c
        h = h_next
    return rms_norm(h, final_g)
```

```python
import contextlib
import numpy as np
import concourse.bass as bass
import concourse.mybir as mybir
from concourse.bass_utils import run_bass_kernel_spmd

F32 = mybir.dt.float32
BF16 = mybir.dt.bfloat16
ALU = mybir.AluOpType
AF = mybir.ActivationFunctionType
AX = mybir.AxisListType


class _Eng:
    def __init__(self, name, handle, sem, inc):
        self.name = name
        self.h = handle
        self.sem = sem
        self.inc = inc
        self.count = 0
        self.seen = {}


class _Reg:
    __slots__ = ("w", "r")

    def __init__(self):
        self.w = None
        self.r = {}


class K:
    def __init__(self, nc, stack, n_dma=10):
        self.nc = nc
        self.stack = stack
        self.regs = {}
        mk = lambda n: stack.enter_context(nc.semaphore(n))
        self.pe = _Eng("pe", nc.tensor, mk("s_pe"), 1)
        self.act = _Eng("act", nc.scalar, mk("s_act"), 1)
        self.dve = _Eng("dve", nc.vector, mk("s_dve"), 1)
        self.pool = _Eng("pool", nc.gpsimd, mk("s_pool"), 1)
        self.compute = [self.pe, self.act, self.dve, self.pool]
        self.dq = {}
        for qn, qh in (("sync", nc.sync), ("gpsimd", nc.gpsimd), ("scalar", nc.scalar)):
            self.dq[qn] = [
                _Eng(f"d_{qn}{i}", qh, mk(f"s_d{qn}{i}"), 16) for i in range(n_dma)
            ]
        self.dq_rr = {qn: 0 for qn in self.dq}
        self.qseen = {"sync": {}, "gpsimd": self.pool.seen, "scalar": self.act.seen}

    def reg(self, key):
        r = self.regs.get(key)
        if r is None:
            r = self.regs[key] = _Reg()
        return r

    def _deps(self, reads, writes):
        deps = {}
        for k in reads:
            r = self.reg(k)
            if r.w is not None:
                e, c = r.w
                deps[e] = max(deps.get(e, 0), c)
        for k in writes:
            r = self.reg(k)
            if r.w is not None:
                e, c = r.w
                deps[e] = max(deps.get(e, 0), c)
            for e, c in r.r.items():
                deps[e] = max(deps.get(e, 0), c)
        return deps

    def _commit(self, eng, reads, writes):
        c = eng.count
        for k in reads:
            self.reg(k).r[eng] = c
        for k in writes:
            r = self.reg(k)
            r.w = (eng, c)
            r.r = {}

    def op(self, eng, fn, reads=(), writes=()):
        deps = self._deps(reads, writes)
        for e, c in deps.items():
            if e is eng and eng is self.pe:
                continue
            if eng.seen.get(e, 0) < c:
                eng.h.wait_ge(e.sem, c * e.inc)
                eng.seen[e] = c
        ins = fn()
        eng.count += 1
        ins.then_inc(eng.sem, eng.inc)
        self._commit(eng, reads, writes)
        return ins

    def dma(self, q, out, in_, reads=(), writes=(), **kw):
        lst = self.dq[q]
        i = self.dq_rr[q]
        self.dq_rr[q] = (i + 1) % len(lst)
        d = lst[i]
        seen = self.qseen[q]
        if d.count > 0 and seen.get(d, 0) < d.count:
            d.h.wait_ge(d.sem, d.count * 16)
            seen[d] = d.count
        deps = self._deps(reads, writes)
        for e, c in deps.items():
            if seen.get(e, 0) < c:
                d.h.wait_ge(e.sem, c * e.inc)
                seen[e] = c
        ins = d.h.dma_start(out=out, in_=in_, **kw)
        d.count += 1
        ins.then_inc(d.sem, 16)
        self._commit(d, reads, writes)
        return ins

    def finish(self):
        seen = self.qseen["sync"]
        allengs = list(self.compute)
        for lst in self.dq.values():
            allengs += lst
        for e in allengs:
            if e.count > 0 and seen.get(e, 0) < e.count:
                self.nc.sync.wait_ge(e.sem, e.count * e.inc)
                seen[e] = e.count

    def barrier(self):
        allengs = list(self.compute)
        for lst in self.dq.values():
            allengs += lst
        for h, seen in ((self.nc.tensor, self.pe.seen), (self.nc.scalar, self.act.seen),
                        (self.nc.vector, self.dve.seen), (self.nc.gpsimd, self.pool.seen),
                        (self.nc.sync, self.qseen["sync"])):
            for e in allengs:
                if e.count > 0 and seen.get(e, 0) < e.count:
                    h.wait_ge(e.sem, e.count * e.inc)
                    seen[e] = e.count
        self.regs.clear()


D = 1024
T = 4096
TC = 256
TT = T + TC
NTT = TT // 128
DEPTH = 2
EPS = 1e-6
NCH = TT // 16
WCOL = {"ua": (0, 256), "k": (256, 512), "ga": (768, 256), "gs": (1024, 256), "q": (1280, 512),
        "v": (1792, 512), "gd": (2304, 512), "us": (2816, 256)}
TWO_PI = 2.0 * np.pi


class Builder:
    def __init__(self, debug=False, layers=(0, 1), phases=("M", "P", "L", "S", "A")):
        self.debug = debug
        self.layers = layers
        self.phases = phases
        self.nc = bass.Bass("TRN2", target_bir_lowering=False)
        self.ins = {}
        self.scr = {}

    def din(self, name, shape, dt=F32):
        ap = self.nc.dram_tensor(name, list(shape), dt, kind="ExternalInput").ap()
        self.ins[name] = ap
        return ap

    def dscr(self, name, shape, dt=F32):
        kind = "ExternalOutput" if self.debug else "Internal"
        ap = self.nc.dram_tensor(name, list(shape), dt, kind=kind).ap()
        self.scr[name] = ap
        return ap

    def _uname(self, name):
        self._uid = getattr(self, "_uid", 0) + 1
        return f"{name}_u{self._uid}"

    def sb(self, st, name, shape, dt=F32):
        return st.enter_context(self.nc.sbuf_tensor(self._uname(name), list(shape), dt))

    def ps(self, st, name, shape, dt=F32):
        return st.enter_context(self.nc.psum_tensor(self._uname(name), list(shape), dt))

    def declare(self):
        d = self.din
        self.hin = d("hin", [TT, D])
        self.cvec = d("cvec", [128, 8, 2])
        self.ident = d("ident", [128, 128])
        self.perm = d("perm", [128, 128])
        self.sel2 = d("sel2", [2, 2, 128])
        self.ropeC = d("ropeC", [128, T])
        self.ropeS = d("ropeS", [128, T])
        self.finalg = d("finalg", [1, D])
        self.s5idx = d("s5idx", [128, 16, NCH + 1])
        self.s5exp = d("s5exp", [128, 3, 16, 16])
        self.s5mask = d("s5mask", [128, 2, 2, 256])
        self.L = []
        for l in range(DEPTH):
            p = {}
            p["w_mod"] = d(f"w_mod{l}", [D, 3 * D])
            p["bmodR"] = d(f"bmodR{l}", [2, 3 * D])
            p["normg"] = d(f"normg{l}", [128, 8, 2])
            p["w_in"] = d(f"w_in{l}", [D, 3 * D])
            p["w_out"] = d(f"w_out{l}", [D, D])
            p["convw"] = d(f"convw{l}", [128, 2, 4])
            p["convb"] = d(f"convb{l}", [128, 2])
            p["wax"] = d(f"wax{l}", [128, 2, 2, 2, 128])
            p["bax"] = d(f"bax{l}", [128, 2, 2, 2])
            p["lrulam"] = d(f"lrulam{l}", [128, 2, 2])
            p["s5lam"] = d(f"s5lam{l}", [128, 3, 16])
            p["s5b"] = d(f"s5b{l}", [128, 2, 16, 16])
            p["s5c"] = d(f"s5c{l}", [128, 2, 16, 16])
            p["s5d"] = d(f"s5d{l}", [1, 256])
            p["wglu"] = d(f"wglu{l}", [256, 256])
            p["bglu"] = d(f"bglu{l}", [128, 2])
            p["dalam"] = d(f"dalam{l}", [1, 256])
            p["dag"] = d(f"dag{l}", [128, 1])
            self.L.append(p)
        s = self.dscr
        self.uaT = s("uaT", [256, TT])
        self.gaT = s("gaT", [256, TT])
        self.gsT = s("gsT", [256, TT])
        self.kT = s("kT", [512, TT], BF16)
        self.qT = s("qT", [512, TT], BF16)
        self.v_s = s("v_s", [TT, 512], BF16)
        self.gdT = s("gdT", [512, TT])
        self.us_s = s("us_s", [TT, 256])
        self.yas = s("yas", [512, TT], BF16)
        self.hbuf = s("hbuf", [TT, D])
        self.zs = s("zs", [TT, 256])
        self.out = self.nc.dram_tensor("out", [T, D], F32, kind="ExternalOutput").ap()
        if self.debug:
            self.dbg_mod = s("dbg_mod", [128, 24, 2])
            self.dbg_gate = s("dbg_gate", [128, 2, D])
            self.dbg_hl = s("dbg_hl", [2, 2, 128, TT])

    def build(self):
        nc = self.nc
        self.declare()
        with contextlib.ExitStack() as st:
            self.k = K(nc, st)
            k = self.k
            g = lambda name, shape, dt=F32: self.sb(st, name, shape, dt)
            self.identf = g("identf", [128, 128])
            self.identb = g("identb", [128, 128], BF16)
            self.permf = g("permf", [128, 128])
            self.ones1 = g("ones1", [1, 128])
            self.sc = g("sc", [128, 8, 2])
            self.modv = g("modv", [128, 24, 2])
            self.gmul = g("gmul", [128, 8, 2])
            self.gateb = g("gateb", [128, 2, D])
            self.nlam = g("nlam", [128, 1])
            k.dma("sync", self.identf[:], self.ident, writes=["identf"])
            k.dma("sync", self.permf[:], self.perm, writes=["permf"])
            k.dma("sync", self.sc[:], self.cvec, writes=["sc"])
            k.op(k.dve, lambda: nc.vector.tensor_copy(self.identb[:], self.identf[:]), reads=["identf"], writes=["identb"])
            k.op(k.dve, lambda: nc.vector.memset(self.ones1[:], 1.0), writes=["ones1"])
            k.op(k.act, lambda: nc.scalar.activation(self.sc[:], self.sc[:], AF.Silu), reads=["sc"], writes=["sc"])
            k.barrier()
            for l in self.layers:
                last = (l == DEPTH - 1)
                hsrc = self.hin if l == 0 else self.hbuf
                if "M" in self.phases:
                    self.phase_mod(l)
                    k.barrier()
                if "P" in self.phases:
                    self.phase_proj(l, hsrc)
                    k.barrier()
                if "L" in self.phases:
                    self.phase_lru(l)
                    k.barrier()
                if "S" in self.phases:
                    self.phase_s5(l)
                    k.barrier()
                with contextlib.ExitStack() as ast:
                    pre = None
                    if "A" in self.phases:
                        pre = self.att_prefetch(l, ast)
                    if "S" in self.phases and getattr(self, "s5_stop", 9) > 4:
                        self.phase_s5b(l)
                        k.barrier()
                    if "A" in self.phases:
                        self.phase_att(l, hsrc, last, pre)
                        k.barrier()
            k.finish()
        return nc

    def phase_mod(self, l):
        nc, k, p = self.nc, self.k, self.L[l]
        with contextlib.ExitStack() as ph:
            wb = [self.sb(ph, f"m_wb{i}", [128, 8, 512]) for i in range(3)]
            bmr = self.sb(ph, "m_bmr", [2, 3 * D])
            ng = self.sb(ph, "m_ng", [128, 8, 2])
            sel = self.sb(ph, "m_sel", [2, 2, 128])
            rowv = self.sb(ph, "m_rowv", [2, 3 * D])
            psr = [self.ps(ph, f"m_psr{i}", [128, 512]) for i in range(2)]
            psT = self.ps(ph, "m_psT", [128, 512])
            psb = [self.ps(ph, f"m_psb{i}", [128, 512]) for i in range(2)]
            k.dma("scalar", bmr[:], p["bmodR"], writes=["bmr"])
            k.dma("scalar", ng[:], p["normg"], writes=["ng"])
            k.dma("scalar", sel[:], self.sel2, writes=["sel"])
            wsrc = p["w_mod"].rearrange("(k p) n -> p k n", p=128)
            for cb in range(6):
                w = wb[cb % 3]
                wk = f"wb{cb % 3}"
                for k4 in range(2):
                    k.dma("sync" if k4 == 0 else "scalar", w[:, k4 * 4:(k4 + 1) * 4, :],
                          wsrc[:, k4 * 4:(k4 + 1) * 4, cb * 512:(cb + 1) * 512], writes=[wk + f"_{k4}"])
                pr_ = psr[cb % 2]
                for k8 in range(8):
                    k.op(k.pe, lambda: nc.tensor.matmul(pr_[0:2, :], self.sc[:, k8, :], w[:, k8, :], start=(k8 == 0), stop=(k8 == 7)),
                         reads=[wk + f"_{k8 // 4}", "sc"], writes=[f"psr{cb % 2}"])
                cs = slice(cb * 512, (cb + 1) * 512)
                k.op(k.dve, lambda: nc.vector.tensor_tensor(rowv[:, cs], pr_[0:2, :], bmr[:, cs], ALU.add),
                     reads=[f"psr{cb % 2}", "bmr"], writes=[f"rowv{cb}"])
                for j in range(4):
                    ft = cb * 4 + j
                    k.op(k.pe, lambda: nc.tensor.transpose(psT[:, ft * 2:(ft + 1) * 2], rowv[0:2, ft * 128:(ft + 1) * 128], self.identf[0:2, 0:2]),
                         reads=[f"rowv{cb}", "identf"], writes=["psT"])
            k.op(k.dve, lambda: nc.vector.tensor_copy(self.modv[:], psT[:, 0:48].rearrange("p (a b) -> p a b", b=2)),
                 reads=["psT"], writes=["modv"])
            k.op(k.dve, lambda: nc.vector.scalar_tensor_tensor(self.gmul[:], self.modv[:, 8:16, :], 1.0, ng[:], ALU.add, ALU.mult),
                 reads=["modv", "ng"], writes=["gmul"])
            for v in range(2):
                for hf in range(2):
                    pb = psb[hf]
                    k.op(k.pe, lambda: nc.tensor.matmul(pb[:], sel[0:2, v, :], rowv[0:2, 2 * D + hf * 512:2 * D + (hf + 1) * 512], start=True, stop=True),
                         reads=["sel", f"rowv{4 + hf}"], writes=[f"psb{hf}"])
                    k.op(k.dve, lambda: nc.vector.tensor_copy(self.gateb[:, v, hf * 512:(hf + 1) * 512], pb[:]),
                         reads=[f"psb{hf}"], writes=[f"gateb{v}{hf}"])
            if self.debug:
                k.dma("gpsimd", self.dbg_mod, self.modv[:], reads=["modv"])
                k.dma("gpsimd", self.dbg_gate, self.gateb[:], reads=["gateb00", "gateb01", "gateb10", "gateb11"])

    def phase_proj(self, l, hsrc):
        nc, k, p = self.nc, self.k, self.L[l]
        with contextlib.ExitStack() as ph:
            wbf = self.sb(ph, "p_wbf", [128, 8, 3 * D], BF16)
            stg = [self.sb(ph, f"p_stg{i}", [128, 8, 512]) for i in range(2)]
            wsrc = p["w_in"].rearrange("(k p) n -> p k n", p=128)
            for cb in range(6):
                s_ = stg[cb % 2]
                k.dma("sync", s_[:], wsrc[:, :, cb * 512:(cb + 1) * 512], writes=[f"stg{cb % 2}"])
                eng = k.dve if cb % 2 == 0 else k.pool
                k.op(eng, lambda s_=s_, cb=cb, eng=eng: eng.h.tensor_copy(wbf[:, :, cb * 512:(cb + 1) * 512], s_[:]),
                     reads=[f"stg{cb % 2}"], writes=[f"wbf{cb}"])
            wkeys = [f"wbf{cb}" for cb in range(6)]
            xb = [self.sb(ph, f"p_x{i}", [128, D]) for i in range(4)]
            junk = self.sb(ph, "p_junk", [128, D], BF16)
            xh = [self.sb(ph, f"p_xh{i}", [128, D], BF16) for i in range(4)]
            ss = self.sb(ph, "p_ss", [128, 4])
            nT = [self.sb(ph, f"p_nT{i}", [128, 8, 512], BF16) for i in range(2)]
            cosb = [self.sb(ph, f"p_cos{i}", [128, 512]) for i in range(2)]
            sinb = [self.sb(ph, f"p_sin{i}", [128, 512]) for i in range(2)]
            kf = [self.sb(ph, f"p_kf{i}", [128, 512]) for i in range(2)]
            t1 = [self.sb(ph, f"p_t1{i}", [128, 512]) for i in range(2)]
            t2 = [self.sb(ph, f"p_t2{i}", [128, 512]) for i in range(2)]
            of32 = [self.sb(ph, f"p_of{i}", [128, 512]) for i in range(3)]
            obf = [self.sb(ph, f"p_ob{i}", [128, 512], BF16) for i in range(3)]
            pT = [self.ps(ph, f"p_pT{i}", [128, 1024], BF16) for i in range(2)]
            pm = [self.ps(ph, f"p_pm{i}", [128, 512]) for i in range(4)]
            pr = [self.ps(ph, f"p_pr{i}", [128, 512]) for i in range(2)]
            blocks = [(0, 256)] + [(256 + 512 * i, 512) for i in range(8)]
            cnt = {"x": 0, "pm": 0, "pr": 0, "of": 0, "ob": 0, "kf": 0, "ev": 0}

            def rot(name, n):
                i = cnt[name] % n
                cnt[name] += 1
                return i

            def stage1a(bi):
                t0, nt = blocks[bi]
                lat = t0 >= TC
                if lat:
                    cb_, sb_ = cosb[bi % 2], sinb[bi % 2]
                    k.dma("sync", cb_[:], self.ropeC[:, t0 - TC:t0 - TC + 512], writes=[f"cos{bi % 2}"])
                    k.dma("sync", sb_[:], self.ropeS[:, t0 - TC:t0 - TC + 512], writes=[f"sin{bi % 2}"])
                for j in range(nt // 128):
                    xi = (bi * 4 + j) % 4
                    x_, xh_ = xb[xi], xh[xi]
                    k.dma("sync", x_[:], hsrc[t0 + j * 128:t0 + (j + 1) * 128, :], writes=[f"x{xi}"])
                    k.op(k.act, lambda: nc.scalar.activation(junk[:], x_[:], AF.Square, accum_out=ss[:, xi:xi + 1]),
                         reads=[f"x{xi}"], writes=["junk", f"ss{xi}"])
                    k.op(k.act, lambda: nc.scalar.activation(ss[:, xi:xi + 1], ss[:, xi:xi + 1], AF.Sqrt, scale=1.0 / D, bias=EPS),
                         reads=[f"ss{xi}"], writes=[f"ss{xi}"])
                    k.op(k.dve, lambda: nc.vector.reciprocal(ss[:, xi:xi + 1], ss[:, xi:xi + 1]),
                         reads=[f"ss{xi}"], writes=[f"ss{xi}"])
                    k.op(k.dve, lambda: nc.vector.tensor_scalar_mul(xh_[:], x_[:], ss[:, xi:xi + 1]),
                         reads=[f"x{xi}", f"ss{xi}"], writes=[f"xh{xi}"])

            def stage1b(bi):
                t0, nt = blocks[bi]
                v = 1 if t0 < TC else 0
                nTb = nT[bi % 2]
                nk = f"nT{bi % 2}"
                for j in range(nt // 128):
                    xi = (bi * 4 + j) % 4
                    xh_ = xh[xi]
                    pi_ = rot("x", 2)
                    pT_ = pT[pi_]
                    for k8 in range(8):
                        k.op(k.pe, lambda: nc.tensor.transpose(pT_[:, k8 * 128:(k8 + 1) * 128], xh_[:, k8 * 128:(k8 + 1) * 128], self.identb[:]),
                             reads=[f"xh{xi}", "identb"], writes=[f"pT{pi_}"])
                    for k8 in range(8):
                        if k8 % 2 == 0:
                            k.op(k.dve, lambda: nc.vector.tensor_scalar(
                                nTb[:, k8, j * 128:(j + 1) * 128], pT_[:, k8 * 128:(k8 + 1) * 128],
                                self.gmul[:, k8, v:v + 1], self.modv[:, k8, v:v + 1], ALU.mult, ALU.add),
                                reads=[f"pT{pi_}", "gmul", "modv"], writes=[nk])
                        else:
                            k.op(k.act, lambda: nc.scalar.activation(
                                nTb[:, k8, j * 128:(j + 1) * 128], pT_[:, k8 * 128:(k8 + 1) * 128], AF.Identity,
                                scale=self.gmul[:, k8, v:v + 1], bias=self.modv[:, k8, v:v + 1]),
                                reads=[f"pT{pi_}", "gmul", "modv"], writes=[nk])

            def stage2(bi, part):
                t0, nt = blocks[bi]
                ntile = nt // 128
                nTb = nT[bi % 2]
                nk = f"nT{bi % 2}"
                lat = t0 >= TC
                cb_, sb_ = cosb[bi % 2], sinb[bi % 2]
                pending = []
                for name, dst in ((("ua", self.uaT), ("ga", self.gaT), ("gs", self.gsT), ("gd", self.gdT), ("k", self.kT), ("q", self.qT)) if part == 0 else []):
                    co, w = WCOL[name]
                    for mt in range(w // 128):
                        pi = rot("pm", 4)
                        pm_ = pm[pi]
                        for k8 in range(8):
                            k.op(k.pe, lambda k8=k8, pm_=pm_, co=co, mt=mt: nc.tensor.matmul(
                                pm_[:, :nt], wbf[:, k8, co + mt * 128:co + (mt + 1) * 128], nTb[:, k8, :nt],
                                start=(k8 == 0), stop=(k8 == 7)), reads=wkeys + [nk], writes=[f"pm{pi}"])
                        while pending:
                            pending.pop(0)()
                        rows = slice(mt * 128, (mt + 1) * 128)
                        if name in ("ua", "ga", "gs", "gd"):
                            oi = rot("of", 3)
                            o_ = of32[oi]
                            k.op(k.act, lambda o_=o_, pm_=pm_: nc.scalar.copy(o_[:, :nt], pm_[:, :nt]),
                                 reads=[f"pm{pi}"], writes=[f"of{oi}"])
                            k.dma("gpsimd", dst[rows, t0:t0 + nt], o_[:, :nt], reads=[f"of{oi}"])
                        elif not lat:
                            oi = rot("ob", 3)
                            o_ = obf[oi]
                            k.op(k.act, lambda o_=o_, pm_=pm_: nc.scalar.copy(o_[:, :nt], pm_[:, :nt]),
                                 reads=[f"pm{pi}"], writes=[f"ob{oi}"])
                            k.dma("gpsimd", dst[rows, t0:t0 + nt], o_[:, :nt], reads=[f"ob{oi}"])
                        else:
                            ki = rot("kf", 2)
                            kf_, t1_, t2_, pr_ = kf[ki], t1[ki], t2[ki], pr[ki]
                            k.op(k.act, lambda kf_=kf_, pm_=pm_: nc.scalar.copy(kf_[:], pm_[:]),
                                 reads=[f"pm{pi}"], writes=[f"kf{ki}"])
                            k.op(k.pool, lambda kf_=kf_, t1_=t1_: nc.gpsimd.tensor_tensor(t1_[:], kf_[:], cb_[:], ALU.mult),
                                 reads=[f"kf{ki}", f"cos{bi % 2}"], writes=[f"t1{ki}"])

                            def part_b(kf_=kf_, t1_=t1_, t2_=t2_, pr_=pr_, ki=ki, dst=dst, rows=rows):
                                k.op(k.pe, lambda: nc.tensor.matmul(pr_[:], self.permf[:], kf_[:], start=True, stop=True),
                                     reads=[f"kf{ki}", "permf"], writes=[f"pr{ki}"])
                                k.op(k.dve, lambda: nc.vector.tensor_tensor(t2_[:], pr_[:], sb_[:], ALU.mult),
                                     reads=[f"pr{ki}", f"sin{bi % 2}"], writes=[f"t2{ki}"])
                                oi = rot("ob", 3)
                                o_ = obf[oi]
                                k.op(k.dve, lambda: nc.vector.tensor_tensor(o_[:], t1_[:], t2_[:], ALU.add),
                                     reads=[f"t1{ki}", f"t2{ki}"], writes=[f"ob{oi}"])
                                k.dma("gpsimd", dst[rows, t0:t0 + nt], o_[:, :nt], reads=[f"ob{oi}"])
                            pending.append(part_b)
                while pending:
                    pending.pop(0)()
                for j in (range(ntile) if part == 1 else []):
                    r0 = t0 + j * 128
                    for name, dst in (("v", self.v_s), ("us", self.us_s)):
                        co, w = WCOL[name]
                        pi = rot("pm", 4)
                        pm_ = pm[pi]
                        for k8 in range(8):
                            k.op(k.pe, lambda k8=k8, pm_=pm_, co=co, w=w, j=j: nc.tensor.matmul(
                                pm_[:, :w], nTb[:, k8, j * 128:(j + 1) * 128], wbf[:, k8, co:co + w],
                                start=(k8 == 0), stop=(k8 == 7)), reads=wkeys + [nk], writes=[f"pm{pi}"])
                        while pending:
                            pending.pop(0)()
                        ev = rot("ev", 2)
                        eng = k.act if ev == 0 else k.dve
                        if name == "v":
                            oi = rot("ob", 3)
                            o_ = obf[oi]
                            key = f"ob{oi}"
                        else:
                            oi = rot("of", 3)
                            o_ = of32[oi]
                            key = f"of{oi}"
                        if eng is k.act:
                            k.op(eng, lambda o_=o_, pm_=pm_, w=w: nc.scalar.copy(o_[:, :w], pm_[:, :w]), reads=[f"pm{pi}"], writes=[key])
                        else:
                            k.op(eng, lambda o_=o_, pm_=pm_, w=w: nc.vector.tensor_copy(o_[:, :w], pm_[:, :w]), reads=[f"pm{pi}"], writes=[key])
                        k.dma("gpsimd", dst[r0:r0 + 128, :], o_[:, :w], reads=[key])

            stage1a(0)
            stage1b(0)
            for bi in range(len(blocks)):
                if bi + 1 < len(blocks):
                    stage1a(bi + 1)
                stage2(bi, 0)
                if bi + 1 < len(blocks):
                    stage1b(bi + 1)
                stage2(bi, 1)


def _fp(a):
    return np.ascontiguousarray(a, dtype=np.float32)


def _const_tables():
    c = {}
    c["ident"] = np.eye(128, dtype=np.float32)
    r = np.arange(128)
    partner = np.where((r % 32) < 16, r + 16, r - 16)
    perm = np.zeros((128, 128), np.float32)
    perm[partner, r] = 1.0
    c["perm"] = perm
    sel2 = np.zeros((2, 2, 128), np.float32)
    sel2[0, 0, :] = 1.0
    sel2[1, 1, :] = 1.0
    c["sel2"] = sel2
    f = np.arange(16, dtype=np.float32)
    inv_freq = np.exp(np.float32(-np.log(10000.0)) * f / np.float32(16)).astype(np.float32)
    t = np.arange(T)
    row_pos = (t // 64).astype(np.float32)
    col_pos = (t % 64).astype(np.float32)
    d64 = r % 64
    is_row = d64 < 32
    fidx = (d64 % 32) % 16
    ang = np.where(is_row[:, None], row_pos[None, :], col_pos[None, :]).astype(np.float32) * inv_freq[fidx][:, None]
    ang = ang.astype(np.float32)
    sign = np.where((d64 % 32) < 16, -1.0, 1.0).astype(np.float32)
    c["ropeC"] = np.cos(ang).astype(np.float32)
    c["ropeS"] = (np.sin(ang).astype(np.float32) * sign[:, None]).astype(np.float32)
    m = np.arange(NCH + 1, dtype=np.float32)
    idx = np.zeros((128, 16, NCH + 1), np.float32)
    idx[:, 0:8, :] = m[None, None, :]
    idx[:, 8:16, :] = -m[None, None, :]
    c["s5idx"] = idx
    e = np.zeros((128, 3, 16, 16), np.float32)
    j = np.arange(16, dtype=np.float32)
    e[:, 0, 0:8, :] = 15 - j
    e[:, 0, 8:16, :] = j
    e[:, 1, 0:8, :] = j + 1
    e[:, 1, 8:16, :] = 16 - j
    e[:, 2, 0:8, :] = j - 15
    e[:, 2, 8:16, :] = -j
    c["s5exp"] = e
    pidx = np.arange(128)
    i8 = pidx // 16
    jj = np.repeat(np.arange(16), 16)
    msk = np.zeros((128, 2, 2, 256), np.float32)
    for kt in range(2):
        i = (8 * kt + i8)[:, None]
        msk[:, 0, kt, :] = (jj[None, :] >= i)
        msk[:, 1, kt, :] = (i >= jj[None, :])
    c["s5mask"] = msk
    return c


_PERM_COLS = None


def _wcol_perm():
    o = {"ua": (0, 256), "us": (256, 256), "k": (512, 512), "v": (1024, 512),
         "ga": (1536, 256), "gs": (1792, 256), "q": (2048, 512), "gd": (2560, 512)}
    idx = np.zeros(3 * D, np.int64)
    for name, (off, w) in WCOL.items():
        so, sw = o[name]
        idx[off:off + w] = np.arange(so, so + sw)
    return idx


def _fpart(vec, ntile):
    return _fp(np.asarray(vec).reshape(ntile, 128).T)


def _layer_inputs(inp, l):
    o = {}
    o[f"w_mod{l}"] = _fp(inp["w_mod"][l])
    o[f"bmodR{l}"] = _fp(np.repeat(inp["b_mod"][l][None, :], 2, axis=0))
    ng = _fpart(inp["norm_g"][l], 8)
    o[f"normg{l}"] = _fp(np.repeat(ng[:, :, None], 2, axis=2))
    o[f"w_in{l}"] = _fp(inp["w_in"][l][:, _wcol_perm()])
    o[f"w_out{l}"] = _fp(inp["w_out"][l])
    cw = inp["lru_conv_w"][l]
    o[f"convw{l}"] = _fp(cw.T.reshape(2, 128, 4).transpose(1, 0, 2))
    o[f"convb{l}"] = _fpart(inp["lru_conv_b"][l], 2)
    wax = np.zeros((128, 2, 2, 2, 128), np.float32)
    for ai, nm in enumerate(("lru_wa", "lru_wx")):
        w = inp[nm][l]
        for d in range(2):
            for ct in range(2):
                for h in range(2):
                    wax[h * 64:(h + 1) * 64, ct, ai, d, h * 64:(h + 1) * 64] = w[d, 2 * ct + h]
    o[f"wax{l}"] = wax
    bax = np.zeros((128, 2, 2, 2), np.float32)
    for ai, nm in enumerate(("lru_ba", "lru_bx")):
        for d in range(2):
            bax[:, :, ai, d] = _fpart(inp[nm][l][d], 2)
    o[f"bax{l}"] = bax
    ll = np.zeros((128, 2, 2), np.float32)
    for d in range(2):
        ll[:, :, d] = _fpart(inp["lru_lam"][l][d], 2)
    o[f"lrulam{l}"] = ll

    def tp_layout(a):
        a = np.asarray(a)
        rest = a.shape[3:]
        a = a.reshape((2, 8, 2, 64) + rest)
        a = np.moveaxis(a, (2, 3), (0, 1))
        return a.reshape((128, 16) + rest)

    s5lam = np.zeros((128, 3, 16), np.float32)
    s5lam[:, 0] = tp_layout(inp["s5_lam_re"][l])
    s5lam[:, 1] = tp_layout(inp["s5_lam_im"][l])
    s5lam[:, 2] = tp_layout(np.repeat(inp["s5_log_dt"][l][:, :, None], 64, axis=2))
    o[f"s5lam{l}"] = s5lam
    sb_ = np.zeros((128, 2, 16, 16), np.float32)
    sb_[:, 0] = tp_layout(inp["s5_b_re"][l])
    sb_[:, 1] = tp_layout(inp["s5_b_im"][l])
    o[f"s5b{l}"] = sb_
    sc_ = np.zeros((128, 2, 16, 16), np.float32)
    sc_[:, 0] = tp_layout(np.swapaxes(inp["s5_c_re"][l], 2, 3))
    sc_[:, 1] = tp_layout(np.swapaxes(inp["s5_c_im"][l], 2, 3))
    o[f"s5c{l}"] = sc_
    o[f"s5d{l}"] = _fp(inp["s5_d"][l][None, :])
    o[f"wglu{l}"] = _fp(inp["s5_w_glu"][l])
    o[f"bglu{l}"] = _fpart(inp["s5_b_glu"][l], 2)
    o[f"dalam{l}"] = _fp(inp["da_lam"][l].reshape(1, 256))
    o[f"dag{l}"] = _fp(inp["da_norm_g"][l][:, None])
    return o


def prep_inputs(inp):
    shared = dict(_const_tables())
    shared["finalg"] = _fp(inp["final_g"][None, :])
    for l in range(DEPTH):
        shared.update(_layer_inputs(inp, l))
    maps = []
    for b in range(8):
        m = dict(shared)
        m["hin"] = _fp(np.concatenate([inp["ctx"][b], inp["x"][b]], axis=0))
        cv = np.zeros((128, 8, 2), np.float32)
        cv[:, :, 0] = _fpart(inp["c"][b], 8)
        cv[:, :, 1] = _fpart(inp["c_ctx"], 8)
        m["cvec"] = cv
        maps.append(m)
    return maps


_NC_CACHE = {}


def kernel(**inputs):
    inp = {k_: np.asarray(v) for k_, v in inputs.items()}
    maps = prep_inputs(inp)
    if "nc" not in _NC_CACHE:
        _NC_CACHE["nc"] = Builder().build()
    res = run_bass_kernel_spmd(_NC_CACHE["nc"], maps, core_ids=list(range(8)))
    return np.stack([np.asarray(r["out"]) for r in res.results], axis=0).astype(np.float32)


def _phase_lru(self, l):
    nc, k, p = self.nc, self.k, self.L[l]
    with contextlib.ExitStack() as ph:
        N = TT
        convw = self.sb(ph, "l_convw", [128, 2, 4])
        convb = self.sb(ph, "l_convb", [128, 2])
        waxf = self.sb(ph, "l_waxf", [128, 2, 2, 2, 128])
        waxb = self.sb(ph, "l_waxb", [128, 2, 2, 2, 128], BF16)
        bax = self.sb(ph, "l_bax", [128, 2, 2, 2])
        lam = self.sb(ph, "l_lam", [128, 2, 2])
        cl = self.sb(ph, "l_cl", [128, 2, 2])
        cl2 = self.sb(ph, "l_cl2", [128, 2, 2])
        ua = self.sb(ph, "l_ua", [128, N])
        xc = self.sb(ph, "l_xc", [128, N])
        xcb = self.sb(ph, "l_xcb", [128, N], BF16)
        sg = self.sb(ph, "l_sg", [128, N])
        gr = [self.sb(ph, f"l_gr{d}", [128, N]) for d in range(2)]
        gi = [self.sb(ph, f"l_gi{d}", [128, N]) for d in range(2)]
        a2 = [self.sb(ph, f"l_a2{d}", [128, N]) for d in range(2)]
        hd = [self.sb(ph, "l_h0", [128, N]), ua]
        yb = xcb
        pg = [self.ps(ph, f"l_pg{i}", [128, 512]) for i in range(4)]
        k.dma("sync", convw[:], p["convw"], writes=["convw"])
        k.dma("sync", convb[:], p["convb"], writes=["convb"])
        k.dma("sync", waxf[:], p["wax"], writes=["waxf"])
        k.dma("sync", bax[:], p["bax"], writes=["bax"])
        k.dma("sync", lam[:], p["lrulam"], writes=["lam"])
        k.op(k.dve, lambda: nc.vector.tensor_copy(waxb[:], waxf[:]), reads=["waxf"], writes=["waxb"])
        k.op(k.act, lambda: nc.scalar.activation(cl[:], lam[:], AF.Exp, scale=-1.0), reads=["lam"], writes=["cl"])
        k.op(k.act, lambda: nc.scalar.activation(cl[:], cl[:], AF.Ln, scale=1.0, bias=1.0), reads=["cl"], writes=["cl"])
        k.op(k.dve, lambda: nc.vector.tensor_scalar_mul(cl2[:], cl[:], -16.0), reads=["cl"], writes=["cl2"])
        k.op(k.dve, lambda: nc.vector.tensor_scalar_mul(cl[:], cl[:], -8.0), reads=["cl", "cl2"], writes=["cl"])
        segs = [(0, TC), (TC, TT)]
        nblk = [(i * 512, min(512, N - i * 512)) for i in range((N + 511) // 512)]
        pgi = 0
        for ct in range(2):
            rows = slice(ct * 128, (ct + 1) * 128)
            k.dma("sync", ua[:], self.uaT[rows, :], writes=["ua"])
            k.dma("scalar", sg[:], self.gaT[rows, :], writes=["sg"])
            k.op(k.dve, lambda: nc.vector.tensor_scalar(xc[:], ua[:], convw[:, ct, 2:3], convb[:, ct:ct + 1], ALU.mult, ALU.add),
                 reads=["ua", "convw", "convb"], writes=["xc"])
            for (s0, s1) in segs:
                for tap, off in ((0, -2), (1, -1), (3, 1)):
                    lo = max(s0, s0 - off)
                    hi = min(s1, s1 - off)
                    k.op(k.dve, lambda: nc.vector.scalar_tensor_tensor(
                        xc[:, lo:hi], ua[:, lo + off:hi + off], convw[:, ct, tap:tap + 1], xc[:, lo:hi], ALU.mult, ALU.add),
                        reads=["ua", "xc", "convw"], writes=["xc"])
            k.op(k.act, lambda: nc.scalar.copy(xcb[:], xc[:]), reads=["xc"], writes=["xcb"])
            for d in range(2):
                for bi, (c0, w) in enumerate(nblk):
                    pr_, pi_ = pg[pgi % 4], pg[(pgi + 1) % 4]
                    kr, ki = f"pg{pgi % 4}", f"pg{(pgi + 1) % 4}"
                    pgi += 2
                    k.op(k.pe, lambda: nc.tensor.matmul(pr_[:, :w], waxb[:, ct, 0, d, :], xcb[:, c0:c0 + w], start=True, stop=True),
                         reads=["waxb", "xcb"], writes=[kr])
                    k.op(k.pe, lambda: nc.tensor.matmul(pi_[:, :w], waxb[:, ct, 1, d, :], xcb[:, c0:c0 + w], start=True, stop=True),
                         reads=["waxb", "xcb"], writes=[ki])
                    k.op(k.act, lambda: nc.scalar.activation(gr[d][:, c0:c0 + w], pr_[:, :w], AF.Sigmoid, bias=bax[:, ct, 0, d:d + 1]),
                         reads=[kr, "bax"], writes=[f"gr{d}"])
                    k.op(k.act, lambda: nc.scalar.activation(gi[d][:, c0:c0 + w], pi_[:, :w], AF.Sigmoid, bias=bax[:, ct, 1, d:d + 1]),
                         reads=[ki, "bax"], writes=[f"gi{d}"])
                k.op(k.dve, lambda: nc.vector.tensor_tensor(gi[d][:], gi[d][:], xc[:], ALU.mult), reads=[f"gi{d}", "xc"], writes=[f"gi{d}"])
            for d in range(2):
                k.op(k.act, lambda: nc.scalar.activation(a2[d][:], gr[d][:], AF.Exp, scale=cl2[:, ct, d:d + 1]), reads=[f"gr{d}", "cl2"], writes=[f"a2{d}"])
                k.op(k.act, lambda: nc.scalar.activation(gr[d][:], gr[d][:], AF.Exp, scale=cl[:, ct, d:d + 1]), reads=[f"gr{d}", "cl", f"a2{d}"], writes=[f"gr{d}"])
            for d in range(2):
                k.op(k.act, lambda: nc.scalar.activation(a2[d][:], a2[d][:], AF.Sqrt, scale=-1.0, bias=1.0), reads=[f"a2{d}"], writes=[f"a2{d}"])
                k.op(k.dve, lambda: nc.vector.tensor_tensor(gi[d][:], gi[d][:], a2[d][:], ALU.mult), reads=[f"gi{d}", f"a2{d}"], writes=[f"gi{d}"])
            k.op(k.act, lambda: nc.scalar.activation(sg[:], sg[:], AF.Silu), reads=["sg"], writes=["sg"])
            k.op(k.dve, lambda: nc.vector.tensor_tensor_scan(hd[0][:], gr[0][:], gi[0][:], 0.0, ALU.mult, ALU.add),
                 reads=["gr0", "gi0"], writes=["h0"])
            k.op(k.dve, lambda: nc.vector.tensor_tensor_scan(hd[1][:, 0:TC][:, ::-1], gr[1][:, 0:TC][:, ::-1], gi[1][:, 0:TC][:, ::-1],
                                                              0.0, ALU.mult, ALU.add), reads=["gr1", "gi1"], writes=["ua"])
            k.op(k.dve, lambda: nc.vector.tensor_tensor_scan(hd[1][:, TC:TT][:, ::-1], gr[1][:, TC:TT][:, ::-1], gi[1][:, TC:TT][:, ::-1],
                                                              hd[1][:, 0:1], ALU.mult, ALU.add), reads=["gr1", "gi1", "ua"], writes=["ua"])
            if self.debug:
                for d in range(2):
                    k.dma("gpsimd", self.dbg_hl[ct, d], hd[d][:], reads=[("h0" if d == 0 else "ua")])
            k.op(k.pool, lambda: nc.gpsimd.tensor_tensor(hd[0][:], hd[0][:], sg[:], ALU.mult), reads=["h0", "sg"], writes=["h0"])
            k.op(k.dve, lambda: nc.vector.tensor_tensor(hd[1][:], hd[1][:], sg[:], ALU.mult), reads=["ua", "sg"], writes=["ua"])
            k.op(k.dve, lambda: nc.vector.tensor_tensor(yb[:], hd[0][:], hd[1][:], ALU.add), reads=["h0", "ua"], writes=["xcb"])
            k.dma("gpsimd", self.yas[rows, :], yb[:], reads=["xcb"])


Builder.phase_lru = _phase_lru


def _att_prefetch(self, l, st_):
    nc, k, p = self.nc, self.k, self.L[l]
    kTs = self.sb(st_, "a_kT", [128, 4, TT], BF16)
    vs = self.sb(st_, "a_v", [128, NTT, 512], BF16)
    if True:
        k.dma("scalar", kTs[:], self.kT.rearrange("(j p) t -> p j t", p=128), writes=["kTs"])
        vsrc = self.v_s.rearrange("(tt p) e -> p tt e", p=128)
        for q4 in range(0, NTT, 9):
            hi_ = min(NTT, q4 + 9)
            k.dma("scalar", vs[:, q4:hi_, :], vsrc[:, q4:hi_, :], writes=[f"vs{q4}"])
    return kTs, vs


def _phase_att(self, l, hsrc, last, pre):
    nc, k, p = self.nc, self.k, self.L[l]
    lam_init = 0.8 - 0.6 * float(np.exp(-0.3 * l))
    with contextlib.ExitStack() as ph:
        kTs, vs = pre
        woutb = self.sb(ph, "a_wout", [128, 8, D], BF16)
        with contextlib.ExitStack() as tmp_:
            stg = [self.sb(tmp_, f"a_stg{i}", [128, 8, 256]) for i in range(2)]
            wsrc = p["w_out"].rearrange("(k p) n -> p k n", p=128)
            for q4 in range(4):
                k.dma("sync", stg[q4 % 2][:], wsrc[:, :, q4 * 256:(q4 + 1) * 256], writes=[f"stg{q4 % 2}"])
                k.op(k.pool, lambda: nc.gpsimd.tensor_copy(woutb[:, :, q4 * 256:(q4 + 1) * 256], stg[q4 % 2][:]),
                     reads=[f"stg{q4 % 2}"], writes=[f"wout{q4}"])
            k.barrier()
        dal = self.sb(ph, "a_dal", [1, 256])
        prod = self.sb(ph, "a_prod", [1, 2, 64])
        e2 = self.sb(ph, "a_e2", [1, 2])
        nl1 = self.sb(ph, "a_nl1", [1, 1])
        dagc = self.sb(ph, "a_dagc", [128, 1])
        onesf = self.sb(ph, "a_onesf", [128, 128])
        onesb = self.sb(ph, "a_onesb", [128, 128], BF16)
        fgb = self.sb(ph, "a_fgb", [128, D])
        qTb = [self.sb(ph, f"a_q{i}", [128, 4, 512], BF16) for i in range(2)]
        yasb = [self.sb(ph, f"a_yas{i}", [128, 4, 512], BF16) for i in range(2)]
        gdb = [self.sb(ph, f"a_gdb{i}", [128, 4, 512]) for i in range(2)]
        PT = [[self.sb(ph, f"a_pt{m}{i}", [128, 512], BF16) for i in range(3)] for m in range(2)]
        acc = [self.sb(ph, f"a_acc{m}", [128, 512]) for m in range(2)]
        rden = [self.sb(ph, f"a_rden{m}", [128, 512]) for m in range(2)]
        obT = [self.sb(ph, f"a_obT{i}", [128, 4, 512]) for i in range(2)]
        cO = [self.sb(ph, f"a_cO{i}", [128, 512]) for i in range(2)]
        t1 = self.sb(ph, "a_t1", [128, 512])
        sqb = self.sb(ph, "a_sqb", [128, 4, 512])
        rstd1 = self.sb(ph, "a_rstd", [128, 512])
        ydT = self.sb(ph, "a_ydT", [128, 4, 512], BF16)
        hx = [self.sb(ph, f"a_hx{i}", [128, D]) for i in range(2)]
        hn = [self.sb(ph, f"a_hn{i}", [128, D]) for i in range(2)]
        junk = self.sb(ph, "a_junk", [128, D], BF16)
        fs = self.sb(ph, "a_fs", [128, 2])
        psc = [self.ps(ph, f"a_psc{i}", [128, 512]) for i in range(4)]
        pO = [self.ps(ph, f"a_pO{i}", [128, 512]) for i in range(2)]
        pden = [self.ps(ph, f"a_pden{i}", [128, 512]) for i in range(2)]
        po = psc[0:2]
        pokeys = ["psc0", "psc1"]

        k.dma("sync", dal[:], p["dalam"], writes=["dal"])
        dv_ = dal[:].rearrange("p (m t e) -> p m t e", m=2, t=2)
        k.op(k.dve, lambda: nc.vector.tensor_tensor(prod[:], dv_[:, :, 0, :], dv_[:, :, 1, :], ALU.mult), reads=["dal"], writes=["prod"])
        k.op(k.dve, lambda: nc.vector.reduce_sum(e2[:], prod[:], axis=AX.X), reads=["prod"], writes=["e2"])
        k.op(k.act, lambda: nc.scalar.activation(e2[:], e2[:], AF.Exp), reads=["e2"], writes=["e2"])
        k.op(k.dve, lambda: nc.vector.tensor_tensor(nl1[:], e2[:, 1:2], e2[:, 0:1], ALU.subtract), reads=["e2"], writes=["nl1"])
        k.op(k.dve, lambda: nc.vector.tensor_scalar_add(nl1[:], nl1[:], -lam_init), reads=["nl1"], writes=["nl1"])
        k.op(k.pe, lambda: nc.tensor.matmul(pden[0][:, 0:1], self.ones1[:], nl1[:], start=True, stop=True), reads=["ones1", "nl1"], writes=["pden0"])
        k.op(k.dve, lambda: nc.vector.tensor_copy(self.nlam[:], pden[0][:, 0:1]), reads=["pden0"], writes=["nlam"])
        k.dma("sync", dagc[:], p["dag"], writes=["dagc"])
        k.op(k.dve, lambda: nc.vector.tensor_scalar_mul(dagc[:], dagc[:], 1.0 - lam_init), reads=["dagc"], writes=["dagc"])
        k.op(k.dve, lambda: nc.vector.memset(onesf[:], 1.0), writes=["onesf"])
        k.op(k.dve, lambda: nc.vector.memset(onesb[:], 1.0), writes=["onesb"])
        if last:
            k.dma("sync", fgb[:], self.finalg.partition_broadcast(128), writes=["fgb"])

        qblocks = [] if last else [(0, 256, [0, 1])]
        qblocks += [(TC + 512 * i, 512, list(range(NTT))) for i in range(8)]
        qTv = self.qT.rearrange("(j p) t -> p j t", p=128)
        yasv = self.yas.rearrange("(j p) t -> p j t", p=128)
        gdv = self.gdT.rearrange("(j p) t -> p j t", p=128)
        pt_i = 0
        pair_i = 0
        tile_i = [0]

        def epilogue1(bi):
            t0, nq, _ = qblocks[bi]
            gd_, ob_ = gdb[bi % 2], obT[bi % 2]
            banks = [(pden[0], "pden0"), (pden[1], "pden1"), (pO[0], "pO0"), (pO[1], "pO1")]
            tmpb = [(rstd1, "rstd"), (t1, "t1"), (cO[0], "cO0"), (cO[1], "cO1")]
            for h in range(4):
                pd, pdk = banks[h]
                k.op(k.pe, lambda: nc.tensor.matmul(pd[:, :nq], onesf[:], sqb[:, h, :nq], start=True, stop=True),
                     reads=["onesf", f"sqb{h}"], writes=[pdk])
            for h in range(4):
                pd, pdk = banks[h]
                rs_, rk = tmpb[h]
                k.op(k.act, lambda: nc.scalar.activation(rs_[:, :nq], pd[:, :nq], AF.Sqrt, scale=1.0 / 128, bias=EPS),
                     reads=[pdk], writes=[rk])
            for h in range(4):
                rs_, rk = tmpb[h]
                k.op(k.dve, lambda: nc.vector.reciprocal(rs_[:, :nq], rs_[:, :nq]), reads=[rk], writes=[rk])
                k.op(k.pool, lambda: nc.gpsimd.tensor_tensor(rs_[:, :nq], rs_[:, :nq], ob_[:, h, :nq], ALU.mult),
                     reads=[rk, f"obT{bi % 2}{h}"], writes=[rk])
                k.op(k.dve, lambda: nc.vector.scalar_tensor_tensor(ydT[:, h, :nq], rs_[:, :nq], dagc[:, 0:1], gd_[:, h, :nq], ALU.mult, ALU.mult),
                     reads=[rk, "dagc", f"gdb{bi % 2}"], writes=["ydT"])

        def epilogue2(bi):
            t0, nq, _ = qblocks[bi]
            v = 1 if t0 < TC else 0
            yb_ = yasb[bi % 2]
            for qs in range(nq // 128):
                r0 = t0 + qs * 128
                gi_ = tile_i[0] % 2
                tile_i[0] += 1
                hx_, hn_ = hx[gi_], hn[gi_]
                k.dma("sync", hx_[:], hsrc[r0:r0 + 128, :], writes=[f"hx{gi_}"])
                for hf in range(2):
                    for mt in range(8):
                        lhs = yb_[:, mt, qs * 128:(qs + 1) * 128] if mt < 4 else ydT[:, mt - 4, qs * 128:(qs + 1) * 128]
                        k.op(k.pe, lambda: nc.tensor.matmul(po[hf][:], lhs, woutb[:, mt, hf * 512:(hf + 1) * 512],
                                                            start=(mt == 0), stop=(mt == 7)),
                             reads=[f"yas{bi % 2}", "ydT"], writes=[pokeys[hf]])
                    cs = slice(hf * 512, (hf + 1) * 512)
                    k.op(k.dve, lambda: nc.vector.tensor_tensor(hn_[:, cs], po[hf][:], self.gateb[:, v, cs], ALU.mult),
                         reads=[pokeys[hf], f"gateb{v}{hf}"], writes=[f"hn{gi_}{hf}"])
                    k.op(k.pool, lambda: nc.gpsimd.tensor_tensor(hn_[:, cs], hn_[:, cs], hx_[:, cs], ALU.add),
                         reads=[f"hn{gi_}{hf}", f"hx{gi_}"], writes=[f"hn{gi_}{hf}"])
                hk = [f"hn{gi_}0", f"hn{gi_}1"]
                if not last:
                    k.dma("gpsimd", self.hbuf[r0:r0 + 128, :], hn_[:], reads=hk)
                else:
                    k.op(k.act, lambda: nc.scalar.activation(junk[:], hn_[:], AF.Square, accum_out=fs[:, gi_:gi_ + 1]),
                         reads=hk, writes=["junk", f"fs{gi_}"])
                    k.op(k.act, lambda: nc.scalar.activation(fs[:, gi_:gi_ + 1], fs[:, gi_:gi_ + 1], AF.Sqrt, scale=1.0 / D, bias=EPS),
                         reads=[f"fs{gi_}"], writes=[f"fs{gi_}"])
                    k.op(k.dve, lambda: nc.vector.reciprocal(fs[:, gi_:gi_ + 1], fs[:, gi_:gi_ + 1]), reads=[f"fs{gi_}"], writes=[f"fs{gi_}"])
                    k.op(k.dve, lambda: nc.vector.scalar_tensor_tensor(hn_[:], hn_[:], fs[:, gi_:gi_ + 1], fgb[:], ALU.mult, ALU.mult),
                         reads=hk + [f"fs{gi_}", "fgb"], writes=hk)
                    k.dma("gpsimd", self.out[r0 - TC:r0 - TC + 128, :], hn_[:], reads=hk)

        for bi, (t0, nq, ktl) in enumerate(qblocks):
            qb_, yb_, gd_, ob_ = qTb[bi % 2], yasb[bi % 2], gdb[bi % 2], obT[bi % 2]
            k.dma("sync", qb_[:, :, :nq], qTv[:, :, t0:t0 + nq], writes=[f"q{bi % 2}"])
            k.dma("sync", yb_[:, :, :nq], yasv[:, :, t0:t0 + nq], writes=[f"yas{bi % 2}"])
            k.dma("sync", gd_[:, :, :nq], gdv[:, :, t0:t0 + nq], writes=[f"gdb{bi % 2}"])
            k.op(k.act, lambda: nc.scalar.activation(gd_[:, :, :nq], gd_[:, :, :nq], AF.Silu), reads=[f"gdb{bi % 2}"], writes=[f"gdb{bi % 2}"])
            its = [(h, ki_, kt) for h in range(4) for ki_, kt in enumerate(ktl)]

            def emit_scores(i):
                h, ki_, kt = its[i]
                for m in range(2):
                    prt = slice(m * 64, (m + 1) * 64)
                    bnk = 2 * ((pair_i + i) % 2) + m
                    k.op(k.pe, lambda: nc.tensor.matmul(psc[bnk][:, :nq], kTs[prt, h, kt * 128:(kt + 1) * 128], qb_[prt, h, :nq],
                                                        start=True, stop=True), reads=["kTs", f"q{bi % 2}"], writes=[f"psc{bnk}"])

            emit_scores(0)
            for i, (h, ki_, kt) in enumerate(its):
                first, lastk = (ki_ == 0), (ki_ == len(ktl) - 1)
                defer_here = lastk and bi > 0 and h == 0
                if i + 1 < len(its) and not defer_here:
                    emit_scores(i + 1)
                for m in range(2):
                    bnk = 2 * ((pair_i + i) % 2) + m
                    pt_ = PT[m][pt_i % 3]
                    pk = f"pt{m}{pt_i % 3}"
                    k.op(k.act, lambda: nc.scalar.activation(pt_[:, :nq], psc[bnk][:, :nq], AF.Exp, scale=0.125), reads=[f"psc{bnk}"], writes=[pk])
                    k.op(k.pe, lambda: nc.tensor.matmul(pO[m][:, :nq], vs[:, kt, h * 128:(h + 1) * 128], pt_[:, :nq], start=first, stop=lastk),
                         reads=[pk], writes=[f"pO{m}"])
                    if m == 0:
                        k.op(k.pe, lambda: nc.tensor.matmul(pden[0][:, :nq], onesb[:], pt_[:, :nq], start=first, stop=lastk),
                             reads=[pk, "onesb"], writes=["pden0"])
                    else:
                        e_ = ki_ % 2
                        eng = k.dve if e_ == 0 else k.pool
                        if ki_ < 2:
                            k.op(eng, lambda: eng.h.tensor_copy(acc[e_][:, :nq], pt_[:, :nq]), reads=[pk], writes=[f"acc{e_}"])
                        else:
                            k.op(eng, lambda: eng.h.tensor_tensor(acc[e_][:, :nq], acc[e_][:, :nq], pt_[:, :nq], ALU.add),
                                 reads=[pk, f"acc{e_}"], writes=[f"acc{e_}"])
                pt_i += 1
                if not lastk:
                    continue
                k.op(k.act, lambda: nc.scalar.copy(cO[0][:, :nq], pO[0][:, :nq]), reads=["pO0"], writes=["cO0"])
                k.op(k.dve, lambda: nc.vector.tensor_copy(cO[1][:, :nq], pO[1][:, :nq]), reads=["pO1"], writes=["cO1"])
                k.op(k.dve, lambda: nc.vector.reciprocal(rden[0][:, :nq], pden[0][:, :nq]), reads=["pden0"], writes=["rden0"])
                k.op(k.pe, lambda: nc.tensor.matmul(pden[1][:, :nq], onesf[:], acc[0][:, :nq], start=True, stop=False),
                     reads=["onesf", "acc0"], writes=["pden1"])
                k.op(k.pe, lambda: nc.tensor.matmul(pden[1][:, :nq], onesf[:], acc[1][:, :nq], start=False, stop=True),
                     reads=["onesf", "acc1"], writes=["pden1"])
                k.op(k.dve, lambda: nc.vector.reciprocal(rden[1][:, :nq], pden[1][:, :nq]), reads=["pden1"], writes=["rden1"])
                k.op(k.dve, lambda: nc.vector.tensor_scalar_mul(rden[1][:, :nq], rden[1][:, :nq], self.nlam[:, 0:1]), reads=["rden1", "nlam"], writes=["rden1"])
                k.op(k.pool, lambda: nc.gpsimd.tensor_tensor(t1[:, :nq], cO[1][:, :nq], rden[1][:, :nq], ALU.mult), reads=["cO1", "rden1"], writes=["t1"])
                k.op(k.dve, lambda: nc.vector.tensor_tensor(ob_[:, h, :nq], cO[0][:, :nq], rden[0][:, :nq], ALU.mult), reads=["cO0", "rden0"], writes=[f"obT{bi % 2}{h}"])
                k.op(k.pool, lambda: nc.gpsimd.tensor_tensor(ob_[:, h, :nq], ob_[:, h, :nq], t1[:, :nq], ALU.add), reads=[f"obT{bi % 2}{h}", "t1"], writes=[f"obT{bi % 2}{h}"])
                k.op(k.pool, lambda: nc.gpsimd.tensor_tensor(sqb[:, h, :nq], ob_[:, h, :nq], ob_[:, h, :nq], ALU.mult),
                     reads=[f"obT{bi % 2}{h}"], writes=[f"sqb{h}"])
                if h == 3:
                    epilogue1(bi)
                if bi > 0 and h == 0:
                    epilogue2(bi - 1)
                    if i + 1 < len(its):
                        emit_scores(i + 1)
            pair_i += len(its)
        epilogue2(len(qblocks) - 1)


Builder.phase_att = _phase_att
Builder.att_prefetch = _att_prefetch
Builder.phase_s5 = lambda self, l: None


def _phase_s5(self, l):
    nc, k, p = self.nc, self.k, self.L[l]
    PI = float(np.pi)
    uid = [0]

    def nm(s_):
        uid[0] += 1
        return f"s_{s_}{uid[0]}"

    def dv(fn, reads, writes):
        return k.op(k.dve, fn, reads=reads, writes=writes)

    I32 = mybir.dt.int32
    PI_LO = 3.1415925

    tcache = {}

    def reduce_pi(st_, out_t, src, shape, rkeys, tmps=None):
        if tmps is None:
            u = self.sb(st_, nm("ru"), shape)
            qi = self.sb(st_, nm("rq"), shape, I32)
        else:
            u, qi = tmps
        dv(lambda: nc.vector.tensor_scalar_mul(u[:], src, 1.0 / TWO_PI), rkeys, [u.name])
        dv(lambda: nc.vector.tensor_copy(qi[:], u[:]), [u.name], [qi.name])
        dv(lambda: nc.vector.tensor_copy(u[:], qi[:]), [qi.name], [u.name])
        dv(lambda: nc.vector.scalar_tensor_tensor(out_t[:], u[:], -TWO_PI, src, ALU.mult, ALU.add), [u.name] + rkeys, [out_t.name])
        dv(lambda: nc.vector.tensor_scalar(out_t[:], out_t[:], -PI_LO, PI_LO, ALU.max, ALU.min), [out_t.name], [out_t.name])

    def sincos(st_, ang, shape, K_, key):
        sn = self.sb(st_, nm("sn"), shape)
        cs = self.sb(st_, nm("cs"), shape)
        ck = (id(st_), tuple(shape))
        if ck not in tcache:
            tcache[ck] = (self.sb(st_, nm("ah"), shape), self.sb(st_, nm("ru"), shape), self.sb(st_, nm("rq"), shape, I32))
        ah, u_, q_ = tcache[ck]
        reduce_pi(st_, sn, ang, shape, [key], (u_, q_))
        dv(lambda: nc.vector.tensor_scalar_add(ah[:], ang, PI / 2), [key], [ah.name])
        reduce_pi(st_, cs, ah[:], shape, [ah.name], (u_, q_))
        for t_ in (sn, cs):
            k.op(k.act, lambda: nc.scalar.activation(t_[:], t_[:], AF.Sin), reads=[t_.name], writes=[t_.name])
        return sn, cs

    with contextlib.ExitStack() as ph:
        BLt = self.sb(ph, "s_BLt", [128, 128, 64], BF16)
        DLt = self.sb(ph, "s_DLt", [128, 32, 256], BF16)
        CLR = self.sb(ph, "s_CLR", [128, 16, 256], BF16)
        CLI = self.sb(ph, "s_CLI", [128, 16, 256], BF16)
        lamp = self.sb(ph, "s_lamp", [128, 3, 16])
        lrd = self.sb(ph, "s_lrd", [128, 16])
        ang = self.sb(ph, "s_ang", [128, 16])
        k.dma("sync", lamp[:], p["s5lam"], writes=["lamp"])
        dt = self.sb(ph, "s_dt", [128, 16])
        k.op(k.act, lambda: nc.scalar.activation(dt[:], lamp[:, 2, :], AF.Exp), reads=["lamp"], writes=["dt"])
        dv(lambda: nc.vector.tensor_tensor(lrd[:], lamp[:, 0, :], dt[:], ALU.mult), ["lamp", "dt"], ["lrd"])
        dv(lambda: nc.vector.tensor_tensor(ang[:], lamp[:, 1, :], dt[:], ALU.mult), ["lamp", "dt"], ["ang"])

        with contextlib.ExitStack() as sa:
            S2 = [128, 16]
            bsrc = self.sb(sa, "s_bsrc", [128, 2, 16, 16])
            csrc = self.sb(sa, "s_csrc", [128, 2, 16, 16])
            expt = self.sb(sa, "s_expt", [128, 3, 16, 16])
            mask = self.sb(sa, "s_mask", [128, 2, 2, 256])
            k.dma("sync", bsrc[:], p["s5b"], writes=["bsrc"])
            k.dma("sync", csrc[:], p["s5c"], writes=["csrc"])
            k.dma("sync", expt[:], self.s5exp, writes=["expt"])
            k.dma("sync", mask[:], self.s5mask, writes=["mask"])
            mag = self.sb(sa, "s_mag", S2)
            k.op(k.act, lambda: nc.scalar.activation(mag[:], lrd[:], AF.Exp), reads=["lrd"], writes=["mag"])
            sn, cs = sincos(sa, ang[:], S2, 1, "ang")
            nr = self.sb(sa, "s_nr", S2); ni = self.sb(sa, "s_ni", S2); den = self.sb(sa, "s_den", S2)
            t1 = self.sb(sa, "s_t1", S2); t2 = self.sb(sa, "s_t2", S2)
            cfr = self.sb(sa, "s_cfr", S2); cfi = self.sb(sa, "s_cfi", S2)
            dv(lambda: nc.vector.tensor_tensor(nr[:], mag[:], cs[:], ALU.mult), ["mag", cs.name], ["nr"])
            dv(lambda: nc.vector.tensor_scalar_add(nr[:], nr[:], -1.0), ["nr"], ["nr"])
            dv(lambda: nc.vector.tensor_tensor(ni[:], mag[:], sn[:], ALU.mult), ["mag", sn.name], ["ni"])
            lre, lim = lamp[:, 0, :], lamp[:, 1, :]
            dv(lambda: nc.vector.tensor_tensor(den[:], lre, lre, ALU.mult), ["lamp"], ["den"])
            dv(lambda: nc.vector.tensor_tensor(t1[:], lim, lim, ALU.mult), ["lamp"], ["t1"])
            dv(lambda: nc.vector.tensor_tensor(den[:], den[:], t1[:], ALU.add), ["den", "t1"], ["den"])
            dv(lambda: nc.vector.reciprocal(den[:], den[:]), ["den"], ["den"])
            dv(lambda: nc.vector.tensor_tensor(t1[:], nr[:], lre, ALU.mult), ["nr", "lamp"], ["t1"])
            dv(lambda: nc.vector.tensor_tensor(t2[:], ni[:], lim, ALU.mult), ["ni", "lamp"], ["t2"])
            dv(lambda: nc.vector.tensor_tensor(cfr[:], t1[:], t2[:], ALU.add), ["t1", "t2"], ["cfr"])
            dv(lambda: nc.vector.tensor_tensor(cfr[:], cfr[:], den[:], ALU.mult), ["cfr", "den"], ["cfr"])
            dv(lambda: nc.vector.tensor_tensor(t1[:], ni[:], lre, ALU.mult), ["ni", "lamp", "cfr"], ["t1"])
            dv(lambda: nc.vector.tensor_tensor(t2[:], nr[:], lim, ALU.mult), ["nr", "lamp", "cfr"], ["t2"])
            dv(lambda: nc.vector.tensor_tensor(cfi[:], t1[:], t2[:], ALU.subtract), ["t1", "t2"], ["cfi"])
            dv(lambda: nc.vector.tensor_tensor(cfi[:], cfi[:], den[:], ALU.mult), ["cfi", "den"], ["cfi"])
            S3 = [128, 16, 16]
            S4 = [128, 16, 16, 16]
            bbr = self.sb(sa, "s_bbr", S3); bbi = self.sb(sa, "s_bbi", S3); u1 = self.sb(sa, "s_u1", S3)
            cfrb = cfr[:].unsqueeze(2).broadcast_to(S3)
            cfib = cfi[:].unsqueeze(2).broadcast_to(S3)
            dv(lambda: nc.vector.tensor_tensor(bbr[:], bsrc[:, 0], cfrb, ALU.mult), ["bsrc", "cfr"], ["bbr"])
            dv(lambda: nc.vector.tensor_tensor(u1[:], bsrc[:, 1], cfib, ALU.mult), ["bsrc", "cfi"], ["u1"])
            dv(lambda: nc.vector.tensor_tensor(bbr[:], bbr[:], u1[:], ALU.subtract), ["bbr", "u1"], ["bbr"])
            dv(lambda: nc.vector.tensor_tensor(bbi[:], bsrc[:, 1], cfrb, ALU.mult), ["bsrc", "cfr", "bbr"], ["bbi"])
            dv(lambda: nc.vector.tensor_tensor(u1[:], bsrc[:, 0], cfib, ALU.mult), ["bsrc", "cfi", "bbr"], ["u1"])
            dv(lambda: nc.vector.tensor_tensor(bbi[:], bbi[:], u1[:], ALU.add), ["bbi", "u1"], ["bbi"])

            def cpow(e):
                lr = self.sb(sa, nm("lr"), S3)
                an = self.sb(sa, nm("an"), S3)
                dv(lambda: nc.vector.tensor_tensor(lr[:], expt[:, e], lrd[:].unsqueeze(2).broadcast_to(S3), ALU.mult), ["expt", "lrd"], [lr.name])
                k.op(k.act, lambda: nc.scalar.activation(lr[:], lr[:], AF.Exp), reads=[lr.name], writes=[lr.name])
                dv(lambda: nc.vector.tensor_tensor(an[:], expt[:, e], ang[:].unsqueeze(2).broadcast_to(S3), ALU.mult), ["expt", "ang"], [an.name])
                s_, c_ = sincos(sa, an[:], S3, 60, an.name)
                dv(lambda: nc.vector.tensor_tensor(c_[:], c_[:], lr[:], ALU.mult), [c_.name, lr.name], [c_.name])
                dv(lambda: nc.vector.tensor_tensor(s_[:], s_[:], lr[:], ALU.mult), [s_.name, lr.name], [s_.name])
                return c_, s_

            w1 = self.sb(sa, "s_w1", S4)
            w2 = self.sb(sa, "s_w2", S4)

            def cmul(out_re, out_im_neg, out_im, pw, vr, vi, vkeys):
                pr_, pi_ = pw
                prb = pr_[:].unsqueeze(3).broadcast_to(S4)
                pib = pi_[:].unsqueeze(3).broadcast_to(S4)
                vrb = vr.unsqueeze(2).broadcast_to(S4)
                vib = vi.unsqueeze(2).broadcast_to(S4)
                rk = [pr_.name, pi_.name] + vkeys
                dv(lambda: nc.vector.tensor_tensor(w1[:], prb, vrb, ALU.mult), rk, ["w1"])
                dv(lambda: nc.vector.tensor_tensor(w2[:], pib, vib, ALU.mult), rk, ["w2"])
                dv(lambda: nc.vector.tensor_tensor(out_re, w1[:], w2[:], ALU.subtract), ["w1", "w2"], [nm("o")])
                dv(lambda: nc.vector.tensor_tensor(w1[:], prb, vib, ALU.mult), rk, ["w1"])
                dv(lambda: nc.vector.tensor_tensor(w2[:], pib, vrb, ALU.mult), rk, ["w2"])
                if out_im is not None:
                    dv(lambda: nc.vector.tensor_tensor(out_im, w1[:], w2[:], ALU.add), ["w1", "w2"], [nm("o")])
                else:
                    dv(lambda: nc.vector.scalar_tensor_tensor(out_im_neg, w1[:], -1.0, w2[:], ALU.mult, ALU.subtract), ["w1", "w2"], [nm("o")])

            PBr = self.sb(sa, "s_PBr", S4); PBi = self.sb(sa, "s_PBi", S4)
            QCr = self.sb(sa, "s_QCr", S4); QCn = self.sb(sa, "s_QCn", S4)
            cmul(PBr[:], None, PBi[:], cpow(0), bbr[:], bbi[:], ["bbr", "bbi"])
            cmul(CLR[:].rearrange("p t (j h) -> p t j h", h=16), CLI[:].rearrange("p t (j h) -> p t j h", h=16), None,
                 cpow(1), csrc[:, 0], csrc[:, 1], ["csrc"])
            cmul(QCr[:], QCn[:], None, cpow(2), csrc[:, 0], csrc[:, 1], ["csrc"])
            k.barrier()
            pst = [self.ps(sa, f"s_pst{i}", [128, 512]) for i in range(2)]
            psd = [self.ps(sa, f"s_psd{i}", [128, 512]) for i in range(4)]
            n_ = 0
            for tp in range(16):
                for gl in range(2):
                    pt_ = pst[n_ % 2]
                    pk = f"pst{n_ % 2}"
                    n_ += 1
                    prt = slice(gl * 64, (gl + 1) * 64)
                    for kt in range(2):
                        for pl_, PB in enumerate((PBr, PBi)):
                            q_ = kt * 2 + pl_
                            k.op(k.pe, lambda: nc.tensor.transpose(pt_[:, q_ * 64:(q_ + 1) * 64], PB[prt, tp, kt * 8:(kt + 1) * 8, :],
                                                                   self.identf[prt, prt]), reads=["identf"], writes=[pk])
                    s0 = (tp * 2 + gl) * 4
                    k.op(k.act, lambda: nc.scalar.copy(BLt[:, s0:s0 + 4, :], pt_[:, 0:256].rearrange("p (a b) -> p a b", b=64)),
                         reads=[pk], writes=["BLt"])
            n_ = 0
            for g in range(16):
                gp, gl = g // 2, g % 2
                prt = slice(gl * 64, (gl + 1) * 64)
                for kt in range(2):
                    pf, pb = psd[(2 * n_) % 4], psd[(2 * n_ + 1) % 4]
                    kf_, kb_ = f"psd{(2 * n_) % 4}", f"psd{(2 * n_ + 1) % 4}"
                    n_ += 1
                    for ps_, pk, tp in ((pf, kf_, gp), (pb, kb_, 8 + gp)):
                        k.op(k.pe, lambda: nc.tensor.matmul(ps_[:, 0:256], PBr[prt, tp, kt * 8:(kt + 1) * 8, :], QCr[prt, tp, :, :],
                                                            start=True, stop=False), reads=[], writes=[pk])
                        k.op(k.pe, lambda: nc.tensor.matmul(ps_[:, 0:256], PBi[prt, tp, kt * 8:(kt + 1) * 8, :], QCn[prt, tp, :, :],
                                                            start=False, stop=True), reads=[], writes=[pk])
                    m1 = self.sb(sa, nm("m"), [128, 256]) if n_ <= 2 else m1s[n_ % 2]
                    if n_ <= 2:
                        m1s = (m1s + [m1]) if n_ == 2 else [m1]
                    dv(lambda: nc.vector.tensor_tensor(m1[:], pf[:, 0:256], mask[:, 0, kt, :], ALU.mult), [kf_, "mask"], [m1.name])
                    dv(lambda: nc.vector.tensor_tensor(pb[:, 256:512], pb[:, 0:256], mask[:, 1, kt, :], ALU.mult), [kb_, "mask"], [kb_])
                    dv(lambda: nc.vector.tensor_tensor(DLt[:, g * 2 + kt, :], pb[:, 256:512], m1[:], ALU.add), [kb_, m1.name], ["DLt"])
            k.barrier()

        if getattr(self, "s5_stop", 9) <= 1:
            return
        Ut = self.sb(ph, "s_Ut", [128, 32, NCH], BF16)
        SFR = self.sb(ph, "s_SFR", [128, 8, NCH + 1], BF16); SFI = self.sb(ph, "s_SFI", [128, 8, NCH + 1], BF16)
        SBR = self.sb(ph, "s_SBR", [128, 8, NCH + 1], BF16); SBI = self.sb(ph, "s_SBI", [128, 8, NCH + 1], BF16)
        P16 = self.sb(ph, "s_P16", [128, 16])
        phr = self.sb(ph, "s_phr", [128, 16])
        k.op(k.act, lambda: nc.scalar.activation(P16[:], lrd[:], AF.Exp, scale=16.0), reads=["lrd"], writes=["P16"])
        ph16 = self.sb(ph, "s_ph16", [128, 16])
        dv(lambda: nc.vector.tensor_scalar_mul(ph16[:], ang[:], 16.0), ["ang"], ["ph16"])
        reduce_pi(ph, phr, ph16[:], [128, 16], ["ph16"])
        CT = [(0, 16), (16, 128), (144, 128)]
        usv = self.us_s.rearrange("(c j) ch -> c j ch", j=16)
        with contextlib.ExitStack() as su:
            ucm = [self.sb(su, f"s_ucm{i}", [128, 16, 256]) for i in range(3)]
            psu = [self.ps(su, f"s_psu{i}", [128, 512]) for i in range(4)]
            n_ = 0
            for ci, (c0, n) in enumerate(CT):
                k.dma("sync", ucm[ci][0:n], usv[c0:c0 + n], writes=[f"ucm{ci}"])
                ucp_ = self.sb(su, f"s_ucp{ci}", [128, 16, 16, 16])
                k.op(k.pool if ci == 1 else k.dve,
                     lambda: (nc.gpsimd if ci == 1 else nc.vector).tensor_copy(
                         ucp_[0:n], ucm[ci][0:n].rearrange("p i (g h) -> p g i h", h=16)),
                     reads=[f"ucm{ci}"], writes=[f"ucp{ci}"])
                for q4 in range(8):
                    pu = psu[n_ % 4]
                    pk = f"psu{n_ % 4}"
                    n_ += 1
                    for a in range(4):
                        s_ = q4 * 4 + a
                        g, kt = s_ // 2, s_ % 2
                        k.op(k.pe, lambda: nc.tensor.transpose(pu[:, a * 128:a * 128 + n], ucp_[0:n, g, kt * 8:(kt + 1) * 8, :],
                                                               self.identf[0:n, 0:n]), reads=[f"ucp{ci}", "identf"], writes=[pk])
                    eng = k.act if n_ % 2 == 0 else k.dve
                    src = pu[:].rearrange("p (a b) -> p a b", b=128)[:, :, 0:n]
                    dst = Ut[:, q4 * 4:(q4 + 1) * 4, c0:c0 + n]
                    if eng is k.act:
                        k.op(eng, lambda: nc.scalar.copy(dst, src), reads=[pk], writes=["Ut"])
                    else:
                        k.op(eng, lambda: nc.vector.tensor_copy(dst, src), reads=[pk], writes=["Ut"])
        k.barrier()
        if getattr(self, "s5_stop", 9) <= 2:
            return
        for d in range(2):
            with contextlib.ExitStack() as sd:
                S8 = [128, 8, NCH]
                idx = self.sb(sd, "s_idx", S8)
                k.dma("sync", idx[:], self.s5idx[:, d * 8:(d + 1) * 8, 0:NCH], writes=["idx"])
                dv(lambda: nc.vector.tensor_tensor(idx[:], idx[:], phr[:, d * 8:(d + 1) * 8].unsqueeze(2).broadcast_to(S8), ALU.mult),
                   ["idx", phr.name], ["idx"])
                SN, CS = sincos(sd, idx[:], S8, 140, "idx")
                PCO = self.sb(sd, "s_PCO", S8)
                XR = self.sb(sd, "s_XR", S8); XI = self.sb(sd, "s_XI", S8)
                VR = self.sb(sd, "s_VR", S8); VI = self.sb(sd, "s_VI", S8)
                WR = self.sb(sd, "s_WR", S8); WI = self.sb(sd, "s_WI", S8)
                psx = [self.ps(sd, f"s_psx{i}", [128, 512]) for i in range(4)]
                dv(lambda: nc.vector.memset(PCO[:], 1.0), [], ["PCO"])
                dv(lambda: nc.vector.tensor_tensor(PCO[:], PCO[:], P16[:, d * 8:(d + 1) * 8].unsqueeze(2).broadcast_to(S8), ALU.mult),
                   ["PCO", "P16"], ["PCO"])
                zc = 0 if d == 0 else NCH - 1
                dv(lambda: nc.vector.memset(PCO[:, :, zc:zc + 1], 0.0), ["PCO"], ["PCO"])
                n_ = 0
                for a in range(8):
                    tp = d * 8 + a
                    for pl_, X in enumerate((XR, XI)):
                        px = psx[n_ % 4]
                        pk = f"psx{n_ % 4}"
                        n_ += 1
                        for gl in range(2):
                            g = a * 2 + gl
                            prt = slice(gl * 64, (gl + 1) * 64)
                            segs = [(0, NCH, 0)] if d == 0 else [(16, NCH, 0), (0, 16, 256)]
                            for (u0, u1_, o0) in segs:
                                for kt in range(2):
                                    slot = ((tp * 2 + gl) * 2 + kt) * 2 + pl_
                                    k.op(k.pe, lambda: nc.tensor.matmul(px[prt, o0:o0 + (u1_ - u0)], BLt[:, slot, :], Ut[:, g * 2 + kt, u0:u1_],
                                                                        start=(kt == 0), stop=(kt == 1)), reads=["BLt", "Ut"], writes=[pk])
                        k.op(k.act, lambda: nc.scalar.copy(X[:, a, :], px[:, 0:NCH]), reads=[pk], writes=[X.name])
                dv(lambda: nc.vector.tensor_tensor(VR[:], XR[:], CS[:], ALU.mult), [XR.name, CS.name], ["VR"])
                k.op(k.pool, lambda: nc.gpsimd.tensor_tensor(WR[:], XI[:], SN[:], ALU.mult), reads=[XI.name, SN.name], writes=["WR"])
                dv(lambda: nc.vector.tensor_tensor(VR[:], VR[:], WR[:], ALU.add), ["VR", "WR"], ["VR"])
                dv(lambda: nc.vector.tensor_tensor(VI[:], XI[:], CS[:], ALU.mult), [XI.name, CS.name], ["VI"])
                k.op(k.pool, lambda: nc.gpsimd.tensor_tensor(WI[:], XR[:], SN[:], ALU.mult), reads=[XR.name, SN.name], writes=["WI"])
                dv(lambda: nc.vector.tensor_tensor(VI[:], VI[:], WI[:], ALU.subtract), ["VI", "WI"], ["VI"])
                fl = lambda t_: t_[:].rearrange("p a b -> p (a b)")
                rv = (lambda ap: ap) if d == 0 else (lambda ap: ap[:, ::-1])
                dv(lambda: nc.vector.tensor_tensor_scan(rv(fl(WR)), rv(fl(PCO)), rv(fl(VR)), 0.0, ALU.mult, ALU.add), ["PCO", "VR", "WR"], ["WR"])
                dv(lambda: nc.vector.tensor_tensor_scan(rv(fl(WI)), rv(fl(PCO)), rv(fl(VI)), 0.0, ALU.mult, ALU.add), ["PCO", "VI", "WI"], ["WI"])
                SR_, SI_ = (SFR, SFI) if d == 0 else (SBR, SBI)
                o_ = 1 if d == 0 else 0
                zcol = 0 if d == 0 else NCH
                dv(lambda: nc.vector.memset(SR_[:, :, zcol:zcol + 1], 0.0), [], [SR_.name])
                dv(lambda: nc.vector.memset(SI_[:, :, zcol:zcol + 1], 0.0), [], [SI_.name])
                dv(lambda: nc.vector.tensor_tensor(VR[:], WR[:], CS[:], ALU.mult), ["WR", CS.name, "VR"], ["VR"])
                k.op(k.pool, lambda: nc.gpsimd.tensor_tensor(VI[:], WI[:], SN[:], ALU.mult), reads=["WI", SN.name, "VI"], writes=["VI"])
                dv(lambda: nc.vector.tensor_tensor(SR_[:, :, o_:o_ + NCH], VR[:], VI[:], ALU.subtract), ["VR", "VI", SR_.name], [SR_.name])
                dv(lambda: nc.vector.tensor_tensor(VR[:], WI[:], CS[:], ALU.mult), ["WI", CS.name, "VR", SR_.name], ["VR"])
                k.op(k.pool, lambda: nc.gpsimd.tensor_tensor(VI[:], WR[:], SN[:], ALU.mult), reads=["WR", SN.name, "VI", SR_.name], writes=["VI"])
                dv(lambda: nc.vector.tensor_tensor(SI_[:, :, o_:o_ + NCH], VR[:], VI[:], ALU.add), ["VR", "VI", SI_.name], [SI_.name])
            k.barrier()
        if getattr(self, "s5_stop", 9) <= 3:
            return
        with contextlib.ExitStack() as sy:
            dsk = self.sb(sy, "s_dsk", [128, 256])
            k.dma("sync", dsk[:], p["s5d"].partition_broadcast(128), writes=["dsk"])
            ucm = [self.sb(sy, f"s_ucy{i}", [128, 16, 256]) for i in range(2)]
            ycm = [self.sb(sy, f"s_ycm{i}", [128, 16, 256]) for i in range(2)]
            tq = [self.sb(sy, f"s_tq{i}", [128, 16, 256]) for i in range(2)]
            psy = [self.ps(sy, f"s_psy{i}", [128, 512]) for i in range(4)]
            zsv = self.zs.rearrange("(c j) ch -> c j ch", j=16)
            n_ = 0
            for ci, (c0, n) in enumerate(CT):
                u_, y_, t_ = ucm[ci % 2], ycm[ci % 2], tq[ci % 2]
                uk, yk, tk = f"ucy{ci % 2}", f"ycm{ci % 2}", f"tq{ci % 2}"
                k.dma("sync", u_[0:n], usv[c0:c0 + n], writes=[uk])
                mb0 = (256 if c0 < 16 else c0 - 16) + 1
                for g2 in range(8):
                    py = psy[n_ % 4]
                    pk = f"psy{n_ % 4}"
                    n_ += 1
                    for gg in range(2):
                        g = g2 * 2 + gg
                        gp, gl = g // 2, g % 2
                        prt = slice(gl * 64, (gl + 1) * 64)
                        o = py[0:n, gg * 256:(gg + 1) * 256]
                        mm = [(Ut[:, g * 2 + 0, c0:c0 + n], DLt[:, g * 2 + 0, :]),
                              (Ut[:, g * 2 + 1, c0:c0 + n], DLt[:, g * 2 + 1, :]),
                              (SFR[prt, gp, c0:c0 + n], CLR[prt, gp, :]),
                              (SFI[prt, gp, c0:c0 + n], CLI[prt, gp, :]),
                              (SBR[prt, gp, mb0:mb0 + n], CLR[prt, 8 + gp, :]),
                              (SBI[prt, gp, mb0:mb0 + n], CLI[prt, 8 + gp, :])]
                        for mi, (lh, rh) in enumerate(mm):
                            k.op(k.pe, lambda: nc.tensor.matmul(o, lh, rh, start=(mi == 0), stop=(mi == 5)),
                                 reads=["Ut", "DLt", "CLR", "CLI", SFR.name, SFI.name, SBR.name, SBI.name], writes=[pk])
                    src = py[0:n, :].rearrange("p (g j h) -> p g j h", g=2, h=16)
                    for gg in range(2):
                        g = g2 * 2 + gg
                        eng = k.act if gg == 0 else k.dve
                        dst = y_[0:n, :, g * 16:(g + 1) * 16]
                        if eng is k.act:
                            k.op(eng, lambda: nc.scalar.copy(dst, src[:, gg]), reads=[pk], writes=[yk])
                        else:
                            k.op(eng, lambda: nc.vector.tensor_copy(dst, src[:, gg]), reads=[pk], writes=[yk])
                dsb = dsk[0:n].unsqueeze(1).broadcast_to([n, 16, 256])
                k.op(k.pool, lambda: nc.gpsimd.tensor_tensor(u_[0:n], u_[0:n], dsb, ALU.mult), reads=[uk, "dsk"], writes=[uk])
                dv(lambda: nc.vector.tensor_tensor(y_[0:n], y_[0:n], u_[0:n], ALU.add), [yk, uk], [yk])
                k.op(k.pool, lambda: nc.gpsimd.tensor_tensor(t_[0:n], y_[0:n], y_[0:n], ALU.mult), reads=[yk], writes=[tk])
                dv(lambda: nc.vector.tensor_scalar(t_[0:n], t_[0:n], 0.044715, 1.0, ALU.mult, ALU.add), [tk], [tk])
                k.op(k.pool, lambda: nc.gpsimd.tensor_tensor(t_[0:n], t_[0:n], y_[0:n], ALU.mult), reads=[tk, yk], writes=[tk])
                k.op(k.act, lambda: nc.scalar.activation(t_[0:n], t_[0:n], AF.Sigmoid, scale=1.5957691216057308), reads=[tk], writes=[tk])
                dv(lambda: nc.vector.tensor_tensor(y_[0:n], y_[0:n], t_[0:n], ALU.mult), [yk, tk], [yk])
                k.dma("gpsimd", zsv[c0:c0 + n], y_[0:n], reads=[yk])


def _phase_s5b(self, l):
    nc, k, p = self.nc, self.k, self.L[l]

    def dv(fn, reads, writes):
        return k.op(k.dve, fn, reads=reads, writes=writes)

    with contextlib.ExitStack() as sb_:
        wgf = self.sb(sb_, "g_wgf", [128, 2, 256])
        wgb = self.sb(sb_, "g_wgb", [128, 2, 256], BF16)
        bgl = self.sb(sb_, "g_bgl", [128, 2])
        zT = self.sb(sb_, "g_zT", [128, 2, TT])
        zTb = self.sb(sb_, "g_zTb", [128, 2, TT], BF16)
        zt = [self.sb(sb_, f"g_zt{i}", [128, 256]) for i in range(2)]
        gsl = self.sb(sb_, "g_gs", [128, TT])
        sgl = self.sb(sb_, "g_sg", [128, TT])
        yb = self.sb(sb_, "g_yb", [128, TT], BF16)
        pz = [self.ps(sb_, f"g_pz{i}", [128, 512]) for i in range(2)]
        pg = [self.ps(sb_, f"g_pg{i}", [128, 512]) for i in range(2)]
        k.dma("sync", wgf[:], p["wglu"].rearrange("(k p) n -> p k n", p=128), writes=["wgf"])
        k.dma("sync", bgl[:], p["bglu"], writes=["bgl"])
        dv(lambda: nc.vector.tensor_copy(wgb[:], wgf[:]), ["wgf"], ["wgb"])
        for tt in range(NTT):
            z_ = zt[tt % 2]
            zk = f"zt{tt % 2}"
            pz_ = pz[tt % 2]
            k.dma("sync", z_[:], self.zs[tt * 128:(tt + 1) * 128, :], writes=[zk])
            for c_ in range(2):
                k.op(k.pe, lambda: nc.tensor.matmul(pz_[:, c_ * 128:(c_ + 1) * 128], z_[:, c_ * 128:(c_ + 1) * 128], self.identf[:],
                                                    start=True, stop=True),
                     reads=[zk, "identf"], writes=[f"pz{tt % 2}"])
            for c_ in range(2):
                k.op(k.dve, lambda: nc.vector.tensor_copy(zT[:, c_, tt * 128:(tt + 1) * 128], pz_[:, c_ * 128:(c_ + 1) * 128]),
                     reads=[f"pz{tt % 2}"], writes=["zT"])
                dv(lambda: nc.vector.tensor_copy(zTb[:, c_, tt * 128:(tt + 1) * 128], pz_[:, c_ * 128:(c_ + 1) * 128]),
                   [f"pz{tt % 2}"], ["zTb"])
        if getattr(self, "s5_stop", 9) <= 5:
            return
        nblk = [(i * 512, min(512, TT - i * 512)) for i in range((TT + 511) // 512)]
        for co in range(2):
            rows = slice(256 + co * 128, 256 + (co + 1) * 128)
            k.dma("sync", gsl[:], self.gsT[co * 128:(co + 1) * 128, :], writes=["gsl"])
            k.op(k.act, lambda: nc.scalar.activation(gsl[:], gsl[:], AF.Silu), reads=["gsl"], writes=["gsl"])
            for bi, (c0, w) in enumerate(nblk):
                pg_ = pg[bi % 2]
                for ci_ in range(2):
                    k.op(k.pe, lambda: nc.tensor.matmul(pg_[:, :w], wgb[:, ci_, co * 128:(co + 1) * 128], zTb[:, ci_, c0:c0 + w],
                                                        start=(ci_ == 0), stop=(ci_ == 1)), reads=["wgb", "zTb"], writes=[f"pg{bi % 2}"])
                k.op(k.act, lambda: nc.scalar.activation(sgl[:, c0:c0 + w], pg_[:, :w], AF.Sigmoid, bias=bgl[:, co:co + 1]),
                     reads=[f"pg{bi % 2}", "bgl"], writes=["sgl"])
            dv(lambda: nc.vector.tensor_tensor(sgl[:], sgl[:], zT[:, co, :], ALU.mult), ["sgl", "zT"], ["sgl"])
            k.op(k.pool, lambda: nc.gpsimd.tensor_tensor(yb[:], sgl[:], gsl[:], ALU.mult), reads=["sgl", "gsl"], writes=["yb"])
            k.dma("gpsimd", self.yas[rows, :], yb[:], reads=["yb"])


Builder.phase_s5 = _phase_s5
Builder.phase_s5b = _phase_s5b
```

```python
import contextlib
import numpy as np
import concourse.bass as bass
import concourse.mybir as mybir
from concourse.bass_utils import run_bass_kernel_spmd

F32 = mybir.dt.float32
BF16 = mybir.dt.bfloat16
ALU = mybir.AluOpType
AF = mybir.ActivationFunctionType
AX = mybir.AxisListType


class _Eng:
    def __init__(self, name, handle, sem, inc):
        self.name = name
        self.h = handle
        self.sem = sem
        self.inc = inc
        self.count = 0
        self.seen = {}


class _Reg:
    __slots__ = ("w", "r")

    def __init__(self):
        self.w = None
        self.r = {}


class K:
    def __init__(self, nc, stack, n_dma=10):
        self.nc = nc
        self.stack = stack
        self.regs = {}
        mk = lambda n: stack.enter_context(nc.semaphore(n))
        self.pe = _Eng("pe", nc.tensor, mk("s_pe"), 1)
        self.act = _Eng("act", nc.scalar, mk("s_act"), 1)
        self.dve = _Eng("dve", nc.vector, mk("s_dve"), 1)
        self.pool = _Eng("pool", nc.gpsimd, mk("s_pool"), 1)
        self.compute = [self.pe, self.act, self.dve, self.pool]
        self.dq = {}
        for qn, qh in (("sync", nc.sync), ("gpsimd", nc.gpsimd), ("scalar", nc.scalar)):
            self.dq[qn] = [
                _Eng(f"d_{qn}{i}", qh, mk(f"s_d{qn}{i}"), 16) for i in range(n_dma)
            ]
        self.dq_rr = {qn: 0 for qn in self.dq}
        self.qseen = {"sync": {}, "gpsimd": self.pool.seen, "scalar": self.act.seen}

    def reg(self, key):
        r = self.regs.get(key)
        if r is None:
            r = self.regs[key] = _Reg()
        return r

    def _deps(self, reads, writes):
        deps = {}
        for k in reads:
            r = self.reg(k)
            if r.w is not None:
                e, c = r.w
                deps[e] = max(deps.get(e, 0), c)
        for k in writes:
            r = self.reg(k)
            if r.w is not None:
                e, c = r.w
                deps[e] = max(deps.get(e, 0), c)
            for e, c in r.r.items():
                deps[e] = max(deps.get(e, 0), c)
        return deps

    def _commit(self, eng, reads, writes):
        c = eng.count
        for k in reads:
            self.reg(k).r[eng] = c
        for k in writes:
            r = self.reg(k)
            r.w = (eng, c)
            r.r = {}

    def op(self, eng, fn, reads=(), writes=()):
        deps = self._deps(reads, writes)
        for e, c in deps.items():
            if e is eng and eng is self.pe:
                continue
            if eng.seen.get(e, 0) < c:
                eng.h.wait_ge(e.sem, c * e.inc)
                eng.seen[e] = c
        ins = fn()
        eng.count += 1
        ins.then_inc(eng.sem, eng.inc)
        self._commit(eng, reads, writes)
        return ins

    def dma(self, q, out, in_, reads=(), writes=(), **kw):
        lst = self.dq[q]
        i = self.dq_rr[q]
        self.dq_rr[q] = (i + 1) % len(lst)
        d = lst[i]
        seen = self.qseen[q]
        if d.count > 0 and seen.get(d, 0) < d.count:
            d.h.wait_ge(d.sem, d.count * 16)
            seen[d] = d.count
        deps = self._deps(reads, writes)
        for e, c in deps.items():
            if seen.get(e, 0) < c:
                d.h.wait_ge(e.sem, c * e.inc)
                seen[e] = c
        ins = d.h.dma_start(out=out, in_=in_, **kw)
        d.count += 1
        ins.then_inc(d.sem, 16)
        self._commit(d, reads, writes)
        return ins

    def finish(self):
        seen = self.qseen["sync"]
        allengs = list(self.compute)
        for lst in self.dq.values():
            allengs += lst
        for e in allengs:
            if e.count > 0 and seen.get(e, 0) < e.count:
                self.nc.sync.wait_ge(e.sem, e.count * e.inc)
                seen[e] = e.count

    def barrier(self):
        allengs = list(self.compute)
        for lst in self.dq.values():
            allengs += lst
        for h, seen in ((self.nc.tensor, self.pe.seen), (self.nc.scalar, self.act.seen),
                        (self.nc.vector, self.dve.seen), (self.nc.gpsimd, self.pool.seen),
                        (self.nc.sync, self.qseen["sync"])):
            for e in allengs:
                if e.count > 0 and seen.get(e, 0) < e.count:
                    h.wait_ge(e.sem, e.count * e.inc)
                    seen[e] = e.count
        self.regs.clear()


D = 1024
T = 4096
TC = 256
TT = T + TC
NTT = TT // 128
DEPTH = 2
EPS = 1e-6
NCH = TT // 16
WCOL = {"ua": (0, 256), "k": (256, 512), "ga": (768, 256), "gs": (1024, 256), "q": (1280, 512),
        "v": (1792, 512), "gd": (2304, 512), "us": (2816, 256)}
TWO_PI = 2.0 * np.pi


class Builder:
    def __init__(self, debug=False, layers=(0, 1), phases=("M", "P", "L", "S", "A")):
        self.debug = debug
        self.layers = layers
        self.phases = phases
        self.nc = bass.Bass("TRN2", target_bir_lowering=False)
        self.ins = {}
        self.scr = {}

    def din(self, name, shape, dt=F32):
        ap = self.nc.dram_tensor(name, list(shape), dt, kind="ExternalInput").ap()
        self.ins[name] = ap
        return ap

    def dscr(self, name, shape, dt=F32):
        kind = "ExternalOutput" if self.debug else "Internal"
        ap = self.nc.dram_tensor(name, list(shape), dt, kind=kind).ap()
        self.scr[name] = ap
        return ap

    def _uname(self, name):
        self._uid = getattr(self, "_uid", 0) + 1
        return f"{name}_u{self._uid}"

    def sb(self, st, name, shape, dt=F32):
        return st.enter_context(self.nc.sbuf_tensor(self._uname(name), list(shape), dt))

    def ps(self, st, name, shape, dt=F32):
        return st.enter_context(self.nc.psum_tensor(self._uname(name), list(shape), dt))

    def declare(self):
        d = self.din
        self.hin = d("hin", [TT, D])
        self.cvec = d("cvec", [128, 8, 2])
        self.ident = d("ident", [128, 128])
        self.perm = d("perm", [128, 128])
        self.sel2 = d("sel2", [2, 2, 128])
        self.ropeC = d("ropeC", [128, T])
        self.ropeS = d("ropeS", [128, T])
        self.finalg = d("finalg", [1, D])
        self.s5idx = d("s5idx", [128, 16, NCH + 1])
        self.s5exp = d("s5exp", [128, 3, 16, 16])
        self.s5mask = d("s5mask", [128, 2, 2, 256])
        self.L = []
        for l in range(DEPTH):
            p = {}
            p["w_mod"] = d(f"w_mod{l}", [D, 3 * D])
            p["bmodR"] = d(f"bmodR{l}", [2, 3 * D])
            p["normg"] = d(f"normg{l}", [128, 8, 2])
            p["w_in"] = d(f"w_in{l}", [D, 3 * D])
            p["w_out"] = d(f"w_out{l}", [D, D])
            p["convw"] = d(f"convw{l}", [128, 2, 4])
            p["convb"] = d(f"convb{l}", [128, 2])
            p["wax"] = d(f"wax{l}", [128, 2, 2, 2, 128])
            p["bax"] = d(f"bax{l}", [128, 2, 2, 2])
            p["lrulam"] = d(f"lrulam{l}", [128, 2, 2])
            p["s5lam"] = d(f"s5lam{l}", [128, 3, 16])
            p["s5b"] = d(f"s5b{l}", [128, 2, 16, 16])
            p["s5c"] = d(f"s5c{l}", [128, 2, 16, 16])
            p["s5d"] = d(f"s5d{l}", [1, 256])
            p["wglu"] = d(f"wglu{l}", [256, 256])
            p["bglu"] = d(f"bglu{l}", [128, 2])
            p["dalam"] = d(f"dalam{l}", [1, 256])
            p["dag"] = d(f"dag{l}", [128, 1])
            self.L.append(p)
        s = self.dscr
        self.uaT = s("uaT", [256, TT])
        self.gaT = s("gaT", [256, TT])
        self.gsT = s("gsT", [256, TT])
        self.kT = s("kT", [512, TT], BF16)
        self.qT = s("qT", [512, TT], BF16)
        self.v_s = s("v_s", [TT, 512], BF16)
        self.gdT = s("gdT", [512, TT])
        self.us_s = s("us_s", [TT, 256])
        self.yas = s("yas", [512, TT], BF16)
        self.hbuf = s("hbuf", [TT, D])
        self.zs = s("zs", [TT, 256])
        self.out = self.nc.dram_tensor("out", [T, D], F32, kind="ExternalOutput").ap()
        if self.debug:
            self.dbg_mod = s("dbg_mod", [128, 24, 2])
            self.dbg_gate = s("dbg_gate", [128, 2, D])
            self.dbg_hl = s("dbg_hl", [2, 2, 128, TT])

    def build(self):
        nc = self.nc
        self.declare()
        with contextlib.ExitStack() as st:
            self.k = K(nc, st)
            k = self.k
            g = lambda name, shape, dt=F32: self.sb(st, name, shape, dt)
            self.identf = g("identf", [128, 128])
            self.identb = g("identb", [128, 128], BF16)
            self.permf = g("permf", [128, 128])
            self.ones1 = g("ones1", [1, 128])
            self.sc = g("sc", [128, 8, 2])
            self.modv = g("modv", [128, 24, 2])
            self.gmul = g("gmul", [128, 8, 2])
            self.gateb = g("gateb", [128, 2, D])
            self.nlam = g("nlam", [128, 1])
            k.dma("sync", self.identf[:], self.ident, writes=["identf"])
            k.dma("sync", self.permf[:], self.perm, writes=["permf"])
            k.dma("sync", self.sc[:], self.cvec, writes=["sc"])
            k.op(k.dve, lambda: nc.vector.tensor_copy(self.identb[:], self.identf[:]), reads=["identf"], writes=["identb"])
            k.op(k.dve, lambda: nc.vector.memset(self.ones1[:], 1.0), writes=["ones1"])
            k.op(k.act, lambda: nc.scalar.activation(self.sc[:], self.sc[:], AF.Silu), reads=["sc"], writes=["sc"])
            k.barrier()
            for l in self.layers:
                last = (l == DEPTH - 1)
                hsrc = self.hin if l == 0 else self.hbuf
                with contextlib.ExitStack() as pst:
                    wbf = self.proj_prefetch(l, pst) if "P" in self.phases else None
                    if "M" in self.phases:
                        self.phase_mod(l)
                        k.barrier()
                    if "P" in self.phases:
                        self.phase_proj(l, hsrc, wbf)
                        k.barrier()
                if "L" in self.phases:
                    self.phase_lru(l)
                    k.barrier()
                if "S" in self.phases:
                    self.phase_s5(l)
                    k.barrier()
                with contextlib.ExitStack() as ast:
                    pre = None
                    if "A" in self.phases:
                        pre = self.att_prefetch(l, ast)
                    if "S" in self.phases and getattr(self, "s5_stop", 9) > 4:
                        self.phase_s5b(l)
                        k.barrier()
                    if "A" in self.phases:
                        self.phase_att(l, hsrc, last, pre)
                        k.barrier()
            k.finish()
        return nc

    def phase_mod(self, l):
        nc, k, p = self.nc, self.k, self.L[l]
        with contextlib.ExitStack() as ph:
            wb = [self.sb(ph, f"m_wb{i}", [128, 8, 512]) for i in range(3)]
            bmr = self.sb(ph, "m_bmr", [2, 3 * D])
            ng = self.sb(ph, "m_ng", [128, 8, 2])
            sel = self.sb(ph, "m_sel", [2, 2, 128])
            rowv = self.sb(ph, "m_rowv", [2, 3 * D])
            psr = [self.ps(ph, f"m_psr{i}", [128, 512]) for i in range(2)]
            psT = self.ps(ph, "m_psT", [128, 512])
            psb = [self.ps(ph, f"m_psb{i}", [128, 512]) for i in range(2)]
            k.dma("scalar", bmr[:], p["bmodR"], writes=["bmr"])
            k.dma("scalar", ng[:], p["normg"], writes=["ng"])
            k.dma("scalar", sel[:], self.sel2, writes=["sel"])
            wsrc = p["w_mod"].rearrange("(k p) n -> p k n", p=128)
            for cb in range(6):
                w = wb[cb % 3]
                wk = f"wb{cb % 3}"
                for k4 in range(2):
                    k.dma("sync", w[:, k4 * 4:(k4 + 1) * 4, :],
                          wsrc[:, k4 * 4:(k4 + 1) * 4, cb * 512:(cb + 1) * 512], writes=[wk + f"_{k4}"])
                pr_ = psr[cb % 2]
                for k8 in range(8):
                    k.op(k.pe, lambda: nc.tensor.matmul(pr_[0:2, :], self.sc[:, k8, :], w[:, k8, :], start=(k8 == 0), stop=(k8 == 7)),
                         reads=[wk + f"_{k8 // 4}", "sc"], writes=[f"psr{cb % 2}"])
                cs = slice(cb * 512, (cb + 1) * 512)
                k.op(k.dve, lambda: nc.vector.tensor_tensor(rowv[:, cs], pr_[0:2, :], bmr[:, cs], ALU.add),
                     reads=[f"psr{cb % 2}", "bmr"], writes=[f"rowv{cb}"])
                for j in range(4):
                    ft = cb * 4 + j
                    k.op(k.pe, lambda: nc.tensor.transpose(psT[:, ft * 2:(ft + 1) * 2], rowv[0:2, ft * 128:(ft + 1) * 128], self.identf[0:2, 0:2]),
                         reads=[f"rowv{cb}", "identf"], writes=["psT"])
            k.op(k.dve, lambda: nc.vector.tensor_copy(self.modv[:], psT[:, 0:48].rearrange("p (a b) -> p a b", b=2)),
                 reads=["psT"], writes=["modv"])
            k.op(k.dve, lambda: nc.vector.scalar_tensor_tensor(self.gmul[:], self.modv[:, 8:16, :], 1.0, ng[:], ALU.add, ALU.mult),
                 reads=["modv", "ng"], writes=["gmul"])
            for v in range(2):
                for hf in range(2):
                    pb = psb[hf]
                    k.op(k.pe, lambda: nc.tensor.matmul(pb[:], sel[0:2, v, :], rowv[0:2, 2 * D + hf * 512:2 * D + (hf + 1) * 512], start=True, stop=True),
                         reads=["sel", f"rowv{4 + hf}"], writes=[f"psb{hf}"])
                    k.op(k.dve, lambda: nc.vector.tensor_copy(self.gateb[:, v, hf * 512:(hf + 1) * 512], pb[:]),
                         reads=[f"psb{hf}"], writes=[f"gateb{v}{hf}"])
            if self.debug:
                k.dma("gpsimd", self.dbg_mod, self.modv[:], reads=["modv"])
                k.dma("gpsimd", self.dbg_gate, self.gateb[:], reads=["gateb00", "gateb01", "gateb10", "gateb11"])

    def proj_prefetch(self, l, st_):
        nc, k, p = self.nc, self.k, self.L[l]
        wbf = self.sb(st_, "p_wbf", [128, 8, 3 * D], BF16)
        stg = [self.sb(st_, f"p_stg{i}", [128, 8, 256]) for i in range(3)]
        wsrc = p["w_in"].rearrange("(k p) n -> p k n", p=128)
        for cb in range(12):
            s_ = stg[cb % 3]
            k.dma("scalar", s_[:], wsrc[:, :, cb * 256:(cb + 1) * 256], writes=[f"pstg{cb % 3}"])
            if cb % 3 == 2:
                k.op(k.pool, lambda: nc.gpsimd.tensor_copy(wbf[:, :, cb * 256:(cb + 1) * 256], s_[:]), reads=[f"pstg{cb % 3}"], writes=[f"wbf{cb}"])
            else:
                k.op(k.dve, lambda: nc.vector.tensor_copy(wbf[:, :, cb * 256:(cb + 1) * 256], s_[:]), reads=[f"pstg{cb % 3}"], writes=[f"wbf{cb}"])
        return wbf

    def phase_proj(self, l, hsrc, wbf):
        nc, k, p = self.nc, self.k, self.L[l]
        with contextlib.ExitStack() as ph:
            wkeys = []
            xb = [self.sb(ph, f"p_x{i}", [128, D]) for i in range(4)]
            junk = self.sb(ph, "p_junk", [128, D], BF16)
            xh = [self.sb(ph, f"p_xh{i}", [128, D], BF16) for i in range(4)]
            ss = self.sb(ph, "p_ss", [128, 4])
            nT = [self.sb(ph, f"p_nT{i}", [128, 8, 512], BF16) for i in range(2)]
            cosb = [self.sb(ph, f"p_cos{i}", [128, 512]) for i in range(2)]
            sinb = [self.sb(ph, f"p_sin{i}", [128, 512]) for i in range(2)]
            kf = [self.sb(ph, f"p_kf{i}", [128, 512]) for i in range(2)]
            t1 = [self.sb(ph, f"p_t1{i}", [128, 512]) for i in range(2)]
            t2 = [self.sb(ph, f"p_t2{i}", [128, 512]) for i in range(2)]
            of32 = [self.sb(ph, f"p_of{i}", [128, 512]) for i in range(3)]
            obf = [self.sb(ph, f"p_ob{i}", [128, 512], BF16) for i in range(3)]
            pT = [self.ps(ph, f"p_pT{i}", [128, 1024], BF16) for i in range(2)]
            pm = [self.ps(ph, f"p_pm{i}", [128, 512]) for i in range(4)]
            pr = [self.ps(ph, f"p_pr{i}", [128, 512]) for i in range(2)]
            blocks = [(0, 256)] + [(256 + 512 * i, 512) for i in range(8)]
            cnt = {"x": 0, "pm": 0, "pr": 0, "of": 0, "ob": 0, "kf": 0, "ev": 0}

            def rot(name, n):
                i = cnt[name] % n
                cnt[name] += 1
                return i

            def stage1a(bi):
                t0, nt = blocks[bi]
                lat = t0 >= TC
                if lat:
                    cb_, sb_ = cosb[bi % 2], sinb[bi % 2]
                    k.dma("sync", cb_[:], self.ropeC[:, t0 - TC:t0 - TC + 512], writes=[f"cos{bi % 2}"])
                    k.dma("sync", sb_[:], self.ropeS[:, t0 - TC:t0 - TC + 512], writes=[f"sin{bi % 2}"])
                for j in range(nt // 128):
                    xi = (bi * 4 + j) % 4
                    x_, xh_ = xb[xi], xh[xi]
                    k.dma("sync", x_[:], hsrc[t0 + j * 128:t0 + (j + 1) * 128, :], writes=[f"x{xi}"])
                    k.op(k.act, lambda: nc.scalar.activation(junk[:], x_[:], AF.Square, accum_out=ss[:, xi:xi + 1]),
                         reads=[f"x{xi}"], writes=["junk", f"ss{xi}"])
                    k.op(k.act, lambda: nc.scalar.activation(ss[:, xi:xi + 1], ss[:, xi:xi + 1], AF.Sqrt, scale=1.0 / D, bias=EPS),
                         reads=[f"ss{xi}"], writes=[f"ss{xi}"])
                    k.op(k.dve, lambda: nc.vector.reciprocal(ss[:, xi:xi + 1], ss[:, xi:xi + 1]),
                         reads=[f"ss{xi}"], writes=[f"ss{xi}"])
                    k.op(k.dve, lambda: nc.vector.tensor_scalar_mul(xh_[:], x_[:], ss[:, xi:xi + 1]),
                         reads=[f"x{xi}", f"ss{xi}"], writes=[f"xh{xi}"])

            def stage1b(bi):
                t0, nt = blocks[bi]
                v = 1 if t0 < TC else 0
                nTb = nT[bi % 2]
                nk = f"nT{bi % 2}"
                for j in range(nt // 128):
                    xi = (bi * 4 + j) % 4
                    xh_ = xh[xi]
                    pi_ = rot("x", 2)
                    pT_ = pT[pi_]
                    for k8 in range(8):
                        k.op(k.pe, lambda: nc.tensor.transpose(pT_[:, k8 * 128:(k8 + 1) * 128], xh_[:, k8 * 128:(k8 + 1) * 128], self.identb[:]),
                             reads=[f"xh{xi}", "identb"], writes=[f"pT{pi_}"])
                    for k8 in range(8):
                        if k8 % 2 == 0:
                            k.op(k.dve, lambda: nc.vector.tensor_scalar(
                                nTb[:, k8, j * 128:(j + 1) * 128], pT_[:, k8 * 128:(k8 + 1) * 128],
                                self.gmul[:, k8, v:v + 1], self.modv[:, k8, v:v + 1], ALU.mult, ALU.add),
                                reads=[f"pT{pi_}", "gmul", "modv"], writes=[nk])
                        else:
                            k.op(k.act, lambda: nc.scalar.activation(
                                nTb[:, k8, j * 128:(j + 1) * 128], pT_[:, k8 * 128:(k8 + 1) * 128], AF.Identity,
                                scale=self.gmul[:, k8, v:v + 1], bias=self.modv[:, k8, v:v + 1]),
                                reads=[f"pT{pi_}", "gmul", "modv"], writes=[nk])

            def stage2(bi, part):
                t0, nt = blocks[bi]
                ntile = nt // 128
                nTb = nT[bi % 2]
                nk = f"nT{bi % 2}"
                lat = t0 >= TC
                cb_, sb_ = cosb[bi % 2], sinb[bi % 2]
                pending = []
                for name, dst in ((("ua", self.uaT), ("ga", self.gaT), ("gs", self.gsT), ("gd", self.gdT), ("k", self.kT), ("q", self.qT)) if part == 0 else []):
                    co, w = WCOL[name]
                    for mt in range(w // 128):
                        pi = rot("pm", 4)
                        pm_ = pm[pi]
                        for k8 in range(8):
                            k.op(k.pe, lambda k8=k8, pm_=pm_, co=co, mt=mt: nc.tensor.matmul(
                                pm_[:, :nt], wbf[:, k8, co + mt * 128:co + (mt + 1) * 128], nTb[:, k8, :nt],
                                start=(k8 == 0), stop=(k8 == 7)), reads=wkeys + [nk], writes=[f"pm{pi}"])
                        while pending:
                            pending.pop(0)()
                        rows = slice(mt * 128, (mt + 1) * 128)
                        if name in ("ua", "ga", "gs", "gd"):
                            oi = rot("of", 3)
                            o_ = of32[oi]
                            k.op(k.act, lambda o_=o_, pm_=pm_: nc.scalar.copy(o_[:, :nt], pm_[:, :nt]),
                                 reads=[f"pm{pi}"], writes=[f"of{oi}"])
                            k.dma("gpsimd", dst[rows, t0:t0 + nt], o_[:, :nt], reads=[f"of{oi}"])
                        elif not lat:
                            oi = rot("ob", 3)
                            o_ = obf[oi]
                            k.op(k.act, lambda o_=o_, pm_=pm_: nc.scalar.copy(o_[:, :nt], pm_[:, :nt]),
                                 reads=[f"pm{pi}"], writes=[f"ob{oi}"])
                            k.dma("gpsimd", dst[rows, t0:t0 + nt], o_[:, :nt], reads=[f"ob{oi}"])
                        else:
                            ki = rot("kf", 2)
                            kf_, t1_, t2_, pr_ = kf[ki], t1[ki], t2[ki], pr[ki]
                            k.op(k.act, lambda kf_=kf_, pm_=pm_: nc.scalar.copy(kf_[:], pm_[:]),
                                 reads=[f"pm{pi}"], writes=[f"kf{ki}"])
                            k.op(k.pool, lambda kf_=kf_, t1_=t1_: nc.gpsimd.tensor_tensor(t1_[:], kf_[:], cb_[:], ALU.mult),
                                 reads=[f"kf{ki}", f"cos{bi % 2}"], writes=[f"t1{ki}"])

                            def part_b(kf_=kf_, t1_=t1_, t2_=t2_, pr_=pr_, ki=ki, dst=dst, rows=rows):
                                k.op(k.pe, lambda: nc.tensor.matmul(pr_[:], self.permf[:], kf_[:], start=True, stop=True),
                                     reads=[f"kf{ki}", "permf"], writes=[f"pr{ki}"])
                                k.op(k.dve, lambda: nc.vector.tensor_tensor(t2_[:], pr_[:], sb_[:], ALU.mult),
                                     reads=[f"pr{ki}", f"sin{bi % 2}"], writes=[f"t2{ki}"])
                                oi = rot("ob", 3)
                                o_ = obf[oi]
                                k.op(k.dve, lambda: nc.vector.tensor_tensor(o_[:], t1_[:], t2_[:], ALU.add),
                                     reads=[f"t1{ki}", f"t2{ki}"], writes=[f"ob{oi}"])
                                k.dma("gpsimd", dst[rows, t0:t0 + nt], o_[:, :nt], reads=[f"ob{oi}"])
                            pending.append(part_b)
                while pending:
                    pending.pop(0)()
                for j in (range(ntile) if part == 1 else []):
                    r0 = t0 + j * 128
                    for name, dst in (("v", self.v_s), ("us", self.us_s)):
                        co, w = WCOL[name]
                        pi = rot("pm", 4)
                        pm_ = pm[pi]
                        for k8 in range(8):
                            k.op(k.pe, lambda k8=k8, pm_=pm_, co=co, w=w, j=j: nc.tensor.matmul(
                                pm_[:, :w], nTb[:, k8, j * 128:(j + 1) * 128], wbf[:, k8, co:co + w],
                                start=(k8 == 0), stop=(k8 == 7)), reads=wkeys + [nk], writes=[f"pm{pi}"])
                        while pending:
                            pending.pop(0)()
                        ev = rot("ev", 2)
                        eng = k.act if ev == 0 else k.dve
                        if name == "v":
                            oi = rot("ob", 3)
                            o_ = obf[oi]
                            key = f"ob{oi}"
                        else:
                            oi = rot("of", 3)
                            o_ = of32[oi]
                            key = f"of{oi}"
                        if eng is k.act:
                            k.op(eng, lambda o_=o_, pm_=pm_, w=w: nc.scalar.copy(o_[:, :w], pm_[:, :w]), reads=[f"pm{pi}"], writes=[key])
                        else:
                            k.op(eng, lambda o_=o_, pm_=pm_, w=w: nc.vector.tensor_copy(o_[:, :w], pm_[:, :w]), reads=[f"pm{pi}"], writes=[key])
                        k.dma("gpsimd", dst[r0:r0 + 128, :], o_[:, :w], reads=[key])

            stage1a(0)
            stage1b(0)
            for bi in range(len(blocks)):
                if bi + 1 < len(blocks):
                    stage1a(bi + 1)
                stage2(bi, 0)
                if bi + 1 < len(blocks):
                    stage1b(bi + 1)
                stage2(bi, 1)


def _fp(a):
    return np.ascontiguousarray(a, dtype=np.float32)


def _const_tables():
    c = {}
    c["ident"] = np.eye(128, dtype=np.float32)
    r = np.arange(128)
    partner = np.where((r % 32) < 16, r + 16, r - 16)
    perm = np.zeros((128, 128), np.float32)
    perm[partner, r] = 1.0
    c["perm"] = perm
    sel2 = np.zeros((2, 2, 128), np.float32)
    sel2[0, 0, :] = 1.0
    sel2[1, 1, :] = 1.0
    c["sel2"] = sel2
    f = np.arange(16, dtype=np.float32)
    inv_freq = np.exp(np.float32(-np.log(10000.0)) * f / np.float32(16)).astype(np.float32)
    t = np.arange(T)
    row_pos = (t // 64).astype(np.float32)
    col_pos = (t % 64).astype(np.float32)
    d64 = r % 64
    is_row = d64 < 32
    fidx = (d64 % 32) % 16
    ang = np.where(is_row[:, None], row_pos[None, :], col_pos[None, :]).astype(np.float32) * inv_freq[fidx][:, None]
    ang = ang.astype(np.float32)
    sign = np.where((d64 % 32) < 16, -1.0, 1.0).astype(np.float32)
    c["ropeC"] = np.cos(ang).astype(np.float32)
    c["ropeS"] = (np.sin(ang).astype(np.float32) * sign[:, None]).astype(np.float32)
    m = np.arange(NCH + 1, dtype=np.float32)
    idx = np.zeros((128, 16, NCH + 1), np.float32)
    idx[:, 0:8, :] = m[None, None, :]
    idx[:, 8:16, :] = -m[None, None, :]
    c["s5idx"] = idx
    e = np.zeros((128, 3, 16, 16), np.float32)
    j = np.arange(16, dtype=np.float32)
    e[:, 0, 0:8, :] = 15 - j
    e[:, 0, 8:16, :] = j
    e[:, 1, 0:8, :] = j + 1
    e[:, 1, 8:16, :] = 16 - j
    e[:, 2, 0:8, :] = j - 15
    e[:, 2, 8:16, :] = -j
    c["s5exp"] = e
    pidx = np.arange(128)
    i8 = pidx // 16
    jj = np.repeat(np.arange(16), 16)
    msk = np.zeros((128, 2, 2, 256), np.float32)
    for kt in range(2):
        i = (8 * kt + i8)[:, None]
        msk[:, 0, kt, :] = (jj[None, :] >= i)
        msk[:, 1, kt, :] = (i >= jj[None, :])
    c["s5mask"] = msk
    return c


_PERM_COLS = None


def _wcol_perm():
    o = {"ua": (0, 256), "us": (256, 256), "k": (512, 512), "v": (1024, 512),
         "ga": (1536, 256), "gs": (1792, 256), "q": (2048, 512), "gd": (2560, 512)}
    idx = np.zeros(3 * D, np.int64)
    for name, (off, w) in WCOL.items():
        so, sw = o[name]
        idx[off:off + w] = np.arange(so, so + sw)
    return idx


def _fpart(vec, ntile):
    return _fp(np.asarray(vec).reshape(ntile, 128).T)


def _layer_inputs(inp, l):
    o = {}
    o[f"w_mod{l}"] = _fp(inp["w_mod"][l])
    o[f"bmodR{l}"] = _fp(np.repeat(inp["b_mod"][l][None, :], 2, axis=0))
    ng = _fpart(inp["norm_g"][l], 8)
    o[f"normg{l}"] = _fp(np.repeat(ng[:, :, None], 2, axis=2))
    o[f"w_in{l}"] = _fp(inp["w_in"][l][:, _wcol_perm()])
    o[f"w_out{l}"] = _fp(inp["w_out"][l])
    cw = inp["lru_conv_w"][l]
    o[f"convw{l}"] = _fp(cw.T.reshape(2, 128, 4).transpose(1, 0, 2))
    o[f"convb{l}"] = _fpart(inp["lru_conv_b"][l], 2)
    wax = np.zeros((128, 2, 2, 2, 128), np.float32)
    for ai, nm in enumerate(("lru_wa", "lru_wx")):
        w = inp[nm][l]
        for d in range(2):
            for ct in range(2):
                for h in range(2):
                    wax[h * 64:(h + 1) * 64, ct, ai, d, h * 64:(h + 1) * 64] = w[d, 2 * ct + h]
    o[f"wax{l}"] = wax
    bax = np.zeros((128, 2, 2, 2), np.float32)
    for ai, nm in enumerate(("lru_ba", "lru_bx")):
        for d in range(2):
            bax[:, :, ai, d] = _fpart(inp[nm][l][d], 2)
    o[f"bax{l}"] = bax
    ll = np.zeros((128, 2, 2), np.float32)
    for d in range(2):
        ll[:, :, d] = _fpart(inp["lru_lam"][l][d], 2)
    o[f"lrulam{l}"] = ll

    def tp_layout(a):
        a = np.asarray(a)
        rest = a.shape[3:]
        a = a.reshape((2, 8, 2, 64) + rest)
        a = np.moveaxis(a, (2, 3), (0, 1))
        return a.reshape((128, 16) + rest)

    s5lam = np.zeros((128, 3, 16), np.float32)
    s5lam[:, 0] = tp_layout(inp["s5_lam_re"][l])
    s5lam[:, 1] = tp_layout(inp["s5_lam_im"][l])
    s5lam[:, 2] = tp_layout(np.repeat(inp["s5_log_dt"][l][:, :, None], 64, axis=2))
    o[f"s5lam{l}"] = s5lam
    sb_ = np.zeros((128, 2, 16, 16), np.float32)
    sb_[:, 0] = tp_layout(inp["s5_b_re"][l])
    sb_[:, 1] = tp_layout(inp["s5_b_im"][l])
    o[f"s5b{l}"] = sb_
    sc_ = np.zeros((128, 2, 16, 16), np.float32)
    sc_[:, 0] = tp_layout(np.swapaxes(inp["s5_c_re"][l], 2, 3))
    sc_[:, 1] = tp_layout(np.swapaxes(inp["s5_c_im"][l], 2, 3))
    o[f"s5c{l}"] = sc_
    o[f"s5d{l}"] = _fp(inp["s5_d"][l][None, :])
    o[f"wglu{l}"] = _fp(inp["s5_w_glu"][l])
    o[f"bglu{l}"] = _fpart(inp["s5_b_glu"][l], 2)
    o[f"dalam{l}"] = _fp(inp["da_lam"][l].reshape(1, 256))
    o[f"dag{l}"] = _fp(inp["da_norm_g"][l][:, None])
    return o


def prep_inputs(inp):
    shared = dict(_const_tables())
    shared["finalg"] = _fp(inp["final_g"][None, :])
    for l in range(DEPTH):
        shared.update(_layer_inputs(inp, l))
    maps = []
    for b in range(8):
        m = dict(shared)
        m["hin"] = _fp(np.concatenate([inp["ctx"][b], inp["x"][b]], axis=0))
        cv = np.zeros((128, 8, 2), np.float32)
        cv[:, :, 0] = _fpart(inp["c"][b], 8)
        cv[:, :, 1] = _fpart(inp["c_ctx"], 8)
        m["cvec"] = cv
        maps.append(m)
    return maps


_NC_CACHE = {}


def kernel(**inputs):
    inp = {k_: np.asarray(v) for k_, v in inputs.items()}
    maps = prep_inputs(inp)
    if "nc" not in _NC_CACHE:
        _NC_CACHE["nc"] = Builder().build()
    res = run_bass_kernel_spmd(_NC_CACHE["nc"], maps, core_ids=list(range(8)))
    return np.stack([np.asarray(r["out"]) for r in res.results], axis=0).astype(np.float32)


def _phase_lru(self, l):
    nc, k, p = self.nc, self.k, self.L[l]
    with contextlib.ExitStack() as ph:
        N = TT
        convw = self.sb(ph, "l_convw", [128, 2, 4])
        convb = self.sb(ph, "l_convb", [128, 2])
        waxf = self.sb(ph, "l_waxf", [128, 2, 2, 2, 128])
        waxb = self.sb(ph, "l_waxb", [128, 2, 2, 2, 128], BF16)
        bax = self.sb(ph, "l_bax", [128, 2, 2, 2])
        lam = self.sb(ph, "l_lam", [128, 2, 2])
        cl = self.sb(ph, "l_cl", [128, 2, 2])
        cl2 = self.sb(ph, "l_cl2", [128, 2, 2])
        ua = self.sb(ph, "l_ua", [128, N])
        xc = self.sb(ph, "l_xc", [128, N])
        xcb = self.sb(ph, "l_xcb", [128, N], BF16)
        sg = self.sb(ph, "l_sg", [128, N])
        gr = [self.sb(ph, f"l_gr{d}", [128, N]) for d in range(2)]
        gi = [self.sb(ph, f"l_gi{d}", [128, N]) for d in range(2)]
        a2 = [self.sb(ph, f"l_a2{d}", [128, N]) for d in range(2)]
        hd = [self.sb(ph, "l_h0", [128, N]), ua]
        yb = xcb
        pg = [self.ps(ph, f"l_pg{i}", [128, 512]) for i in range(4)]
        k.dma("sync", convw[:], p["convw"], writes=["convw"])
        k.dma("sync", convb[:], p["convb"], writes=["convb"])
        k.dma("sync", waxf[:], p["wax"], writes=["waxf"])
        k.dma("sync", bax[:], p["bax"], writes=["bax"])
        k.dma("sync", lam[:], p["lrulam"], writes=["lam"])
        k.op(k.dve, lambda: nc.vector.tensor_copy(waxb[:], waxf[:]), reads=["waxf"], writes=["waxb"])
        k.op(k.act, lambda: nc.scalar.activation(cl[:], lam[:], AF.Exp, scale=-1.0), reads=["lam"], writes=["cl"])
        k.op(k.act, lambda: nc.scalar.activation(cl[:], cl[:], AF.Ln, scale=1.0, bias=1.0), reads=["cl"], writes=["cl"])
        k.op(k.dve, lambda: nc.vector.tensor_scalar_mul(cl2[:], cl[:], -16.0), reads=["cl"], writes=["cl2"])
        k.op(k.dve, lambda: nc.vector.tensor_scalar_mul(cl[:], cl[:], -8.0), reads=["cl", "cl2"], writes=["cl"])
        segs = [(0, TC), (TC, TT)]
        nblk = [(i * 512, min(512, N - i * 512)) for i in range((N + 511) // 512)]
        pgi = 0
        for ct in range(2):
            rows = slice(ct * 128, (ct + 1) * 128)
            k.dma("sync", ua[:], self.uaT[rows, :], writes=["ua"])
            k.dma("scalar", sg[:], self.gaT[rows, :], writes=["sg"])
            k.op(k.dve, lambda: nc.vector.tensor_scalar(xc[:], ua[:], convw[:, ct, 2:3], convb[:, ct:ct + 1], ALU.mult, ALU.add),
                 reads=["ua", "convw", "convb"], writes=["xc"])
            for (s0, s1) in segs:
                for tap, off in ((0, -2), (1, -1), (3, 1)):
                    lo = max(s0, s0 - off)
                    hi = min(s1, s1 - off)
                    k.op(k.dve, lambda: nc.vector.scalar_tensor_tensor(
                        xc[:, lo:hi], ua[:, lo + off:hi + off], convw[:, ct, tap:tap + 1], xc[:, lo:hi], ALU.mult, ALU.add),
                        reads=["ua", "xc", "convw"], writes=["xc"])
            k.op(k.act, lambda: nc.scalar.copy(xcb[:], xc[:]), reads=["xc"], writes=["xcb"])
            for d in range(2):
                for bi, (c0, w) in enumerate(nblk):
                    pr_, pi_ = pg[pgi % 4], pg[(pgi + 1) % 4]
                    kr, ki = f"pg{pgi % 4}", f"pg{(pgi + 1) % 4}"
                    pgi += 2
                    k.op(k.pe, lambda: nc.tensor.matmul(pr_[:, :w], waxb[:, ct, 0, d, :], xcb[:, c0:c0 + w], start=True, stop=True),
                         reads=["waxb", "xcb"], writes=[kr])
                    k.op(k.pe, lambda: nc.tensor.matmul(pi_[:, :w], waxb[:, ct, 1, d, :], xcb[:, c0:c0 + w], start=True, stop=True),
                         reads=["waxb", "xcb"], writes=[ki])
                    k.op(k.act, lambda: nc.scalar.activation(gr[d][:, c0:c0 + w], pr_[:, :w], AF.Sigmoid, bias=bax[:, ct, 0, d:d + 1]),
                         reads=[kr, "bax"], writes=[f"gr{d}"])
                    k.op(k.act, lambda: nc.scalar.activation(gi[d][:, c0:c0 + w], pi_[:, :w], AF.Sigmoid, bias=bax[:, ct, 1, d:d + 1]),
                         reads=[ki, "bax"], writes=[f"gi{d}"])
                k.op(k.dve, lambda: nc.vector.tensor_tensor(gi[d][:], gi[d][:], xc[:], ALU.mult), reads=[f"gi{d}", "xc"], writes=[f"gi{d}"])
            for d in range(2):
                k.op(k.act, lambda: nc.scalar.activation(a2[d][:], gr[d][:], AF.Exp, scale=cl2[:, ct, d:d + 1]), reads=[f"gr{d}", "cl2"], writes=[f"a2{d}"])
                k.op(k.act, lambda: nc.scalar.activation(gr[d][:], gr[d][:], AF.Exp, scale=cl[:, ct, d:d + 1]), reads=[f"gr{d}", "cl", f"a2{d}"], writes=[f"gr{d}"])
            for d in range(2):
                k.op(k.act, lambda: nc.scalar.activation(a2[d][:], a2[d][:], AF.Sqrt, scale=-1.0, bias=1.0), reads=[f"a2{d}"], writes=[f"a2{d}"])
                k.op(k.dve, lambda: nc.vector.tensor_tensor(gi[d][:], gi[d][:], a2[d][:], ALU.mult), reads=[f"gi{d}", f"a2{d}"], writes=[f"gi{d}"])
            k.op(k.act, lambda: nc.scalar.activation(sg[:], sg[:], AF.Silu), reads=["sg"], writes=["sg"])
            k.op(k.dve, lambda: nc.vector.tensor_tensor_scan(hd[0][:], gr[0][:], gi[0][:], 0.0, ALU.mult, ALU.add),
                 reads=["gr0", "gi0"], writes=["h0"])
            k.op(k.dve, lambda: nc.vector.tensor_tensor_scan(hd[1][:, 0:TC][:, ::-1], gr[1][:, 0:TC][:, ::-1], gi[1][:, 0:TC][:, ::-1],
                                                              0.0, ALU.mult, ALU.add), reads=["gr1", "gi1"], writes=["ua"])
            k.op(k.dve, lambda: nc.vector.tensor_tensor_scan(hd[1][:, TC:TT][:, ::-1], gr[1][:, TC:TT][:, ::-1], gi[1][:, TC:TT][:, ::-1],
                                                              hd[1][:, 0:1], ALU.mult, ALU.add), reads=["gr1", "gi1", "ua"], writes=["ua"])
            if self.debug:
                for d in range(2):
                    k.dma("gpsimd", self.dbg_hl[ct, d], hd[d][:], reads=[("h0" if d == 0 else "ua")])
            k.op(k.pool, lambda: nc.gpsimd.tensor_tensor(hd[0][:], hd[0][:], sg[:], ALU.mult), reads=["h0", "sg"], writes=["h0"])
            k.op(k.dve, lambda: nc.vector.tensor_tensor(hd[1][:], hd[1][:], sg[:], ALU.mult), reads=["ua", "sg"], writes=["ua"])
            k.op(k.dve, lambda: nc.vector.tensor_tensor(yb[:], hd[0][:], hd[1][:], ALU.add), reads=["h0", "ua"], writes=["xcb"])
            k.dma("gpsimd", self.yas[rows, :], yb[:], reads=["xcb"])


Builder.phase_lru = _phase_lru


def _att_prefetch(self, l, st_):
    nc, k, p = self.nc, self.k, self.L[l]
    kTs = self.sb(st_, "a_kT", [128, 4, TT], BF16)
    vs = self.sb(st_, "a_v", [128, NTT, 512], BF16)
    if True:
        k.dma("scalar", kTs[:], self.kT.rearrange("(j p) t -> p j t", p=128), writes=["kTs"])
        vsrc = self.v_s.rearrange("(tt p) e -> p tt e", p=128)
        for q4 in range(0, NTT, 9):
            hi_ = min(NTT, q4 + 9)
            k.dma("scalar", vs[:, q4:hi_, :], vsrc[:, q4:hi_, :], writes=[f"vs{q4}"])
    return kTs, vs


def _phase_att(self, l, hsrc, last, pre):
    nc, k, p = self.nc, self.k, self.L[l]
    lam_init = 0.8 - 0.6 * float(np.exp(-0.3 * l))
    with contextlib.ExitStack() as ph:
        kTs, vs = pre
        woutb = self.sb(ph, "a_wout", [128, 8, D], BF16)
        with contextlib.ExitStack() as tmp_:
            stg = [self.sb(tmp_, f"a_stg{i}", [128, 8, 256]) for i in range(2)]
            wsrc = p["w_out"].rearrange("(k p) n -> p k n", p=128)
            for q4 in range(4):
                k.dma("sync", stg[q4 % 2][:], wsrc[:, :, q4 * 256:(q4 + 1) * 256], writes=[f"stg{q4 % 2}"])
                k.op(k.pool, lambda: nc.gpsimd.tensor_copy(woutb[:, :, q4 * 256:(q4 + 1) * 256], stg[q4 % 2][:]),
                     reads=[f"stg{q4 % 2}"], writes=[f"wout{q4}"])
            k.barrier()
        dal = self.sb(ph, "a_dal", [1, 256])
        prod = self.sb(ph, "a_prod", [1, 2, 64])
        e2 = self.sb(ph, "a_e2", [1, 2])
        nl1 = self.sb(ph, "a_nl1", [1, 1])
        dagc = self.sb(ph, "a_dagc", [128, 1])
        onesf = self.sb(ph, "a_onesf", [128, 128])
        onesb = self.sb(ph, "a_onesb", [128, 128], BF16)
        fgb = self.sb(ph, "a_fgb", [128, D])
        qTb = [self.sb(ph, f"a_q{i}", [128, 4, 512], BF16) for i in range(2)]
        yasb = [self.sb(ph, f"a_yas{i}", [128, 4, 512], BF16) for i in range(2)]
        gdb = [self.sb(ph, f"a_gdb{i}", [128, 4, 512]) for i in range(2)]
        PT = [[self.sb(ph, f"a_pt{m}{i}", [128, 512], BF16) for i in range(3)] for m in range(2)]
        acc = [self.sb(ph, f"a_acc{m}", [128, 512]) for m in range(2)]
        rden = [self.sb(ph, f"a_rden{m}", [128, 512]) for m in range(2)]
        obT = [self.sb(ph, f"a_obT{i}", [128, 4, 512]) for i in range(2)]
        cO = [self.sb(ph, f"a_cO{i}", [128, 512]) for i in range(2)]
        t1 = self.sb(ph, "a_t1", [128, 512])
        sqb = self.sb(ph, "a_sqb", [128, 4, 512])
        rstd1 = self.sb(ph, "a_rstd", [128, 512])
        ydT = self.sb(ph, "a_ydT", [128, 4, 512], BF16)
        hx = [self.sb(ph, f"a_hx{i}", [128, D]) for i in range(2)]
        hn = [self.sb(ph, f"a_hn{i}", [128, D]) for i in range(2)]
        junk = self.sb(ph, "a_junk", [128, D], BF16)
        fs = self.sb(ph, "a_fs", [128, 2])
        psc = [self.ps(ph, f"a_psc{i}", [128, 512]) for i in range(4)]
        pO = [self.ps(ph, f"a_pO{i}", [128, 512]) for i in range(2)]
        pden = [self.ps(ph, f"a_pden{i}", [128, 512]) for i in range(2)]
        po = psc[0:2]
        pokeys = ["psc0", "psc1"]

        k.dma("sync", dal[:], p["dalam"], writes=["dal"])
        dv_ = dal[:].rearrange("p (m t e) -> p m t e", m=2, t=2)
        k.op(k.dve, lambda: nc.vector.tensor_tensor(prod[:], dv_[:, :, 0, :], dv_[:, :, 1, :], ALU.mult), reads=["dal"], writes=["prod"])
        k.op(k.dve, lambda: nc.vector.reduce_sum(e2[:], prod[:], axis=AX.X), reads=["prod"], writes=["e2"])
        k.op(k.act, lambda: nc.scalar.activation(e2[:], e2[:], AF.Exp), reads=["e2"], writes=["e2"])
        k.op(k.dve, lambda: nc.vector.tensor_tensor(nl1[:], e2[:, 1:2], e2[:, 0:1], ALU.subtract), reads=["e2"], writes=["nl1"])
        k.op(k.dve, lambda: nc.vector.tensor_scalar_add(nl1[:], nl1[:], -lam_init), reads=["nl1"], writes=["nl1"])
        k.op(k.pe, lambda: nc.tensor.matmul(pden[0][:, 0:1], self.ones1[:], nl1[:], start=True, stop=True), reads=["ones1", "nl1"], writes=["pden0"])
        k.op(k.dve, lambda: nc.vector.tensor_copy(self.nlam[:], pden[0][:, 0:1]), reads=["pden0"], writes=["nlam"])
        k.dma("sync", dagc[:], p["dag"], writes=["dagc"])
        k.op(k.dve, lambda: nc.vector.tensor_scalar_mul(dagc[:], dagc[:], 1.0 - lam_init), reads=["dagc"], writes=["dagc"])
        k.op(k.dve, lambda: nc.vector.memset(onesf[:], 1.0), writes=["onesf"])
        k.op(k.dve, lambda: nc.vector.memset(onesb[:], 1.0), writes=["onesb"])
        if last:
            k.dma("sync", fgb[:], self.finalg.partition_broadcast(128), writes=["fgb"])

        qblocks = [] if last else [(0, 256, [0, 1])]
        qblocks += [(TC + 512 * i, 512, list(range(NTT))) for i in range(8)]
        qTv = self.qT.rearrange("(j p) t -> p j t", p=128)
        yasv = self.yas.rearrange("(j p) t -> p j t", p=128)
        gdv = self.gdT.rearrange("(j p) t -> p j t", p=128)
        pt_i = 0
        pair_i = 0
        tile_i = [0]

        def epilogue1(bi):
            t0, nq, _ = qblocks[bi]
            gd_, ob_ = gdb[bi % 2], obT[bi % 2]
            banks = [(pden[0], "pden0"), (pden[1], "pden1"), (pO[0], "pO0"), (pO[1], "pO1")]
            tmpb = [(rstd1, "rstd"), (t1, "t1"), (cO[0], "cO0"), (cO[1], "cO1")]
            for h in range(4):
                pd, pdk = banks[h]
                k.op(k.pe, lambda: nc.tensor.matmul(pd[:, :nq], onesf[:], sqb[:, h, :nq], start=True, stop=True),
                     reads=["onesf", f"sqb{h}"], writes=[pdk])
            for h in range(4):
                pd, pdk = banks[h]
                rs_, rk = tmpb[h]
                k.op(k.act, lambda: nc.scalar.activation(rs_[:, :nq], pd[:, :nq], AF.Sqrt, scale=1.0 / 128, bias=EPS),
                     reads=[pdk], writes=[rk])
            for h in range(4):
                rs_, rk = tmpb[h]
                k.op(k.dve, lambda: nc.vector.reciprocal(rs_[:, :nq], rs_[:, :nq]), reads=[rk], writes=[rk])
                k.op(k.pool, lambda: nc.gpsimd.tensor_tensor(rs_[:, :nq], rs_[:, :nq], ob_[:, h, :nq], ALU.mult),
                     reads=[rk, f"obT{bi % 2}{h}"], writes=[rk])
                k.op(k.dve, lambda: nc.vector.scalar_tensor_tensor(ydT[:, h, :nq], rs_[:, :nq], dagc[:, 0:1], gd_[:, h, :nq], ALU.mult, ALU.mult),
                     reads=[rk, "dagc", f"gdb{bi % 2}"], writes=["ydT"])

        def epilogue2(bi):
            t0, nq, _ = qblocks[bi]
            v = 1 if t0 < TC else 0
            yb_ = yasb[bi % 2]
            for qs in range(nq // 128):
                r0 = t0 + qs * 128
                gi_ = tile_i[0] % 2
                tile_i[0] += 1
                hx_, hn_ = hx[gi_], hn[gi_]
                k.dma("sync", hx_[:], hsrc[r0:r0 + 128, :], writes=[f"hx{gi_}"])
                for hf in range(2):
                    for mt in range(8):
                        lhs = yb_[:, mt, qs * 128:(qs + 1) * 128] if mt < 4 else ydT[:, mt - 4, qs * 128:(qs + 1) * 128]
                        k.op(k.pe, lambda: nc.tensor.matmul(po[hf][:], lhs, woutb[:, mt, hf * 512:(hf + 1) * 512],
                                                            start=(mt == 0), stop=(mt == 7)),
                             reads=[f"yas{bi % 2}", "ydT"], writes=[pokeys[hf]])
                    cs = slice(hf * 512, (hf + 1) * 512)
                    k.op(k.dve, lambda: nc.vector.tensor_tensor(hn_[:, cs], po[hf][:], self.gateb[:, v, cs], ALU.mult),
                         reads=[pokeys[hf], f"gateb{v}{hf}"], writes=[f"hn{gi_}{hf}"])
                    k.op(k.pool, lambda: nc.gpsimd.tensor_tensor(hn_[:, cs], hn_[:, cs], hx_[:, cs], ALU.add),
                         reads=[f"hn{gi_}{hf}", f"hx{gi_}"], writes=[f"hn{gi_}{hf}"])
                hk = [f"hn{gi_}0", f"hn{gi_}1"]
                if not last:
                    k.dma("gpsimd", self.hbuf[r0:r0 + 128, :], hn_[:], reads=hk)
                else:
                    k.op(k.act, lambda: nc.scalar.activation(junk[:], hn_[:], AF.Square, accum_out=fs[:, gi_:gi_ + 1]),
                         reads=hk, writes=["junk", f"fs{gi_}"])
                    k.op(k.act, lambda: nc.scalar.activation(fs[:, gi_:gi_ + 1], fs[:, gi_:gi_ + 1], AF.Sqrt, scale=1.0 / D, bias=EPS),
                         reads=[f"fs{gi_}"], writes=[f"fs{gi_}"])
                    k.op(k.dve, lambda: nc.vector.reciprocal(fs[:, gi_:gi_ + 1], fs[:, gi_:gi_ + 1]), reads=[f"fs{gi_}"], writes=[f"fs{gi_}"])
                    k.op(k.dve, lambda: nc.vector.scalar_tensor_tensor(hn_[:], hn_[:], fs[:, gi_:gi_ + 1], fgb[:], ALU.mult, ALU.mult),
                         reads=hk + [f"fs{gi_}", "fgb"], writes=hk)
                    k.dma("gpsimd", self.out[r0 - TC:r0 - TC + 128, :], hn_[:], reads=hk)

        for bi, (t0, nq, ktl) in enumerate(qblocks):
            qb_, yb_, gd_, ob_ = qTb[bi % 2], yasb[bi % 2], gdb[bi % 2], obT[bi % 2]
            k.dma("sync", qb_[:, :, :nq], qTv[:, :, t0:t0 + nq], writes=[f"q{bi % 2}"])
            k.dma("sync", yb_[:, :, :nq], yasv[:, :, t0:t0 + nq], writes=[f"yas{bi % 2}"])
            k.dma("sync", gd_[:, :, :nq], gdv[:, :, t0:t0 + nq], writes=[f"gdb{bi % 2}"])
            k.op(k.act, lambda: nc.scalar.activation(gd_[:, :, :nq], gd_[:, :, :nq], AF.Silu), reads=[f"gdb{bi % 2}"], writes=[f"gdb{bi % 2}"])
            its = [(h, ki_, kt) for h in range(4) for ki_, kt in enumerate(ktl)]

            def emit_scores(i):
                h, ki_, kt = its[i]
                for m in range(2):
                    prt = slice(m * 64, (m + 1) * 64)
                    bnk = 2 * ((pair_i + i) % 2) + m
                    k.op(k.pe, lambda: nc.tensor.matmul(psc[bnk][:, :nq], kTs[prt, h, kt * 128:(kt + 1) * 128], qb_[prt, h, :nq],
                                                        start=True, stop=True), reads=["kTs", f"q{bi % 2}"], writes=[f"psc{bnk}"])

            emit_scores(0)
            for i, (h, ki_, kt) in enumerate(its):
                first, lastk = (ki_ == 0), (ki_ == len(ktl) - 1)
                defer_here = lastk and bi > 0 and h == 0
                if i + 1 < len(its) and not defer_here:
                    emit_scores(i + 1)
                for m in range(2):
                    bnk = 2 * ((pair_i + i) % 2) + m
                    pt_ = PT[m][pt_i % 3]
                    pk = f"pt{m}{pt_i % 3}"
                    k.op(k.act, lambda: nc.scalar.activation(pt_[:, :nq], psc[bnk][:, :nq], AF.Exp, scale=0.125), reads=[f"psc{bnk}"], writes=[pk])
                    k.op(k.pe, lambda: nc.tensor.matmul(pO[m][:, :nq], vs[:, kt, h * 128:(h + 1) * 128], pt_[:, :nq], start=first, stop=lastk),
                         reads=[pk], writes=[f"pO{m}"])
                    if m == 0:
                        k.op(k.pe, lambda: nc.tensor.matmul(pden[0][:, :nq], onesb[:], pt_[:, :nq], start=first, stop=lastk),
                             reads=[pk, "onesb"], writes=["pden0"])
                    else:
                        e_ = ki_ % 2
                        eng = k.dve if e_ == 0 else k.pool
                        if ki_ < 2:
                            k.op(eng, lambda: eng.h.tensor_copy(acc[e_][:, :nq], pt_[:, :nq]), reads=[pk], writes=[f"acc{e_}"])
                        else:
                            k.op(eng, lambda: eng.h.tensor_tensor(acc[e_][:, :nq], acc[e_][:, :nq], pt_[:, :nq], ALU.add),
                                 reads=[pk, f"acc{e_}"], writes=[f"acc{e_}"])
                pt_i += 1
                if not lastk:
                    continue
                k.op(k.act, lambda: nc.scalar.copy(cO[0][:, :nq], pO[0][:, :nq]), reads=["pO0"], writes=["cO0"])
                k.op(k.dve, lambda: nc.vector.tensor_copy(cO[1][:, :nq], pO[1][:, :nq]), reads=["pO1"], writes=["cO1"])
                k.op(k.dve, lambda: nc.vector.reciprocal(rden[0][:, :nq], pden[0][:, :nq]), reads=["pden0"], writes=["rden0"])
                k.op(k.pe, lambda: nc.tensor.matmul(pden[1][:, :nq], onesf[:], acc[0][:, :nq], start=True, stop=False),
                     reads=["onesf", "acc0"], writes=["pden1"])
                k.op(k.pe, lambda: nc.tensor.matmul(pden[1][:, :nq], onesf[:], acc[1][:, :nq], start=False, stop=True),
                     reads=["onesf", "acc1"], writes=["pden1"])
                k.op(k.dve, lambda: nc.vector.reciprocal(rden[1][:, :nq], pden[1][:, :nq]), reads=["pden1"], writes=["rden1"])
                k.op(k.dve, lambda: nc.vector.tensor_scalar_mul(rden[1][:, :nq], rden[1][:, :nq], self.nlam[:, 0:1]), reads=["rden1", "nlam"], writes=["rden1"])
                k.op(k.pool, lambda: nc.gpsimd.tensor_tensor(t1[:, :nq], cO[1][:, :nq], rden[1][:, :nq], ALU.mult), reads=["cO1", "rden1"], writes=["t1"])
                k.op(k.dve, lambda: nc.vector.tensor_tensor(ob_[:, h, :nq], cO[0][:, :nq], rden[0][:, :nq], ALU.mult), reads=["cO0", "rden0"], writes=[f"obT{bi % 2}{h}"])
                k.op(k.pool, lambda: nc.gpsimd.tensor_tensor(ob_[:, h, :nq], ob_[:, h, :nq], t1[:, :nq], ALU.add), reads=[f"obT{bi % 2}{h}", "t1"], writes=[f"obT{bi % 2}{h}"])
                k.op(k.pool, lambda: nc.gpsimd.tensor_tensor(sqb[:, h, :nq], ob_[:, h, :nq], ob_[:, h, :nq], ALU.mult),
                     reads=[f"obT{bi % 2}{h}"], writes=[f"sqb{h}"])
                if h == 3:
                    epilogue1(bi)
                if bi > 0 and h == 0:
                    epilogue2(bi - 1)
                    if i + 1 < len(its):
                        emit_scores(i + 1)
            pair_i += len(its)
        epilogue2(len(qblocks) - 1)


Builder.phase_att = _phase_att
Builder.att_prefetch = _att_prefetch
Builder.phase_s5 = lambda self, l: None


def _phase_s5(self, l):
    nc, k, p = self.nc, self.k, self.L[l]
    PI = float(np.pi)
    uid = [0]

    def nm(s_):
        uid[0] += 1
        return f"s_{s_}{uid[0]}"

    def dv(fn, reads, writes):
        return k.op(k.dve, fn, reads=reads, writes=writes)

    I32 = mybir.dt.int32
    PI_LO = 3.1415925

    tcache = {}

    def reduce_pi(st_, out_t, src, shape, rkeys, tmps=None):
        if tmps is None:
            u = self.sb(st_, nm("ru"), shape)
            qi = self.sb(st_, nm("rq"), shape, I32)
        else:
            u, qi = tmps
        dv(lambda: nc.vector.tensor_scalar_mul(u[:], src, 1.0 / TWO_PI), rkeys, [u.name])
        dv(lambda: nc.vector.tensor_copy(qi[:], u[:]), [u.name], [qi.name])
        dv(lambda: nc.vector.tensor_copy(u[:], qi[:]), [qi.name], [u.name])
        dv(lambda: nc.vector.scalar_tensor_tensor(out_t[:], u[:], -TWO_PI, src, ALU.mult, ALU.add), [u.name] + rkeys, [out_t.name])
        dv(lambda: nc.vector.tensor_scalar(out_t[:], out_t[:], -PI_LO, PI_LO, ALU.max, ALU.min), [out_t.name], [out_t.name])

    def sincos(st_, ang, shape, K_, key):
        sn = self.sb(st_, nm("sn"), shape)
        cs = self.sb(st_, nm("cs"), shape)
        ck = (id(st_), tuple(shape))
        if ck not in tcache:
            tcache[ck] = (self.sb(st_, nm("ah"), shape), self.sb(st_, nm("ru"), shape), self.sb(st_, nm("rq"), shape, I32))
        ah, u_, q_ = tcache[ck]
        reduce_pi(st_, sn, ang, shape, [key], (u_, q_))
        dv(lambda: nc.vector.tensor_scalar_add(ah[:], ang, PI / 2), [key], [ah.name])
        reduce_pi(st_, cs, ah[:], shape, [ah.name], (u_, q_))
        for t_ in (sn, cs):
            k.op(k.act, lambda: nc.scalar.activation(t_[:], t_[:], AF.Sin), reads=[t_.name], writes=[t_.name])
        return sn, cs

    with contextlib.ExitStack() as ph:
        BLt = self.sb(ph, "s_BLt", [128, 128, 64], BF16)
        DLt = self.sb(ph, "s_DLt", [128, 32, 256], BF16)
        CLR = self.sb(ph, "s_CLR", [128, 16, 256], BF16)
        CLI = self.sb(ph, "s_CLI", [128, 16, 256], BF16)
        lamp = self.sb(ph, "s_lamp", [128, 3, 16])
        lrd = self.sb(ph, "s_lrd", [128, 16])
        ang = self.sb(ph, "s_ang", [128, 16])
        k.dma("sync", lamp[:], p["s5lam"], writes=["lamp"])
        dt = self.sb(ph, "s_dt", [128, 16])
        k.op(k.act, lambda: nc.scalar.activation(dt[:], lamp[:, 2, :], AF.Exp), reads=["lamp"], writes=["dt"])
        dv(lambda: nc.vector.tensor_tensor(lrd[:], lamp[:, 0, :], dt[:], ALU.mult), ["lamp", "dt"], ["lrd"])
        dv(lambda: nc.vector.tensor_tensor(ang[:], lamp[:, 1, :], dt[:], ALU.mult), ["lamp", "dt"], ["ang"])

        with contextlib.ExitStack() as sa:
            S2 = [128, 16]
            bsrc = self.sb(sa, "s_bsrc", [128, 2, 16, 16])
            csrc = self.sb(sa, "s_csrc", [128, 2, 16, 16])
            expt = self.sb(sa, "s_expt", [128, 3, 16, 16])
            mask = self.sb(sa, "s_mask", [128, 2, 2, 256])
            k.dma("sync", bsrc[:], p["s5b"], writes=["bsrc"])
            k.dma("sync", csrc[:], p["s5c"], writes=["csrc"])
            k.dma("sync", expt[:], self.s5exp, writes=["expt"])
            k.dma("sync", mask[:], self.s5mask, writes=["mask"])
            mag = self.sb(sa, "s_mag", S2)
            k.op(k.act, lambda: nc.scalar.activation(mag[:], lrd[:], AF.Exp), reads=["lrd"], writes=["mag"])
            sn, cs = sincos(sa, ang[:], S2, 1, "ang")
            nr = self.sb(sa, "s_nr", S2); ni = self.sb(sa, "s_ni", S2); den = self.sb(sa, "s_den", S2)
            t1 = self.sb(sa, "s_t1", S2); t2 = self.sb(sa, "s_t2", S2)
            cfr = self.sb(sa, "s_cfr", S2); cfi = self.sb(sa, "s_cfi", S2)
            dv(lambda: nc.vector.tensor_tensor(nr[:], mag[:], cs[:], ALU.mult), ["mag", cs.name], ["nr"])
            dv(lambda: nc.vector.tensor_scalar_add(nr[:], nr[:], -1.0), ["nr"], ["nr"])
            dv(lambda: nc.vector.tensor_tensor(ni[:], mag[:], sn[:], ALU.mult), ["mag", sn.name], ["ni"])
            lre, lim = lamp[:, 0, :], lamp[:, 1, :]
            dv(lambda: nc.vector.tensor_tensor(den[:], lre, lre, ALU.mult), ["lamp"], ["den"])
            dv(lambda: nc.vector.tensor_tensor(t1[:], lim, lim, ALU.mult), ["lamp"], ["t1"])
            dv(lambda: nc.vector.tensor_tensor(den[:], den[:], t1[:], ALU.add), ["den", "t1"], ["den"])
            dv(lambda: nc.vector.reciprocal(den[:], den[:]), ["den"], ["den"])
            dv(lambda: nc.vector.tensor_tensor(t1[:], nr[:], lre, ALU.mult), ["nr", "lamp"], ["t1"])
            dv(lambda: nc.vector.tensor_tensor(t2[:], ni[:], lim, ALU.mult), ["ni", "lamp"], ["t2"])
            dv(lambda: nc.vector.tensor_tensor(cfr[:], t1[:], t2[:], ALU.add), ["t1", "t2"], ["cfr"])
            dv(lambda: nc.vector.tensor_tensor(cfr[:], cfr[:], den[:], ALU.mult), ["cfr", "den"], ["cfr"])
            dv(lambda: nc.vector.tensor_tensor(t1[:], ni[:], lre, ALU.mult), ["ni", "lamp", "cfr"], ["t1"])
            dv(lambda: nc.vector.tensor_tensor(t2[:], nr[:], lim, ALU.mult), ["nr", "lamp", "cfr"], ["t2"])
            dv(lambda: nc.vector.tensor_tensor(cfi[:], t1[:], t2[:], ALU.subtract), ["t1", "t2"], ["cfi"])
            dv(lambda: nc.vector.tensor_tensor(cfi[:], cfi[:], den[:], ALU.mult), ["cfi", "den"], ["cfi"])
            S3 = [128, 16, 16]
            S4 = [128, 16, 16, 16]
            bbr = self.sb(sa, "s_bbr", S3); bbi = self.sb(sa, "s_bbi", S3); u1 = self.sb(sa, "s_u1", S3)
            cfrb = cfr[:].unsqueeze(2).broadcast_to(S3)
            cfib = cfi[:].unsqueeze(2).broadcast_to(S3)
            dv(lambda: nc.vector.tensor_tensor(bbr[:], bsrc[:, 0], cfrb, ALU.mult), ["bsrc", "cfr"], ["bbr"])
            dv(lambda: nc.vector.tensor_tensor(u1[:], bsrc[:, 1], cfib, ALU.mult), ["bsrc", "cfi"], ["u1"])
            dv(lambda: nc.vector.tensor_tensor(bbr[:], bbr[:], u1[:], ALU.subtract), ["bbr", "u1"], ["bbr"])
            dv(lambda: nc.vector.tensor_tensor(bbi[:], bsrc[:, 1], cfrb, ALU.mult), ["bsrc", "cfr", "bbr"], ["bbi"])
            dv(lambda: nc.vector.tensor_tensor(u1[:], bsrc[:, 0], cfib, ALU.mult), ["bsrc", "cfi", "bbr"], ["u1"])
            dv(lambda: nc.vector.tensor_tensor(bbi[:], bbi[:], u1[:], ALU.add), ["bbi", "u1"], ["bbi"])

            def cpow(e):
                lr = self.sb(sa, nm("lr"), S3)
                an = self.sb(sa, nm("an"), S3)
                dv(lambda: nc.vector.tensor_tensor(lr[:], expt[:, e], lrd[:].unsqueeze(2).broadcast_to(S3), ALU.mult), ["expt", "lrd"], [lr.name])
                k.op(k.act, lambda: nc.scalar.activation(lr[:], lr[:], AF.Exp), reads=[lr.name], writes=[lr.name])
                dv(lambda: nc.vector.tensor_tensor(an[:], expt[:, e], ang[:].unsqueeze(2).broadcast_to(S3), ALU.mult), ["expt", "ang"], [an.name])
                s_, c_ = sincos(sa, an[:], S3, 60, an.name)
                dv(lambda: nc.vector.tensor_tensor(c_[:], c_[:], lr[:], ALU.mult), [c_.name, lr.name], [c_.name])
                dv(lambda: nc.vector.tensor_tensor(s_[:], s_[:], lr[:], ALU.mult), [s_.name, lr.name], [s_.name])
                return c_, s_

            w1 = self.sb(sa, "s_w1", S4)
            w2 = self.sb(sa, "s_w2", S4)

            def cmul(out_re, out_im_neg, out_im, pw, vr, vi, vkeys):
                pr_, pi_ = pw
                prb = pr_[:].unsqueeze(3).broadcast_to(S4)
                pib = pi_[:].unsqueeze(3).broadcast_to(S4)
                vrb = vr.unsqueeze(2).broadcast_to(S4)
                vib = vi.unsqueeze(2).broadcast_to(S4)
                rk = [pr_.name, pi_.name] + vkeys
                dv(lambda: nc.vector.tensor_tensor(w1[:], prb, vrb, ALU.mult), rk, ["w1"])
                dv(lambda: nc.vector.tensor_tensor(w2[:], pib, vib, ALU.mult), rk, ["w2"])
                dv(lambda: nc.vector.tensor_tensor(out_re, w1[:], w2[:], ALU.subtract), ["w1", "w2"], [nm("o")])
                dv(lambda: nc.vector.tensor_tensor(w1[:], prb, vib, ALU.mult), rk, ["w1"])
                dv(lambda: nc.vector.tensor_tensor(w2[:], pib, vrb, ALU.mult), rk, ["w2"])
                if out_im is not None:
                    dv(lambda: nc.vector.tensor_tensor(out_im, w1[:], w2[:], ALU.add), ["w1", "w2"], [nm("o")])
                else:
                    dv(lambda: nc.vector.scalar_tensor_tensor(out_im_neg, w1[:], -1.0, w2[:], ALU.mult, ALU.subtract), ["w1", "w2"], [nm("o")])

            PBr = self.sb(sa, "s_PBr", S4); PBi = self.sb(sa, "s_PBi", S4)
            QCr = self.sb(sa, "s_QCr", S4); QCn = self.sb(sa, "s_QCn", S4)
            cmul(PBr[:], None, PBi[:], cpow(0), bbr[:], bbi[:], ["bbr", "bbi"])
            cmul(CLR[:].rearrange("p t (j h) -> p t j h", h=16), CLI[:].rearrange("p t (j h) -> p t j h", h=16), None,
                 cpow(1), csrc[:, 0], csrc[:, 1], ["csrc"])
            cmul(QCr[:], QCn[:], None, cpow(2), csrc[:, 0], csrc[:, 1], ["csrc"])
            k.barrier()
            pst = [self.ps(sa, f"s_pst{i}", [128, 512]) for i in range(4)]
            psd = [self.ps(sa, f"s_psd{i}", [128, 512]) for i in range(4)]
            for tp in range(16):
                pts = [pst[(tp % 2) * 2 + gl] for gl in range(2)]
                pks = [f"pst{(tp % 2) * 2 + gl}" for gl in range(2)]
                for kt in range(2):
                    for pl_, PB in enumerate((PBr, PBi)):
                        q_ = kt * 2 + pl_
                        for gl in range(2):
                            prt = slice(gl * 64, (gl + 1) * 64)
                            k.op(k.pe, lambda: nc.tensor.transpose(pts[gl][:, q_ * 64:(q_ + 1) * 64], PB[prt, tp, kt * 8:(kt + 1) * 8, :],
                                                                   self.identf[prt, prt]), reads=["identf"], writes=[pks[gl]])
                for gl in range(2):
                    s0 = (tp * 2 + gl) * 4
                    k.op(k.act, lambda: nc.scalar.copy(BLt[:, s0:s0 + 4, :], pts[gl][:, 0:256].rearrange("p (a b) -> p a b", b=64)),
                         reads=[pks[gl]], writes=["BLt"])
            m1s = [self.sb(sa, nm("m"), [128, 256]) for _ in range(2)]
            for gp in range(8):
                for kt in range(2):
                    for dr, tp in ((0, gp), (1, 8 + gp)):
                        for PBx, QCx, st_, sp_ in ((PBr, QCr, True, False), (PBi, QCn, False, True)):
                            for gl in range(2):
                                prt = slice(gl * 64, (gl + 1) * 64)
                                ps_ = psd[gl * 2 + dr]
                                k.op(k.pe, lambda: nc.tensor.matmul(ps_[:, 0:256], PBx[prt, tp, kt * 8:(kt + 1) * 8, :], QCx[prt, tp, :, :],
                                                                    start=st_, stop=sp_), reads=[], writes=[f"psd{gl * 2 + dr}"])
                    for gl in range(2):
                        g = gp * 2 + gl
                        pf, pb = psd[gl * 2], psd[gl * 2 + 1]
                        kf_, kb_ = f"psd{gl * 2}", f"psd{gl * 2 + 1}"
                        m1 = m1s[gl]
                        dv(lambda: nc.vector.tensor_tensor(m1[:], pf[:, 0:256], mask[:, 0, kt, :], ALU.mult), [kf_, "mask"], [m1.name])
                        dv(lambda: nc.vector.tensor_tensor(pb[:, 256:512], pb[:, 0:256], mask[:, 1, kt, :], ALU.mult), [kb_, "mask"], [kb_])
                        dv(lambda: nc.vector.tensor_tensor(DLt[:, g * 2 + kt, :], pb[:, 256:512], m1[:], ALU.add), [kb_, m1.name], ["DLt"])
            k.barrier()

        if getattr(self, "s5_stop", 9) <= 1:
            return
        Ut = self.sb(ph, "s_Ut", [128, 32, NCH], BF16)
        SFR = self.sb(ph, "s_SFR", [128, 8, NCH + 1], BF16); SFI = self.sb(ph, "s_SFI", [128, 8, NCH + 1], BF16)
        SBR = self.sb(ph, "s_SBR", [128, 8, NCH + 1], BF16); SBI = self.sb(ph, "s_SBI", [128, 8, NCH + 1], BF16)
        P16 = self.sb(ph, "s_P16", [128, 16])
        phr = self.sb(ph, "s_phr", [128, 16])
        k.op(k.act, lambda: nc.scalar.activation(P16[:], lrd[:], AF.Exp, scale=16.0), reads=["lrd"], writes=["P16"])
        ph16 = self.sb(ph, "s_ph16", [128, 16])
        dv(lambda: nc.vector.tensor_scalar_mul(ph16[:], ang[:], 16.0), ["ang"], ["ph16"])
        reduce_pi(ph, phr, ph16[:], [128, 16], ["ph16"])
        CT = [(0, 16), (16, 128), (144, 128)]
        usv = self.us_s.rearrange("(c j) ch -> c j ch", j=16)
        with contextlib.ExitStack() as su:
            ucm = [self.sb(su, f"s_ucm{i}", [128, 16, 256]) for i in range(3)]
            psu = [self.ps(su, f"s_psu{i}", [128, 512]) for i in range(4)]
            n_ = 0
            for ci, (c0, n) in enumerate(CT):
                k.dma("sync", ucm[ci][0:n], usv[c0:c0 + n], writes=[f"ucm{ci}"])
                ucp_ = self.sb(su, f"s_ucp{ci}", [128, 16, 16, 16])
                k.op(k.pool if ci == 1 else k.dve,
                     lambda: (nc.gpsimd if ci == 1 else nc.vector).tensor_copy(
                         ucp_[0:n], ucm[ci][0:n].rearrange("p i (g h) -> p g i h", h=16)),
                     reads=[f"ucm{ci}"], writes=[f"ucp{ci}"])
                for q4 in range(8):
                    pu = psu[n_ % 4]
                    pk = f"psu{n_ % 4}"
                    n_ += 1
                    for a in range(4):
                        s_ = q4 * 4 + a
                        g, kt = s_ // 2, s_ % 2
                        k.op(k.pe, lambda: nc.tensor.transpose(pu[:, a * 128:a * 128 + n], ucp_[0:n, g, kt * 8:(kt + 1) * 8, :],
                                                               self.identf[0:n, 0:n]), reads=[f"ucp{ci}", "identf"], writes=[pk])
                    eng = k.act if n_ % 2 == 0 else k.dve
                    src = pu[:].rearrange("p (a b) -> p a b", b=128)[:, :, 0:n]
                    dst = Ut[:, q4 * 4:(q4 + 1) * 4, c0:c0 + n]
                    if eng is k.act:
                        k.op(eng, lambda: nc.scalar.copy(dst, src), reads=[pk], writes=["Ut"])
                    else:
                        k.op(eng, lambda: nc.vector.tensor_copy(dst, src), reads=[pk], writes=["Ut"])
        k.barrier()
        if getattr(self, "s5_stop", 9) <= 2:
            return
        for d in range(2):
            with contextlib.ExitStack() as sd:
                S8 = [128, 8, NCH]
                idx = self.sb(sd, "s_idx", S8)
                k.dma("sync", idx[:], self.s5idx[:, d * 8:(d + 1) * 8, 0:NCH], writes=["idx"])
                dv(lambda: nc.vector.tensor_tensor(idx[:], idx[:], phr[:, d * 8:(d + 1) * 8].unsqueeze(2).broadcast_to(S8), ALU.mult),
                   ["idx", phr.name], ["idx"])
                SN, CS = sincos(sd, idx[:], S8, 140, "idx")
                PCO = self.sb(sd, "s_PCO", S8)
                XR = self.sb(sd, "s_XR", S8); XI = self.sb(sd, "s_XI", S8)
                VR = self.sb(sd, "s_VR", S8); VI = self.sb(sd, "s_VI", S8)
                WR = self.sb(sd, "s_WR", S8); WI = self.sb(sd, "s_WI", S8)
                psx = [self.ps(sd, f"s_psx{i}", [128, 512]) for i in range(4)]
                dv(lambda: nc.vector.memset(PCO[:], 1.0), [], ["PCO"])
                dv(lambda: nc.vector.tensor_tensor(PCO[:], PCO[:], P16[:, d * 8:(d + 1) * 8].unsqueeze(2).broadcast_to(S8), ALU.mult),
                   ["PCO", "P16"], ["PCO"])
                zc = 0 if d == 0 else NCH - 1
                dv(lambda: nc.vector.memset(PCO[:, :, zc:zc + 1], 0.0), ["PCO"], ["PCO"])
                n_ = 0
                for a in range(8):
                    tp = d * 8 + a
                    for pl_, X in enumerate((XR, XI)):
                        px = psx[n_ % 4]
                        pk = f"psx{n_ % 4}"
                        n_ += 1
                        for gl in range(2):
                            g = a * 2 + gl
                            prt = slice(gl * 64, (gl + 1) * 64)
                            segs = [(0, NCH, 0)] if d == 0 else [(16, NCH, 0), (0, 16, 256)]
                            for (u0, u1_, o0) in segs:
                                for kt in range(2):
                                    slot = ((tp * 2 + gl) * 2 + kt) * 2 + pl_
                                    k.op(k.pe, lambda: nc.tensor.matmul(px[prt, o0:o0 + (u1_ - u0)], BLt[:, slot, :], Ut[:, g * 2 + kt, u0:u1_],
                                                                        start=(kt == 0), stop=(kt == 1)), reads=["BLt", "Ut"], writes=[pk])
                        k.op(k.act, lambda: nc.scalar.copy(X[:, a, :], px[:, 0:NCH]), reads=[pk], writes=[X.name])
                dv(lambda: nc.vector.tensor_tensor(VR[:], XR[:], CS[:], ALU.mult), [XR.name, CS.name], ["VR"])
                k.op(k.pool, lambda: nc.gpsimd.tensor_tensor(WR[:], XI[:], SN[:], ALU.mult), reads=[XI.name, SN.name], writes=["WR"])
                dv(lambda: nc.vector.tensor_tensor(VR[:], VR[:], WR[:], ALU.add), ["VR", "WR"], ["VR"])
                dv(lambda: nc.vector.tensor_tensor(VI[:], XI[:], CS[:], ALU.mult), [XI.name, CS.name], ["VI"])
                k.op(k.pool, lambda: nc.gpsimd.tensor_tensor(WI[:], XR[:], SN[:], ALU.mult), reads=[XR.name, SN.name], writes=["WI"])
                dv(lambda: nc.vector.tensor_tensor(VI[:], VI[:], WI[:], ALU.subtract), ["VI", "WI"], ["VI"])
                fl = lambda t_: t_[:].rearrange("p a b -> p (a b)")
                rv = (lambda ap: ap) if d == 0 else (lambda ap: ap[:, ::-1])
                dv(lambda: nc.vector.tensor_tensor_scan(rv(fl(WR)), rv(fl(PCO)), rv(fl(VR)), 0.0, ALU.mult, ALU.add), ["PCO", "VR", "WR"], ["WR"])
                dv(lambda: nc.vector.tensor_tensor_scan(rv(fl(WI)), rv(fl(PCO)), rv(fl(VI)), 0.0, ALU.mult, ALU.add), ["PCO", "VI", "WI"], ["WI"])
                SR_, SI_ = (SFR, SFI) if d == 0 else (SBR, SBI)
                o_ = 1 if d == 0 else 0
                zcol = 0 if d == 0 else NCH
                dv(lambda: nc.vector.memset(SR_[:, :, zcol:zcol + 1], 0.0), [], [SR_.name])
                dv(lambda: nc.vector.memset(SI_[:, :, zcol:zcol + 1], 0.0), [], [SI_.name])
                dv(lambda: nc.vector.tensor_tensor(VR[:], WR[:], CS[:], ALU.mult), ["WR", CS.name, "VR"], ["VR"])
                k.op(k.pool, lambda: nc.gpsimd.tensor_tensor(VI[:], WI[:], SN[:], ALU.mult), reads=["WI", SN.name, "VI"], writes=["VI"])
                dv(lambda: nc.vector.tensor_tensor(SR_[:, :, o_:o_ + NCH], VR[:], VI[:], ALU.subtract), ["VR", "VI", SR_.name], [SR_.name])
                dv(lambda: nc.vector.tensor_tensor(VR[:], WI[:], CS[:], ALU.mult), ["WI", CS.name, "VR", SR_.name], ["VR"])
                k.op(k.pool, lambda: nc.gpsimd.tensor_tensor(VI[:], WR[:], SN[:], ALU.mult), reads=["WR", SN.name, "VI", SR_.name], writes=["VI"])
                dv(lambda: nc.vector.tensor_tensor(SI_[:, :, o_:o_ + NCH], VR[:], VI[:], ALU.add), ["VR", "VI", SI_.name], [SI_.name])
            k.barrier()
        if getattr(self, "s5_stop", 9) <= 3:
            return
        with contextlib.ExitStack() as sy:
            dsk = self.sb(sy, "s_dsk", [128, 256])
            k.dma("sync", dsk[:], p["s5d"].partition_broadcast(128), writes=["dsk"])
            ucm = [self.sb(sy, f"s_ucy{i}", [128, 16, 256]) for i in range(2)]
            ycm = [self.sb(sy, f"s_ycm{i}", [128, 16, 256]) for i in range(2)]
            tq = [self.sb(sy, f"s_tq{i}", [128, 16, 256]) for i in range(2)]
            psy = [self.ps(sy, f"s_psy{i}", [128, 512]) for i in range(4)]
            zsv = self.zs.rearrange("(c j) ch -> c j ch", j=16)
            n_ = 0
            for ci, (c0, n) in enumerate(CT):
                u_, y_, t_ = ucm[ci % 2], ycm[ci % 2], tq[ci % 2]
                uk, yk, tk = f"ucy{ci % 2}", f"ycm{ci % 2}", f"tq{ci % 2}"
                k.dma("sync", u_[0:n], usv[c0:c0 + n], writes=[uk])
                mb0 = (256 if c0 < 16 else c0 - 16) + 1
                for g2 in range(8):
                    py = psy[n_ % 4]
                    pk = f"psy{n_ % 4}"
                    n_ += 1
                    for gg in range(2):
                        g = g2 * 2 + gg
                        gp, gl = g // 2, g % 2
                        prt = slice(gl * 64, (gl + 1) * 64)
                        o = py[0:n, gg * 256:(gg + 1) * 256]
                        mm = [(Ut[:, g * 2 + 0, c0:c0 + n], DLt[:, g * 2 + 0, :]),
                              (Ut[:, g * 2 + 1, c0:c0 + n], DLt[:, g * 2 + 1, :]),
                              (SFR[prt, gp, c0:c0 + n], CLR[prt, gp, :]),
                              (SFI[prt, gp, c0:c0 + n], CLI[prt, gp, :]),
                              (SBR[prt, gp, mb0:mb0 + n], CLR[prt, 8 + gp, :]),
                              (SBI[prt, gp, mb0:mb0 + n], CLI[prt, 8 + gp, :])]
                        for mi, (lh, rh) in enumerate(mm):
                            k.op(k.pe, lambda: nc.tensor.matmul(o, lh, rh, start=(mi == 0), stop=(mi == 5)),
                                 reads=["Ut", "DLt", "CLR", "CLI", SFR.name, SFI.name, SBR.name, SBI.name], writes=[pk])
                    src = py[0:n, :].rearrange("p (g j h) -> p g j h", g=2, h=16)
                    for gg in range(2):
                        g = g2 * 2 + gg
                        eng = k.act if gg == 0 else k.dve
                        dst = y_[0:n, :, g * 16:(g + 1) * 16]
                        if eng is k.act:
                            k.op(eng, lambda: nc.scalar.copy(dst, src[:, gg]), reads=[pk], writes=[yk])
                        else:
                            k.op(eng, lambda: nc.vector.tensor_copy(dst, src[:, gg]), reads=[pk], writes=[yk])
                dsb = dsk[0:n].unsqueeze(1).broadcast_to([n, 16, 256])
                k.op(k.pool, lambda: nc.gpsimd.tensor_tensor(u_[0:n], u_[0:n], dsb, ALU.mult), reads=[uk, "dsk"], writes=[uk])
                dv(lambda: nc.vector.tensor_tensor(y_[0:n], y_[0:n], u_[0:n], ALU.add), [yk, uk], [yk])
                k.op(k.pool, lambda: nc.gpsimd.tensor_tensor(t_[0:n], y_[0:n], y_[0:n], ALU.mult), reads=[yk], writes=[tk])
                dv(lambda: nc.vector.tensor_scalar(t_[0:n], t_[0:n], 0.044715, 1.0, ALU.mult, ALU.add), [tk], [tk])
                k.op(k.pool, lambda: nc.gpsimd.tensor_tensor(t_[0:n], t_[0:n], y_[0:n], ALU.mult), reads=[tk, yk], writes=[tk])
                k.op(k.act, lambda: nc.scalar.activation(t_[0:n], t_[0:n], AF.Sigmoid, scale=1.5957691216057308), reads=[tk], writes=[tk])
                dv(lambda: nc.vector.tensor_tensor(y_[0:n], y_[0:n], t_[0:n], ALU.mult), [yk, tk], [yk])
                k.dma("gpsimd", zsv[c0:c0 + n], y_[0:n], reads=[yk])


def _phase_s5b(self, l):
    nc, k, p = self.nc, self.k, self.L[l]

    def dv(fn, reads, writes):
        return k.op(k.dve, fn, reads=reads, writes=writes)

    with contextlib.ExitStack() as sb_:
        wgf = self.sb(sb_, "g_wgf", [128, 2, 256])
        wgb = self.sb(sb_, "g_wgb", [128, 2, 256], BF16)
        bgl = self.sb(sb_, "g_bgl", [128, 2])
        zT = self.sb(sb_, "g_zT", [128, 2, TT])
        zTb = self.sb(sb_, "g_zTb", [128, 2, TT], BF16)
        zt = [self.sb(sb_, f"g_zt{i}", [128, 256]) for i in range(2)]
        gsl = self.sb(sb_, "g_gs", [128, TT])
        sgl = self.sb(sb_, "g_sg", [128, TT])
        yb = self.sb(sb_, "g_yb", [128, TT], BF16)
        pz = [self.ps(sb_, f"g_pz{i}", [128, 512]) for i in range(2)]
        pg = [self.ps(sb_, f"g_pg{i}", [128, 512]) for i in range(2)]
        k.dma("sync", wgf[:], p["wglu"].rearrange("(k p) n -> p k n", p=128), writes=["wgf"])
        k.dma("sync", bgl[:], p["bglu"], writes=["bgl"])
        dv(lambda: nc.vector.tensor_copy(wgb[:], wgf[:]), ["wgf"], ["wgb"])
        for tt in range(NTT):
            z_ = zt[tt % 2]
            zk = f"zt{tt % 2}"
            pz_ = pz[tt % 2]
            k.dma("sync", z_[:], self.zs[tt * 128:(tt + 1) * 128, :], writes=[zk])
            for c_ in range(2):
                k.op(k.pe, lambda: nc.tensor.matmul(pz_[:, c_ * 128:(c_ + 1) * 128], z_[:, c_ * 128:(c_ + 1) * 128], self.identf[:],
                                                    start=True, stop=True),
                     reads=[zk, "identf"], writes=[f"pz{tt % 2}"])
            for c_ in range(2):
                k.op(k.dve, lambda: nc.vector.tensor_copy(zT[:, c_, tt * 128:(tt + 1) * 128], pz_[:, c_ * 128:(c_ + 1) * 128]),
                     reads=[f"pz{tt % 2}"], writes=["zT"])
                dv(lambda: nc.vector.tensor_copy(zTb[:, c_, tt * 128:(tt + 1) * 128], pz_[:, c_ * 128:(c_ + 1) * 128]),
                   [f"pz{tt % 2}"], ["zTb"])
        if getattr(self, "s5_stop", 9) <= 5:
            return
        nblk = [(i * 512, min(512, TT - i * 512)) for i in range((TT + 511) // 512)]
        for co in range(2):
            rows = slice(256 + co * 128, 256 + (co + 1) * 128)
            k.dma("sync", gsl[:], self.gsT[co * 128:(co + 1) * 128, :], writes=["gsl"])
            k.op(k.act, lambda: nc.scalar.activation(gsl[:], gsl[:], AF.Silu), reads=["gsl"], writes=["gsl"])
            for bi, (c0, w) in enumerate(nblk):
                pg_ = pg[bi % 2]
                for ci_ in range(2):
                    k.op(k.pe, lambda: nc.tensor.matmul(pg_[:, :w], wgb[:, ci_, co * 128:(co + 1) * 128], zTb[:, ci_, c0:c0 + w],
                                                        start=(ci_ == 0), stop=(ci_ == 1)), reads=["wgb", "zTb"], writes=[f"pg{bi % 2}"])
                k.op(k.act, lambda: nc.scalar.activation(sgl[:, c0:c0 + w], pg_[:, :w], AF.Sigmoid, bias=bgl[:, co:co + 1]),
                     reads=[f"pg{bi % 2}", "bgl"], writes=["sgl"])
            dv(lambda: nc.vector.tensor_tensor(sgl[:], sgl[:], zT[:, co, :], ALU.mult), ["sgl", "zT"], ["sgl"])
            k.op(k.pool, lambda: nc.gpsimd.tensor_tensor(yb[:], sgl[:], gsl[:], ALU.mult), reads=["sgl", "gsl"], writes=["yb"])
            k.dma("gpsimd", self.yas[rows, :], yb[:], reads=["yb"])


Builder.phase_s5 = _phase_s5
Builder.phase_s5b = _phase_s5b
```

```python
import contextlib
import numpy as np
import concourse.bass as bass
import concourse.mybir as mybir
from concourse.bass_utils import run_bass_kernel_spmd

F32 = mybir.dt.float32
BF16 = mybir.dt.bfloat16
ALU = mybir.AluOpType
AF = mybir.ActivationFunctionType
AX = mybir.AxisListType


class _Eng:
    def __init__(self, name, handle, sem, inc):
        self.name = name
        self.h = handle
        self.sem = sem
        self.inc = inc
        self.count = 0
        self.seen = {}


class _Reg:
    __slots__ = ("w", "r")

    def __init__(self):
        self.w = None
        self.r = {}


class K:
    def __init__(self, nc, stack, n_dma=10):
        self.nc = nc
        self.stack = stack
        self.regs = {}
        mk = lambda n: stack.enter_context(nc.semaphore(n))
        self.pe = _Eng("pe", nc.tensor, mk("s_pe"), 1)
        self.act = _Eng("act", nc.scalar, mk("s_act"), 1)
        self.dve = _Eng("dve", nc.vector, mk("s_dve"), 1)
        self.pool = _Eng("pool", nc.gpsimd, mk("s_pool"), 1)
        self.compute = [self.pe, self.act, self.dve, self.pool]
        self.dq = {}
        for qn, qh in (("sync", nc.sync), ("gpsimd", nc.gpsimd), ("scalar", nc.scalar)):
            self.dq[qn] = [
                _Eng(f"d_{qn}{i}", qh, mk(f"s_d{qn}{i}"), 16) for i in range(n_dma)
            ]
        self.dq_rr = {qn: 0 for qn in self.dq}
        self.qseen = {"sync": {}, "gpsimd": self.pool.seen, "scalar": self.act.seen}

    def reg(self, key):
        r = self.regs.get(key)
        if r is None:
            r = self.regs[key] = _Reg()
        return r

    def _deps(self, reads, writes):
        deps = {}
        for k in reads:
            r = self.reg(k)
            if r.w is not None:
                e, c = r.w
                deps[e] = max(deps.get(e, 0), c)
        for k in writes:
            r = self.reg(k)
            if r.w is not None:
                e, c = r.w
                deps[e] = max(deps.get(e, 0), c)
            for e, c in r.r.items():
                deps[e] = max(deps.get(e, 0), c)
        return deps

    def _commit(self, eng, reads, writes):
        c = eng.count
        for k in reads:
            self.reg(k).r[eng] = c
        for k in writes:
            r = self.reg(k)
            r.w = (eng, c)
            r.r = {}

    def op(self, eng, fn, reads=(), writes=()):
        deps = self._deps(reads, writes)
        for e, c in deps.items():
            if e is eng and eng is self.pe:
                continue
            if eng.seen.get(e, 0) < c:
                eng.h.wait_ge(e.sem, c * e.inc)
                eng.seen[e] = c
        ins = fn()
        eng.count += 1
        ins.then_inc(eng.sem, eng.inc)
        self._commit(eng, reads, writes)
        return ins

    def dma(self, q, out, in_, reads=(), writes=(), **kw):
        lst = self.dq[q]
        i = self.dq_rr[q]
        self.dq_rr[q] = (i + 1) % len(lst)
        d = lst[i]
        seen = self.qseen[q]
        if d.count > 0 and seen.get(d, 0) < d.count:
            d.h.wait_ge(d.sem, d.count * 16)
            seen[d] = d.count
        deps = self._deps(reads, writes)
        for e, c in deps.items():
            if seen.get(e, 0) < c:
                d.h.wait_ge(e.sem, c * e.inc)
                seen[e] = c
        ins = d.h.dma_start(out=out, in_=in_, **kw)
        d.count += 1
        ins.then_inc(d.sem, 16)
        self._commit(d, reads, writes)
        return ins

    def finish(self):
        seen = self.qseen["sync"]
        allengs = list(self.compute)
        for lst in self.dq.values():
            allengs += lst
        for e in allengs:
            if e.count > 0 and seen.get(e, 0) < e.count:
                self.nc.sync.wait_ge(e.sem, e.count * e.inc)
                seen[e] = e.count

    def barrier(self):
        allengs = list(self.compute)
        for lst in self.dq.values():
            allengs += lst
        for h, seen in ((self.nc.tensor, self.pe.seen), (self.nc.scalar, self.act.seen),
                        (self.nc.vector, self.dve.seen), (self.nc.gpsimd, self.pool.seen),
                        (self.nc.sync, self.qseen["sync"])):
            for e in allengs:
                if e.count > 0 and seen.get(e, 0) < e.count:
                    h.wait_ge(e.sem, e.count * e.inc)
                    seen[e] = e.count
        self.regs.clear()


D = 1024
T = 4096
TC = 256
TT = T + TC
NTT = TT // 128
DEPTH = 2
EPS = 1e-6
NCH = TT // 16
WCOL = {"ua": (0, 256), "k": (256, 512), "ga": (768, 256), "gs": (1024, 256), "q": (1280, 512),
        "v": (1792, 512), "gd": (2304, 512), "us": (2816, 256)}
TWO_PI = 2.0 * np.pi


class Builder:
    def __init__(self, debug=False, layers=(0, 1), phases=("M", "P", "L", "S", "A")):
        self.debug = debug
        self.layers = layers
        self.phases = phases
        self.nc = bass.Bass("TRN2", target_bir_lowering=False)
        self.ins = {}
        self.scr = {}

    def din(self, name, shape, dt=F32):
        ap = self.nc.dram_tensor(name, list(shape), dt, kind="ExternalInput").ap()
        self.ins[name] = ap
        return ap

    def dscr(self, name, shape, dt=F32):
        kind = "ExternalOutput" if self.debug else "Internal"
        ap = self.nc.dram_tensor(name, list(shape), dt, kind=kind).ap()
        self.scr[name] = ap
        return ap

    def _uname(self, name):
        self._uid = getattr(self, "_uid", 0) + 1
        return f"{name}_u{self._uid}"

    def sb(self, st, name, shape, dt=F32):
        return st.enter_context(self.nc.sbuf_tensor(self._uname(name), list(shape), dt))

    def ps(self, st, name, shape, dt=F32):
        return st.enter_context(self.nc.psum_tensor(self._uname(name), list(shape), dt))

    def declare(self):
        d = self.din
        self.hin = d("hin", [TT, D])
        self.cvec = d("cvec", [128, 8, 2])
        self.ident = d("ident", [128, 128])
        self.perm = d("perm", [128, 128])
        self.sel2 = d("sel2", [2, 2, 128])
        self.ropeC = d("ropeC", [128, T])
        self.ropeS = d("ropeS", [128, T])
        self.finalg = d("finalg", [1, D])
        self.s5idx = d("s5idx", [128, 16, NCH + 1])
        self.s5exp = d("s5exp", [128, 3, 16, 16])
        self.s5mask = d("s5mask", [128, 2, 2, 256])
        self.L = []
        for l in range(DEPTH):
            p = {}
            p["w_mod"] = d(f"w_mod{l}", [D, 3 * D])
            p["bmodR"] = d(f"bmodR{l}", [2, 3 * D])
            p["normg"] = d(f"normg{l}", [128, 8, 2])
            p["w_in"] = d(f"w_in{l}", [D, 3 * D])
            p["w_out"] = d(f"w_out{l}", [D, D])
            p["convw"] = d(f"convw{l}", [128, 2, 4])
            p["convb"] = d(f"convb{l}", [128, 2])
            p["wax"] = d(f"wax{l}", [128, 2, 2, 2, 128])
            p["bax"] = d(f"bax{l}", [128, 2, 2, 2])
            p["lrulam"] = d(f"lrulam{l}", [128, 2, 2])
            p["s5lam"] = d(f"s5lam{l}", [128, 3, 16])
            p["s5b"] = d(f"s5b{l}", [128, 2, 16, 16])
            p["s5c"] = d(f"s5c{l}", [128, 2, 16, 16])
            p["s5d"] = d(f"s5d{l}", [1, 256])
            p["wglu"] = d(f"wglu{l}", [256, 256])
            p["bglu"] = d(f"bglu{l}", [128, 2])
            p["dalam"] = d(f"dalam{l}", [1, 256])
            p["dag"] = d(f"dag{l}", [128, 1])
            self.L.append(p)
        s = self.dscr
        self.uaT = s("uaT", [256, TT])
        self.gaT = s("gaT", [256, TT])
        self.gsT = s("gsT", [256, TT])
        self.kT = s("kT", [512, TT], BF16)
        self.qT = s("qT", [512, TT], BF16)
        self.v_s = s("v_s", [TT, 512], BF16)
        self.gdT = s("gdT", [512, TT])
        self.us_s = s("us_s", [TT, 256])
        self.yas = s("yas", [512, TT], BF16)
        self.hbuf = s("hbuf", [TT, D])
        self.zs = s("zs", [TT, 256])
        self.out = self.nc.dram_tensor("out", [T, D], F32, kind="ExternalOutput").ap()
        if self.debug:
            self.dbg_mod = s("dbg_mod", [128, 24, 2])
            self.dbg_gate = s("dbg_gate", [128, 2, D])
            self.dbg_hl = s("dbg_hl", [2, 2, 128, TT])

    def build(self):
        nc = self.nc
        self.declare()
        with contextlib.ExitStack() as st:
            self.k = K(nc, st)
            k = self.k
            g = lambda name, shape, dt=F32: self.sb(st, name, shape, dt)
            self.identf = g("identf", [128, 128])
            self.identb = g("identb", [128, 128], BF16)
            self.permf = g("permf", [128, 128])
            self.ones1 = g("ones1", [1, 128])
            self.sc = g("sc", [128, 8, 2])
            self.modv = g("modv", [128, 24, 2])
            self.gmul = g("gmul", [128, 8, 2])
            self.gateb = g("gateb", [128, 2, D])
            self.nlam = g("nlam", [128, 1])
            k.dma("sync", self.identf[:], self.ident, writes=["identf"])
            k.dma("sync", self.permf[:], self.perm, writes=["permf"])
            k.dma("sync", self.sc[:], self.cvec, writes=["sc"])
            k.op(k.dve, lambda: nc.vector.tensor_copy(self.identb[:], self.identf[:]), reads=["identf"], writes=["identb"])
            k.op(k.dve, lambda: nc.vector.memset(self.ones1[:], 1.0), writes=["ones1"])
            k.op(k.act, lambda: nc.scalar.activation(self.sc[:], self.sc[:], AF.Silu), reads=["sc"], writes=["sc"])
            k.barrier()
            for l in self.layers:
                last = (l == DEPTH - 1)
                hsrc = self.hin if l == 0 else self.hbuf
                with contextlib.ExitStack() as pst:
                    wbf = self.proj_prefetch(l, pst) if "P" in self.phases else None
                    if "M" in self.phases:
                        self.phase_mod(l)
                        k.barrier()
                    if "P" in self.phases:
                        self.phase_proj(l, hsrc, wbf)
                        k.barrier()
                if "L" in self.phases:
                    self.phase_lru(l)
                    k.barrier()
                if "S" in self.phases:
                    self.phase_s5(l)
                    k.barrier()
                with contextlib.ExitStack() as ast:
                    pre = None
                    if "A" in self.phases:
                        pre = self.att_prefetch(l, ast)
                    if "S" in self.phases and getattr(self, "s5_stop", 9) > 4:
                        self.phase_s5b(l)
                    k.barrier()
                    if pre is not None:
                        self._att_tmp.close()
                    if "A" in self.phases:
                        self.phase_att(l, hsrc, last, pre)
                        k.barrier()
            k.finish()
        return nc

    def phase_mod(self, l):
        nc, k, p = self.nc, self.k, self.L[l]
        with contextlib.ExitStack() as ph:
            wb = [self.sb(ph, f"m_wb{i}", [128, 8, 512]) for i in range(3)]
            bmr = self.sb(ph, "m_bmr", [2, 3 * D])
            ng = self.sb(ph, "m_ng", [128, 8, 2])
            sel = self.sb(ph, "m_sel", [2, 2, 128])
            rowv = self.sb(ph, "m_rowv", [2, 3 * D])
            psr = [self.ps(ph, f"m_psr{i}", [128, 512]) for i in range(2)]
            psT = self.ps(ph, "m_psT", [128, 512])
            psb = [self.ps(ph, f"m_psb{i}", [128, 512]) for i in range(2)]
            k.dma("scalar", bmr[:], p["bmodR"], writes=["bmr"])
            k.dma("scalar", ng[:], p["normg"], writes=["ng"])
            k.dma("scalar", sel[:], self.sel2, writes=["sel"])
            wsrc = p["w_mod"].rearrange("(k p) n -> p k n", p=128)
            for cb in range(6):
                w = wb[cb % 3]
                wk = f"wb{cb % 3}"
                for k4 in range(2):
                    k.dma("sync", w[:, k4 * 4:(k4 + 1) * 4, :],
                          wsrc[:, k4 * 4:(k4 + 1) * 4, cb * 512:(cb + 1) * 512], writes=[wk + f"_{k4}"])
                pr_ = psr[cb % 2]
                for k8 in range(8):
                    k.op(k.pe, lambda: nc.tensor.matmul(pr_[0:2, :], self.sc[:, k8, :], w[:, k8, :], start=(k8 == 0), stop=(k8 == 7)),
                         reads=[wk + f"_{k8 // 4}", "sc"], writes=[f"psr{cb % 2}"])
                cs = slice(cb * 512, (cb + 1) * 512)
                k.op(k.dve, lambda: nc.vector.tensor_tensor(rowv[:, cs], pr_[0:2, :], bmr[:, cs], ALU.add),
                     reads=[f"psr{cb % 2}", "bmr"], writes=[f"rowv{cb}"])
                for j in range(4):
                    ft = cb * 4 + j
                    k.op(k.pe, lambda: nc.tensor.transpose(psT[:, ft * 2:(ft + 1) * 2], rowv[0:2, ft * 128:(ft + 1) * 128], self.identf[0:2, 0:2]),
                         reads=[f"rowv{cb}", "identf"], writes=["psT"])
            k.op(k.dve, lambda: nc.vector.tensor_copy(self.modv[:], psT[:, 0:48].rearrange("p (a b) -> p a b", b=2)),
                 reads=["psT"], writes=["modv"])
            k.op(k.dve, lambda: nc.vector.scalar_tensor_tensor(self.gmul[:], self.modv[:, 8:16, :], 1.0, ng[:], ALU.add, ALU.mult),
                 reads=["modv", "ng"], writes=["gmul"])
            for v in range(2):
                for hf in range(2):
                    pb = psb[hf]
                    k.op(k.pe, lambda: nc.tensor.matmul(pb[:], sel[0:2, v, :], rowv[0:2, 2 * D + hf * 512:2 * D + (hf + 1) * 512], start=True, stop=True),
                         reads=["sel", f"rowv{4 + hf}"], writes=[f"psb{hf}"])
                    k.op(k.dve, lambda: nc.vector.tensor_copy(self.gateb[:, v, hf * 512:(hf + 1) * 512], pb[:]),
                         reads=[f"psb{hf}"], writes=[f"gateb{v}{hf}"])
            if self.debug:
                k.dma("gpsimd", self.dbg_mod, self.modv[:], reads=["modv"])
                k.dma("gpsimd", self.dbg_gate, self.gateb[:], reads=["gateb00", "gateb01", "gateb10", "gateb11"])

    def proj_prefetch(self, l, st_):
        nc, k, p = self.nc, self.k, self.L[l]
        wbf = self.sb(st_, "p_wbf", [128, 8, 3 * D], BF16)
        stg = [self.sb(st_, f"p_stg{i}", [128, 8, 256]) for i in range(3)]
        wsrc = p["w_in"].rearrange("(k p) n -> p k n", p=128)
        for cb in range(12):
            s_ = stg[cb % 3]
            k.dma("scalar", s_[:], wsrc[:, :, cb * 256:(cb + 1) * 256], writes=[f"pstg{cb % 3}"])
            if cb % 3 == 2:
                k.op(k.pool, lambda: nc.gpsimd.tensor_copy(wbf[:, :, cb * 256:(cb + 1) * 256], s_[:]), reads=[f"pstg{cb % 3}"], writes=[f"wbf{cb}"])
            else:
                k.op(k.dve, lambda: nc.vector.tensor_copy(wbf[:, :, cb * 256:(cb + 1) * 256], s_[:]), reads=[f"pstg{cb % 3}"], writes=[f"wbf{cb}"])
        return wbf

    def phase_proj(self, l, hsrc, wbf):
        nc, k, p = self.nc, self.k, self.L[l]
        with contextlib.ExitStack() as ph:
            wkeys = []
            xb = [self.sb(ph, f"p_x{i}", [128, D]) for i in range(4)]
            junk = self.sb(ph, "p_junk", [128, D], BF16)
            xh = [self.sb(ph, f"p_xh{i}", [128, D], BF16) for i in range(4)]
            ss = self.sb(ph, "p_ss", [128, 4])
            nT = [self.sb(ph, f"p_nT{i}", [128, 8, 512], BF16) for i in range(2)]
            cosb = [self.sb(ph, f"p_cos{i}", [128, 512]) for i in range(2)]
            sinb = [self.sb(ph, f"p_sin{i}", [128, 512]) for i in range(2)]
            kf = [self.sb(ph, f"p_kf{i}", [128, 512]) for i in range(2)]
            t1 = [self.sb(ph, f"p_t1{i}", [128, 512]) for i in range(2)]
            t2 = [self.sb(ph, f"p_t2{i}", [128, 512]) for i in range(2)]
            of32 = [self.sb(ph, f"p_of{i}", [128, 512]) for i in range(3)]
            obf = [self.sb(ph, f"p_ob{i}", [128, 512], BF16) for i in range(3)]
            pT = [self.ps(ph, f"p_pT{i}", [128, 1024], BF16) for i in range(2)]
            pm = [self.ps(ph, f"p_pm{i}", [128, 512]) for i in range(4)]
            pr = [self.ps(ph, f"p_pr{i}", [128, 512]) for i in range(2)]
            blocks = [(0, 256)] + [(256 + 512 * i, 512) for i in range(8)]
            cnt = {"x": 0, "pm": 0, "pr": 0, "of": 0, "ob": 0, "kf": 0, "ev": 0}

            def rot(name, n):
                i = cnt[name] % n
                cnt[name] += 1
                return i

            def stage1a(bi):
                t0, nt = blocks[bi]
                lat = t0 >= TC
                if lat:
                    cb_, sb_ = cosb[bi % 2], sinb[bi % 2]
                    k.dma("sync", cb_[:], self.ropeC[:, t0 - TC:t0 - TC + 512], writes=[f"cos{bi % 2}"])
                    k.dma("sync", sb_[:], self.ropeS[:, t0 - TC:t0 - TC + 512], writes=[f"sin{bi % 2}"])
                for j in range(nt // 128):
                    xi = (bi * 4 + j) % 4
                    x_, xh_ = xb[xi], xh[xi]
                    k.dma("sync", x_[:], hsrc[t0 + j * 128:t0 + (j + 1) * 128, :], writes=[f"x{xi}"])
                    k.op(k.act, lambda: nc.scalar.activation(junk[:], x_[:], AF.Square, accum_out=ss[:, xi:xi + 1]),
                         reads=[f"x{xi}"], writes=["junk", f"ss{xi}"])
                    k.op(k.act, lambda: nc.scalar.activation(ss[:, xi:xi + 1], ss[:, xi:xi + 1], AF.Sqrt, scale=1.0 / D, bias=EPS),
                         reads=[f"ss{xi}"], writes=[f"ss{xi}"])
                    k.op(k.dve, lambda: nc.vector.reciprocal(ss[:, xi:xi + 1], ss[:, xi:xi + 1]),
                         reads=[f"ss{xi}"], writes=[f"ss{xi}"])
                    k.op(k.dve, lambda: nc.vector.tensor_scalar_mul(xh_[:], x_[:], ss[:, xi:xi + 1]),
                         reads=[f"x{xi}", f"ss{xi}"], writes=[f"xh{xi}"])

            def stage1b(bi):
                t0, nt = blocks[bi]
                v = 1 if t0 < TC else 0
                nTb = nT[bi % 2]
                nk = f"nT{bi % 2}"
                for j in range(nt // 128):
                    xi = (bi * 4 + j) % 4
                    xh_ = xh[xi]
                    pi_ = rot("x", 2)
                    pT_ = pT[pi_]
                    for k8 in range(8):
                        k.op(k.pe, lambda: nc.tensor.transpose(pT_[:, k8 * 128:(k8 + 1) * 128], xh_[:, k8 * 128:(k8 + 1) * 128], self.identb[:]),
                             reads=[f"xh{xi}", "identb"], writes=[f"pT{pi_}"])
                    for k8 in range(8):
                        if k8 % 2 == 0:
                            k.op(k.dve, lambda: nc.vector.tensor_scalar(
                                nTb[:, k8, j * 128:(j + 1) * 128], pT_[:, k8 * 128:(k8 + 1) * 128],
                                self.gmul[:, k8, v:v + 1], self.modv[:, k8, v:v + 1], ALU.mult, ALU.add),
                                reads=[f"pT{pi_}", "gmul", "modv"], writes=[nk])
                        else:
                            k.op(k.act, lambda: nc.scalar.activation(
                                nTb[:, k8, j * 128:(j + 1) * 128], pT_[:, k8 * 128:(k8 + 1) * 128], AF.Identity,
                                scale=self.gmul[:, k8, v:v + 1], bias=self.modv[:, k8, v:v + 1]),
                                reads=[f"pT{pi_}", "gmul", "modv"], writes=[nk])

            def stage2(bi, part):
                t0, nt = blocks[bi]
                ntile = nt // 128
                nTb = nT[bi % 2]
                nk = f"nT{bi % 2}"
                lat = t0 >= TC
                cb_, sb_ = cosb[bi % 2], sinb[bi % 2]
                pending = []
                for name, dst in ((("ua", self.uaT), ("ga", self.gaT), ("gs", self.gsT), ("gd", self.gdT), ("k", self.kT), ("q", self.qT)) if part == 0 else []):
                    co, w = WCOL[name]
                    for mt in range(w // 128):
                        pi = rot("pm", 4)
                        pm_ = pm[pi]
                        for k8 in range(8):
                            k.op(k.pe, lambda k8=k8, pm_=pm_, co=co, mt=mt: nc.tensor.matmul(
                                pm_[:, :nt], wbf[:, k8, co + mt * 128:co + (mt + 1) * 128], nTb[:, k8, :nt],
                                start=(k8 == 0), stop=(k8 == 7)), reads=wkeys + [nk], writes=[f"pm{pi}"])
                        while pending:
                            pending.pop(0)()
                        rows = slice(mt * 128, (mt + 1) * 128)
                        if name in ("ua", "ga", "gs", "gd"):
                            oi = rot("of", 3)
                            o_ = of32[oi]
                            k.op(k.act, lambda o_=o_, pm_=pm_: nc.scalar.copy(o_[:, :nt], pm_[:, :nt]),
                                 reads=[f"pm{pi}"], writes=[f"of{oi}"])
                            k.dma("gpsimd", dst[rows, t0:t0 + nt], o_[:, :nt], reads=[f"of{oi}"])
                        elif not lat:
                            oi = rot("ob", 3)
                            o_ = obf[oi]
                            k.op(k.act, lambda o_=o_, pm_=pm_: nc.scalar.copy(o_[:, :nt], pm_[:, :nt]),
                                 reads=[f"pm{pi}"], writes=[f"ob{oi}"])
                            k.dma("gpsimd", dst[rows, t0:t0 + nt], o_[:, :nt], reads=[f"ob{oi}"])
                        else:
                            ki = rot("kf", 2)
                            kf_, t1_, t2_, pr_ = kf[ki], t1[ki], t2[ki], pr[ki]
                            k.op(k.act, lambda kf_=kf_, pm_=pm_: nc.scalar.copy(kf_[:], pm_[:]),
                                 reads=[f"pm{pi}"], writes=[f"kf{ki}"])
                            k.op(k.pool, lambda kf_=kf_, t1_=t1_: nc.gpsimd.tensor_tensor(t1_[:], kf_[:], cb_[:], ALU.mult),
                                 reads=[f"kf{ki}", f"cos{bi % 2}"], writes=[f"t1{ki}"])

                            def part_b(kf_=kf_, t1_=t1_, t2_=t2_, pr_=pr_, ki=ki, dst=dst, rows=rows):
                                k.op(k.pe, lambda: nc.tensor.matmul(pr_[:], self.permf[:], kf_[:], start=True, stop=True),
                                     reads=[f"kf{ki}", "permf"], writes=[f"pr{ki}"])
                                k.op(k.dve, lambda: nc.vector.tensor_tensor(t2_[:], pr_[:], sb_[:], ALU.mult),
                                     reads=[f"pr{ki}", f"sin{bi % 2}"], writes=[f"t2{ki}"])
                                oi = rot("ob", 3)
                                o_ = obf[oi]
                                k.op(k.dve, lambda: nc.vector.tensor_tensor(o_[:], t1_[:], t2_[:], ALU.add),
                                     reads=[f"t1{ki}", f"t2{ki}"], writes=[f"ob{oi}"])
                                k.dma("gpsimd", dst[rows, t0:t0 + nt], o_[:, :nt], reads=[f"ob{oi}"])
                            pending.append(part_b)
                while pending:
                    pending.pop(0)()
                for j in (range(ntile) if part == 1 else []):
                    r0 = t0 + j * 128
                    for name, dst in (("v", self.v_s), ("us", self.us_s)):
                        co, w = WCOL[name]
                        pi = rot("pm", 4)
                        pm_ = pm[pi]
                        for k8 in range(8):
                            k.op(k.pe, lambda k8=k8, pm_=pm_, co=co, w=w, j=j: nc.tensor.matmul(
                                pm_[:, :w], nTb[:, k8, j * 128:(j + 1) * 128], wbf[:, k8, co:co + w],
                                start=(k8 == 0), stop=(k8 == 7)), reads=wkeys + [nk], writes=[f"pm{pi}"])
                        while pending:
                            pending.pop(0)()
                        ev = rot("ev", 2)
                        eng = k.act if ev == 0 else k.dve
                        if name == "v":
                            oi = rot("ob", 3)
                            o_ = obf[oi]
                            key = f"ob{oi}"
                        else:
                            oi = rot("of", 3)
                            o_ = of32[oi]
                            key = f"of{oi}"
                        if eng is k.act:
                            k.op(eng, lambda o_=o_, pm_=pm_, w=w: nc.scalar.copy(o_[:, :w], pm_[:, :w]), reads=[f"pm{pi}"], writes=[key])
                        else:
                            k.op(eng, lambda o_=o_, pm_=pm_, w=w: nc.vector.tensor_copy(o_[:, :w], pm_[:, :w]), reads=[f"pm{pi}"], writes=[key])
                        k.dma("gpsimd", dst[r0:r0 + 128, :], o_[:, :w], reads=[key])

            stage1a(0)
            stage1b(0)
            for bi in range(len(blocks)):
                if bi + 1 < len(blocks):
                    stage1a(bi + 1)
                stage2(bi, 0)
                if bi + 1 < len(blocks):
                    stage1b(bi + 1)
                stage2(bi, 1)


def _fp(a):
    return np.ascontiguousarray(a, dtype=np.float32)


def _const_tables():
    c = {}
    c["ident"] = np.eye(128, dtype=np.float32)
    r = np.arange(128)
    partner = np.where((r % 32) < 16, r + 16, r - 16)
    perm = np.zeros((128, 128), np.float32)
    perm[partner, r] = 1.0
    c["perm"] = perm
    sel2 = np.zeros((2, 2, 128), np.float32)
    sel2[0, 0, :] = 1.0
    sel2[1, 1, :] = 1.0
    c["sel2"] = sel2
    f = np.arange(16, dtype=np.float32)
    inv_freq = np.exp(np.float32(-np.log(10000.0)) * f / np.float32(16)).astype(np.float32)
    t = np.arange(T)
    row_pos = (t // 64).astype(np.float32)
    col_pos = (t % 64).astype(np.float32)
    d64 = r % 64
    is_row = d64 < 32
    fidx = (d64 % 32) % 16
    ang = np.where(is_row[:, None], row_pos[None, :], col_pos[None, :]).astype(np.float32) * inv_freq[fidx][:, None]
    ang = ang.astype(np.float32)
    sign = np.where((d64 % 32) < 16, -1.0, 1.0).astype(np.float32)
    c["ropeC"] = np.cos(ang).astype(np.float32)
    c["ropeS"] = (np.sin(ang).astype(np.float32) * sign[:, None]).astype(np.float32)
    m = np.arange(NCH + 1, dtype=np.float32)
    idx = np.zeros((128, 16, NCH + 1), np.float32)
    idx[:, 0:8, :] = m[None, None, :]
    idx[:, 8:16, :] = -m[None, None, :]
    c["s5idx"] = idx
    e = np.zeros((128, 3, 16, 16), np.float32)
    j = np.arange(16, dtype=np.float32)
    e[:, 0, 0:8, :] = 15 - j
    e[:, 0, 8:16, :] = j
    e[:, 1, 0:8, :] = j + 1
    e[:, 1, 8:16, :] = 16 - j
    e[:, 2, 0:8, :] = j - 15
    e[:, 2, 8:16, :] = -j
    c["s5exp"] = e
    pidx = np.arange(128)
    i8 = pidx // 16
    jj = np.repeat(np.arange(16), 16)
    msk = np.zeros((128, 2, 2, 256), np.float32)
    for kt in range(2):
        i = (8 * kt + i8)[:, None]
        msk[:, 0, kt, :] = (jj[None, :] >= i)
        msk[:, 1, kt, :] = (i >= jj[None, :])
    c["s5mask"] = msk
    return c


_PERM_COLS = None


def _wcol_perm():
    o = {"ua": (0, 256), "us": (256, 256), "k": (512, 512), "v": (1024, 512),
         "ga": (1536, 256), "gs": (1792, 256), "q": (2048, 512), "gd": (2560, 512)}
    idx = np.zeros(3 * D, np.int64)
    for name, (off, w) in WCOL.items():
        so, sw = o[name]
        idx[off:off + w] = np.arange(so, so + sw)
    return idx


def _fpart(vec, ntile):
    return _fp(np.asarray(vec).reshape(ntile, 128).T)


def _layer_inputs(inp, l):
    o = {}
    o[f"w_mod{l}"] = _fp(inp["w_mod"][l])
    o[f"bmodR{l}"] = _fp(np.repeat(inp["b_mod"][l][None, :], 2, axis=0))
    ng = _fpart(inp["norm_g"][l], 8)
    o[f"normg{l}"] = _fp(np.repeat(ng[:, :, None], 2, axis=2))
    o[f"w_in{l}"] = _fp(inp["w_in"][l][:, _wcol_perm()])
    o[f"w_out{l}"] = _fp(inp["w_out"][l])
    cw = inp["lru_conv_w"][l]
    o[f"convw{l}"] = _fp(cw.T.reshape(2, 128, 4).transpose(1, 0, 2))
    o[f"convb{l}"] = _fpart(inp["lru_conv_b"][l], 2)
    wax = np.zeros((128, 2, 2, 2, 128), np.float32)
    for ai, nm in enumerate(("lru_wa", "lru_wx")):
        w = inp[nm][l]
        for d in range(2):
            for ct in range(2):
                for h in range(2):
                    wax[h * 64:(h + 1) * 64, ct, ai, d, h * 64:(h + 1) * 64] = w[d, 2 * ct + h]
    o[f"wax{l}"] = wax
    bax = np.zeros((128, 2, 2, 2), np.float32)
    for ai, nm in enumerate(("lru_ba", "lru_bx")):
        for d in range(2):
            bax[:, :, ai, d] = _fpart(inp[nm][l][d], 2)
    o[f"bax{l}"] = bax
    ll = np.zeros((128, 2, 2), np.float32)
    for d in range(2):
        ll[:, :, d] = _fpart(inp["lru_lam"][l][d], 2)
    o[f"lrulam{l}"] = ll

    def tp_layout(a):
        a = np.asarray(a)
        rest = a.shape[3:]
        a = a.reshape((2, 8, 2, 64) + rest)
        a = np.moveaxis(a, (2, 3), (0, 1))
        return a.reshape((128, 16) + rest)

    s5lam = np.zeros((128, 3, 16), np.float32)
    s5lam[:, 0] = tp_layout(inp["s5_lam_re"][l])
    s5lam[:, 1] = tp_layout(inp["s5_lam_im"][l])
    s5lam[:, 2] = tp_layout(np.repeat(inp["s5_log_dt"][l][:, :, None], 64, axis=2))
    o[f"s5lam{l}"] = s5lam
    sb_ = np.zeros((128, 2, 16, 16), np.float32)
    sb_[:, 0] = tp_layout(inp["s5_b_re"][l])
    sb_[:, 1] = tp_layout(inp["s5_b_im"][l])
    o[f"s5b{l}"] = sb_
    sc_ = np.zeros((128, 2, 16, 16), np.float32)
    sc_[:, 0] = tp_layout(np.swapaxes(inp["s5_c_re"][l], 2, 3))
    sc_[:, 1] = tp_layout(np.swapaxes(inp["s5_c_im"][l], 2, 3))
    o[f"s5c{l}"] = sc_
    o[f"s5d{l}"] = _fp(inp["s5_d"][l][None, :])
    o[f"wglu{l}"] = _fp(inp["s5_w_glu"][l])
    o[f"bglu{l}"] = _fpart(inp["s5_b_glu"][l], 2)
    o[f"dalam{l}"] = _fp(inp["da_lam"][l].reshape(1, 256))
    o[f"dag{l}"] = _fp(inp["da_norm_g"][l][:, None])
    return o


def prep_inputs(inp):
    shared = dict(_const_tables())
    shared["finalg"] = _fp(inp["final_g"][None, :])
    for l in range(DEPTH):
        shared.update(_layer_inputs(inp, l))
    maps = []
    for b in range(8):
        m = dict(shared)
        m["hin"] = _fp(np.concatenate([inp["ctx"][b], inp["x"][b]], axis=0))
        cv = np.zeros((128, 8, 2), np.float32)
        cv[:, :, 0] = _fpart(inp["c"][b], 8)
        cv[:, :, 1] = _fpart(inp["c_ctx"], 8)
        m["cvec"] = cv
        maps.append(m)
    return maps


_NC_CACHE = {}


def kernel(**inputs):
    inp = {k_: np.asarray(v) for k_, v in inputs.items()}
    maps = prep_inputs(inp)
    if "nc" not in _NC_CACHE:
        _NC_CACHE["nc"] = Builder().build()
    res = run_bass_kernel_spmd(_NC_CACHE["nc"], maps, core_ids=list(range(8)))
    return np.stack([np.asarray(r["out"]) for r in res.results], axis=0).astype(np.float32)


def _phase_lru(self, l):
    nc, k, p = self.nc, self.k, self.L[l]
    with contextlib.ExitStack() as ph:
        N = TT
        convw = self.sb(ph, "l_convw", [128, 2, 4])
        convb = self.sb(ph, "l_convb", [128, 2])
        waxf = self.sb(ph, "l_waxf", [128, 2, 2, 2, 128])
        waxb = self.sb(ph, "l_waxb", [128, 2, 2, 2, 128], BF16)
        bax = self.sb(ph, "l_bax", [128, 2, 2, 2])
        lam = self.sb(ph, "l_lam", [128, 2, 2])
        cl = self.sb(ph, "l_cl", [128, 2, 2])
        cl2 = self.sb(ph, "l_cl2", [128, 2, 2])
        ua = self.sb(ph, "l_ua", [128, N])
        xc = self.sb(ph, "l_xc", [128, N])
        xcb = self.sb(ph, "l_xcb", [128, N], BF16)
        sg = self.sb(ph, "l_sg", [128, N])
        gr = [self.sb(ph, f"l_gr{d}", [128, N]) for d in range(2)]
        gi = [self.sb(ph, f"l_gi{d}", [128, N]) for d in range(2)]
        a2 = [self.sb(ph, f"l_a2{d}", [128, N]) for d in range(2)]
        hd = [self.sb(ph, "l_h0", [128, N]), ua]
        yb = xcb
        pg = [self.ps(ph, f"l_pg{i}", [128, 512]) for i in range(4)]
        k.dma("sync", convw[:], p["convw"], writes=["convw"])
        k.dma("sync", convb[:], p["convb"], writes=["convb"])
        k.dma("sync", waxf[:], p["wax"], writes=["waxf"])
        k.dma("sync", bax[:], p["bax"], writes=["bax"])
        k.dma("sync", lam[:], p["lrulam"], writes=["lam"])
        k.op(k.dve, lambda: nc.vector.tensor_copy(waxb[:], waxf[:]), reads=["waxf"], writes=["waxb"])
        k.op(k.act, lambda: nc.scalar.activation(cl[:], lam[:], AF.Exp, scale=-1.0), reads=["lam"], writes=["cl"])
        k.op(k.act, lambda: nc.scalar.activation(cl[:], cl[:], AF.Ln, scale=1.0, bias=1.0), reads=["cl"], writes=["cl"])
        k.op(k.dve, lambda: nc.vector.tensor_scalar_mul(cl2[:], cl[:], -16.0), reads=["cl"], writes=["cl2"])
        k.op(k.dve, lambda: nc.vector.tensor_scalar_mul(cl[:], cl[:], -8.0), reads=["cl", "cl2"], writes=["cl"])
        segs = [(0, TC), (TC, TT)]
        nblk = [(i * 512, min(512, N - i * 512)) for i in range((N + 511) // 512)]
        pgi = 0
        for ct in range(2):
            rows = slice(ct * 128, (ct + 1) * 128)
            k.dma("sync", ua[:], self.uaT[rows, :], writes=["ua"])
            k.dma("scalar", sg[:], self.gaT[rows, :], writes=["sg"])
            k.op(k.dve, lambda: nc.vector.tensor_scalar(xc[:], ua[:], convw[:, ct, 2:3], convb[:, ct:ct + 1], ALU.mult, ALU.add),
                 reads=["ua", "convw", "convb"], writes=["xc"])
            for (s0, s1) in segs:
                for tap, off in ((0, -2), (1, -1), (3, 1)):
                    lo = max(s0, s0 - off)
                    hi = min(s1, s1 - off)
                    k.op(k.dve, lambda: nc.vector.scalar_tensor_tensor(
                        xc[:, lo:hi], ua[:, lo + off:hi + off], convw[:, ct, tap:tap + 1], xc[:, lo:hi], ALU.mult, ALU.add),
                        reads=["ua", "xc", "convw"], writes=["xc"])
            k.op(k.act, lambda: nc.scalar.copy(xcb[:], xc[:]), reads=["xc"], writes=["xcb"])
            for d in range(2):
                for bi, (c0, w) in enumerate(nblk):
                    pr_, pi_ = pg[pgi % 4], pg[(pgi + 1) % 4]
                    kr, ki = f"pg{pgi % 4}", f"pg{(pgi + 1) % 4}"
                    pgi += 2
                    k.op(k.pe, lambda: nc.tensor.matmul(pr_[:, :w], waxb[:, ct, 0, d, :], xcb[:, c0:c0 + w], start=True, stop=True),
                         reads=["waxb", "xcb"], writes=[kr])
                    k.op(k.pe, lambda: nc.tensor.matmul(pi_[:, :w], waxb[:, ct, 1, d, :], xcb[:, c0:c0 + w], start=True, stop=True),
                         reads=["waxb", "xcb"], writes=[ki])
                    k.op(k.act, lambda: nc.scalar.activation(gr[d][:, c0:c0 + w], pr_[:, :w], AF.Sigmoid, bias=bax[:, ct, 0, d:d + 1]),
                         reads=[kr, "bax"], writes=[f"gr{d}"])
                    k.op(k.act, lambda: nc.scalar.activation(gi[d][:, c0:c0 + w], pi_[:, :w], AF.Sigmoid, bias=bax[:, ct, 1, d:d + 1]),
                         reads=[ki, "bax"], writes=[f"gi{d}"])
                k.op(k.dve, lambda: nc.vector.tensor_tensor(gi[d][:], gi[d][:], xc[:], ALU.mult), reads=[f"gi{d}", "xc"], writes=[f"gi{d}"])
            for d in range(2):
                k.op(k.act, lambda: nc.scalar.activation(a2[d][:], gr[d][:], AF.Exp, scale=cl2[:, ct, d:d + 1]), reads=[f"gr{d}", "cl2"], writes=[f"a2{d}"])
                k.op(k.act, lambda: nc.scalar.activation(gr[d][:], gr[d][:], AF.Exp, scale=cl[:, ct, d:d + 1]), reads=[f"gr{d}", "cl", f"a2{d}"], writes=[f"gr{d}"])
            for d in range(2):
                k.op(k.act, lambda: nc.scalar.activation(a2[d][:], a2[d][:], AF.Sqrt, scale=-1.0, bias=1.0), reads=[f"a2{d}"], writes=[f"a2{d}"])
                k.op(k.dve, lambda: nc.vector.tensor_tensor(gi[d][:], gi[d][:], a2[d][:], ALU.mult), reads=[f"gi{d}", f"a2{d}"], writes=[f"gi{d}"])
            k.op(k.act, lambda: nc.scalar.activation(sg[:], sg[:], AF.Silu), reads=["sg"], writes=["sg"])
            k.op(k.dve, lambda: nc.vector.tensor_tensor_scan(hd[0][:], gr[0][:], gi[0][:], 0.0, ALU.mult, ALU.add),
                 reads=["gr0", "gi0"], writes=["h0"])
            k.op(k.dve, lambda: nc.vector.tensor_tensor_scan(hd[1][:, 0:TC][:, ::-1], gr[1][:, 0:TC][:, ::-1], gi[1][:, 0:TC][:, ::-1],
                                                              0.0, ALU.mult, ALU.add), reads=["gr1", "gi1"], writes=["ua"])
            k.op(k.dve, lambda: nc.vector.tensor_tensor_scan(hd[1][:, TC:TT][:, ::-1], gr[1][:, TC:TT][:, ::-1], gi[1][:, TC:TT][:, ::-1],
                                                              hd[1][:, 0:1], ALU.mult, ALU.add), reads=["gr1", "gi1", "ua"], writes=["ua"])
            if self.debug:
                for d in range(2):
                    k.dma("gpsimd", self.dbg_hl[ct, d], hd[d][:], reads=[("h0" if d == 0 else "ua")])
            k.op(k.pool, lambda: nc.gpsimd.tensor_tensor(hd[0][:], hd[0][:], sg[:], ALU.mult), reads=["h0", "sg"], writes=["h0"])
            k.op(k.dve, lambda: nc.vector.tensor_tensor(hd[1][:], hd[1][:], sg[:], ALU.mult), reads=["ua", "sg"], writes=["ua"])
            k.op(k.dve, lambda: nc.vector.tensor_tensor(yb[:], hd[0][:], hd[1][:], ALU.add), reads=["h0", "ua"], writes=["xcb"])
            k.dma("gpsimd", self.yas[rows, :], yb[:], reads=["xcb"])


Builder.phase_lru = _phase_lru


def _att_prefetch(self, l, st_):
    nc, k, p = self.nc, self.k, self.L[l]
    kTs = self.sb(st_, "a_kT", [128, 4, TT], BF16)
    vs = self.sb(st_, "a_v", [128, NTT, 512], BF16)
    woutb = self.sb(st_, "a_wout", [128, 8, D], BF16)
    self._att_tmp = contextlib.ExitStack()
    stg = self.sb(self._att_tmp, "a_stg", [128, 8, 256])
    wsrc = p["w_out"].rearrange("(k p) n -> p k n", p=128)
    for q4 in range(4):
        k.dma("scalar", stg[:], wsrc[:, :, q4 * 256:(q4 + 1) * 256], writes=["astg"])
        k.op(k.pool, lambda: nc.gpsimd.tensor_copy(woutb[:, :, q4 * 256:(q4 + 1) * 256], stg[:]), reads=["astg"], writes=[f"wout{q4}"])
    if True:
        k.dma("scalar", kTs[:], self.kT.rearrange("(j p) t -> p j t", p=128), writes=["kTs"])
        vsrc = self.v_s.rearrange("(tt p) e -> p tt e", p=128)
        for q4 in range(0, NTT, 9):
            hi_ = min(NTT, q4 + 9)
            k.dma("scalar", vs[:, q4:hi_, :], vsrc[:, q4:hi_, :], writes=[f"vs{q4}"])
    return kTs, vs, woutb


def _phase_att(self, l, hsrc, last, pre):
    nc, k, p = self.nc, self.k, self.L[l]
    lam_init = 0.8 - 0.6 * float(np.exp(-0.3 * l))
    with contextlib.ExitStack() as ph:
        kTs, vs, woutb = pre
        dal = self.sb(ph, "a_dal", [1, 256])
        prod = self.sb(ph, "a_prod", [1, 2, 64])
        e2 = self.sb(ph, "a_e2", [1, 2])
        nl1 = self.sb(ph, "a_nl1", [1, 1])
        dagc = self.sb(ph, "a_dagc", [128, 1])
        onesf = self.sb(ph, "a_onesf", [128, 128])
        onesb = self.sb(ph, "a_onesb", [128, 128], BF16)
        fgb = self.sb(ph, "a_fgb", [128, D])
        qTb = [self.sb(ph, f"a_q{i}", [128, 4, 512], BF16) for i in range(2)]
        yasb = [self.sb(ph, f"a_yas{i}", [128, 4, 512], BF16) for i in range(2)]
        gdb = [self.sb(ph, f"a_gdb{i}", [128, 4, 512]) for i in range(2)]
        PT = [[self.sb(ph, f"a_pt{m}{i}", [128, 512], BF16) for i in range(3)] for m in range(2)]
        acc = [self.sb(ph, f"a_acc{m}", [128, 512]) for m in range(2)]
        rden = [self.sb(ph, f"a_rden{m}", [128, 512]) for m in range(2)]
        obT = [self.sb(ph, f"a_obT{i}", [128, 4, 512]) for i in range(2)]
        cO = [self.sb(ph, f"a_cO{i}", [128, 512]) for i in range(2)]
        t1 = self.sb(ph, "a_t1", [128, 512])
        sqb = self.sb(ph, "a_sqb", [128, 4, 512])
        rstd1 = self.sb(ph, "a_rstd", [128, 512])
        ydT = self.sb(ph, "a_ydT", [128, 4, 512], BF16)
        hx = [self.sb(ph, f"a_hx{i}", [128, D]) for i in range(2)]
        hn = [self.sb(ph, f"a_hn{i}", [128, D]) for i in range(2)]
        junk = self.sb(ph, "a_junk", [128, D], BF16)
        fs = self.sb(ph, "a_fs", [128, 2])
        psc = [self.ps(ph, f"a_psc{i}", [128, 512]) for i in range(4)]
        pO = [self.ps(ph, f"a_pO{i}", [128, 512]) for i in range(2)]
        pden = [self.ps(ph, f"a_pden{i}", [128, 512]) for i in range(2)]
        po = psc[0:2]
        pokeys = ["psc0", "psc1"]

        k.dma("sync", dal[:], p["dalam"], writes=["dal"])
        dv_ = dal[:].rearrange("p (m t e) -> p m t e", m=2, t=2)
        k.op(k.dve, lambda: nc.vector.tensor_tensor(prod[:], dv_[:, :, 0, :], dv_[:, :, 1, :], ALU.mult), reads=["dal"], writes=["prod"])
        k.op(k.dve, lambda: nc.vector.reduce_sum(e2[:], prod[:], axis=AX.X), reads=["prod"], writes=["e2"])
        k.op(k.act, lambda: nc.scalar.activation(e2[:], e2[:], AF.Exp), reads=["e2"], writes=["e2"])
        k.op(k.dve, lambda: nc.vector.tensor_tensor(nl1[:], e2[:, 1:2], e2[:, 0:1], ALU.subtract), reads=["e2"], writes=["nl1"])
        k.op(k.dve, lambda: nc.vector.tensor_scalar_add(nl1[:], nl1[:], -lam_init), reads=["nl1"], writes=["nl1"])
        k.op(k.pe, lambda: nc.tensor.matmul(pden[0][:, 0:1], self.ones1[:], nl1[:], start=True, stop=True), reads=["ones1", "nl1"], writes=["pden0"])
        k.op(k.dve, lambda: nc.vector.tensor_copy(self.nlam[:], pden[0][:, 0:1]), reads=["pden0"], writes=["nlam"])
        k.dma("sync", dagc[:], p["dag"], writes=["dagc"])
        k.op(k.dve, lambda: nc.vector.tensor_scalar_mul(dagc[:], dagc[:], 1.0 - lam_init), reads=["dagc"], writes=["dagc"])
        k.op(k.dve, lambda: nc.vector.memset(onesf[:], 1.0), writes=["onesf"])
        k.op(k.dve, lambda: nc.vector.memset(onesb[:], 1.0), writes=["onesb"])
        if last:
            k.dma("sync", fgb[:], self.finalg.partition_broadcast(128), writes=["fgb"])

        qblocks = [] if last else [(0, 256, [0, 1])]
        qblocks += [(TC + 512 * i, 512, list(range(NTT))) for i in range(8)]
        qTv = self.qT.rearrange("(j p) t -> p j t", p=128)
        yasv = self.yas.rearrange("(j p) t -> p j t", p=128)
        gdv = self.gdT.rearrange("(j p) t -> p j t", p=128)
        pt_i = 0
        pair_i = 0
        tile_i = [0]

        def epilogue1(bi):
            t0, nq, _ = qblocks[bi]
            gd_, ob_ = gdb[bi % 2], obT[bi % 2]
            banks = [(pden[0], "pden0"), (pden[1], "pden1"), (pO[0], "pO0"), (pO[1], "pO1")]
            tmpb = [(rstd1, "rstd"), (t1, "t1"), (cO[0], "cO0"), (cO[1], "cO1")]
            for h in range(4):
                pd, pdk = banks[h]
                k.op(k.pe, lambda: nc.tensor.matmul(pd[:, :nq], onesf[:], sqb[:, h, :nq], start=True, stop=True),
                     reads=["onesf", f"sqb{h}"], writes=[pdk])
            for h in range(4):
                pd, pdk = banks[h]
                rs_, rk = tmpb[h]
                k.op(k.act, lambda: nc.scalar.activation(rs_[:, :nq], pd[:, :nq], AF.Sqrt, scale=1.0 / 128, bias=EPS),
                     reads=[pdk], writes=[rk])
            for h in range(4):
                rs_, rk = tmpb[h]
                k.op(k.dve, lambda: nc.vector.reciprocal(rs_[:, :nq], rs_[:, :nq]), reads=[rk], writes=[rk])
                k.op(k.pool, lambda: nc.gpsimd.tensor_tensor(rs_[:, :nq], rs_[:, :nq], ob_[:, h, :nq], ALU.mult),
                     reads=[rk, f"obT{bi % 2}{h}"], writes=[rk])
                k.op(k.dve, lambda: nc.vector.scalar_tensor_tensor(ydT[:, h, :nq], rs_[:, :nq], dagc[:, 0:1], gd_[:, h, :nq], ALU.mult, ALU.mult),
                     reads=[rk, "dagc", f"gdb{bi % 2}"], writes=["ydT"])

        def epilogue2(bi):
            t0, nq, _ = qblocks[bi]
            v = 1 if t0 < TC else 0
            yb_ = yasb[bi % 2]
            for qs in range(nq // 128):
                r0 = t0 + qs * 128
                gi_ = tile_i[0] % 2
                tile_i[0] += 1
                hx_, hn_ = hx[gi_], hn[gi_]
                k.dma("sync", hx_[:], hsrc[r0:r0 + 128, :], writes=[f"hx{gi_}"])
                for hf in range(2):
                    for mt in range(8):
                        lhs = yb_[:, mt, qs * 128:(qs + 1) * 128] if mt < 4 else ydT[:, mt - 4, qs * 128:(qs + 1) * 128]
                        k.op(k.pe, lambda: nc.tensor.matmul(po[hf][:], lhs, woutb[:, mt, hf * 512:(hf + 1) * 512],
                                                            start=(mt == 0), stop=(mt == 7)),
                             reads=[f"yas{bi % 2}", "ydT"], writes=[pokeys[hf]])
                    cs = slice(hf * 512, (hf + 1) * 512)
                    k.op(k.dve, lambda: nc.vector.tensor_tensor(hn_[:, cs], po[hf][:], self.gateb[:, v, cs], ALU.mult),
                         reads=[pokeys[hf], f"gateb{v}{hf}"], writes=[f"hn{gi_}{hf}"])
                    k.op(k.pool, lambda: nc.gpsimd.tensor_tensor(hn_[:, cs], hn_[:, cs], hx_[:, cs], ALU.add),
                         reads=[f"hn{gi_}{hf}", f"hx{gi_}"], writes=[f"hn{gi_}{hf}"])
                hk = [f"hn{gi_}0", f"hn{gi_}1"]
                if not last:
                    k.dma("gpsimd", self.hbuf[r0:r0 + 128, :], hn_[:], reads=hk)
                else:
                    k.op(k.act, lambda: nc.scalar.activation(junk[:], hn_[:], AF.Square, accum_out=fs[:, gi_:gi_ + 1]),
                         reads=hk, writes=["junk", f"fs{gi_}"])
                    k.op(k.act, lambda: nc.scalar.activation(fs[:, gi_:gi_ + 1], fs[:, gi_:gi_ + 1], AF.Sqrt, scale=1.0 / D, bias=EPS),
                         reads=[f"fs{gi_}"], writes=[f"fs{gi_}"])
                    k.op(k.dve, lambda: nc.vector.reciprocal(fs[:, gi_:gi_ + 1], fs[:, gi_:gi_ + 1]), reads=[f"fs{gi_}"], writes=[f"fs{gi_}"])
                    k.op(k.dve, lambda: nc.vector.scalar_tensor_tensor(hn_[:], hn_[:], fs[:, gi_:gi_ + 1], fgb[:], ALU.mult, ALU.mult),
                         reads=hk + [f"fs{gi_}", "fgb"], writes=hk)
                    k.dma("gpsimd", self.out[r0 - TC:r0 - TC + 128, :], hn_[:], reads=hk)

        for bi, (t0, nq, ktl) in enumerate(qblocks):
            qb_, yb_, gd_, ob_ = qTb[bi % 2], yasb[bi % 2], gdb[bi % 2], obT[bi % 2]
            k.dma("sync", qb_[:, :, :nq], qTv[:, :, t0:t0 + nq], writes=[f"q{bi % 2}"])
            k.dma("sync", yb_[:, :, :nq], yasv[:, :, t0:t0 + nq], writes=[f"yas{bi % 2}"])
            k.dma("sync", gd_[:, :, :nq], gdv[:, :, t0:t0 + nq], writes=[f"gdb{bi % 2}"])
            k.op(k.act, lambda: nc.scalar.activation(gd_[:, :, :nq], gd_[:, :, :nq], AF.Silu), reads=[f"gdb{bi % 2}"], writes=[f"gdb{bi % 2}"])
            its = [(h, ki_, kt) for h in range(4) for ki_, kt in enumerate(ktl)]

            def emit_scores(i):
                h, ki_, kt = its[i]
                for m in range(2):
                    prt = slice(m * 64, (m + 1) * 64)
                    bnk = 2 * ((pair_i + i) % 2) + m
                    k.op(k.pe, lambda: nc.tensor.matmul(psc[bnk][:, :nq], kTs[prt, h, kt * 128:(kt + 1) * 128], qb_[prt, h, :nq],
                                                        start=True, stop=True), reads=["kTs", f"q{bi % 2}"], writes=[f"psc{bnk}"])

            emit_scores(0)
            for i, (h, ki_, kt) in enumerate(its):
                first, lastk = (ki_ == 0), (ki_ == len(ktl) - 1)
                defer_here = lastk and bi > 0 and h == 0
                if i + 1 < len(its) and not defer_here:
                    emit_scores(i + 1)
                for m in range(2):
                    bnk = 2 * ((pair_i + i) % 2) + m
                    pt_ = PT[m][pt_i % 3]
                    pk = f"pt{m}{pt_i % 3}"
                    k.op(k.act, lambda: nc.scalar.activation(pt_[:, :nq], psc[bnk][:, :nq], AF.Exp, scale=0.125), reads=[f"psc{bnk}"], writes=[pk])
                    k.op(k.pe, lambda: nc.tensor.matmul(pO[m][:, :nq], vs[:, kt, h * 128:(h + 1) * 128], pt_[:, :nq], start=first, stop=lastk),
                         reads=[pk], writes=[f"pO{m}"])
                    if m == 0:
                        k.op(k.pe, lambda: nc.tensor.matmul(pden[0][:, :nq], onesb[:], pt_[:, :nq], start=first, stop=lastk),
                             reads=[pk, "onesb"], writes=["pden0"])
                    else:
                        e_ = ki_ % 2
                        eng = k.dve if e_ == 0 else k.pool
                        if ki_ < 2:
                            k.op(eng, lambda: eng.h.tensor_copy(acc[e_][:, :nq], pt_[:, :nq]), reads=[pk], writes=[f"acc{e_}"])
                        else:
                            k.op(eng, lambda: eng.h.tensor_tensor(acc[e_][:, :nq], acc[e_][:, :nq], pt_[:, :nq], ALU.add),
                                 reads=[pk, f"acc{e_}"], writes=[f"acc{e_}"])
                pt_i += 1
                if not lastk:
                    continue
                k.op(k.act, lambda: nc.scalar.copy(cO[0][:, :nq], pO[0][:, :nq]), reads=["pO0"], writes=["cO0"])
                k.op(k.dve, lambda: nc.vector.tensor_copy(cO[1][:, :nq], pO[1][:, :nq]), reads=["pO1"], writes=["cO1"])
                k.op(k.dve, lambda: nc.vector.reciprocal(rden[0][:, :nq], pden[0][:, :nq]), reads=["pden0"], writes=["rden0"])
                k.op(k.pe, lambda: nc.tensor.matmul(pden[1][:, :nq], onesf[:], acc[0][:, :nq], start=True, stop=False),
                     reads=["onesf", "acc0"], writes=["pden1"])
                k.op(k.pe, lambda: nc.tensor.matmul(pden[1][:, :nq], onesf[:], acc[1][:, :nq], start=False, stop=True),
                     reads=["onesf", "acc1"], writes=["pden1"])
                k.op(k.dve, lambda: nc.vector.reciprocal(rden[1][:, :nq], pden[1][:, :nq]), reads=["pden1"], writes=["rden1"])
                k.op(k.dve, lambda: nc.vector.tensor_scalar_mul(rden[1][:, :nq], rden[1][:, :nq], self.nlam[:, 0:1]), reads=["rden1", "nlam"], writes=["rden1"])
                k.op(k.pool, lambda: nc.gpsimd.tensor_tensor(t1[:, :nq], cO[1][:, :nq], rden[1][:, :nq], ALU.mult), reads=["cO1", "rden1"], writes=["t1"])
                k.op(k.dve, lambda: nc.vector.tensor_tensor(ob_[:, h, :nq], cO[0][:, :nq], rden[0][:, :nq], ALU.mult), reads=["cO0", "rden0"], writes=[f"obT{bi % 2}{h}"])
                k.op(k.pool, lambda: nc.gpsimd.tensor_tensor(ob_[:, h, :nq], ob_[:, h, :nq], t1[:, :nq], ALU.add), reads=[f"obT{bi % 2}{h}", "t1"], writes=[f"obT{bi % 2}{h}"])
                k.op(k.pool, lambda: nc.gpsimd.tensor_tensor(sqb[:, h, :nq], ob_[:, h, :nq], ob_[:, h, :nq], ALU.mult),
                     reads=[f"obT{bi % 2}{h}"], writes=[f"sqb{h}"])
                if h == 3:
                    epilogue1(bi)
                if bi > 0 and h == 0:
                    epilogue2(bi - 1)
                    if i + 1 < len(its):
                        emit_scores(i + 1)
            pair_i += len(its)
        epilogue2(len(qblocks) - 1)


Builder.phase_att = _phase_att
Builder.att_prefetch = _att_prefetch
Builder.phase_s5 = lambda self, l: None


def _phase_s5(self, l):
    nc, k, p = self.nc, self.k, self.L[l]
    PI = float(np.pi)
    uid = [0]

    def nm(s_):
        uid[0] += 1
        return f"s_{s_}{uid[0]}"

    def dv(fn, reads, writes):
        return k.op(k.dve, fn, reads=reads, writes=writes)

    I32 = mybir.dt.int32
    PI_LO = 3.1415925

    tcache = {}

    def reduce_pi(st_, out_t, src, shape, rkeys, tmps=None):
        if tmps is None:
            u = self.sb(st_, nm("ru"), shape)
            qi = self.sb(st_, nm("rq"), shape, I32)
        else:
            u, qi = tmps
        dv(lambda: nc.vector.tensor_scalar_mul(u[:], src, 1.0 / TWO_PI), rkeys, [u.name])
        dv(lambda: nc.vector.tensor_copy(qi[:], u[:]), [u.name], [qi.name])
        dv(lambda: nc.vector.tensor_copy(u[:], qi[:]), [qi.name], [u.name])
        dv(lambda: nc.vector.scalar_tensor_tensor(out_t[:], u[:], -TWO_PI, src, ALU.mult, ALU.add), [u.name] + rkeys, [out_t.name])
        dv(lambda: nc.vector.tensor_scalar(out_t[:], out_t[:], -PI_LO, PI_LO, ALU.max, ALU.min), [out_t.name], [out_t.name])

    def sincos(st_, ang, shape, K_, key):
        sn = self.sb(st_, nm("sn"), shape)
        cs = self.sb(st_, nm("cs"), shape)
        ck = (id(st_), tuple(shape))
        if ck not in tcache:
            tcache[ck] = (self.sb(st_, nm("ah"), shape), self.sb(st_, nm("ru"), shape), self.sb(st_, nm("rq"), shape, I32))
        ah, u_, q_ = tcache[ck]
        reduce_pi(st_, sn, ang, shape, [key], (u_, q_))
        dv(lambda: nc.vector.tensor_scalar_add(ah[:], ang, PI / 2), [key], [ah.name])
        reduce_pi(st_, cs, ah[:], shape, [ah.name], (u_, q_))
        for t_ in (sn, cs):
            k.op(k.act, lambda: nc.scalar.activation(t_[:], t_[:], AF.Sin), reads=[t_.name], writes=[t_.name])
        return sn, cs

    with contextlib.ExitStack() as ph:
        BLt = self.sb(ph, "s_BLt", [128, 128, 64], BF16)
        DLt = self.sb(ph, "s_DLt", [128, 32, 256], BF16)
        CLR = self.sb(ph, "s_CLR", [128, 16, 256], BF16)
        CLI = self.sb(ph, "s_CLI", [128, 16, 256], BF16)
        lamp = self.sb(ph, "s_lamp", [128, 3, 16])
        lrd = self.sb(ph, "s_lrd", [128, 16])
        ang = self.sb(ph, "s_ang", [128, 16])
        k.dma("sync", lamp[:], p["s5lam"], writes=["lamp"])
        dt = self.sb(ph, "s_dt", [128, 16])
        k.op(k.act, lambda: nc.scalar.activation(dt[:], lamp[:, 2, :], AF.Exp), reads=["lamp"], writes=["dt"])
        dv(lambda: nc.vector.tensor_tensor(lrd[:], lamp[:, 0, :], dt[:], ALU.mult), ["lamp", "dt"], ["lrd"])
        dv(lambda: nc.vector.tensor_tensor(ang[:], lamp[:, 1, :], dt[:], ALU.mult), ["lamp", "dt"], ["ang"])

        with contextlib.ExitStack() as sa:
            S2 = [128, 16]
            bsrc = self.sb(sa, "s_bsrc", [128, 2, 16, 16])
            csrc = self.sb(sa, "s_csrc", [128, 2, 16, 16])
            expt = self.sb(sa, "s_expt", [128, 3, 16, 16])
            mask = self.sb(sa, "s_mask", [128, 2, 2, 256])
            k.dma("sync", bsrc[:], p["s5b"], writes=["bsrc"])
            k.dma("sync", csrc[:], p["s5c"], writes=["csrc"])
            k.dma("sync", expt[:], self.s5exp, writes=["expt"])
            k.dma("sync", mask[:], self.s5mask, writes=["mask"])
            mag = self.sb(sa, "s_mag", S2)
            k.op(k.act, lambda: nc.scalar.activation(mag[:], lrd[:], AF.Exp), reads=["lrd"], writes=["mag"])
            sn, cs = sincos(sa, ang[:], S2, 1, "ang")
            nr = self.sb(sa, "s_nr", S2); ni = self.sb(sa, "s_ni", S2); den = self.sb(sa, "s_den", S2)
            t1 = self.sb(sa, "s_t1", S2); t2 = self.sb(sa, "s_t2", S2)
            cfr = self.sb(sa, "s_cfr", S2); cfi = self.sb(sa, "s_cfi", S2)
            dv(lambda: nc.vector.tensor_tensor(nr[:], mag[:], cs[:], ALU.mult), ["mag", cs.name], ["nr"])
            dv(lambda: nc.vector.tensor_scalar_add(nr[:], nr[:], -1.0), ["nr"], ["nr"])
            dv(lambda: nc.vector.tensor_tensor(ni[:], mag[:], sn[:], ALU.mult), ["mag", sn.name], ["ni"])
            lre, lim = lamp[:, 0, :], lamp[:, 1, :]
            dv(lambda: nc.vector.tensor_tensor(den[:], lre, lre, ALU.mult), ["lamp"], ["den"])
            dv(lambda: nc.vector.tensor_tensor(t1[:], lim, lim, ALU.mult), ["lamp"], ["t1"])
            dv(lambda: nc.vector.tensor_tensor(den[:], den[:], t1[:], ALU.add), ["den", "t1"], ["den"])
            dv(lambda: nc.vector.reciprocal(den[:], den[:]), ["den"], ["den"])
            dv(lambda: nc.vector.tensor_tensor(t1[:], nr[:], lre, ALU.mult), ["nr", "lamp"], ["t1"])
            dv(lambda: nc.vector.tensor_tensor(t2[:], ni[:], lim, ALU.mult), ["ni", "lamp"], ["t2"])
            dv(lambda: nc.vector.tensor_tensor(cfr[:], t1[:], t2[:], ALU.add), ["t1", "t2"], ["cfr"])
            dv(lambda: nc.vector.tensor_tensor(cfr[:], cfr[:], den[:], ALU.mult), ["cfr", "den"], ["cfr"])
            dv(lambda: nc.vector.tensor_tensor(t1[:], ni[:], lre, ALU.mult), ["ni", "lamp", "cfr"], ["t1"])
            dv(lambda: nc.vector.tensor_tensor(t2[:], nr[:], lim, ALU.mult), ["nr", "lamp", "cfr"], ["t2"])
            dv(lambda: nc.vector.tensor_tensor(cfi[:], t1[:], t2[:], ALU.subtract), ["t1", "t2"], ["cfi"])
            dv(lambda: nc.vector.tensor_tensor(cfi[:], cfi[:], den[:], ALU.mult), ["cfi", "den"], ["cfi"])
            S3 = [128, 16, 16]
            S4 = [128, 16, 16, 16]
            bbr = self.sb(sa, "s_bbr", S3); bbi = self.sb(sa, "s_bbi", S3); u1 = self.sb(sa, "s_u1", S3)
            cfrb = cfr[:].unsqueeze(2).broadcast_to(S3)
            cfib = cfi[:].unsqueeze(2).broadcast_to(S3)
            dv(lambda: nc.vector.tensor_tensor(bbr[:], bsrc[:, 0], cfrb, ALU.mult), ["bsrc", "cfr"], ["bbr"])
            dv(lambda: nc.vector.tensor_tensor(u1[:], bsrc[:, 1], cfib, ALU.mult), ["bsrc", "cfi"], ["u1"])
            dv(lambda: nc.vector.tensor_tensor(bbr[:], bbr[:], u1[:], ALU.subtract), ["bbr", "u1"], ["bbr"])
            dv(lambda: nc.vector.tensor_tensor(bbi[:], bsrc[:, 1], cfrb, ALU.mult), ["bsrc", "cfr", "bbr"], ["bbi"])
            dv(lambda: nc.vector.tensor_tensor(u1[:], bsrc[:, 0], cfib, ALU.mult), ["bsrc", "cfi", "bbr"], ["u1"])
            dv(lambda: nc.vector.tensor_tensor(bbi[:], bbi[:], u1[:], ALU.add), ["bbi", "u1"], ["bbi"])

            def cpow(e):
                lr = self.sb(sa, nm("lr"), S3)
                an = self.sb(sa, nm("an"), S3)
                dv(lambda: nc.vector.tensor_tensor(lr[:], expt[:, e], lrd[:].unsqueeze(2).broadcast_to(S3), ALU.mult), ["expt", "lrd"], [lr.name])
                k.op(k.act, lambda: nc.scalar.activation(lr[:], lr[:], AF.Exp), reads=[lr.name], writes=[lr.name])
                dv(lambda: nc.vector.tensor_tensor(an[:], expt[:, e], ang[:].unsqueeze(2).broadcast_to(S3), ALU.mult), ["expt", "ang"], [an.name])
                s_, c_ = sincos(sa, an[:], S3, 60, an.name)
                dv(lambda: nc.vector.tensor_tensor(c_[:], c_[:], lr[:], ALU.mult), [c_.name, lr.name], [c_.name])
                dv(lambda: nc.vector.tensor_tensor(s_[:], s_[:], lr[:], ALU.mult), [s_.name, lr.name], [s_.name])
                return c_, s_

            w1 = self.sb(sa, "s_w1", S4)
            w2 = self.sb(sa, "s_w2", S4)

            def cmul(out_re, out_im_neg, out_im, pw, vr, vi, vkeys):
                pr_, pi_ = pw
                prb = pr_[:].unsqueeze(3).broadcast_to(S4)
                pib = pi_[:].unsqueeze(3).broadcast_to(S4)
                vrb = vr.unsqueeze(2).broadcast_to(S4)
                vib = vi.unsqueeze(2).broadcast_to(S4)
                rk = [pr_.name, pi_.name] + vkeys
                dv(lambda: nc.vector.tensor_tensor(w1[:], prb, vrb, ALU.mult), rk, ["w1"])
                dv(lambda: nc.vector.tensor_tensor(w2[:], pib, vib, ALU.mult), rk, ["w2"])
                dv(lambda: nc.vector.tensor_tensor(out_re, w1[:], w2[:], ALU.subtract), ["w1", "w2"], [nm("o")])
                dv(lambda: nc.vector.tensor_tensor(w1[:], prb, vib, ALU.mult), rk, ["w1"])
                dv(lambda: nc.vector.tensor_tensor(w2[:], pib, vrb, ALU.mult), rk, ["w2"])
                if out_im is not None:
                    dv(lambda: nc.vector.tensor_tensor(out_im, w1[:], w2[:], ALU.add), ["w1", "w2"], [nm("o")])
                else:
                    dv(lambda: nc.vector.scalar_tensor_tensor(out_im_neg, w1[:], -1.0, w2[:], ALU.mult, ALU.subtract), ["w1", "w2"], [nm("o")])

            PBr = self.sb(sa, "s_PBr", S4); PBi = self.sb(sa, "s_PBi", S4)
            QCr = self.sb(sa, "s_QCr", S4); QCn = self.sb(sa, "s_QCn", S4)
            cmul(PBr[:], None, PBi[:], cpow(0), bbr[:], bbi[:], ["bbr", "bbi"])
            cmul(CLR[:].rearrange("p t (j h) -> p t j h", h=16), CLI[:].rearrange("p t (j h) -> p t j h", h=16), None,
                 cpow(1), csrc[:, 0], csrc[:, 1], ["csrc"])
            cmul(QCr[:], QCn[:], None, cpow(2), csrc[:, 0], csrc[:, 1], ["csrc"])
            k.barrier()
            pst = [self.ps(sa, f"s_pst{i}", [128, 512]) for i in range(4)]
            psd = [self.ps(sa, f"s_psd{i}", [128, 512]) for i in range(4)]
            for tp in range(16):
                pts = [pst[(tp % 2) * 2 + gl] for gl in range(2)]
                pks = [f"pst{(tp % 2) * 2 + gl}" for gl in range(2)]
                for kt in range(2):
                    for pl_, PB in enumerate((PBr, PBi)):
                        q_ = kt * 2 + pl_
                        for gl in range(2):
                            prt = slice(gl * 64, (gl + 1) * 64)
                            k.op(k.pe, lambda: nc.tensor.transpose(pts[gl][:, q_ * 64:(q_ + 1) * 64], PB[prt, tp, kt * 8:(kt + 1) * 8, :],
                                                                   self.identf[prt, prt]), reads=["identf"], writes=[pks[gl]])
                for gl in range(2):
                    s0 = (tp * 2 + gl) * 4
                    k.op(k.act, lambda: nc.scalar.copy(BLt[:, s0:s0 + 4, :], pts[gl][:, 0:256].rearrange("p (a b) -> p a b", b=64)),
                         reads=[pks[gl]], writes=["BLt"])
            m1s = [self.sb(sa, nm("m"), [128, 256]) for _ in range(2)]
            for gp in range(8):
                for kt in range(2):
                    for dr, tp in ((0, gp), (1, 8 + gp)):
                        for PBx, QCx, st_, sp_ in ((PBr, QCr, True, False), (PBi, QCn, False, True)):
                            for gl in range(2):
                                prt = slice(gl * 64, (gl + 1) * 64)
                                ps_ = psd[gl * 2 + dr]
                                k.op(k.pe, lambda: nc.tensor.matmul(ps_[:, 0:256], PBx[prt, tp, kt * 8:(kt + 1) * 8, :], QCx[prt, tp, :, :],
                                                                    start=st_, stop=sp_), reads=[], writes=[f"psd{gl * 2 + dr}"])
                    for gl in range(2):
                        g = gp * 2 + gl
                        pf, pb = psd[gl * 2], psd[gl * 2 + 1]
                        kf_, kb_ = f"psd{gl * 2}", f"psd{gl * 2 + 1}"
                        m1 = m1s[gl]
                        dv(lambda: nc.vector.tensor_tensor(m1[:], pf[:, 0:256], mask[:, 0, kt, :], ALU.mult), [kf_, "mask"], [m1.name])
                        dv(lambda: nc.vector.tensor_tensor(pb[:, 256:512], pb[:, 0:256], mask[:, 1, kt, :], ALU.mult), [kb_, "mask"], [kb_])
                        dv(lambda: nc.vector.tensor_tensor(DLt[:, g * 2 + kt, :], pb[:, 256:512], m1[:], ALU.add), [kb_, m1.name], ["DLt"])
            k.barrier()

        if getattr(self, "s5_stop", 9) <= 1:
            return
        Ut = self.sb(ph, "s_Ut", [128, 32, NCH], BF16)
        SFR = self.sb(ph, "s_SFR", [128, 8, NCH + 1], BF16); SFI = self.sb(ph, "s_SFI", [128, 8, NCH + 1], BF16)
        SBR = self.sb(ph, "s_SBR", [128, 8, NCH + 1], BF16); SBI = self.sb(ph, "s_SBI", [128, 8, NCH + 1], BF16)
        P16 = self.sb(ph, "s_P16", [128, 16])
        phr = self.sb(ph, "s_phr", [128, 16])
        k.op(k.act, lambda: nc.scalar.activation(P16[:], lrd[:], AF.Exp, scale=16.0), reads=["lrd"], writes=["P16"])
        ph16 = self.sb(ph, "s_ph16", [128, 16])
        dv(lambda: nc.vector.tensor_scalar_mul(ph16[:], ang[:], 16.0), ["ang"], ["ph16"])
        reduce_pi(ph, phr, ph16[:], [128, 16], ["ph16"])
        CT = [(0, 16), (16, 128), (144, 128)]
        usv = self.us_s.rearrange("(c j) ch -> c j ch", j=16)
        with contextlib.ExitStack() as su:
            ucm = [self.sb(su, f"s_ucm{i}", [128, 16, 256]) for i in range(3)]
            psu = [self.ps(su, f"s_psu{i}", [128, 512]) for i in range(4)]
            n_ = 0
            for ci, (c0, n) in enumerate(CT):
                k.dma("sync", ucm[ci][0:n], usv[c0:c0 + n], writes=[f"ucm{ci}"])
                ucp_ = self.sb(su, f"s_ucp{ci}", [128, 16, 16, 16])
                k.op(k.pool if ci == 1 else k.dve,
                     lambda: (nc.gpsimd if ci == 1 else nc.vector).tensor_copy(
                         ucp_[0:n], ucm[ci][0:n].rearrange("p i (g h) -> p g i h", h=16)),
                     reads=[f"ucm{ci}"], writes=[f"ucp{ci}"])
                for q4 in range(8):
                    pu = psu[n_ % 4]
                    pk = f"psu{n_ % 4}"
                    n_ += 1
                    for a in range(4):
                        s_ = q4 * 4 + a
                        g, kt = s_ // 2, s_ % 2
                        k.op(k.pe, lambda: nc.tensor.transpose(pu[:, a * 128:a * 128 + n], ucp_[0:n, g, kt * 8:(kt + 1) * 8, :],
                                                               self.identf[0:n, 0:n]), reads=[f"ucp{ci}", "identf"], writes=[pk])
                    eng = k.act if n_ % 2 == 0 else k.dve
                    src = pu[:].rearrange("p (a b) -> p a b", b=128)[:, :, 0:n]
                    dst = Ut[:, q4 * 4:(q4 + 1) * 4, c0:c0 + n]
                    if eng is k.act:
                        k.op(eng, lambda: nc.scalar.copy(dst, src), reads=[pk], writes=["Ut"])
                    else:
                        k.op(eng, lambda: nc.vector.tensor_copy(dst, src), reads=[pk], writes=["Ut"])
        k.barrier()
        if getattr(self, "s5_stop", 9) <= 2:
            return
        for d in range(2):
            with contextlib.ExitStack() as sd:
                S8 = [128, 8, NCH]
                idx = self.sb(sd, "s_idx", S8)
                k.dma("sync", idx[:], self.s5idx[:, d * 8:(d + 1) * 8, 0:NCH], writes=["idx"])
                dv(lambda: nc.vector.tensor_tensor(idx[:], idx[:], phr[:, d * 8:(d + 1) * 8].unsqueeze(2).broadcast_to(S8), ALU.mult),
                   ["idx", phr.name], ["idx"])
                SN, CS = sincos(sd, idx[:], S8, 140, "idx")
                PCO = self.sb(sd, "s_PCO", S8)
                XR = self.sb(sd, "s_XR", S8); XI = self.sb(sd, "s_XI", S8)
                VR = self.sb(sd, "s_VR", S8); VI = self.sb(sd, "s_VI", S8)
                WR = self.sb(sd, "s_WR", S8); WI = self.sb(sd, "s_WI", S8)
                psx = [self.ps(sd, f"s_psx{i}", [128, 512]) for i in range(4)]
                dv(lambda: nc.vector.memset(PCO[:], 1.0), [], ["PCO"])
                dv(lambda: nc.vector.tensor_tensor(PCO[:], PCO[:], P16[:, d * 8:(d + 1) * 8].unsqueeze(2).broadcast_to(S8), ALU.mult),
                   ["PCO", "P16"], ["PCO"])
                zc = 0 if d == 0 else NCH - 1
                dv(lambda: nc.vector.memset(PCO[:, :, zc:zc + 1], 0.0), ["PCO"], ["PCO"])
                n_ = 0
                for a in range(8):
                    tp = d * 8 + a
                    for pl_, X in enumerate((XR, XI)):
                        px = psx[n_ % 4]
                        pk = f"psx{n_ % 4}"
                        n_ += 1
                        for gl in range(2):
                            g = a * 2 + gl
                            prt = slice(gl * 64, (gl + 1) * 64)
                            segs = [(0, NCH, 0)] if d == 0 else [(16, NCH, 0), (0, 16, 256)]
                            for (u0, u1_, o0) in segs:
                                for kt in range(2):
                                    slot = ((tp * 2 + gl) * 2 + kt) * 2 + pl_
                                    k.op(k.pe, lambda: nc.tensor.matmul(px[prt, o0:o0 + (u1_ - u0)], BLt[:, slot, :], Ut[:, g * 2 + kt, u0:u1_],
                                                                        start=(kt == 0), stop=(kt == 1)), reads=["BLt", "Ut"], writes=[pk])
                        k.op(k.act, lambda: nc.scalar.copy(X[:, a, :], px[:, 0:NCH]), reads=[pk], writes=[X.name])
                dv(lambda: nc.vector.tensor_tensor(VR[:], XR[:], CS[:], ALU.mult), [XR.name, CS.name], ["VR"])
                k.op(k.pool, lambda: nc.gpsimd.tensor_tensor(WR[:], XI[:], SN[:], ALU.mult), reads=[XI.name, SN.name], writes=["WR"])
                dv(lambda: nc.vector.tensor_tensor(VR[:], VR[:], WR[:], ALU.add), ["VR", "WR"], ["VR"])
                dv(lambda: nc.vector.tensor_tensor(VI[:], XI[:], CS[:], ALU.mult), [XI.name, CS.name], ["VI"])
                k.op(k.pool, lambda: nc.gpsimd.tensor_tensor(WI[:], XR[:], SN[:], ALU.mult), reads=[XR.name, SN.name], writes=["WI"])
                dv(lambda: nc.vector.tensor_tensor(VI[:], VI[:], WI[:], ALU.subtract), ["VI", "WI"], ["VI"])
                fl = lambda t_: t_[:].rearrange("p a b -> p (a b)")
                rv = (lambda ap: ap) if d == 0 else (lambda ap: ap[:, ::-1])
                dv(lambda: nc.vector.tensor_tensor_scan(rv(fl(WR)), rv(fl(PCO)), rv(fl(VR)), 0.0, ALU.mult, ALU.add), ["PCO", "VR", "WR"], ["WR"])
                dv(lambda: nc.vector.tensor_tensor_scan(rv(fl(WI)), rv(fl(PCO)), rv(fl(VI)), 0.0, ALU.mult, ALU.add), ["PCO", "VI", "WI"], ["WI"])
                SR_, SI_ = (SFR, SFI) if d == 0 else (SBR, SBI)
                o_ = 1 if d == 0 else 0
                zcol = 0 if d == 0 else NCH
                dv(lambda: nc.vector.memset(SR_[:, :, zcol:zcol + 1], 0.0), [], [SR_.name])
                dv(lambda: nc.vector.memset(SI_[:, :, zcol:zcol + 1], 0.0), [], [SI_.name])
                dv(lambda: nc.vector.tensor_tensor(VR[:], WR[:], CS[:], ALU.mult), ["WR", CS.name, "VR"], ["VR"])
                k.op(k.pool, lambda: nc.gpsimd.tensor_tensor(VI[:], WI[:], SN[:], ALU.mult), reads=["WI", SN.name, "VI"], writes=["VI"])
                dv(lambda: nc.vector.tensor_tensor(SR_[:, :, o_:o_ + NCH], VR[:], VI[:], ALU.subtract), ["VR", "VI", SR_.name], [SR_.name])
                dv(lambda: nc.vector.tensor_tensor(VR[:], WI[:], CS[:], ALU.mult), ["WI", CS.name, "VR", SR_.name], ["VR"])
                k.op(k.pool, lambda: nc.gpsimd.tensor_tensor(VI[:], WR[:], SN[:], ALU.mult), reads=["WR", SN.name, "VI", SR_.name], writes=["VI"])
                dv(lambda: nc.vector.tensor_tensor(SI_[:, :, o_:o_ + NCH], VR[:], VI[:], ALU.add), ["VR", "VI", SI_.name], [SI_.name])
            k.barrier()
        if getattr(self, "s5_stop", 9) <= 3:
            return
        with contextlib.ExitStack() as sy:
            dsk = self.sb(sy, "s_dsk", [128, 256])
            k.dma("sync", dsk[:], p["s5d"].partition_broadcast(128), writes=["dsk"])
            ucm = [self.sb(sy, f"s_ucy{i}", [128, 16, 256]) for i in range(2)]
            ycm = [self.sb(sy, f"s_ycm{i}", [128, 16, 256]) for i in range(2)]
            tq = [self.sb(sy, f"s_tq{i}", [128, 16, 256]) for i in range(2)]
            psy = [self.ps(sy, f"s_psy{i}", [128, 512]) for i in range(4)]
            zsv = self.zs.rearrange("(c j) ch -> c j ch", j=16)
            n_ = 0
            for ci, (c0, n) in enumerate(CT):
                u_, y_, t_ = ucm[ci % 2], ycm[ci % 2], tq[ci % 2]
                uk, yk, tk = f"ucy{ci % 2}", f"ycm{ci % 2}", f"tq{ci % 2}"
                k.dma("sync", u_[0:n], usv[c0:c0 + n], writes=[uk])
                mb0 = (256 if c0 < 16 else c0 - 16) + 1
                for g2 in range(8):
                    py = psy[n_ % 4]
                    pk = f"psy{n_ % 4}"
                    n_ += 1
                    for gg in range(2):
                        g = g2 * 2 + gg
                        gp, gl = g // 2, g % 2
                        prt = slice(gl * 64, (gl + 1) * 64)
                        o = py[0:n, gg * 256:(gg + 1) * 256]
                        mm = [(Ut[:, g * 2 + 0, c0:c0 + n], DLt[:, g * 2 + 0, :]),
                              (Ut[:, g * 2 + 1, c0:c0 + n], DLt[:, g * 2 + 1, :]),
                              (SFR[prt, gp, c0:c0 + n], CLR[prt, gp, :]),
                              (SFI[prt, gp, c0:c0 + n], CLI[prt, gp, :]),
                              (SBR[prt, gp, mb0:mb0 + n], CLR[prt, 8 + gp, :]),
                              (SBI[prt, gp, mb0:mb0 + n], CLI[prt, 8 + gp, :])]
                        for mi, (lh, rh) in enumerate(mm):
                            k.op(k.pe, lambda: nc.tensor.matmul(o, lh, rh, start=(mi == 0), stop=(mi == 5)),
                                 reads=["Ut", "DLt", "CLR", "CLI", SFR.name, SFI.name, SBR.name, SBI.name], writes=[pk])
                    src = py[0:n, :].rearrange("p (g j h) -> p g j h", g=2, h=16)
                    for gg in range(2):
                        g = g2 * 2 + gg
                        eng = k.act if gg == 0 else k.dve
                        dst = y_[0:n, :, g * 16:(g + 1) * 16]
                        if eng is k.act:
                            k.op(eng, lambda: nc.scalar.copy(dst, src[:, gg]), reads=[pk], writes=[yk])
                        else:
                            k.op(eng, lambda: nc.vector.tensor_copy(dst, src[:, gg]), reads=[pk], writes=[yk])
                dsb = dsk[0:n].unsqueeze(1).broadcast_to([n, 16, 256])
                k.op(k.pool, lambda: nc.gpsimd.tensor_tensor(u_[0:n], u_[0:n], dsb, ALU.mult), reads=[uk, "dsk"], writes=[uk])
                dv(lambda: nc.vector.tensor_tensor(y_[0:n], y_[0:n], u_[0:n], ALU.add), [yk, uk], [yk])
                k.op(k.pool, lambda: nc.gpsimd.tensor_tensor(t_[0:n], y_[0:n], y_[0:n], ALU.mult), reads=[yk], writes=[tk])
                dv(lambda: nc.vector.tensor_scalar(t_[0:n], t_[0:n], 0.044715, 1.0, ALU.mult, ALU.add), [tk], [tk])
                k.op(k.pool, lambda: nc.gpsimd.tensor_tensor(t_[0:n], t_[0:n], y_[0:n], ALU.mult), reads=[tk, yk], writes=[tk])
                k.op(k.act, lambda: nc.scalar.activation(t_[0:n], t_[0:n], AF.Sigmoid, scale=1.5957691216057308), reads=[tk], writes=[tk])
                dv(lambda: nc.vector.tensor_tensor(y_[0:n], y_[0:n], t_[0:n], ALU.mult), [yk, tk], [yk])
                k.dma("gpsimd", zsv[c0:c0 + n], y_[0:n], reads=[yk])


def _phase_s5b(self, l):
    nc, k, p = self.nc, self.k, self.L[l]

    def dv(fn, reads, writes):
        return k.op(k.dve, fn, reads=reads, writes=writes)

    with contextlib.ExitStack() as sb_:
        wgf = self.sb(sb_, "g_wgf", [128, 2, 256])
        wgb = self.sb(sb_, "g_wgb", [128, 2, 256], BF16)
        bgl = self.sb(sb_, "g_bgl", [128, 2])
        zT = self.sb(sb_, "g_zT", [128, 2, TT])
        zTb = self.sb(sb_, "g_zTb", [128, 2, TT], BF16)
        zt = [self.sb(sb_, f"g_zt{i}", [128, 256]) for i in range(2)]
        gsl = self.sb(sb_, "g_gs", [128, TT])
        sgl = self.sb(sb_, "g_sg", [128, TT])
        yb = self.sb(sb_, "g_yb", [128, TT], BF16)
        pz = [self.ps(sb_, f"g_pz{i}", [128, 512]) for i in range(2)]
        pg = [self.ps(sb_, f"g_pg{i}", [128, 512]) for i in range(2)]
        k.dma("sync", wgf[:], p["wglu"].rearrange("(k p) n -> p k n", p=128), writes=["wgf"])
        k.dma("sync", bgl[:], p["bglu"], writes=["bgl"])
        dv(lambda: nc.vector.tensor_copy(wgb[:], wgf[:]), ["wgf"], ["wgb"])
        for tt in range(NTT):
            z_ = zt[tt % 2]
            zk = f"zt{tt % 2}"
            pz_ = pz[tt % 2]
            k.dma("sync", z_[:], self.zs[tt * 128:(tt + 1) * 128, :], writes=[zk])
            for c_ in range(2):
                k.op(k.pe, lambda: nc.tensor.matmul(pz_[:, c_ * 128:(c_ + 1) * 128], z_[:, c_ * 128:(c_ + 1) * 128], self.identf[:],
                                                    start=True, stop=True),
                     reads=[zk, "identf"], writes=[f"pz{tt % 2}"])
            for c_ in range(2):
                k.op(k.dve, lambda: nc.vector.tensor_copy(zT[:, c_, tt * 128:(tt + 1) * 128], pz_[:, c_ * 128:(c_ + 1) * 128]),
                     reads=[f"pz{tt % 2}"], writes=["zT"])
                dv(lambda: nc.vector.tensor_copy(zTb[:, c_, tt * 128:(tt + 1) * 128], pz_[:, c_ * 128:(c_ + 1) * 128]),
                   [f"pz{tt % 2}"], ["zTb"])
        if getattr(self, "s5_stop", 9) <= 5:
            return
        nblk = [(i * 512, min(512, TT - i * 512)) for i in range((TT + 511) // 512)]
        for co in range(2):
            rows = slice(256 + co * 128, 256 + (co + 1) * 128)
            k.dma("sync", gsl[:], self.gsT[co * 128:(co + 1) * 128, :], writes=["gsl"])
            k.op(k.act, lambda: nc.scalar.activation(gsl[:], gsl[:], AF.Silu), reads=["gsl"], writes=["gsl"])
            for bi, (c0, w) in enumerate(nblk):
                pg_ = pg[bi % 2]
                for ci_ in range(2):
                    k.op(k.pe, lambda: nc.tensor.matmul(pg_[:, :w], wgb[:, ci_, co * 128:(co + 1) * 128], zTb[:, ci_, c0:c0 + w],
                                                        start=(ci_ == 0), stop=(ci_ == 1)), reads=["wgb", "zTb"], writes=[f"pg{bi % 2}"])
                k.op(k.act, lambda: nc.scalar.activation(sgl[:, c0:c0 + w], pg_[:, :w], AF.Sigmoid, bias=bgl[:, co:co + 1]),
                     reads=[f"pg{bi % 2}", "bgl"], writes=["sgl"])
            dv(lambda: nc.vector.tensor_tensor(sgl[:], sgl[:], zT[:, co, :], ALU.mult), ["sgl", "zT"], ["sgl"])
            k.op(k.pool, lambda: nc.gpsimd.tensor_tensor(yb[:], sgl[:], gsl[:], ALU.mult), reads=["sgl", "gsl"], writes=["yb"])
            k.dma("gpsimd", self.yas[rows, :], yb[:], reads=["yb"])


Builder.phase_s5 = _phase_s5
Builder.phase_s5b = _phase_s5b
```

```python
import contextlib
import numpy as np
import concourse.bass as bass
import concourse.mybir as mybir
from concourse.bass_utils import run_bass_kernel_spmd

F32 = mybir.dt.float32
BF16 = mybir.dt.bfloat16
ALU = mybir.AluOpType
AF = mybir.ActivationFunctionType
AX = mybir.AxisListType


class _Eng:
    def __init__(self, name, handle, sem, inc):
        self.name = name
        self.h = handle
        self.sem = sem
        self.inc = inc
        self.count = 0
        self.seen = {}


class _Reg:
    __slots__ = ("w", "r")

    def __init__(self):
        self.w = None
        self.r = {}


class K:
    def __init__(self, nc, stack, n_dma=10):
        self.nc = nc
        self.stack = stack
        self.regs = {}
        mk = lambda n: stack.enter_context(nc.semaphore(n))
        self.pe = _Eng("pe", nc.tensor, mk("s_pe"), 1)
        self.act = _Eng("act", nc.scalar, mk("s_act"), 1)
        self.dve = _Eng("dve", nc.vector, mk("s_dve"), 1)
        self.pool = _Eng("pool", nc.gpsimd, mk("s_pool"), 1)
        self.compute = [self.pe, self.act, self.dve, self.pool]
        self.dq = {}
        for qn, qh in (("sync", nc.sync), ("gpsimd", nc.gpsimd), ("scalar", nc.scalar)):
            self.dq[qn] = [
                _Eng(f"d_{qn}{i}", qh, mk(f"s_d{qn}{i}"), 16) for i in range(n_dma)
            ]
        self.dq_rr = {qn: 0 for qn in self.dq}
        self.qseen = {"sync": {}, "gpsimd": self.pool.seen, "scalar": self.act.seen}

    def reg(self, key):
        r = self.regs.get(key)
        if r is None:
            r = self.regs[key] = _Reg()
        return r

    def _deps(self, reads, writes):
        deps = {}
        for k in reads:
            r = self.reg(k)
            if r.w is not None:
                e, c = r.w
                deps[e] = max(deps.get(e, 0), c)
        for k in writes:
            r = self.reg(k)
            if r.w is not None:
                e, c = r.w
                deps[e] = max(deps.get(e, 0), c)
            for e, c in r.r.items():
                deps[e] = max(deps.get(e, 0), c)
        return deps

    def _commit(self, eng, reads, writes):
        c = eng.count
        for k in reads:
            self.reg(k).r[eng] = c
        for k in writes:
            r = self.reg(k)
            r.w = (eng, c)
            r.r = {}

    def op(self, eng, fn, reads=(), writes=()):
        deps = self._deps(reads, writes)
        for e, c in deps.items():
            if e is eng and eng is self.pe:
                continue
            if eng.seen.get(e, 0) < c:
                eng.h.wait_ge(e.sem, c * e.inc)
                eng.seen[e] = c
        ins = fn()
        eng.count += 1
        ins.then_inc(eng.sem, eng.inc)
        self._commit(eng, reads, writes)
        return ins

    def dma(self, q, out, in_, reads=(), writes=(), **kw):
        lst = self.dq[q]
        i = self.dq_rr[q]
        self.dq_rr[q] = (i + 1) % len(lst)
        d = lst[i]
        seen = self.qseen[q]
        if d.count > 0 and seen.get(d, 0) < d.count:
            d.h.wait_ge(d.sem, d.count * 16)
            seen[d] = d.count
        deps = self._deps(reads, writes)
        for e, c in deps.items():
            if seen.get(e, 0) < c:
                d.h.wait_ge(e.sem, c * e.inc)
                seen[e] = c
        ins = d.h.dma_start(out=out, in_=in_, **kw)
        d.count += 1
        ins.then_inc(d.sem, 16)
        self._commit(d, reads, writes)
        return ins

    def finish(self):
        seen = self.qseen["sync"]
        allengs = list(self.compute)
        for lst in self.dq.values():
            allengs += lst
        for e in allengs:
            if e.count > 0 and seen.get(e, 0) < e.count:
                self.nc.sync.wait_ge(e.sem, e.count * e.inc)
                seen[e] = e.count

    def barrier(self):
        allengs = list(self.compute)
        for lst in self.dq.values():
            allengs += lst
        for h, seen in ((self.nc.tensor, self.pe.seen), (self.nc.scalar, self.act.seen),
                        (self.nc.vector, self.dve.seen), (self.nc.gpsimd, self.pool.seen),
                        (self.nc.sync, self.qseen["sync"])):
            for e in allengs:
                if e.count > 0 and seen.get(e, 0) < e.count:
                    h.wait_ge(e.sem, e.count * e.inc)
                    seen[e] = e.count
        self.regs.clear()


D = 1024
T = 4096
TC = 256
TT = T + TC
NTT = TT // 128
DEPTH = 2
EPS = 1e-6
NCH = TT // 16
WCOL = {"ua": (0, 256), "k": (256, 512), "ga": (768, 256), "gs": (1024, 256), "q": (1280, 512),
        "v": (1792, 512), "gd": (2304, 512), "us": (2816, 256)}
TWO_PI = 2.0 * np.pi


class Builder:
    def __init__(self, debug=False, layers=(0, 1), phases=("M", "P", "L", "S", "A")):
        self.debug = debug
        self.layers = layers
        self.phases = phases
        self.nc = bass.Bass("TRN2", target_bir_lowering=False)
        self.ins = {}
        self.scr = {}

    def din(self, name, shape, dt=F32):
        ap = self.nc.dram_tensor(name, list(shape), dt, kind="ExternalInput").ap()
        self.ins[name] = ap
        return ap

    def dscr(self, name, shape, dt=F32):
        kind = "ExternalOutput" if self.debug else "Internal"
        ap = self.nc.dram_tensor(name, list(shape), dt, kind=kind).ap()
        self.scr[name] = ap
        return ap

    def _uname(self, name):
        self._uid = getattr(self, "_uid", 0) + 1
        return f"{name}_u{self._uid}"

    def sb(self, st, name, shape, dt=F32):
        return st.enter_context(self.nc.sbuf_tensor(self._uname(name), list(shape), dt))

    def ps(self, st, name, shape, dt=F32):
        return st.enter_context(self.nc.psum_tensor(self._uname(name), list(shape), dt))

    def declare(self):
        d = self.din
        self.hin = d("hin", [TT, D])
        self.cvec = d("cvec", [128, 8, 2])
        self.ident = d("ident", [128, 128])
        self.perm = d("perm", [128, 128])
        self.sel2 = d("sel2", [2, 2, 128])
        self.ropeC = d("ropeC", [128, T])
        self.ropeS = d("ropeS", [128, T])
        self.finalg = d("finalg", [1, D])
        self.s5idx = d("s5idx", [128, 16, NCH + 1])
        self.s5exp = d("s5exp", [128, 3, 16, 16])
        self.s5mask = d("s5mask", [128, 2, 2, 256])
        self.L = []
        for l in range(DEPTH):
            p = {}
            p["w_mod"] = d(f"w_mod{l}", [D, 3 * D])
            p["bmodR"] = d(f"bmodR{l}", [2, 3 * D])
            p["normg"] = d(f"normg{l}", [128, 8, 2])
            p["w_in"] = d(f"w_in{l}", [D, 3 * D])
            p["w_out"] = d(f"w_out{l}", [D, D])
            p["convw"] = d(f"convw{l}", [128, 2, 4])
            p["convb"] = d(f"convb{l}", [128, 2])
            p["wax"] = d(f"wax{l}", [128, 2, 2, 2, 128])
            p["bax"] = d(f"bax{l}", [128, 2, 2, 2])
            p["lrulam"] = d(f"lrulam{l}", [128, 2, 2])
            p["s5lam"] = d(f"s5lam{l}", [128, 3, 16])
            p["s5b"] = d(f"s5b{l}", [128, 2, 16, 16])
            p["s5c"] = d(f"s5c{l}", [128, 2, 16, 16])
            p["s5d"] = d(f"s5d{l}", [1, 256])
            p["wglu"] = d(f"wglu{l}", [256, 256])
            p["bglu"] = d(f"bglu{l}", [128, 2])
            p["dalam"] = d(f"dalam{l}", [1, 256])
            p["dag"] = d(f"dag{l}", [128, 1])
            self.L.append(p)
        s = self.dscr
        self.uaT = s("uaT", [256, TT])
        self.gaT = s("gaT", [256, TT])
        self.gsT = s("gsT", [256, TT])
        self.kT = s("kT", [512, TT], BF16)
        self.qT = s("qT", [512, TT], BF16)
        self.v_s = s("v_s", [TT, 512], BF16)
        self.gdT = s("gdT", [512, TT])
        self.us_s = s("us_s", [TT, 256])
        self.yas = s("yas", [512, TT], BF16)
        self.hbuf = s("hbuf", [TT, D])
        self.zs = s("zs", [TT, 256])
        self.out = self.nc.dram_tensor("out", [T, D], F32, kind="ExternalOutput").ap()
        if self.debug:
            self.dbg_mod = s("dbg_mod", [128, 24, 2])
            self.dbg_gate = s("dbg_gate", [128, 2, D])
            self.dbg_hl = s("dbg_hl", [2, 2, 128, TT])

    def build(self):
        nc = self.nc
        self.declare()
        with contextlib.ExitStack() as st:
            self.k = K(nc, st)
            k = self.k
            g = lambda name, shape, dt=F32: self.sb(st, name, shape, dt)
            self.identf = g("identf", [128, 128])
            self.identb = g("identb", [128, 128], BF16)
            self.permf = g("permf", [128, 128])
            self.ones1 = g("ones1", [1, 128])
            self.sc = g("sc", [128, 8, 2])
            self.modv = g("modv", [128, 24, 2])
            self.gmul = g("gmul", [128, 8, 2])
            self.gateb = g("gateb", [128, 2, D])
            self.nlam = g("nlam", [128, 1])
            k.dma("sync", self.identf[:], self.ident, writes=["identf"])
            k.dma("sync", self.permf[:], self.perm, writes=["permf"])
            k.dma("sync", self.sc[:], self.cvec, writes=["sc"])
            k.op(k.dve, lambda: nc.vector.tensor_copy(self.identb[:], self.identf[:]), reads=["identf"], writes=["identb"])
            k.op(k.dve, lambda: nc.vector.memset(self.ones1[:], 1.0), writes=["ones1"])
            k.op(k.act, lambda: nc.scalar.activation(self.sc[:], self.sc[:], AF.Silu), reads=["sc"], writes=["sc"])
            k.barrier()
            for l in self.layers:
                last = (l == DEPTH - 1)
                hsrc = self.hin if l == 0 else self.hbuf
                with contextlib.ExitStack() as pst:
                    wbf = self.proj_prefetch(l, pst) if "P" in self.phases else None
                    if "M" in self.phases:
                        self.phase_mod(l)
                        k.barrier()
                    if "P" in self.phases:
                        self.phase_proj(l, hsrc, wbf)
                        k.barrier()
                if "L" in self.phases:
                    self.phase_lru(l)
                    k.barrier()
                if "S" in self.phases:
                    self.phase_s5(l)
                    k.barrier()
                with contextlib.ExitStack() as ast:
                    pre = None
                    if "A" in self.phases:
                        pre = self.att_prefetch(l, ast)
                    if "S" in self.phases and getattr(self, "s5_stop", 9) > 4:
                        self.phase_s5b(l)
                    k.barrier()
                    if pre is not None:
                        self._att_tmp.close()
                    if "A" in self.phases:
                        self.phase_att(l, hsrc, last, pre)
                        k.barrier()
            k.finish()
        return nc

    def phase_mod(self, l):
        nc, k, p = self.nc, self.k, self.L[l]
        with contextlib.ExitStack() as ph:
            wb = [self.sb(ph, f"m_wb{i}", [128, 8, 512]) for i in range(3)]
            bmr = self.sb(ph, "m_bmr", [2, 3 * D])
            ng = self.sb(ph, "m_ng", [128, 8, 2])
            sel = self.sb(ph, "m_sel", [2, 2, 128])
            rowv = self.sb(ph, "m_rowv", [2, 3 * D])
            psr = [self.ps(ph, f"m_psr{i}", [128, 512]) for i in range(2)]
            psT = self.ps(ph, "m_psT", [128, 512])
            psb = [self.ps(ph, f"m_psb{i}", [128, 512]) for i in range(2)]
            k.dma("scalar", bmr[:], p["bmodR"], writes=["bmr"])
            k.dma("scalar", ng[:], p["normg"], writes=["ng"])
            k.dma("scalar", sel[:], self.sel2, writes=["sel"])
            wsrc = p["w_mod"].rearrange("(k p) n -> p k n", p=128)
            for cb in range(6):
                w = wb[cb % 3]
                wk = f"wb{cb % 3}"
                for k4 in range(2):
                    k.dma("sync", w[:, k4 * 4:(k4 + 1) * 4, :],
                          wsrc[:, k4 * 4:(k4 + 1) * 4, cb * 512:(cb + 1) * 512], writes=[wk + f"_{k4}"])
                pr_ = psr[cb % 2]
                for k8 in range(8):
                    k.op(k.pe, lambda: nc.tensor.matmul(pr_[0:2, :], self.sc[:, k8, :], w[:, k8, :], start=(k8 == 0), stop=(k8 == 7)),
                         reads=[wk + f"_{k8 // 4}", "sc"], writes=[f"psr{cb % 2}"])
                cs = slice(cb * 512, (cb + 1) * 512)
                k.op(k.dve, lambda: nc.vector.tensor_tensor(rowv[:, cs], pr_[0:2, :], bmr[:, cs], ALU.add),
                     reads=[f"psr{cb % 2}", "bmr"], writes=[f"rowv{cb}"])
                for j in range(4):
                    ft = cb * 4 + j
                    k.op(k.pe, lambda: nc.tensor.transpose(psT[:, ft * 2:(ft + 1) * 2], rowv[0:2, ft * 128:(ft + 1) * 128], self.identf[0:2, 0:2]),
                         reads=[f"rowv{cb}", "identf"], writes=["psT"])
            k.op(k.dve, lambda: nc.vector.tensor_copy(self.modv[:], psT[:, 0:48].rearrange("p (a b) -> p a b", b=2)),
                 reads=["psT"], writes=["modv"])
            k.op(k.dve, lambda: nc.vector.scalar_tensor_tensor(self.gmul[:], self.modv[:, 8:16, :], 1.0, ng[:], ALU.add, ALU.mult),
                 reads=["modv", "ng"], writes=["gmul"])
            for v in range(2):
                for hf in range(2):
                    pb = psb[hf]
                    k.op(k.pe, lambda: nc.tensor.matmul(pb[:], sel[0:2, v, :], rowv[0:2, 2 * D + hf * 512:2 * D + (hf + 1) * 512], start=True, stop=True),
                         reads=["sel", f"rowv{4 + hf}"], writes=[f"psb{hf}"])
                    k.op(k.dve, lambda: nc.vector.tensor_copy(self.gateb[:, v, hf * 512:(hf + 1) * 512], pb[:]),
                         reads=[f"psb{hf}"], writes=[f"gateb{v}{hf}"])
            if self.debug:
                k.dma("gpsimd", self.dbg_mod, self.modv[:], reads=["modv"])
                k.dma("gpsimd", self.dbg_gate, self.gateb[:], reads=["gateb00", "gateb01", "gateb10", "gateb11"])

    def proj_prefetch(self, l, st_):
        nc, k, p = self.nc, self.k, self.L[l]
        wbf = self.sb(st_, "p_wbf", [128, 8, 3 * D], BF16)
        stg = [self.sb(st_, f"p_stg{i}", [128, 8, 256]) for i in range(3)]
        wsrc = p["w_in"].rearrange("(k p) n -> p k n", p=128)
        for cb in range(12):
            s_ = stg[cb % 3]
            k.dma("scalar", s_[:], wsrc[:, :, cb * 256:(cb + 1) * 256], writes=[f"pstg{cb % 3}"])
            if cb % 3 == 2:
                k.op(k.pool, lambda: nc.gpsimd.tensor_copy(wbf[:, :, cb * 256:(cb + 1) * 256], s_[:]), reads=[f"pstg{cb % 3}"], writes=[f"wbf{cb}"])
            else:
                k.op(k.dve, lambda: nc.vector.tensor_copy(wbf[:, :, cb * 256:(cb + 1) * 256], s_[:]), reads=[f"pstg{cb % 3}"], writes=[f"wbf{cb}"])
        return wbf

    def phase_proj(self, l, hsrc, wbf):
        nc, k, p = self.nc, self.k, self.L[l]
        with contextlib.ExitStack() as ph:
            wkeys = []
            xb = [self.sb(ph, f"p_x{i}", [128, D]) for i in range(4)]
            junk = self.sb(ph, "p_junk", [128, D], BF16)
            xh = [self.sb(ph, f"p_xh{i}", [128, D], BF16) for i in range(4)]
            ss = self.sb(ph, "p_ss", [128, 4])
            nT = [self.sb(ph, f"p_nT{i}", [128, 8, 512], BF16) for i in range(2)]
            cosb = [self.sb(ph, f"p_cos{i}", [128, 512]) for i in range(2)]
            sinb = [self.sb(ph, f"p_sin{i}", [128, 512]) for i in range(2)]
            kf = [self.sb(ph, f"p_kf{i}", [128, 512]) for i in range(2)]
            t1 = [self.sb(ph, f"p_t1{i}", [128, 512]) for i in range(2)]
            t2 = [self.sb(ph, f"p_t2{i}", [128, 512]) for i in range(2)]
            of32 = [self.sb(ph, f"p_of{i}", [128, 512]) for i in range(3)]
            obf = [self.sb(ph, f"p_ob{i}", [128, 512], BF16) for i in range(3)]
            pT = [self.ps(ph, f"p_pT{i}", [128, 1024], BF16) for i in range(2)]
            pm = [self.ps(ph, f"p_pm{i}", [128, 512]) for i in range(4)]
            pr = [self.ps(ph, f"p_pr{i}", [128, 512]) for i in range(2)]
            blocks = [(0, 256)] + [(256 + 512 * i, 512) for i in range(8)]
            cnt = {"x": 0, "pm": 0, "pr": 0, "of": 0, "ob": 0, "kf": 0, "ev": 0}

            def rot(name, n):
                i = cnt[name] % n
                cnt[name] += 1
                return i

            def stage1a(bi):
                t0, nt = blocks[bi]
                lat = t0 >= TC
                if lat:
                    cb_, sb_ = cosb[bi % 2], sinb[bi % 2]
                    k.dma("sync", cb_[:], self.ropeC[:, t0 - TC:t0 - TC + 512], writes=[f"cos{bi % 2}"])
                    k.dma("sync", sb_[:], self.ropeS[:, t0 - TC:t0 - TC + 512], writes=[f"sin{bi % 2}"])
                for j in range(nt // 128):
                    xi = (bi * 4 + j) % 4
                    x_, xh_ = xb[xi], xh[xi]
                    k.dma("sync", x_[:], hsrc[t0 + j * 128:t0 + (j + 1) * 128, :], writes=[f"x{xi}"])
                    k.op(k.act, lambda: nc.scalar.activation(junk[:], x_[:], AF.Square, accum_out=ss[:, xi:xi + 1]),
                         reads=[f"x{xi}"], writes=["junk", f"ss{xi}"])
                    k.op(k.act, lambda: nc.scalar.activation(ss[:, xi:xi + 1], ss[:, xi:xi + 1], AF.Sqrt, scale=1.0 / D, bias=EPS),
                         reads=[f"ss{xi}"], writes=[f"ss{xi}"])
                    k.op(k.dve, lambda: nc.vector.reciprocal(ss[:, xi:xi + 1], ss[:, xi:xi + 1]),
                         reads=[f"ss{xi}"], writes=[f"ss{xi}"])
                    k.op(k.dve, lambda: nc.vector.tensor_scalar_mul(xh_[:], x_[:], ss[:, xi:xi + 1]),
                         reads=[f"x{xi}", f"ss{xi}"], writes=[f"xh{xi}"])

            def stage1b(bi):
                t0, nt = blocks[bi]
                v = 1 if t0 < TC else 0
                nTb = nT[bi % 2]
                nk = f"nT{bi % 2}"
                for j in range(nt // 128):
                    xi = (bi * 4 + j) % 4
                    xh_ = xh[xi]
                    pi_ = rot("x", 2)
                    pT_ = pT[pi_]
                    for k8 in range(8):
                        k.op(k.pe, lambda: nc.tensor.transpose(pT_[:, k8 * 128:(k8 + 1) * 128], xh_[:, k8 * 128:(k8 + 1) * 128], self.identb[:]),
                             reads=[f"xh{xi}", "identb"], writes=[f"pT{pi_}"])
                    for k8 in range(8):
                        if k8 % 2 == 0:
                            k.op(k.dve, lambda: nc.vector.tensor_scalar(
                                nTb[:, k8, j * 128:(j + 1) * 128], pT_[:, k8 * 128:(k8 + 1) * 128],
                                self.gmul[:, k8, v:v + 1], self.modv[:, k8, v:v + 1], ALU.mult, ALU.add),
                                reads=[f"pT{pi_}", "gmul", "modv"], writes=[nk])
                        else:
                            k.op(k.act, lambda: nc.scalar.activation(
                                nTb[:, k8, j * 128:(j + 1) * 128], pT_[:, k8 * 128:(k8 + 1) * 128], AF.Identity,
                                scale=self.gmul[:, k8, v:v + 1], bias=self.modv[:, k8, v:v + 1]),
                                reads=[f"pT{pi_}", "gmul", "modv"], writes=[nk])

            def stage2(bi, part):
                t0, nt = blocks[bi]
                ntile = nt // 128
                nTb = nT[bi % 2]
                nk = f"nT{bi % 2}"
                lat = t0 >= TC
                cb_, sb_ = cosb[bi % 2], sinb[bi % 2]
                pending = []
                for name, dst in ((("ua", self.uaT), ("ga", self.gaT), ("gs", self.gsT), ("gd", self.gdT), ("k", self.kT), ("q", self.qT)) if part == 0 else []):
                    co, w = WCOL[name]
                    for mt in range(w // 128):
                        pi = rot("pm", 4)
                        pm_ = pm[pi]
                        for k8 in range(8):
                            k.op(k.pe, lambda k8=k8, pm_=pm_, co=co, mt=mt: nc.tensor.matmul(
                                pm_[:, :nt], wbf[:, k8, co + mt * 128:co + (mt + 1) * 128], nTb[:, k8, :nt],
                                start=(k8 == 0), stop=(k8 == 7)), reads=wkeys + [nk], writes=[f"pm{pi}"])
                        while pending:
                            pending.pop(0)()
                        rows = slice(mt * 128, (mt + 1) * 128)
                        if name in ("ua", "ga", "gs", "gd"):
                            oi = rot("of", 3)
                            o_ = of32[oi]
                            k.op(k.act, lambda o_=o_, pm_=pm_: nc.scalar.copy(o_[:, :nt], pm_[:, :nt]),
                                 reads=[f"pm{pi}"], writes=[f"of{oi}"])
                            k.dma("gpsimd", dst[rows, t0:t0 + nt], o_[:, :nt], reads=[f"of{oi}"])
                        elif not lat:
                            oi = rot("ob", 3)
                            o_ = obf[oi]
                            k.op(k.act, lambda o_=o_, pm_=pm_: nc.scalar.copy(o_[:, :nt], pm_[:, :nt]),
                                 reads=[f"pm{pi}"], writes=[f"ob{oi}"])
                            k.dma("gpsimd", dst[rows, t0:t0 + nt], o_[:, :nt], reads=[f"ob{oi}"])
                        else:
                            ki = rot("kf", 2)
                            kf_, t1_, t2_, pr_ = kf[ki], t1[ki], t2[ki], pr[ki]
                            k.op(k.act, lambda kf_=kf_, pm_=pm_: nc.scalar.copy(kf_[:], pm_[:]),
                                 reads=[f"pm{pi}"], writes=[f"kf{ki}"])
                            k.op(k.pool, lambda kf_=kf_, t1_=t1_: nc.gpsimd.tensor_tensor(t1_[:], kf_[:], cb_[:], ALU.mult),
                                 reads=[f"kf{ki}", f"cos{bi % 2}"], writes=[f"t1{ki}"])

                            def part_b(kf_=kf_, t1_=t1_, t2_=t2_, pr_=pr_, ki=ki, dst=dst, rows=rows):
                                k.op(k.pe, lambda: nc.tensor.matmul(pr_[:], self.permf[:], kf_[:], start=True, stop=True),
                                     reads=[f"kf{ki}", "permf"], writes=[f"pr{ki}"])
                                k.op(k.dve, lambda: nc.vector.tensor_tensor(t2_[:], pr_[:], sb_[:], ALU.mult),
                                     reads=[f"pr{ki}", f"sin{bi % 2}"], writes=[f"t2{ki}"])
                                oi = rot("ob", 3)
                                o_ = obf[oi]
                                k.op(k.dve, lambda: nc.vector.tensor_tensor(o_[:], t1_[:], t2_[:], ALU.add),
                                     reads=[f"t1{ki}", f"t2{ki}"], writes=[f"ob{oi}"])
                                k.dma("gpsimd", dst[rows, t0:t0 + nt], o_[:, :nt], reads=[f"ob{oi}"])
                            pending.append(part_b)
                while pending:
                    pending.pop(0)()
                for j in (range(ntile) if part == 1 else []):
                    r0 = t0 + j * 128
                    for name, dst in (("v", self.v_s), ("us", self.us_s)):
                        co, w = WCOL[name]
                        pi = rot("pm", 4)
                        pm_ = pm[pi]
                        for k8 in range(8):
                            k.op(k.pe, lambda k8=k8, pm_=pm_, co=co, w=w, j=j: nc.tensor.matmul(
                                pm_[:, :w], nTb[:, k8, j * 128:(j + 1) * 128], wbf[:, k8, co:co + w],
                                start=(k8 == 0), stop=(k8 == 7)), reads=wkeys + [nk], writes=[f"pm{pi}"])
                        while pending:
                            pending.pop(0)()
                        ev = rot("ev", 2)
                        eng = k.act if ev == 0 else k.dve
                        if name == "v":
                            oi = rot("ob", 3)
                            o_ = obf[oi]
                            key = f"ob{oi}"
                        else:
                            oi = rot("of", 3)
                            o_ = of32[oi]
                            key = f"of{oi}"
                        if eng is k.act:
                            k.op(eng, lambda o_=o_, pm_=pm_, w=w: nc.scalar.copy(o_[:, :w], pm_[:, :w]), reads=[f"pm{pi}"], writes=[key])
                        else:
                            k.op(eng, lambda o_=o_, pm_=pm_, w=w: nc.vector.tensor_copy(o_[:, :w], pm_[:, :w]), reads=[f"pm{pi}"], writes=[key])
                        k.dma("gpsimd", dst[r0:r0 + 128, :], o_[:, :w], reads=[key])

            stage1a(0)
            stage1b(0)
            for bi in range(len(blocks)):
                if bi + 1 < len(blocks):
                    stage1a(bi + 1)
                stage2(bi, 0)
                if bi + 1 < len(blocks):
                    stage1b(bi + 1)
                stage2(bi, 1)


def _fp(a):
    return np.ascontiguousarray(a, dtype=np.float32)


def _const_tables():
    c = {}
    c["ident"] = np.eye(128, dtype=np.float32)
    r = np.arange(128)
    partner = np.where((r % 32) < 16, r + 16, r - 16)
    perm = np.zeros((128, 128), np.float32)
    perm[partner, r] = 1.0
    c["perm"] = perm
    sel2 = np.zeros((2, 2, 128), np.float32)
    sel2[0, 0, :] = 1.0
    sel2[1, 1, :] = 1.0
    c["sel2"] = sel2
    f = np.arange(16, dtype=np.float32)
    inv_freq = np.exp(np.float32(-np.log(10000.0)) * f / np.float32(16)).astype(np.float32)
    t = np.arange(T)
    row_pos = (t // 64).astype(np.float32)
    col_pos = (t % 64).astype(np.float32)
    d64 = r % 64
    is_row = d64 < 32
    fidx = (d64 % 32) % 16
    ang = np.where(is_row[:, None], row_pos[None, :], col_pos[None, :]).astype(np.float32) * inv_freq[fidx][:, None]
    ang = ang.astype(np.float32)
    sign = np.where((d64 % 32) < 16, -1.0, 1.0).astype(np.float32)
    c["ropeC"] = np.cos(ang).astype(np.float32)
    c["ropeS"] = (np.sin(ang).astype(np.float32) * sign[:, None]).astype(np.float32)
    m = np.arange(NCH + 1, dtype=np.float32)
    idx = np.zeros((128, 16, NCH + 1), np.float32)
    idx[:, 0:8, :] = m[None, None, :]
    idx[:, 8:16, :] = -m[None, None, :]
    c["s5idx"] = idx
    e = np.zeros((128, 3, 16, 16), np.float32)
    j = np.arange(16, dtype=np.float32)
    e[:, 0, 0:8, :] = 15 - j
    e[:, 0, 8:16, :] = j
    e[:, 1, 0:8, :] = j + 1
    e[:, 1, 8:16, :] = 16 - j
    e[:, 2, 0:8, :] = j - 15
    e[:, 2, 8:16, :] = -j
    c["s5exp"] = e
    pidx = np.arange(128)
    i8 = pidx // 16
    jj = np.repeat(np.arange(16), 16)
    msk = np.zeros((128, 2, 2, 256), np.float32)
    for kt in range(2):
        i = (8 * kt + i8)[:, None]
        msk[:, 0, kt, :] = (jj[None, :] >= i)
        msk[:, 1, kt, :] = (i >= jj[None, :])
    c["s5mask"] = msk
    return c


_PERM_COLS = None


def _wcol_perm():
    o = {"ua": (0, 256), "us": (256, 256), "k": (512, 512), "v": (1024, 512),
         "ga": (1536, 256), "gs": (1792, 256), "q": (2048, 512), "gd": (2560, 512)}
    idx = np.zeros(3 * D, np.int64)
    for name, (off, w) in WCOL.items():
        so, sw = o[name]
        idx[off:off + w] = np.arange(so, so + sw)
    return idx


def _fpart(vec, ntile):
    return _fp(np.asarray(vec).reshape(ntile, 128).T)


def _layer_inputs(inp, l):
    o = {}
    o[f"w_mod{l}"] = _fp(inp["w_mod"][l])
    o[f"bmodR{l}"] = _fp(np.repeat(inp["b_mod"][l][None, :], 2, axis=0))
    ng = _fpart(inp["norm_g"][l], 8)
    o[f"normg{l}"] = _fp(np.repeat(ng[:, :, None], 2, axis=2))
    o[f"w_in{l}"] = _fp(inp["w_in"][l][:, _wcol_perm()])
    o[f"w_out{l}"] = _fp(inp["w_out"][l])
    cw = inp["lru_conv_w"][l]
    o[f"convw{l}"] = _fp(cw.T.reshape(2, 128, 4).transpose(1, 0, 2))
    o[f"convb{l}"] = _fpart(inp["lru_conv_b"][l], 2)
    wax = np.zeros((128, 2, 2, 2, 128), np.float32)
    for ai, nm in enumerate(("lru_wa", "lru_wx")):
        w = inp[nm][l]
        for d in range(2):
            for ct in range(2):
                for h in range(2):
                    wax[h * 64:(h + 1) * 64, ct, ai, d, h * 64:(h + 1) * 64] = w[d, 2 * ct + h]
    o[f"wax{l}"] = wax
    bax = np.zeros((128, 2, 2, 2), np.float32)
    for ai, nm in enumerate(("lru_ba", "lru_bx")):
        for d in range(2):
            bax[:, :, ai, d] = _fpart(inp[nm][l][d], 2)
    o[f"bax{l}"] = bax
    ll = np.zeros((128, 2, 2), np.float32)
    for d in range(2):
        ll[:, :, d] = _fpart(inp["lru_lam"][l][d], 2)
    o[f"lrulam{l}"] = ll

    def tp_layout(a):
        a = np.asarray(a)
        rest = a.shape[3:]
        a = a.reshape((2, 8, 2, 64) + rest)
        a = np.moveaxis(a, (2, 3), (0, 1))
        return a.reshape((128, 16) + rest)

    s5lam = np.zeros((128, 3, 16), np.float32)
    s5lam[:, 0] = tp_layout(inp["s5_lam_re"][l])
    s5lam[:, 1] = tp_layout(inp["s5_lam_im"][l])
    s5lam[:, 2] = tp_layout(np.repeat(inp["s5_log_dt"][l][:, :, None], 64, axis=2))
    o[f"s5lam{l}"] = s5lam
    sb_ = np.zeros((128, 2, 16, 16), np.float32)
    sb_[:, 0] = tp_layout(inp["s5_b_re"][l])
    sb_[:, 1] = tp_layout(inp["s5_b_im"][l])
    o[f"s5b{l}"] = sb_
    sc_ = np.zeros((128, 2, 16, 16), np.float32)
    sc_[:, 0] = tp_layout(np.swapaxes(inp["s5_c_re"][l], 2, 3))
    sc_[:, 1] = tp_layout(np.swapaxes(inp["s5_c_im"][l], 2, 3))
    o[f"s5c{l}"] = sc_
    o[f"s5d{l}"] = _fp(inp["s5_d"][l][None, :])
    o[f"wglu{l}"] = _fp(inp["s5_w_glu"][l])
    o[f"bglu{l}"] = _fpart(inp["s5_b_glu"][l], 2)
    o[f"dalam{l}"] = _fp(inp["da_lam"][l].reshape(1, 256))
    o[f"dag{l}"] = _fp(inp["da_norm_g"][l][:, None])
    return o


def prep_inputs(inp):
    shared = dict(_const_tables())
    shared["finalg"] = _fp(inp["final_g"][None, :])
    for l in range(DEPTH):
        shared.update(_layer_inputs(inp, l))
    maps = []
    for b in range(8):
        m = dict(shared)
        m["hin"] = _fp(np.concatenate([inp["ctx"][b], inp["x"][b]], axis=0))
        cv = np.zeros((128, 8, 2), np.float32)
        cv[:, :, 0] = _fpart(inp["c"][b], 8)
        cv[:, :, 1] = _fpart(inp["c_ctx"], 8)
        m["cvec"] = cv
        maps.append(m)
    return maps


_NC_CACHE = {}


def kernel(**inputs):
    inp = {k_: np.asarray(v) for k_, v in inputs.items()}
    maps = prep_inputs(inp)
    if "nc" not in _NC_CACHE:
        _NC_CACHE["nc"] = Builder().build()
    res = run_bass_kernel_spmd(_NC_CACHE["nc"], maps, core_ids=list(range(8)))
    return np.stack([np.asarray(r["out"]) for r in res.results], axis=0).astype(np.float32)


def _phase_lru(self, l):
    nc, k, p = self.nc, self.k, self.L[l]
    with contextlib.ExitStack() as ph:
        N = TT
        convw = self.sb(ph, "l_convw", [128, 2, 4])
        convb = self.sb(ph, "l_convb", [128, 2])
        waxf = self.sb(ph, "l_waxf", [128, 2, 2, 2, 128])
        waxb = self.sb(ph, "l_waxb", [128, 2, 2, 2, 128], BF16)
        bax = self.sb(ph, "l_bax", [128, 2, 2, 2])
        lam = self.sb(ph, "l_lam", [128, 2, 2])
        cl = self.sb(ph, "l_cl", [128, 2, 2])
        cl2 = self.sb(ph, "l_cl2", [128, 2, 2])
        ua = self.sb(ph, "l_ua", [128, N])
        xc = self.sb(ph, "l_xc", [128, N])
        xcb = self.sb(ph, "l_xcb", [128, N], BF16)
        sg = self.sb(ph, "l_sg", [128, N])
        gr = [self.sb(ph, f"l_gr{d}", [128, N]) for d in range(2)]
        gi = [self.sb(ph, f"l_gi{d}", [128, N]) for d in range(2)]
        a2 = [self.sb(ph, f"l_a2{d}", [128, N]) for d in range(2)]
        hd = [self.sb(ph, "l_h0", [128, N]), ua]
        yb = xcb
        pg = [self.ps(ph, f"l_pg{i}", [128, 512]) for i in range(4)]
        k.dma("sync", convw[:], p["convw"], writes=["convw"])
        k.dma("sync", convb[:], p["convb"], writes=["convb"])
        k.dma("sync", waxf[:], p["wax"], writes=["waxf"])
        k.dma("sync", bax[:], p["bax"], writes=["bax"])
        k.dma("sync", lam[:], p["lrulam"], writes=["lam"])
        k.op(k.dve, lambda: nc.vector.tensor_copy(waxb[:], waxf[:]), reads=["waxf"], writes=["waxb"])
        k.op(k.act, lambda: nc.scalar.activation(cl[:], lam[:], AF.Exp, scale=-1.0), reads=["lam"], writes=["cl"])
        k.op(k.act, lambda: nc.scalar.activation(cl[:], cl[:], AF.Ln, scale=1.0, bias=1.0), reads=["cl"], writes=["cl"])
        k.op(k.dve, lambda: nc.vector.tensor_scalar_mul(cl2[:], cl[:], -16.0), reads=["cl"], writes=["cl2"])
        k.op(k.dve, lambda: nc.vector.tensor_scalar_mul(cl[:], cl[:], -8.0), reads=["cl", "cl2"], writes=["cl"])
        segs = [(0, TC), (TC, TT)]
        nblk = [(i * 512, min(512, N - i * 512)) for i in range((N + 511) // 512)]
        pgi = 0
        for ct in range(2):
            rows = slice(ct * 128, (ct + 1) * 128)
            k.dma("sync", ua[:], self.uaT[rows, :], writes=["ua"])
            k.dma("scalar", sg[:], self.gaT[rows, :], writes=["sg"])
            k.op(k.dve, lambda: nc.vector.tensor_scalar(xc[:], ua[:], convw[:, ct, 2:3], convb[:, ct:ct + 1], ALU.mult, ALU.add),
                 reads=["ua", "convw", "convb"], writes=["xc"])
            for (s0, s1) in segs:
                for tap, off in ((0, -2), (1, -1), (3, 1)):
                    lo = max(s0, s0 - off)
                    hi = min(s1, s1 - off)
                    k.op(k.dve, lambda: nc.vector.scalar_tensor_tensor(
                        xc[:, lo:hi], ua[:, lo + off:hi + off], convw[:, ct, tap:tap + 1], xc[:, lo:hi], ALU.mult, ALU.add),
                        reads=["ua", "xc", "convw"], writes=["xc"])
            k.op(k.act, lambda: nc.scalar.copy(xcb[:], xc[:]), reads=["xc"], writes=["xcb"])
            for d in range(2):
                for bi, (c0, w) in enumerate(nblk):
                    pr_, pi_ = pg[pgi % 4], pg[(pgi + 1) % 4]
                    kr, ki = f"pg{pgi % 4}", f"pg{(pgi + 1) % 4}"
                    pgi += 2
                    k.op(k.pe, lambda: nc.tensor.matmul(pr_[:, :w], waxb[:, ct, 0, d, :], xcb[:, c0:c0 + w], start=True, stop=True),
                         reads=["waxb", "xcb"], writes=[kr])
                    k.op(k.pe, lambda: nc.tensor.matmul(pi_[:, :w], waxb[:, ct, 1, d, :], xcb[:, c0:c0 + w], start=True, stop=True),
                         reads=["waxb", "xcb"], writes=[ki])
                    k.op(k.act, lambda: nc.scalar.activation(gr[d][:, c0:c0 + w], pr_[:, :w], AF.Sigmoid, bias=bax[:, ct, 0, d:d + 1]),
                         reads=[kr, "bax"], writes=[f"gr{d}"])
                    k.op(k.act, lambda: nc.scalar.activation(gi[d][:, c0:c0 + w], pi_[:, :w], AF.Sigmoid, bias=bax[:, ct, 1, d:d + 1]),
                         reads=[ki, "bax"], writes=[f"gi{d}"])
                k.op(k.dve, lambda: nc.vector.tensor_tensor(gi[d][:], gi[d][:], xc[:], ALU.mult), reads=[f"gi{d}", "xc"], writes=[f"gi{d}"])
            for d in range(2):
                k.op(k.act, lambda: nc.scalar.activation(a2[d][:], gr[d][:], AF.Exp, scale=cl2[:, ct, d:d + 1]), reads=[f"gr{d}", "cl2"], writes=[f"a2{d}"])
                k.op(k.act, lambda: nc.scalar.activation(gr[d][:], gr[d][:], AF.Exp, scale=cl[:, ct, d:d + 1]), reads=[f"gr{d}", "cl", f"a2{d}"], writes=[f"gr{d}"])
            for d in range(2):
                k.op(k.act, lambda: nc.scalar.activation(a2[d][:], a2[d][:], AF.Sqrt, scale=-1.0, bias=1.0), reads=[f"a2{d}"], writes=[f"a2{d}"])
                k.op(k.dve, lambda: nc.vector.tensor_tensor(gi[d][:], gi[d][:], a2[d][:], ALU.mult), reads=[f"gi{d}", f"a2{d}"], writes=[f"gi{d}"])
            k.op(k.act, lambda: nc.scalar.activation(sg[:], sg[:], AF.Silu), reads=["sg"], writes=["sg"])
            k.op(k.dve, lambda: nc.vector.tensor_tensor_scan(hd[0][:], gr[0][:], gi[0][:], 0.0, ALU.mult, ALU.add),
                 reads=["gr0", "gi0"], writes=["h0"])
            k.op(k.dve, lambda: nc.vector.tensor_tensor_scan(hd[1][:, 0:TC][:, ::-1], gr[1][:, 0:TC][:, ::-1], gi[1][:, 0:TC][:, ::-1],
                                                              0.0, ALU.mult, ALU.add), reads=["gr1", "gi1"], writes=["ua"])
            k.op(k.dve, lambda: nc.vector.tensor_tensor_scan(hd[1][:, TC:TT][:, ::-1], gr[1][:, TC:TT][:, ::-1], gi[1][:, TC:TT][:, ::-1],
                                                              hd[1][:, 0:1], ALU.mult, ALU.add), reads=["gr1", "gi1", "ua"], writes=["ua"])
            if self.debug:
                for d in range(2):
                    k.dma("gpsimd", self.dbg_hl[ct, d], hd[d][:], reads=[("h0" if d == 0 else "ua")])
            k.op(k.pool, lambda: nc.gpsimd.tensor_tensor(hd[0][:], hd[0][:], sg[:], ALU.mult), reads=["h0", "sg"], writes=["h0"])
            k.op(k.dve, lambda: nc.vector.tensor_tensor(hd[1][:], hd[1][:], sg[:], ALU.mult), reads=["ua", "sg"], writes=["ua"])
            k.op(k.dve, lambda: nc.vector.tensor_tensor(yb[:], hd[0][:], hd[1][:], ALU.add), reads=["h0", "ua"], writes=["xcb"])
            k.dma("gpsimd", self.yas[rows, :], yb[:], reads=["xcb"])


Builder.phase_lru = _phase_lru


def _att_prefetch(self, l, st_):
    nc, k, p = self.nc, self.k, self.L[l]
    kTs = self.sb(st_, "a_kT", [128, 4, TT], BF16)
    vs = self.sb(st_, "a_v", [128, NTT, 512], BF16)
    woutb = self.sb(st_, "a_wout", [128, 8, D], BF16)
    self._att_tmp = contextlib.ExitStack()
    stg = self.sb(self._att_tmp, "a_stg", [128, 8, 256])
    wsrc = p["w_out"].rearrange("(k p) n -> p k n", p=128)
    for q4 in range(4):
        k.dma("scalar", stg[:], wsrc[:, :, q4 * 256:(q4 + 1) * 256], writes=["astg"])
        k.op(k.pool, lambda: nc.gpsimd.tensor_copy(woutb[:, :, q4 * 256:(q4 + 1) * 256], stg[:]), reads=["astg"], writes=[f"wout{q4}"])
    if True:
        k.dma("scalar", kTs[:], self.kT.rearrange("(j p) t -> p j t", p=128), writes=["kTs"])
        vsrc = self.v_s.rearrange("(tt p) e -> p tt e", p=128)
        for q4 in range(0, NTT, 9):
            hi_ = min(NTT, q4 + 9)
            k.dma("scalar", vs[:, q4:hi_, :], vsrc[:, q4:hi_, :], writes=[f"vs{q4}"])
    return kTs, vs, woutb


def _phase_att(self, l, hsrc, last, pre):
    nc, k, p = self.nc, self.k, self.L[l]
    lam_init = 0.8 - 0.6 * float(np.exp(-0.3 * l))
    with contextlib.ExitStack() as ph:
        kTs, vs, woutb = pre
        dal = self.sb(ph, "a_dal", [1, 256])
        prod = self.sb(ph, "a_prod", [1, 2, 64])
        e2 = self.sb(ph, "a_e2", [1, 2])
        nl1 = self.sb(ph, "a_nl1", [1, 1])
        dagc = self.sb(ph, "a_dagc", [128, 1])
        onesf = self.sb(ph, "a_onesf", [128, 128])
        onesb = self.sb(ph, "a_onesb", [128, 128], BF16)
        fgb = self.sb(ph, "a_fgb", [128, D])
        qTb = [self.sb(ph, f"a_q{i}", [128, 4, 512], BF16) for i in range(2)]
        yasb = [self.sb(ph, f"a_yas{i}", [128, 4, 512], BF16) for i in range(2)]
        gdb = [self.sb(ph, f"a_gdb{i}", [128, 4, 512]) for i in range(2)]
        PT = [[self.sb(ph, f"a_pt{m}{i}", [128, 512], BF16) for i in range(3)] for m in range(2)]
        acc = [self.sb(ph, f"a_acc{m}", [128, 512]) for m in range(2)]
        rden = [self.sb(ph, f"a_rden{m}", [128, 512]) for m in range(2)]
        obT = [self.sb(ph, f"a_obT{i}", [128, 4, 512]) for i in range(2)]
        cO = [self.sb(ph, f"a_cO{i}", [128, 512]) for i in range(2)]
        t1 = self.sb(ph, "a_t1", [128, 512])
        sqb = self.sb(ph, "a_sqb", [128, 4, 512])
        rstd1 = self.sb(ph, "a_rstd", [128, 512])
        ydT = self.sb(ph, "a_ydT", [128, 4, 512], BF16)
        hx = [self.sb(ph, f"a_hx{i}", [128, D]) for i in range(2)]
        hn = [self.sb(ph, f"a_hn{i}", [128, D]) for i in range(2)]
        junk = self.sb(ph, "a_junk", [128, D], BF16)
        fs = self.sb(ph, "a_fs", [128, 2])
        psc = [self.ps(ph, f"a_psc{i}", [128, 512]) for i in range(4)]
        pO = [self.ps(ph, f"a_pO{i}", [128, 512]) for i in range(2)]
        pden = [self.ps(ph, f"a_pden{i}", [128, 512]) for i in range(2)]
        po = psc[0:2]
        pokeys = ["psc0", "psc1"]

        k.dma("sync", dal[:], p["dalam"], writes=["dal"])
        dv_ = dal[:].rearrange("p (m t e) -> p m t e", m=2, t=2)
        k.op(k.dve, lambda: nc.vector.tensor_tensor(prod[:], dv_[:, :, 0, :], dv_[:, :, 1, :], ALU.mult), reads=["dal"], writes=["prod"])
        k.op(k.dve, lambda: nc.vector.reduce_sum(e2[:], prod[:], axis=AX.X), reads=["prod"], writes=["e2"])
        k.op(k.act, lambda: nc.scalar.activation(e2[:], e2[:], AF.Exp), reads=["e2"], writes=["e2"])
        k.op(k.dve, lambda: nc.vector.tensor_tensor(nl1[:], e2[:, 1:2], e2[:, 0:1], ALU.subtract), reads=["e2"], writes=["nl1"])
        k.op(k.dve, lambda: nc.vector.tensor_scalar_add(nl1[:], nl1[:], -lam_init), reads=["nl1"], writes=["nl1"])
        k.op(k.pe, lambda: nc.tensor.matmul(pden[0][:, 0:1], self.ones1[:], nl1[:], start=True, stop=True), reads=["ones1", "nl1"], writes=["pden0"])
        k.op(k.dve, lambda: nc.vector.tensor_copy(self.nlam[:], pden[0][:, 0:1]), reads=["pden0"], writes=["nlam"])
        k.dma("sync", dagc[:], p["dag"], writes=["dagc"])
        k.op(k.dve, lambda: nc.vector.tensor_scalar_mul(dagc[:], dagc[:], 1.0 - lam_init), reads=["dagc"], writes=["dagc"])
        k.op(k.dve, lambda: nc.vector.memset(onesf[:], 1.0), writes=["onesf"])
        k.op(k.dve, lambda: nc.vector.memset(onesb[:], 1.0), writes=["onesb"])
        if last:
            k.dma("sync", fgb[:], self.finalg.partition_broadcast(128), writes=["fgb"])

        qblocks = [] if last else [(0, 256, [0, 1])]
        qblocks += [(TC + 512 * i, 512, list(range(NTT))) for i in range(8)]
        qTv = self.qT.rearrange("(j p) t -> p j t", p=128)
        yasv = self.yas.rearrange("(j p) t -> p j t", p=128)
        gdv = self.gdT.rearrange("(j p) t -> p j t", p=128)
        pt_i = 0
        pair_i = 0
        tile_i = [0]

        def epilogue1(bi):
            t0, nq, _ = qblocks[bi]
            gd_, ob_ = gdb[bi % 2], obT[bi % 2]
            banks = [(pden[0], "pden0"), (pden[1], "pden1"), (pO[0], "pO0"), (pO[1], "pO1")]
            tmpb = [(rstd1, "rstd"), (t1, "t1"), (cO[0], "cO0"), (cO[1], "cO1")]
            for h in range(4):
                pd, pdk = banks[h]
                k.op(k.pe, lambda: nc.tensor.matmul(pd[:, :nq], onesf[:], sqb[:, h, :nq], start=True, stop=True),
                     reads=["onesf", f"sqb{h}"], writes=[pdk])
            for h in range(4):
                pd, pdk = banks[h]
                rs_, rk = tmpb[h]
                k.op(k.act, lambda: nc.scalar.activation(rs_[:, :nq], pd[:, :nq], AF.Sqrt, scale=1.0 / 128, bias=EPS),
                     reads=[pdk], writes=[rk])
            for h in range(4):
                rs_, rk = tmpb[h]
                k.op(k.dve, lambda: nc.vector.reciprocal(rs_[:, :nq], rs_[:, :nq]), reads=[rk], writes=[rk])
                k.op(k.pool, lambda: nc.gpsimd.tensor_tensor(rs_[:, :nq], rs_[:, :nq], ob_[:, h, :nq], ALU.mult),
                     reads=[rk, f"obT{bi % 2}{h}"], writes=[rk])
                k.op(k.dve, lambda: nc.vector.scalar_tensor_tensor(ydT[:, h, :nq], rs_[:, :nq], dagc[:, 0:1], gd_[:, h, :nq], ALU.mult, ALU.mult),
                     reads=[rk, "dagc", f"gdb{bi % 2}"], writes=["ydT"])

        def epilogue2(bi):
            t0, nq, _ = qblocks[bi]
            v = 1 if t0 < TC else 0
            yb_ = yasb[bi % 2]
            for qs in range(nq // 128):
                r0 = t0 + qs * 128
                gi_ = tile_i[0] % 2
                tile_i[0] += 1
                hx_, hn_ = hx[gi_], hn[gi_]
                k.dma("sync", hx_[:], hsrc[r0:r0 + 128, :], writes=[f"hx{gi_}"])
                for hf in range(2):
                    for mt in range(8):
                        lhs = yb_[:, mt, qs * 128:(qs + 1) * 128] if mt < 4 else ydT[:, mt - 4, qs * 128:(qs + 1) * 128]
                        k.op(k.pe, lambda: nc.tensor.matmul(po[hf][:], lhs, woutb[:, mt, hf * 512:(hf + 1) * 512],
                                                            start=(mt == 0), stop=(mt == 7)),
                             reads=[f"yas{bi % 2}", "ydT"], writes=[pokeys[hf]])
                    cs = slice(hf * 512, (hf + 1) * 512)
                    k.op(k.dve, lambda: nc.vector.tensor_tensor(hn_[:, cs], po[hf][:], self.gateb[:, v, cs], ALU.mult),
                         reads=[pokeys[hf], f"gateb{v}{hf}"], writes=[f"hn{gi_}{hf}"])
                    k.op(k.pool, lambda: nc.gpsimd.tensor_tensor(hn_[:, cs], hn_[:, cs], hx_[:, cs], ALU.add),
                         reads=[f"hn{gi_}{hf}", f"hx{gi_}"], writes=[f"hn{gi_}{hf}"])
                hk = [f"hn{gi_}0", f"hn{gi_}1"]
                if not last:
                    k.dma("gpsimd", self.hbuf[r0:r0 + 128, :], hn_[:], reads=hk)
                else:
                    k.op(k.act, lambda: nc.scalar.activation(junk[:], hn_[:], AF.Square, accum_out=fs[:, gi_:gi_ + 1]),
                         reads=hk, writes=["junk", f"fs{gi_}"])
                    k.op(k.act, lambda: nc.scalar.activation(fs[:, gi_:gi_ + 1], fs[:, gi_:gi_ + 1], AF.Sqrt, scale=1.0 / D, bias=EPS),
                         reads=[f"fs{gi_}"], writes=[f"fs{gi_}"])
                    k.op(k.dve, lambda: nc.vector.reciprocal(fs[:, gi_:gi_ + 1], fs[:, gi_:gi_ + 1]), reads=[f"fs{gi_}"], writes=[f"fs{gi_}"])
                    k.op(k.dve, lambda: nc.vector.scalar_tensor_tensor(hn_[:], hn_[:], fs[:, gi_:gi_ + 1], fgb[:], ALU.mult, ALU.mult),
                         reads=hk + [f"fs{gi_}", "fgb"], writes=hk)
                    k.dma("gpsimd", self.out[r0 - TC:r0 - TC + 128, :], hn_[:], reads=hk)

        for bi, (t0, nq, ktl) in enumerate(qblocks):
            qb_, yb_, gd_, ob_ = qTb[bi % 2], yasb[bi % 2], gdb[bi % 2], obT[bi % 2]
            k.dma("sync", qb_[:, :, :nq], qTv[:, :, t0:t0 + nq], writes=[f"q{bi % 2}"])
            k.dma("sync", yb_[:, :, :nq], yasv[:, :, t0:t0 + nq], writes=[f"yas{bi % 2}"])
            k.dma("sync", gd_[:, :, :nq], gdv[:, :, t0:t0 + nq], writes=[f"gdb{bi % 2}"])
            k.op(k.act, lambda: nc.scalar.activation(gd_[:, :, :nq], gd_[:, :, :nq], AF.Silu), reads=[f"gdb{bi % 2}"], writes=[f"gdb{bi % 2}"])
            its = [(h, ki_, kt) for h in range(4) for ki_, kt in enumerate(ktl)]

            def emit_scores(i):
                h, ki_, kt = its[i]
                for m in range(2):
                    prt = slice(m * 64, (m + 1) * 64)
                    bnk = 2 * ((pair_i + i) % 2) + m
                    k.op(k.pe, lambda: nc.tensor.matmul(psc[bnk][:, :nq], kTs[prt, h, kt * 128:(kt + 1) * 128], qb_[prt, h, :nq],
                                                        start=True, stop=True), reads=["kTs", f"q{bi % 2}"], writes=[f"psc{bnk}"])

            emit_scores(0)
            for i, (h, ki_, kt) in enumerate(its):
                first, lastk = (ki_ == 0), (ki_ == len(ktl) - 1)
                defer_here = lastk and bi > 0 and h == 0
                if i + 1 < len(its) and not defer_here:
                    emit_scores(i + 1)
                for m in range(2):
                    bnk = 2 * ((pair_i + i) % 2) + m
                    pt_ = PT[m][pt_i % 3]
                    pk = f"pt{m}{pt_i % 3}"
                    k.op(k.act, lambda: nc.scalar.activation(pt_[:, :nq], psc[bnk][:, :nq], AF.Exp, scale=0.125), reads=[f"psc{bnk}"], writes=[pk])
                    k.op(k.pe, lambda: nc.tensor.matmul(pO[m][:, :nq], vs[:, kt, h * 128:(h + 1) * 128], pt_[:, :nq], start=first, stop=lastk),
                         reads=[pk], writes=[f"pO{m}"])
                    if m == 0:
                        k.op(k.pe, lambda: nc.tensor.matmul(pden[0][:, :nq], onesb[:], pt_[:, :nq], start=first, stop=lastk),
                             reads=[pk, "onesb"], writes=["pden0"])
                    else:
                        e_ = ki_ % 2
                        eng = k.dve if e_ == 0 else k.pool
                        if ki_ < 2:
                            k.op(eng, lambda: eng.h.tensor_copy(acc[e_][:, :nq], pt_[:, :nq]), reads=[pk], writes=[f"acc{e_}"])
                        else:
                            k.op(eng, lambda: eng.h.tensor_tensor(acc[e_][:, :nq], acc[e_][:, :nq], pt_[:, :nq], ALU.add),
                                 reads=[pk, f"acc{e_}"], writes=[f"acc{e_}"])
                pt_i += 1
                if not lastk:
                    continue
                k.op(k.act, lambda: nc.scalar.copy(cO[0][:, :nq], pO[0][:, :nq]), reads=["pO0"], writes=["cO0"])
                k.op(k.dve, lambda: nc.vector.tensor_copy(cO[1][:, :nq], pO[1][:, :nq]), reads=["pO1"], writes=["cO1"])
                k.op(k.dve, lambda: nc.vector.reciprocal(rden[0][:, :nq], pden[0][:, :nq]), reads=["pden0"], writes=["rden0"])
                k.op(k.pe, lambda: nc.tensor.matmul(pden[1][:, :nq], onesf[:], acc[0][:, :nq], start=True, stop=False),
                     reads=["onesf", "acc0"], writes=["pden1"])
                k.op(k.pe, lambda: nc.tensor.matmul(pden[1][:, :nq], onesf[:], acc[1][:, :nq], start=False, stop=True),
                     reads=["onesf", "acc1"], writes=["pden1"])
                k.op(k.dve, lambda: nc.vector.reciprocal(rden[1][:, :nq], pden[1][:, :nq]), reads=["pden1"], writes=["rden1"])
                k.op(k.dve, lambda: nc.vector.tensor_scalar_mul(rden[1][:, :nq], rden[1][:, :nq], self.nlam[:, 0:1]), reads=["rden1", "nlam"], writes=["rden1"])
                k.op(k.pool, lambda: nc.gpsimd.tensor_tensor(t1[:, :nq], cO[1][:, :nq], rden[1][:, :nq], ALU.mult), reads=["cO1", "rden1"], writes=["t1"])
                k.op(k.dve, lambda: nc.vector.tensor_tensor(ob_[:, h, :nq], cO[0][:, :nq], rden[0][:, :nq], ALU.mult), reads=["cO0", "rden0"], writes=[f"obT{bi % 2}{h}"])
                k.op(k.pool, lambda: nc.gpsimd.tensor_tensor(ob_[:, h, :nq], ob_[:, h, :nq], t1[:, :nq], ALU.add), reads=[f"obT{bi % 2}{h}", "t1"], writes=[f"obT{bi % 2}{h}"])
                k.op(k.pool, lambda: nc.gpsimd.tensor_tensor(sqb[:, h, :nq], ob_[:, h, :nq], ob_[:, h, :nq], ALU.mult),
                     reads=[f"obT{bi % 2}{h}"], writes=[f"sqb{h}"])
                if h == 3:
                    epilogue1(bi)
                if bi > 0 and h == 0:
                    epilogue2(bi - 1)
                    if i + 1 < len(its):
                        emit_scores(i + 1)
            pair_i += len(its)
        epilogue2(len(qblocks) - 1)


Builder.phase_att = _phase_att
Builder.att_prefetch = _att_prefetch
Builder.phase_s5 = lambda self, l: None


def _phase_s5(self, l):
    nc, k, p = self.nc, self.k, self.L[l]
    PI = float(np.pi)
    uid = [0]

    def nm(s_):
        uid[0] += 1
        return f"s_{s_}{uid[0]}"

    def dv(fn, reads, writes):
        return k.op(k.dve, fn, reads=reads, writes=writes)

    I32 = mybir.dt.int32
    PI_LO = 3.1415925

    tcache = {}

    def reduce_pi(st_, out_t, src, shape, rkeys, tmps=None):
        if tmps is None:
            u = self.sb(st_, nm("ru"), shape)
            qi = self.sb(st_, nm("rq"), shape, I32)
        else:
            u, qi = tmps
        dv(lambda: nc.vector.tensor_scalar_mul(u[:], src, 1.0 / TWO_PI), rkeys, [u.name])
        dv(lambda: nc.vector.tensor_copy(qi[:], u[:]), [u.name], [qi.name])
        dv(lambda: nc.vector.tensor_copy(u[:], qi[:]), [qi.name], [u.name])
        dv(lambda: nc.vector.scalar_tensor_tensor(out_t[:], u[:], -TWO_PI, src, ALU.mult, ALU.add), [u.name] + rkeys, [out_t.name])
        dv(lambda: nc.vector.tensor_scalar(out_t[:], out_t[:], -PI_LO, PI_LO, ALU.max, ALU.min), [out_t.name], [out_t.name])

    def sincos(st_, ang, shape, K_, key):
        sn = self.sb(st_, nm("sn"), shape)
        cs = self.sb(st_, nm("cs"), shape)
        ck = (id(st_), tuple(shape))
        if ck not in tcache:
            tcache[ck] = (self.sb(st_, nm("ah"), shape), self.sb(st_, nm("ru"), shape), self.sb(st_, nm("rq"), shape, I32))
        ah, u_, q_ = tcache[ck]
        reduce_pi(st_, sn, ang, shape, [key], (u_, q_))
        dv(lambda: nc.vector.tensor_scalar_add(ah[:], ang, PI / 2), [key], [ah.name])
        reduce_pi(st_, cs, ah[:], shape, [ah.name], (u_, q_))
        for t_ in (sn, cs):
            k.op(k.act, lambda: nc.scalar.activation(t_[:], t_[:], AF.Sin), reads=[t_.name], writes=[t_.name])
        return sn, cs

    with contextlib.ExitStack() as ph:
        BLt = self.sb(ph, "s_BLt", [128, 128, 64], BF16)
        DLt = self.sb(ph, "s_DLt", [128, 32, 256], BF16)
        CLR = self.sb(ph, "s_CLR", [128, 16, 256], BF16)
        CLI = self.sb(ph, "s_CLI", [128, 16, 256], BF16)
        lamp = self.sb(ph, "s_lamp", [128, 3, 16])
        lrd = self.sb(ph, "s_lrd", [128, 16])
        ang = self.sb(ph, "s_ang", [128, 16])
        k.dma("sync", lamp[:], p["s5lam"], writes=["lamp"])
        dt = self.sb(ph, "s_dt", [128, 16])
        k.op(k.act, lambda: nc.scalar.activation(dt[:], lamp[:, 2, :], AF.Exp), reads=["lamp"], writes=["dt"])
        dv(lambda: nc.vector.tensor_tensor(lrd[:], lamp[:, 0, :], dt[:], ALU.mult), ["lamp", "dt"], ["lrd"])
        dv(lambda: nc.vector.tensor_tensor(ang[:], lamp[:, 1, :], dt[:], ALU.mult), ["lamp", "dt"], ["ang"])

        with contextlib.ExitStack() as sa:
            S2 = [128, 16]
            bsrc = self.sb(sa, "s_bsrc", [128, 2, 16, 16])
            csrc = self.sb(sa, "s_csrc", [128, 2, 16, 16])
            expt = self.sb(sa, "s_expt", [128, 3, 16, 16])
            mask = self.sb(sa, "s_mask", [128, 2, 2, 256])
            k.dma("sync", bsrc[:], p["s5b"], writes=["bsrc"])
            k.dma("sync", csrc[:], p["s5c"], writes=["csrc"])
            k.dma("sync", expt[:], self.s5exp, writes=["expt"])
            k.dma("sync", mask[:], self.s5mask, writes=["mask"])
            mag = self.sb(sa, "s_mag", S2)
            k.op(k.act, lambda: nc.scalar.activation(mag[:], lrd[:], AF.Exp), reads=["lrd"], writes=["mag"])
            sn, cs = sincos(sa, ang[:], S2, 1, "ang")
            nr = self.sb(sa, "s_nr", S2); ni = self.sb(sa, "s_ni", S2); den = self.sb(sa, "s_den", S2)
            t1 = self.sb(sa, "s_t1", S2); t2 = self.sb(sa, "s_t2", S2)
            cfr = self.sb(sa, "s_cfr", S2); cfi = self.sb(sa, "s_cfi", S2)
            dv(lambda: nc.vector.tensor_tensor(nr[:], mag[:], cs[:], ALU.mult), ["mag", cs.name], ["nr"])
            dv(lambda: nc.vector.tensor_scalar_add(nr[:], nr[:], -1.0), ["nr"], ["nr"])
            dv(lambda: nc.vector.tensor_tensor(ni[:], mag[:], sn[:], ALU.mult), ["mag", sn.name], ["ni"])
            lre, lim = lamp[:, 0, :], lamp[:, 1, :]
            dv(lambda: nc.vector.tensor_tensor(den[:], lre, lre, ALU.mult), ["lamp"], ["den"])
            dv(lambda: nc.vector.tensor_tensor(t1[:], lim, lim, ALU.mult), ["lamp"], ["t1"])
            dv(lambda: nc.vector.tensor_tensor(den[:], den[:], t1[:], ALU.add), ["den", "t1"], ["den"])
            dv(lambda: nc.vector.reciprocal(den[:], den[:]), ["den"], ["den"])
            dv(lambda: nc.vector.tensor_tensor(t1[:], nr[:], lre, ALU.mult), ["nr", "lamp"], ["t1"])
            dv(lambda: nc.vector.tensor_tensor(t2[:], ni[:], lim, ALU.mult), ["ni", "lamp"], ["t2"])
            dv(lambda: nc.vector.tensor_tensor(cfr[:], t1[:], t2[:], ALU.add), ["t1", "t2"], ["cfr"])
            dv(lambda: nc.vector.tensor_tensor(cfr[:], cfr[:], den[:], ALU.mult), ["cfr", "den"], ["cfr"])
            dv(lambda: nc.vector.tensor_tensor(t1[:], ni[:], lre, ALU.mult), ["ni", "lamp", "cfr"], ["t1"])
            dv(lambda: nc.vector.tensor_tensor(t2[:], nr[:], lim, ALU.mult), ["nr", "lamp", "cfr"], ["t2"])
            dv(lambda: nc.vector.tensor_tensor(cfi[:], t1[:], t2[:], ALU.subtract), ["t1", "t2"], ["cfi"])
            dv(lambda: nc.vector.tensor_tensor(cfi[:], cfi[:], den[:], ALU.mult), ["cfi", "den"], ["cfi"])
            S3 = [128, 16, 16]
            S4 = [128, 16, 16, 16]
            bbr = self.sb(sa, "s_bbr", S3); bbi = self.sb(sa, "s_bbi", S3); u1 = self.sb(sa, "s_u1", S3)
            cfrb = cfr[:].unsqueeze(2).broadcast_to(S3)
            cfib = cfi[:].unsqueeze(2).broadcast_to(S3)
            dv(lambda: nc.vector.tensor_tensor(bbr[:], bsrc[:, 0], cfrb, ALU.mult), ["bsrc", "cfr"], ["bbr"])
            dv(lambda: nc.vector.tensor_tensor(u1[:], bsrc[:, 1], cfib, ALU.mult), ["bsrc", "cfi"], ["u1"])
            dv(lambda: nc.vector.tensor_tensor(bbr[:], bbr[:], u1[:], ALU.subtract), ["bbr", "u1"], ["bbr"])
            dv(lambda: nc.vector.tensor_tensor(bbi[:], bsrc[:, 1], cfrb, ALU.mult), ["bsrc", "cfr", "bbr"], ["bbi"])
            dv(lambda: nc.vector.tensor_tensor(u1[:], bsrc[:, 0], cfib, ALU.mult), ["bsrc", "cfi", "bbr"], ["u1"])
            dv(lambda: nc.vector.tensor_tensor(bbi[:], bbi[:], u1[:], ALU.add), ["bbi", "u1"], ["bbi"])

            def cpow(e):
                lr = self.sb(sa, nm("lr"), S3)
                an = self.sb(sa, nm("an"), S3)
                dv(lambda: nc.vector.tensor_tensor(lr[:], expt[:, e], lrd[:].unsqueeze(2).broadcast_to(S3), ALU.mult), ["expt", "lrd"], [lr.name])
                k.op(k.act, lambda: nc.scalar.activation(lr[:], lr[:], AF.Exp), reads=[lr.name], writes=[lr.name])
                dv(lambda: nc.vector.tensor_tensor(an[:], expt[:, e], ang[:].unsqueeze(2).broadcast_to(S3), ALU.mult), ["expt", "ang"], [an.name])
                s_, c_ = sincos(sa, an[:], S3, 60, an.name)
                dv(lambda: nc.vector.tensor_tensor(c_[:], c_[:], lr[:], ALU.mult), [c_.name, lr.name], [c_.name])
                dv(lambda: nc.vector.tensor_tensor(s_[:], s_[:], lr[:], ALU.mult), [s_.name, lr.name], [s_.name])
                return c_, s_

            w1 = self.sb(sa, "s_w1", S4)
            w2 = self.sb(sa, "s_w2", S4)

            def cmul(out_re, out_im_neg, out_im, pw, vr, vi, vkeys):
                pr_, pi_ = pw
                prb = pr_[:].unsqueeze(3).broadcast_to(S4)
                pib = pi_[:].unsqueeze(3).broadcast_to(S4)
                vrb = vr.unsqueeze(2).broadcast_to(S4)
                vib = vi.unsqueeze(2).broadcast_to(S4)
                rk = [pr_.name, pi_.name] + vkeys
                dv(lambda: nc.vector.tensor_tensor(w1[:], prb, vrb, ALU.mult), rk, ["w1"])
                dv(lambda: nc.vector.tensor_tensor(w2[:], pib, vib, ALU.mult), rk, ["w2"])
                dv(lambda: nc.vector.tensor_tensor(out_re, w1[:], w2[:], ALU.subtract), ["w1", "w2"], [nm("o")])
                dv(lambda: nc.vector.tensor_tensor(w1[:], prb, vib, ALU.mult), rk, ["w1"])
                dv(lambda: nc.vector.tensor_tensor(w2[:], pib, vrb, ALU.mult), rk, ["w2"])
                if out_im is not None:
                    dv(lambda: nc.vector.tensor_tensor(out_im, w1[:], w2[:], ALU.add), ["w1", "w2"], [nm("o")])
                else:
                    dv(lambda: nc.vector.scalar_tensor_tensor(out_im_neg, w1[:], -1.0, w2[:], ALU.mult, ALU.subtract), ["w1", "w2"], [nm("o")])

            PBr = self.sb(sa, "s_PBr", S4); PBi = self.sb(sa, "s_PBi", S4)
            QCr = self.sb(sa, "s_QCr", S4); QCn = self.sb(sa, "s_QCn", S4)
            cmul(PBr[:], None, PBi[:], cpow(0), bbr[:], bbi[:], ["bbr", "bbi"])
            cmul(CLR[:].rearrange("p t (j h) -> p t j h", h=16), CLI[:].rearrange("p t (j h) -> p t j h", h=16), None,
                 cpow(1), csrc[:, 0], csrc[:, 1], ["csrc"])
            cmul(QCr[:], QCn[:], None, cpow(2), csrc[:, 0], csrc[:, 1], ["csrc"])
            k.barrier()
            pst = [self.ps(sa, f"s_pst{i}", [128, 512]) for i in range(4)]
            psd = [self.ps(sa, f"s_psd{i}", [128, 512]) for i in range(4)]
            for tp in range(16):
                pts = [pst[(tp % 2) * 2 + gl] for gl in range(2)]
                pks = [f"pst{(tp % 2) * 2 + gl}" for gl in range(2)]
                for kt in range(2):
                    for pl_, PB in enumerate((PBr, PBi)):
                        q_ = kt * 2 + pl_
                        for gl in range(2):
                            prt = slice(gl * 64, (gl + 1) * 64)
                            k.op(k.pe, lambda: nc.tensor.transpose(pts[gl][:, q_ * 64:(q_ + 1) * 64], PB[prt, tp, kt * 8:(kt + 1) * 8, :],
                                                                   self.identf[prt, prt]), reads=["identf"], writes=[pks[gl]])
                for gl in range(2):
                    s0 = (tp * 2 + gl) * 4
                    k.op(k.act, lambda: nc.scalar.copy(BLt[:, s0:s0 + 4, :], pts[gl][:, 0:256].rearrange("p (a b) -> p a b", b=64)),
                         reads=[pks[gl]], writes=["BLt"])
            m1s = [self.sb(sa, nm("m"), [128, 256]) for _ in range(2)]
            for gp in range(8):
                for kt in range(2):
                    for dr, tp in ((0, gp), (1, 8 + gp)):
                        for PBx, QCx, st_, sp_ in ((PBr, QCr, True, False), (PBi, QCn, False, True)):
                            for gl in range(2):
                                prt = slice(gl * 64, (gl + 1) * 64)
                                ps_ = psd[gl * 2 + dr]
                                k.op(k.pe, lambda: nc.tensor.matmul(ps_[:, 0:256], PBx[prt, tp, kt * 8:(kt + 1) * 8, :], QCx[prt, tp, :, :],
                                                                    start=st_, stop=sp_), reads=[], writes=[f"psd{gl * 2 + dr}"])
                    for gl in range(2):
                        g = gp * 2 + gl
                        pf, pb = psd[gl * 2], psd[gl * 2 + 1]
                        kf_, kb_ = f"psd{gl * 2}", f"psd{gl * 2 + 1}"
                        m1 = m1s[gl]
                        dv(lambda: nc.vector.tensor_tensor(m1[:], pf[:, 0:256], mask[:, 0, kt, :], ALU.mult), [kf_, "mask"], [m1.name])
                        dv(lambda: nc.vector.tensor_tensor(pb[:, 256:512], pb[:, 0:256], mask[:, 1, kt, :], ALU.mult), [kb_, "mask"], [kb_])
                        dv(lambda: nc.vector.tensor_tensor(DLt[:, g * 2 + kt, :], pb[:, 256:512], m1[:], ALU.add), [kb_, m1.name], ["DLt"])
            k.barrier()

        if getattr(self, "s5_stop", 9) <= 1:
            return
        Ut = self.sb(ph, "s_Ut", [128, 32, NCH], BF16)
        SFR = self.sb(ph, "s_SFR", [128, 8, NCH + 1], BF16); SFI = self.sb(ph, "s_SFI", [128, 8, NCH + 1], BF16)
        SBR = self.sb(ph, "s_SBR", [128, 8, NCH + 1], BF16); SBI = self.sb(ph, "s_SBI", [128, 8, NCH + 1], BF16)
        P16 = self.sb(ph, "s_P16", [128, 16])
        phr = self.sb(ph, "s_phr", [128, 16])
        k.op(k.act, lambda: nc.scalar.activation(P16[:], lrd[:], AF.Exp, scale=16.0), reads=["lrd"], writes=["P16"])
        ph16 = self.sb(ph, "s_ph16", [128, 16])
        dv(lambda: nc.vector.tensor_scalar_mul(ph16[:], ang[:], 16.0), ["ang"], ["ph16"])
        reduce_pi(ph, phr, ph16[:], [128, 16], ["ph16"])
        phn = self.sb(ph, "s_phn", [128, 16])
        dv(lambda: nc.vector.tensor_scalar_mul(phn[:], phr[:], 1.0 / TWO_PI), [phr.name], ["phn"])
        CT = [(0, 16), (16, 128), (144, 128)]
        usv = self.us_s.rearrange("(c j) ch -> c j ch", j=16)
        with contextlib.ExitStack() as su:
            ucm = [self.sb(su, f"s_ucm{i}", [128, 16, 256]) for i in range(3)]
            psu = [self.ps(su, f"s_psu{i}", [128, 512]) for i in range(4)]
            n_ = 0
            for ci, (c0, n) in enumerate(CT):
                k.dma("sync", ucm[ci][0:n], usv[c0:c0 + n], writes=[f"ucm{ci}"])
                ucp_ = self.sb(su, f"s_ucp{ci}", [128, 16, 16, 16])
                k.op(k.pool if ci == 1 else k.dve,
                     lambda: (nc.gpsimd if ci == 1 else nc.vector).tensor_copy(
                         ucp_[0:n], ucm[ci][0:n].rearrange("p i (g h) -> p g i h", h=16)),
                     reads=[f"ucm{ci}"], writes=[f"ucp{ci}"])
                for q4 in range(8):
                    pu = psu[n_ % 4]
                    pk = f"psu{n_ % 4}"
                    n_ += 1
                    for a in range(4):
                        s_ = q4 * 4 + a
                        g, kt = s_ // 2, s_ % 2
                        k.op(k.pe, lambda: nc.tensor.transpose(pu[:, a * 128:a * 128 + n], ucp_[0:n, g, kt * 8:(kt + 1) * 8, :],
                                                               self.identf[0:n, 0:n]), reads=[f"ucp{ci}", "identf"], writes=[pk])
                    eng = k.act if n_ % 2 == 0 else k.dve
                    src = pu[:].rearrange("p (a b) -> p a b", b=128)[:, :, 0:n]
                    dst = Ut[:, q4 * 4:(q4 + 1) * 4, c0:c0 + n]
                    if eng is k.act:
                        k.op(eng, lambda: nc.scalar.copy(dst, src), reads=[pk], writes=["Ut"])
                    else:
                        k.op(eng, lambda: nc.vector.tensor_copy(dst, src), reads=[pk], writes=["Ut"])
        k.barrier()
        if getattr(self, "s5_stop", 9) <= 2:
            return
        for d in range(2):
            with contextlib.ExitStack() as sd:
                S8 = [128, 8, NCH]
                idx = self.sb(sd, "s_idx", S8)
                k.dma("sync", idx[:], self.s5idx[:, d * 8:(d + 1) * 8, 0:NCH], writes=["idx"])
                dv(lambda: nc.vector.tensor_tensor(idx[:], idx[:], phn[:, d * 8:(d + 1) * 8].unsqueeze(2).broadcast_to(S8), ALU.mult),
                   ["idx", "phn"], ["idx"])
                SN = self.sb(sd, nm("SN"), S8)
                CS = self.sb(sd, nm("CS"), S8)
                qi8 = self.sb(sd, nm("qi8"), S8, I32)
                dv(lambda: nc.vector.tensor_copy(qi8[:], idx[:]), ["idx"], [qi8.name])
                dv(lambda: nc.vector.tensor_copy(CS[:], qi8[:]), [qi8.name], [CS.name])
                dv(lambda: nc.vector.tensor_tensor(SN[:], idx[:], CS[:], ALU.subtract), ["idx", CS.name], [SN.name])
                dv(lambda: nc.vector.scalar_tensor_tensor(CS[:], SN[:], -1.0, SN[:], ALU.mult, ALU.max), [SN.name], [CS.name])
                k.op(k.act, lambda: nc.scalar.activation(SN[:], SN[:], AF.Sin, scale=6.283179), reads=[SN.name], writes=[SN.name])
                k.op(k.act, lambda: nc.scalar.activation(CS[:], CS[:], AF.Sin, scale=-6.283179, bias=1.5707960), reads=[CS.name], writes=[CS.name])
                PCO = self.sb(sd, "s_PCO", S8)
                XR = self.sb(sd, "s_XR", S8); XI = self.sb(sd, "s_XI", S8)
                VR = self.sb(sd, "s_VR", S8); VI = self.sb(sd, "s_VI", S8)
                WR = self.sb(sd, "s_WR", S8); WI = self.sb(sd, "s_WI", S8)
                psx = [self.ps(sd, f"s_psx{i}", [128, 512]) for i in range(4)]
                dv(lambda: nc.vector.memset(PCO[:], 1.0), [], ["PCO"])
                dv(lambda: nc.vector.tensor_tensor(PCO[:], PCO[:], P16[:, d * 8:(d + 1) * 8].unsqueeze(2).broadcast_to(S8), ALU.mult),
                   ["PCO", "P16"], ["PCO"])
                zc = 0 if d == 0 else NCH - 1
                dv(lambda: nc.vector.memset(PCO[:, :, zc:zc + 1], 0.0), ["PCO"], ["PCO"])
                n_ = 0
                for a in range(8):
                    tp = d * 8 + a
                    for pl_, X in enumerate((XR, XI)):
                        px = psx[n_ % 4]
                        pk = f"psx{n_ % 4}"
                        n_ += 1
                        for gl in range(2):
                            g = a * 2 + gl
                            prt = slice(gl * 64, (gl + 1) * 64)
                            segs = [(0, NCH, 0)] if d == 0 else [(16, NCH, 0), (0, 16, 256)]
                            for (u0, u1_, o0) in segs:
                                for kt in range(2):
                                    slot = ((tp * 2 + gl) * 2 + kt) * 2 + pl_
                                    k.op(k.pe, lambda: nc.tensor.matmul(px[prt, o0:o0 + (u1_ - u0)], BLt[:, slot, :], Ut[:, g * 2 + kt, u0:u1_],
                                                                        start=(kt == 0), stop=(kt == 1)), reads=["BLt", "Ut"], writes=[pk])
                        k.op(k.act, lambda: nc.scalar.copy(X[:, a, :], px[:, 0:NCH]), reads=[pk], writes=[X.name])
                dv(lambda: nc.vector.tensor_tensor(VR[:], XR[:], CS[:], ALU.mult), [XR.name, CS.name], ["VR"])
                k.op(k.pool, lambda: nc.gpsimd.tensor_tensor(WR[:], XI[:], SN[:], ALU.mult), reads=[XI.name, SN.name], writes=["WR"])
                dv(lambda: nc.vector.tensor_tensor(VR[:], VR[:], WR[:], ALU.add), ["VR", "WR"], ["VR"])
                dv(lambda: nc.vector.tensor_tensor(VI[:], XI[:], CS[:], ALU.mult), [XI.name, CS.name], ["VI"])
                k.op(k.pool, lambda: nc.gpsimd.tensor_tensor(WI[:], XR[:], SN[:], ALU.mult), reads=[XR.name, SN.name], writes=["WI"])
                dv(lambda: nc.vector.tensor_tensor(VI[:], VI[:], WI[:], ALU.subtract), ["VI", "WI"], ["VI"])
                fl = lambda t_: t_[:].rearrange("p a b -> p (a b)")
                rv = (lambda ap: ap) if d == 0 else (lambda ap: ap[:, ::-1])
                dv(lambda: nc.vector.tensor_tensor_scan(rv(fl(WR)), rv(fl(PCO)), rv(fl(VR)), 0.0, ALU.mult, ALU.add), ["PCO", "VR", "WR"], ["WR"])
                dv(lambda: nc.vector.tensor_tensor_scan(rv(fl(WI)), rv(fl(PCO)), rv(fl(VI)), 0.0, ALU.mult, ALU.add), ["PCO", "VI", "WI"], ["WI"])
                SR_, SI_ = (SFR, SFI) if d == 0 else (SBR, SBI)
                o_ = 1 if d == 0 else 0
                zcol = 0 if d == 0 else NCH
                dv(lambda: nc.vector.memset(SR_[:, :, zcol:zcol + 1], 0.0), [], [SR_.name])
                dv(lambda: nc.vector.memset(SI_[:, :, zcol:zcol + 1], 0.0), [], [SI_.name])
                dv(lambda: nc.vector.tensor_tensor(VR[:], WR[:], CS[:], ALU.mult), ["WR", CS.name, "VR"], ["VR"])
                k.op(k.pool, lambda: nc.gpsimd.tensor_tensor(VI[:], WI[:], SN[:], ALU.mult), reads=["WI", SN.name, "VI"], writes=["VI"])
                dv(lambda: nc.vector.tensor_tensor(SR_[:, :, o_:o_ + NCH], VR[:], VI[:], ALU.subtract), ["VR", "VI", SR_.name], [SR_.name])
                dv(lambda: nc.vector.tensor_tensor(VR[:], WI[:], CS[:], ALU.mult), ["WI", CS.name, "VR", SR_.name], ["VR"])
                k.op(k.pool, lambda: nc.gpsimd.tensor_tensor(VI[:], WR[:], SN[:], ALU.mult), reads=["WR", SN.name, "VI", SR_.name], writes=["VI"])
                dv(lambda: nc.vector.tensor_tensor(SI_[:, :, o_:o_ + NCH], VR[:], VI[:], ALU.add), ["VR", "VI", SI_.name], [SI_.name])
            k.barrier()
        if getattr(self, "s5_stop", 9) <= 3:
            return
        with contextlib.ExitStack() as sy:
            dsk = self.sb(sy, "s_dsk", [128, 256])
            k.dma("sync", dsk[:], p["s5d"].partition_broadcast(128), writes=["dsk"])
            ucm = [self.sb(sy, f"s_ucy{i}", [128, 16, 256]) for i in range(2)]
            ycm = [self.sb(sy, f"s_ycm{i}", [128, 16, 256]) for i in range(2)]
            tq = [self.sb(sy, f"s_tq{i}", [128, 16, 256]) for i in range(2)]
            psy = [self.ps(sy, f"s_psy{i}", [128, 512]) for i in range(4)]
            zsv = self.zs.rearrange("(c j) ch -> c j ch", j=16)
            n_ = 0
            for ci, (c0, n) in enumerate(CT):
                u_, y_, t_ = ucm[ci % 2], ycm[ci % 2], tq[ci % 2]
                uk, yk, tk = f"ucy{ci % 2}", f"ycm{ci % 2}", f"tq{ci % 2}"
                k.dma("sync", u_[0:n], usv[c0:c0 + n], writes=[uk])
                mb0 = (256 if c0 < 16 else c0 - 16) + 1
                for g2 in range(8):
                    pys = [psy[(2 * n_) % 4], psy[(2 * n_ + 1) % 4]]
                    pks = [f"psy{(2 * n_) % 4}", f"psy{(2 * n_ + 1) % 4}"]
                    n_ += 1
                    mms = []
                    for gg in range(2):
                        g = g2 * 2 + gg
                        gp, gl = g // 2, g % 2
                        prt = slice(gl * 64, (gl + 1) * 64)
                        mms.append([(Ut[:, g * 2 + 0, c0:c0 + n], DLt[:, g * 2 + 0, :]),
                                    (Ut[:, g * 2 + 1, c0:c0 + n], DLt[:, g * 2 + 1, :]),
                                    (SFR[prt, gp, c0:c0 + n], CLR[prt, gp, :]),
                                    (SFI[prt, gp, c0:c0 + n], CLI[prt, gp, :]),
                                    (SBR[prt, gp, mb0:mb0 + n], CLR[prt, 8 + gp, :]),
                                    (SBI[prt, gp, mb0:mb0 + n], CLI[prt, 8 + gp, :])])
                    for mi in range(6):
                        for gg in range(2):
                            lh, rh = mms[gg][mi]
                            k.op(k.pe, lambda: nc.tensor.matmul(pys[gg][0:n, 0:256], lh, rh, start=(mi == 0), stop=(mi == 5)),
                                 reads=["Ut", "DLt", "CLR", "CLI", SFR.name, SFI.name, SBR.name, SBI.name], writes=[pks[gg]])
                    for gg in range(2):
                        g = g2 * 2 + gg
                        src = pys[gg][0:n, 0:256].rearrange("p (j h) -> p j h", h=16)
                        dst = y_[0:n, :, g * 16:(g + 1) * 16]
                        if gg == 0:
                            k.op(k.act, lambda: nc.scalar.copy(dst, src), reads=[pks[gg]], writes=[yk])
                        else:
                            k.op(k.dve, lambda: nc.vector.tensor_copy(dst, src), reads=[pks[gg]], writes=[yk])
                dsb = dsk[0:n].unsqueeze(1).broadcast_to([n, 16, 256])
                k.op(k.pool, lambda: nc.gpsimd.tensor_tensor(u_[0:n], u_[0:n], dsb, ALU.mult), reads=[uk, "dsk"], writes=[uk])
                dv(lambda: nc.vector.tensor_tensor(y_[0:n], y_[0:n], u_[0:n], ALU.add), [yk, uk], [yk])
                k.op(k.pool, lambda: nc.gpsimd.tensor_tensor(t_[0:n], y_[0:n], y_[0:n], ALU.mult), reads=[yk], writes=[tk])
                dv(lambda: nc.vector.tensor_scalar(t_[0:n], t_[0:n], 0.044715, 1.0, ALU.mult, ALU.add), [tk], [tk])
                k.op(k.pool, lambda: nc.gpsimd.tensor_tensor(t_[0:n], t_[0:n], y_[0:n], ALU.mult), reads=[tk, yk], writes=[tk])
                k.op(k.act, lambda: nc.scalar.activation(t_[0:n], t_[0:n], AF.Sigmoid, scale=1.5957691216057308), reads=[tk], writes=[tk])
                dv(lambda: nc.vector.tensor_tensor(y_[0:n], y_[0:n], t_[0:n], ALU.mult), [yk, tk], [yk])
                k.dma("gpsimd", zsv[c0:c0 + n], y_[0:n], reads=[yk])


def _phase_s5b(self, l):
    nc, k, p = self.nc, self.k, self.L[l]

    def dv(fn, reads, writes):
        return k.op(k.dve, fn, reads=reads, writes=writes)

    with contextlib.ExitStack() as sb_:
        wgf = self.sb(sb_, "g_wgf", [128, 2, 256])
        wgb = self.sb(sb_, "g_wgb", [128, 2, 256], BF16)
        bgl = self.sb(sb_, "g_bgl", [128, 2])
        zT = self.sb(sb_, "g_zT", [128, 2, TT])
        zTb = self.sb(sb_, "g_zTb", [128, 2, TT], BF16)
        zt = [self.sb(sb_, f"g_zt{i}", [128, 256]) for i in range(2)]
        gsl = self.sb(sb_, "g_gs", [128, TT])
        sgl = self.sb(sb_, "g_sg", [128, TT])
        yb = self.sb(sb_, "g_yb", [128, TT], BF16)
        pz = [self.ps(sb_, f"g_pz{i}", [128, 512]) for i in range(2)]
        pg = [self.ps(sb_, f"g_pg{i}", [128, 512]) for i in range(2)]
        k.dma("sync", wgf[:], p["wglu"].rearrange("(k p) n -> p k n", p=128), writes=["wgf"])
        k.dma("sync", bgl[:], p["bglu"], writes=["bgl"])
        dv(lambda: nc.vector.tensor_copy(wgb[:], wgf[:]), ["wgf"], ["wgb"])
        for tt in range(NTT):
            z_ = zt[tt % 2]
            zk = f"zt{tt % 2}"
            pz_ = pz[tt % 2]
            k.dma("sync", z_[:], self.zs[tt * 128:(tt + 1) * 128, :], writes=[zk])
            for c_ in range(2):
                k.op(k.pe, lambda: nc.tensor.matmul(pz_[:, c_ * 128:(c_ + 1) * 128], z_[:, c_ * 128:(c_ + 1) * 128], self.identf[:],
                                                    start=True, stop=True),
                     reads=[zk, "identf"], writes=[f"pz{tt % 2}"])
            for c_ in range(2):
                k.op(k.dve, lambda: nc.vector.tensor_copy(zT[:, c_, tt * 128:(tt + 1) * 128], pz_[:, c_ * 128:(c_ + 1) * 128]),
                     reads=[f"pz{tt % 2}"], writes=["zT"])
                dv(lambda: nc.vector.tensor_copy(zTb[:, c_, tt * 128:(tt + 1) * 128], pz_[:, c_ * 128:(c_ + 1) * 128]),
                   [f"pz{tt % 2}"], ["zTb"])
        if getattr(self, "s5_stop", 9) <= 5:
            return
        nblk = [(i * 512, min(512, TT - i * 512)) for i in range((TT + 511) // 512)]
        for co in range(2):
            rows = slice(256 + co * 128, 256 + (co + 1) * 128)
            k.dma("sync", gsl[:], self.gsT[co * 128:(co + 1) * 128, :], writes=["gsl"])
            k.op(k.act, lambda: nc.scalar.activation(gsl[:], gsl[:], AF.Silu), reads=["gsl"], writes=["gsl"])
            for bi, (c0, w) in enumerate(nblk):
                pg_ = pg[bi % 2]
                for ci_ in range(2):
                    k.op(k.pe, lambda: nc.tensor.matmul(pg_[:, :w], wgb[:, ci_, co * 128:(co + 1) * 128], zTb[:, ci_, c0:c0 + w],
                                                        start=(ci_ == 0), stop=(ci_ == 1)), reads=["wgb", "zTb"], writes=[f"pg{bi % 2}"])
                k.op(k.act, lambda: nc.scalar.activation(sgl[:, c0:c0 + w], pg_[:, :w], AF.Sigmoid, bias=bgl[:, co:co + 1]),
                     reads=[f"pg{bi % 2}", "bgl"], writes=["sgl"])
            dv(lambda: nc.vector.tensor_tensor(sgl[:], sgl[:], zT[:, co, :], ALU.mult), ["sgl", "zT"], ["sgl"])
            k.op(k.pool, lambda: nc.gpsimd.tensor_tensor(yb[:], sgl[:], gsl[:], ALU.mult), reads=["sgl", "gsl"], writes=["yb"])
            k.dma("gpsimd", self.yas[rows, :], yb[:], reads=["yb"])


Builder.phase_s5 = _phase_s5
Builder.phase_s5b = _phase_s5b
```

```python
import contextlib
import numpy as np
import concourse.bass as bass
import concourse.mybir as mybir
from concourse.bass_utils import run_bass_kernel_spmd

F32 = mybir.dt.float32
BF16 = mybir.dt.bfloat16
ALU = mybir.AluOpType
AF = mybir.ActivationFunctionType
AX = mybir.AxisListType


class _Eng:
    def __init__(self, name, handle, sem, inc):
        self.name = name
        self.h = handle
        self.sem = sem
        self.inc = inc
        self.count = 0
        self.seen = {}


class _Reg:
    __slots__ = ("w", "r")

    def __init__(self):
        self.w = None
        self.r = {}


class K:
    def __init__(self, nc, stack, n_dma=10):
        self.nc = nc
        self.stack = stack
        self.regs = {}
        mk = lambda n: stack.enter_context(nc.semaphore(n))
        self.pe = _Eng("pe", nc.tensor, mk("s_pe"), 1)
        self.act = _Eng("act", nc.scalar, mk("s_act"), 1)
        self.dve = _Eng("dve", nc.vector, mk("s_dve"), 1)
        self.pool = _Eng("pool", nc.gpsimd, mk("s_pool"), 1)
        self.compute = [self.pe, self.act, self.dve, self.pool]
        self.dq = {}
        for qn, qh in (("sync", nc.sync), ("gpsimd", nc.gpsimd), ("scalar", nc.scalar)):
            self.dq[qn] = [
                _Eng(f"d_{qn}{i}", qh, mk(f"s_d{qn}{i}"), 16) for i in range(n_dma)
            ]
        self.dq_rr = {qn: 0 for qn in self.dq}
        self.qseen = {"sync": {}, "gpsimd": self.pool.seen, "scalar": self.act.seen}

    def reg(self, key):
        r = self.regs.get(key)
        if r is None:
            r = self.regs[key] = _Reg()
        return r

    def _deps(self, reads, writes):
        deps = {}
        for k in reads:
            r = self.reg(k)
            if r.w is not None:
                e, c = r.w
                deps[e] = max(deps.get(e, 0), c)
        for k in writes:
            r = self.reg(k)
            if r.w is not None:
                e, c = r.w
                deps[e] = max(deps.get(e, 0), c)
            for e, c in r.r.items():
                deps[e] = max(deps.get(e, 0), c)
        return deps

    def _commit(self, eng, reads, writes):
        c = eng.count
        for k in reads:
            self.reg(k).r[eng] = c
        for k in writes:
            r = self.reg(k)
            r.w = (eng, c)
            r.r = {}

    def op(self, eng, fn, reads=(), writes=()):
        deps = self._deps(reads, writes)
        for e, c in deps.items():
            if e is eng and eng is self.pe:
                continue
            if eng.seen.get(e, 0) < c:
                eng.h.wait_ge(e.sem, c * e.inc)
                eng.seen[e] = c
        ins = fn()
        eng.count += 1
        ins.then_inc(eng.sem, eng.inc)
        self._commit(eng, reads, writes)
        return ins

    def dma(self, q, out, in_, reads=(), writes=(), **kw):
        lst = self.dq[q]
        i = self.dq_rr[q]
        self.dq_rr[q] = (i + 1) % len(lst)
        d = lst[i]
        seen = self.qseen[q]
        if d.count > 0 and seen.get(d, 0) < d.count:
            d.h.wait_ge(d.sem, d.count * 16)
            seen[d] = d.count
        deps = self._deps(reads, writes)
        for e, c in deps.items():
            if seen.get(e, 0) < c:
                d.h.wait_ge(e.sem, c * e.inc)
                seen[e] = c
        ins = d.h.dma_start(out=out, in_=in_, **kw)
        d.count += 1
        ins.then_inc(d.sem, 16)
        self._commit(d, reads, writes)
        return ins

    def finish(self):
        seen = self.qseen["sync"]
        allengs = list(self.compute)
        for lst in self.dq.values():
            allengs += lst
        for e in allengs:
            if e.count > 0 and seen.get(e, 0) < e.count:
                self.nc.sync.wait_ge(e.sem, e.count * e.inc)
                seen[e] = e.count

    def barrier(self):
        allengs = list(self.compute)
        for lst in self.dq.values():
            allengs += lst
        for h, seen in ((self.nc.tensor, self.pe.seen), (self.nc.scalar, self.act.seen),
                        (self.nc.vector, self.dve.seen), (self.nc.gpsimd, self.pool.seen),
                        (self.nc.sync, self.qseen["sync"])):
            for e in allengs:
                if e.count > 0 and seen.get(e, 0) < e.count:
                    h.wait_ge(e.sem, e.count * e.inc)
                    seen[e] = e.count
        self.regs.clear()


D = 1024
T = 4096
TC = 256
TT = T + TC
NTT = TT // 128
DEPTH = 2
EPS = 1e-6
NCH = TT // 16
WCOL = {"ua": (0, 256), "k": (256, 512), "ga": (768, 256), "gs": (1024, 256), "q": (1280, 512),
        "v": (1792, 512), "gd": (2304, 512), "us": (2816, 256)}
TWO_PI = 2.0 * np.pi


class Builder:
    def __init__(self, debug=False, layers=(0, 1), phases=("M", "P", "L", "S", "A")):
        self.debug = debug
        self.layers = layers
        self.phases = phases
        self.nc = bass.Bass("TRN2", target_bir_lowering=False)
        self.ins = {}
        self.scr = {}

    def din(self, name, shape, dt=F32):
        ap = self.nc.dram_tensor(name, list(shape), dt, kind="ExternalInput").ap()
        self.ins[name] = ap
        return ap

    def dscr(self, name, shape, dt=F32):
        kind = "ExternalOutput" if self.debug else "Internal"
        ap = self.nc.dram_tensor(name, list(shape), dt, kind=kind).ap()
        self.scr[name] = ap
        return ap

    def _uname(self, name):
        self._uid = getattr(self, "_uid", 0) + 1
        return f"{name}_u{self._uid}"

    def sb(self, st, name, shape, dt=F32):
        return st.enter_context(self.nc.sbuf_tensor(self._uname(name), list(shape), dt))

    def ps(self, st, name, shape, dt=F32):
        return st.enter_context(self.nc.psum_tensor(self._uname(name), list(shape), dt))

    def declare(self):
        d = self.din
        self.hin = d("hin", [TT, D])
        self.cvec = d("cvec", [128, 8, 2])
        self.ident = d("ident", [128, 128])
        self.perm = d("perm", [128, 128])
        self.sel2 = d("sel2", [2, 2, 128])
        self.ropeC = d("ropeC", [128, T])
        self.ropeS = d("ropeS", [128, T])
        self.finalg = d("finalg", [1, D])
        self.s5idx = d("s5idx", [128, 16, NCH + 1])
        self.s5exp = d("s5exp", [128, 3, 16, 16])
        self.s5mask = d("s5mask", [128, 2, 2, 256])
        self.L = []
        for l in range(DEPTH):
            p = {}
            p["w_mod"] = d(f"w_mod{l}", [D, 3 * D])
            p["bmodR"] = d(f"bmodR{l}", [2, 3 * D])
            p["normg"] = d(f"normg{l}", [128, 8, 2])
            p["w_in"] = d(f"w_in{l}", [D, 3 * D])
            p["w_out"] = d(f"w_out{l}", [D, D])
            p["convw"] = d(f"convw{l}", [128, 2, 4])
            p["convb"] = d(f"convb{l}", [128, 2])
            p["wax"] = d(f"wax{l}", [128, 2, 2, 2, 128])
            p["bax"] = d(f"bax{l}", [128, 2, 2, 2])
            p["lrulam"] = d(f"lrulam{l}", [128, 2, 2])
            p["s5lam"] = d(f"s5lam{l}", [128, 3, 16])
            p["s5b"] = d(f"s5b{l}", [128, 2, 16, 16])
            p["s5c"] = d(f"s5c{l}", [128, 2, 16, 16])
            p["s5d"] = d(f"s5d{l}", [1, 256])
            p["wglu"] = d(f"wglu{l}", [256, 256])
            p["bglu"] = d(f"bglu{l}", [128, 2])
            p["dalam"] = d(f"dalam{l}", [1, 256])
            p["dag"] = d(f"dag{l}", [128, 1])
            self.L.append(p)
        s = self.dscr
        self.uaT = s("uaT", [256, TT])
        self.gaT = s("gaT", [256, TT])
        self.gsT = s("gsT", [256, TT])
        self.kT = s("kT", [512, TT], BF16)
        self.qT = s("qT", [512, TT], BF16)
        self.v_s = s("v_s", [TT, 512], BF16)
        self.gdT = s("gdT", [512, TT])
        self.us_s = s("us_s", [TT, 256])
        self.yas = s("yas", [512, TT], BF16)
        self.hbuf = s("hbuf", [TT, D])
        self.zs = s("zs", [TT, 256])
        self.out = self.nc.dram_tensor("out", [T, D], F32, kind="ExternalOutput").ap()
        if self.debug:
            self.dbg_mod = s("dbg_mod", [128, 24, 2])
            self.dbg_gate = s("dbg_gate", [128, 2, D])
            self.dbg_hl = s("dbg_hl", [2, 2, 128, TT])

    def build(self):
        nc = self.nc
        self.declare()
        with contextlib.ExitStack() as st:
            self.k = K(nc, st)
            k = self.k
            g = lambda name, shape, dt=F32: self.sb(st, name, shape, dt)
            self.identf = g("identf", [128, 128])
            self.identb = g("identb", [128, 128], BF16)
            self.permf = g("permf", [128, 128])
            self.ones1 = g("ones1", [1, 128])
            self.sc = g("sc", [128, 8, 2])
            self.modv = g("modv", [128, 24, 2])
            self.gmul = g("gmul", [128, 8, 2])
            self.gateb = g("gateb", [128, 2, D])
            self.nlam = g("nlam", [128, 1])
            k.dma("sync", self.identf[:], self.ident, writes=["identf"])
            k.dma("sync", self.permf[:], self.perm, writes=["permf"])
            k.dma("sync", self.sc[:], self.cvec, writes=["sc"])
            k.op(k.dve, lambda: nc.vector.tensor_copy(self.identb[:], self.identf[:]), reads=["identf"], writes=["identb"])
            k.op(k.dve, lambda: nc.vector.memset(self.ones1[:], 1.0), writes=["ones1"])
            k.op(k.act, lambda: nc.scalar.activation(self.sc[:], self.sc[:], AF.Silu), reads=["sc"], writes=["sc"])
            k.barrier()
            for l in self.layers:
                last = (l == DEPTH - 1)
                hsrc = self.hin if l == 0 else self.hbuf
                with contextlib.ExitStack() as pst:
                    wbf = self.proj_prefetch(l, pst) if "P" in self.phases else None
                    if "M" in self.phases:
                        self.phase_mod(l)
                        k.barrier()
                    if "P" in self.phases:
                        self.phase_proj(l, hsrc, wbf)
                        k.barrier()
                if "L" in self.phases:
                    self.phase_lru(l)
                    k.barrier()
                if "S" in self.phases:
                    self.phase_s5(l)
                    k.barrier()
                with contextlib.ExitStack() as ast:
                    pre = None
                    if "A" in self.phases:
                        pre = self.att_prefetch(l, ast)
                    if "S" in self.phases and getattr(self, "s5_stop", 9) > 4:
                        self.phase_s5b(l)
                    k.barrier()
                    if pre is not None:
                        self._att_tmp.close()
                    if "A" in self.phases:
                        self.phase_att(l, hsrc, last, pre)
                        k.barrier()
            k.finish()
        return nc

    def phase_mod(self, l):
        nc, k, p = self.nc, self.k, self.L[l]
        with contextlib.ExitStack() as ph:
            wb = [self.sb(ph, f"m_wb{i}", [128, 8, 512]) for i in range(3)]
            bmr = self.sb(ph, "m_bmr", [2, 3 * D])
            ng = self.sb(ph, "m_ng", [128, 8, 2])
            sel = self.sb(ph, "m_sel", [2, 2, 128])
            rowv = self.sb(ph, "m_rowv", [2, 3 * D])
            psr = [self.ps(ph, f"m_psr{i}", [128, 512]) for i in range(2)]
            psT = self.ps(ph, "m_psT", [128, 512])
            psb = [self.ps(ph, f"m_psb{i}", [128, 512]) for i in range(2)]
            k.dma("scalar", bmr[:], p["bmodR"], writes=["bmr"])
            k.dma("scalar", ng[:], p["normg"], writes=["ng"])
            k.dma("scalar", sel[:], self.sel2, writes=["sel"])
            wsrc = p["w_mod"].rearrange("(k p) n -> p k n", p=128)
            for cb in range(6):
                w = wb[cb % 3]
                wk = f"wb{cb % 3}"
                for k4 in range(2):
                    k.dma("sync", w[:, k4 * 4:(k4 + 1) * 4, :],
                          wsrc[:, k4 * 4:(k4 + 1) * 4, cb * 512:(cb + 1) * 512], writes=[wk + f"_{k4}"])
                pr_ = psr[cb % 2]
                for k8 in range(8):
                    k.op(k.pe, lambda: nc.tensor.matmul(pr_[0:2, :], self.sc[:, k8, :], w[:, k8, :], start=(k8 == 0), stop=(k8 == 7)),
                         reads=[wk + f"_{k8 // 4}", "sc"], writes=[f"psr{cb % 2}"])
                cs = slice(cb * 512, (cb + 1) * 512)
                k.op(k.dve, lambda: nc.vector.tensor_tensor(rowv[:, cs], pr_[0:2, :], bmr[:, cs], ALU.add),
                     reads=[f"psr{cb % 2}", "bmr"], writes=[f"rowv{cb}"])
                for j in range(4):
                    ft = cb * 4 + j
                    k.op(k.pe, lambda: nc.tensor.transpose(psT[:, ft * 2:(ft + 1) * 2], rowv[0:2, ft * 128:(ft + 1) * 128], self.identf[0:2, 0:2]),
                         reads=[f"rowv{cb}", "identf"], writes=["psT"])
            k.op(k.dve, lambda: nc.vector.tensor_copy(self.modv[:], psT[:, 0:48].rearrange("p (a b) -> p a b", b=2)),
                 reads=["psT"], writes=["modv"])
            k.op(k.dve, lambda: nc.vector.scalar_tensor_tensor(self.gmul[:], self.modv[:, 8:16, :], 1.0, ng[:], ALU.add, ALU.mult),
                 reads=["modv", "ng"], writes=["gmul"])
            for v in range(2):
                for hf in range(2):
                    pb = psb[hf]
                    k.op(k.pe, lambda: nc.tensor.matmul(pb[:], sel[0:2, v, :], rowv[0:2, 2 * D + hf * 512:2 * D + (hf + 1) * 512], start=True, stop=True),
                         reads=["sel", f"rowv{4 + hf}"], writes=[f"psb{hf}"])
                    k.op(k.dve, lambda: nc.vector.tensor_copy(self.gateb[:, v, hf * 512:(hf + 1) * 512], pb[:]),
                         reads=[f"psb{hf}"], writes=[f"gateb{v}{hf}"])
            if self.debug:
                k.dma("gpsimd", self.dbg_mod, self.modv[:], reads=["modv"])
                k.dma("gpsimd", self.dbg_gate, self.gateb[:], reads=["gateb00", "gateb01", "gateb10", "gateb11"])

    def proj_prefetch(self, l, st_):
        nc, k, p = self.nc, self.k, self.L[l]
        wbf = self.sb(st_, "p_wbf", [128, 8, 3 * D], BF16)
        stg = [self.sb(st_, f"p_stg{i}", [128, 8, 256]) for i in range(3)]
        wsrc = p["w_in"].rearrange("(k p) n -> p k n", p=128)
        for cb in range(12):
            s_ = stg[cb % 3]
            k.dma("scalar", s_[:], wsrc[:, :, cb * 256:(cb + 1) * 256], writes=[f"pstg{cb % 3}"])
            if cb % 3 == 2:
                k.op(k.pool, lambda: nc.gpsimd.tensor_copy(wbf[:, :, cb * 256:(cb + 1) * 256], s_[:]), reads=[f"pstg{cb % 3}"], writes=[f"wbf{cb}"])
            else:
                k.op(k.dve, lambda: nc.vector.tensor_copy(wbf[:, :, cb * 256:(cb + 1) * 256], s_[:]), reads=[f"pstg{cb % 3}"], writes=[f"wbf{cb}"])
        return wbf

    def phase_proj(self, l, hsrc, wbf):
        nc, k, p = self.nc, self.k, self.L[l]
        with contextlib.ExitStack() as ph:
            wkeys = []
            xb = [self.sb(ph, f"p_x{i}", [128, D]) for i in range(4)]
            junk = self.sb(ph, "p_junk", [128, D], BF16)
            xh = [self.sb(ph, f"p_xh{i}", [128, D], BF16) for i in range(4)]
            ss = self.sb(ph, "p_ss", [128, 4])
            nT = [self.sb(ph, f"p_nT{i}", [128, 8, 512], BF16) for i in range(2)]
            cosb = [self.sb(ph, f"p_cos{i}", [128, 512]) for i in range(2)]
            sinb = [self.sb(ph, f"p_sin{i}", [128, 512]) for i in range(2)]
            kf = [self.sb(ph, f"p_kf{i}", [128, 512]) for i in range(2)]
            t1 = [self.sb(ph, f"p_t1{i}", [128, 512]) for i in range(2)]
            t2 = [self.sb(ph, f"p_t2{i}", [128, 512]) for i in range(2)]
            of32 = [self.sb(ph, f"p_of{i}", [128, 512]) for i in range(3)]
            obf = [self.sb(ph, f"p_ob{i}", [128, 512], BF16) for i in range(3)]
            pT = [self.ps(ph, f"p_pT{i}", [128, 1024], BF16) for i in range(2)]
            pm = [self.ps(ph, f"p_pm{i}", [128, 512]) for i in range(4)]
            pr = [self.ps(ph, f"p_pr{i}", [128, 512]) for i in range(2)]
            blocks = [(0, 256)] + [(256 + 512 * i, 512) for i in range(8)]
            cnt = {"x": 0, "pm": 0, "pr": 0, "of": 0, "ob": 0, "kf": 0, "ev": 0}

            def rot(name, n):
                i = cnt[name] % n
                cnt[name] += 1
                return i

            def stage1a(bi):
                t0, nt = blocks[bi]
                lat = t0 >= TC
                if lat:
                    cb_, sb_ = cosb[bi % 2], sinb[bi % 2]
                    k.dma("sync", cb_[:], self.ropeC[:, t0 - TC:t0 - TC + 512], writes=[f"cos{bi % 2}"])
                    k.dma("sync", sb_[:], self.ropeS[:, t0 - TC:t0 - TC + 512], writes=[f"sin{bi % 2}"])
                for j in range(nt // 128):
                    xi = (bi * 4 + j) % 4
                    x_, xh_ = xb[xi], xh[xi]
                    k.dma("sync", x_[:], hsrc[t0 + j * 128:t0 + (j + 1) * 128, :], writes=[f"x{xi}"])
                    k.op(k.act, lambda: nc.scalar.activation(junk[:], x_[:], AF.Square, accum_out=ss[:, xi:xi + 1]),
                         reads=[f"x{xi}"], writes=["junk", f"ss{xi}"])
                    k.op(k.act, lambda: nc.scalar.activation(ss[:, xi:xi + 1], ss[:, xi:xi + 1], AF.Sqrt, scale=1.0 / D, bias=EPS),
                         reads=[f"ss{xi}"], writes=[f"ss{xi}"])
                    k.op(k.dve, lambda: nc.vector.reciprocal(ss[:, xi:xi + 1], ss[:, xi:xi + 1]),
                         reads=[f"ss{xi}"], writes=[f"ss{xi}"])
                    k.op(k.dve, lambda: nc.vector.tensor_scalar_mul(xh_[:], x_[:], ss[:, xi:xi + 1]),
                         reads=[f"x{xi}", f"ss{xi}"], writes=[f"xh{xi}"])

            def stage1b(bi):
                t0, nt = blocks[bi]
                v = 1 if t0 < TC else 0
                nTb = nT[bi % 2]
                nk = f"nT{bi % 2}"
                for j in range(nt // 128):
                    xi = (bi * 4 + j) % 4
                    xh_ = xh[xi]
                    pi_ = rot("x", 2)
                    pT_ = pT[pi_]
                    for k8 in range(8):
                        k.op(k.pe, lambda: nc.tensor.transpose(pT_[:, k8 * 128:(k8 + 1) * 128], xh_[:, k8 * 128:(k8 + 1) * 128], self.identb[:]),
                             reads=[f"xh{xi}", "identb"], writes=[f"pT{pi_}"])
                    for k8 in range(8):
                        if k8 % 2 == 0:
                            k.op(k.dve, lambda: nc.vector.tensor_scalar(
                                nTb[:, k8, j * 128:(j + 1) * 128], pT_[:, k8 * 128:(k8 + 1) * 128],
                                self.gmul[:, k8, v:v + 1], self.modv[:, k8, v:v + 1], ALU.mult, ALU.add),
                                reads=[f"pT{pi_}", "gmul", "modv"], writes=[nk])
                        else:
                            k.op(k.act, lambda: nc.scalar.activation(
                                nTb[:, k8, j * 128:(j + 1) * 128], pT_[:, k8 * 128:(k8 + 1) * 128], AF.Identity,
                                scale=self.gmul[:, k8, v:v + 1], bias=self.modv[:, k8, v:v + 1]),
                                reads=[f"pT{pi_}", "gmul", "modv"], writes=[nk])

            def stage2(bi, part):
                t0, nt = blocks[bi]
                ntile = nt // 128
                nTb = nT[bi % 2]
                nk = f"nT{bi % 2}"
                lat = t0 >= TC
                cb_, sb_ = cosb[bi % 2], sinb[bi % 2]
                pending = []
                for name, dst in ((("ua", self.uaT), ("ga", self.gaT), ("gs", self.gsT), ("gd", self.gdT), ("k", self.kT), ("q", self.qT)) if part == 0 else []):
                    co, w = WCOL[name]
                    for mt in range(w // 128):
                        pi = rot("pm", 4)
                        pm_ = pm[pi]
                        for k8 in range(8):
                            k.op(k.pe, lambda k8=k8, pm_=pm_, co=co, mt=mt: nc.tensor.matmul(
                                pm_[:, :nt], wbf[:, k8, co + mt * 128:co + (mt + 1) * 128], nTb[:, k8, :nt],
                                start=(k8 == 0), stop=(k8 == 7)), reads=wkeys + [nk], writes=[f"pm{pi}"])
                        while pending:
                            pending.pop(0)()
                        rows = slice(mt * 128, (mt + 1) * 128)
                        if name in ("ua", "ga", "gs", "gd"):
                            oi = rot("of", 3)
                            o_ = of32[oi]
                            k.op(k.act, lambda o_=o_, pm_=pm_: nc.scalar.copy(o_[:, :nt], pm_[:, :nt]),
                                 reads=[f"pm{pi}"], writes=[f"of{oi}"])
                            k.dma("gpsimd", dst[rows, t0:t0 + nt], o_[:, :nt], reads=[f"of{oi}"])
                        elif not lat:
                            oi = rot("ob", 3)
                            o_ = obf[oi]
                            k.op(k.act, lambda o_=o_, pm_=pm_: nc.scalar.copy(o_[:, :nt], pm_[:, :nt]),
                                 reads=[f"pm{pi}"], writes=[f"ob{oi}"])
                            k.dma("gpsimd", dst[rows, t0:t0 + nt], o_[:, :nt], reads=[f"ob{oi}"])
                        else:
                            ki = rot("kf", 2)
                            kf_, t1_, t2_, pr_ = kf[ki], t1[ki], t2[ki], pr[ki]
                            k.op(k.act, lambda kf_=kf_, pm_=pm_: nc.scalar.copy(kf_[:], pm_[:]),
                                 reads=[f"pm{pi}"], writes=[f"kf{ki}"])
                            k.op(k.pool, lambda kf_=kf_, t1_=t1_: nc.gpsimd.tensor_tensor(t1_[:], kf_[:], cb_[:], ALU.mult),
                                 reads=[f"kf{ki}", f"cos{bi % 2}"], writes=[f"t1{ki}"])

                            def part_b(kf_=kf_, t1_=t1_, t2_=t2_, pr_=pr_, ki=ki, dst=dst, rows=rows):
                                k.op(k.pe, lambda: nc.tensor.matmul(pr_[:], self.permf[:], kf_[:], start=True, stop=True),
                                     reads=[f"kf{ki}", "permf"], writes=[f"pr{ki}"])
                                k.op(k.dve, lambda: nc.vector.tensor_tensor(t2_[:], pr_[:], sb_[:], ALU.mult),
                                     reads=[f"pr{ki}", f"sin{bi % 2}"], writes=[f"t2{ki}"])
                                oi = rot("ob", 3)
                                o_ = obf[oi]
                                k.op(k.dve, lambda: nc.vector.tensor_tensor(o_[:], t1_[:], t2_[:], ALU.add),
                                     reads=[f"t1{ki}", f"t2{ki}"], writes=[f"ob{oi}"])
                                k.dma("gpsimd", dst[rows, t0:t0 + nt], o_[:, :nt], reads=[f"ob{oi}"])
                            pending.append(part_b)
                while pending:
                    pending.pop(0)()
                for j in (range(ntile) if part == 1 else []):
                    r0 = t0 + j * 128
                    for name, dst in (("v", self.v_s), ("us", self.us_s)):
                        co, w = WCOL[name]
                        pi = rot("pm", 4)
                        pm_ = pm[pi]
                        for k8 in range(8):
                            k.op(k.pe, lambda k8=k8, pm_=pm_, co=co, w=w, j=j: nc.tensor.matmul(
                                pm_[:, :w], nTb[:, k8, j * 128:(j + 1) * 128], wbf[:, k8, co:co + w],
                                start=(k8 == 0), stop=(k8 == 7)), reads=wkeys + [nk], writes=[f"pm{pi}"])
                        while pending:
                            pending.pop(0)()
                        ev = rot("ev", 2)
                        eng = k.act if ev == 0 else k.dve
                        if name == "v":
                            oi = rot("ob", 3)
                            o_ = obf[oi]
                            key = f"ob{oi}"
                        else:
                            oi = rot("of", 3)
                            o_ = of32[oi]
                            key = f"of{oi}"
                        if eng is k.act:
                            k.op(eng, lambda o_=o_, pm_=pm_, w=w: nc.scalar.copy(o_[:, :w], pm_[:, :w]), reads=[f"pm{pi}"], writes=[key])
                        else:
                            k.op(eng, lambda o_=o_, pm_=pm_, w=w: nc.vector.tensor_copy(o_[:, :w], pm_[:, :w]), reads=[f"pm{pi}"], writes=[key])
                        k.dma("gpsimd", dst[r0:r0 + 128, :], o_[:, :w], reads=[key])

            stage1a(0)
            stage1b(0)
            for bi in range(len(blocks)):
                if bi + 1 < len(blocks):
                    stage1a(bi + 1)
                stage2(bi, 0)
                if bi + 1 < len(blocks):
                    stage1b(bi + 1)
                stage2(bi, 1)


def _fp(a):
    return np.ascontiguousarray(a, dtype=np.float32)


def _const_tables():
    c = {}
    c["ident"] = np.eye(128, dtype=np.float32)
    r = np.arange(128)
    partner = np.where((r % 32) < 16, r + 16, r - 16)
    perm = np.zeros((128, 128), np.float32)
    perm[partner, r] = 1.0
    c["perm"] = perm
    sel2 = np.zeros((2, 2, 128), np.float32)
    sel2[0, 0, :] = 1.0
    sel2[1, 1, :] = 1.0
    c["sel2"] = sel2
    f = np.arange(16, dtype=np.float32)
    inv_freq = np.exp(np.float32(-np.log(10000.0)) * f / np.float32(16)).astype(np.float32)
    t = np.arange(T)
    row_pos = (t // 64).astype(np.float32)
    col_pos = (t % 64).astype(np.float32)
    d64 = r % 64
    is_row = d64 < 32
    fidx = (d64 % 32) % 16
    ang = np.where(is_row[:, None], row_pos[None, :], col_pos[None, :]).astype(np.float32) * inv_freq[fidx][:, None]
    ang = ang.astype(np.float32)
    sign = np.where((d64 % 32) < 16, -1.0, 1.0).astype(np.float32)
    c["ropeC"] = np.cos(ang).astype(np.float32)
    c["ropeS"] = (np.sin(ang).astype(np.float32) * sign[:, None]).astype(np.float32)
    m = np.arange(NCH + 1, dtype=np.float32)
    idx = np.zeros((128, 16, NCH + 1), np.float32)
    idx[:, 0:8, :] = m[None, None, :]
    idx[:, 8:16, :] = -m[None, None, :]
    c["s5idx"] = idx
    e = np.zeros((128, 3, 16, 16), np.float32)
    j = np.arange(16, dtype=np.float32)
    e[:, 0, 0:8, :] = 15 - j
    e[:, 0, 8:16, :] = j
    e[:, 1, 0:8, :] = j + 1
    e[:, 1, 8:16, :] = 16 - j
    e[:, 2, 0:8, :] = j - 15
    e[:, 2, 8:16, :] = -j
    c["s5exp"] = e
    pidx = np.arange(128)
    i8 = pidx // 16
    jj = np.repeat(np.arange(16), 16)
    msk = np.zeros((128, 2, 2, 256), np.float32)
    for kt in range(2):
        i = (8 * kt + i8)[:, None]
        msk[:, 0, kt, :] = (jj[None, :] >= i)
        msk[:, 1, kt, :] = (i >= jj[None, :])
    c["s5mask"] = msk
    return c


_PERM_COLS = None


def _wcol_perm():
    o = {"ua": (0, 256), "us": (256, 256), "k": (512, 512), "v": (1024, 512),
         "ga": (1536, 256), "gs": (1792, 256), "q": (2048, 512), "gd": (2560, 512)}
    idx = np.zeros(3 * D, np.int64)
    for name, (off, w) in WCOL.items():
        so, sw = o[name]
        idx[off:off + w] = np.arange(so, so + sw)
    return idx


def _fpart(vec, ntile):
    return _fp(np.asarray(vec).reshape(ntile, 128).T)


def _layer_inputs(inp, l):
    o = {}
    o[f"w_mod{l}"] = _fp(inp["w_mod"][l])
    o[f"bmodR{l}"] = _fp(np.repeat(inp["b_mod"][l][None, :], 2, axis=0))
    ng = _fpart(inp["norm_g"][l], 8)
    o[f"normg{l}"] = _fp(np.repeat(ng[:, :, None], 2, axis=2))
    o[f"w_in{l}"] = _fp(inp["w_in"][l][:, _wcol_perm()])
    o[f"w_out{l}"] = _fp(inp["w_out"][l])
    cw = inp["lru_conv_w"][l]
    o[f"convw{l}"] = _fp(cw.T.reshape(2, 128, 4).transpose(1, 0, 2))
    o[f"convb{l}"] = _fpart(inp["lru_conv_b"][l], 2)
    wax = np.zeros((128, 2, 2, 2, 128), np.float32)
    for ai, nm in enumerate(("lru_wa", "lru_wx")):
        w = inp[nm][l]
        for d in range(2):
            for ct in range(2):
                for h in range(2):
                    wax[h * 64:(h + 1) * 64, ct, ai, d, h * 64:(h + 1) * 64] = w[d, 2 * ct + h]
    o[f"wax{l}"] = wax
    bax = np.zeros((128, 2, 2, 2), np.float32)
    for ai, nm in enumerate(("lru_ba", "lru_bx")):
        for d in range(2):
            bax[:, :, ai, d] = _fpart(inp[nm][l][d], 2)
    o[f"bax{l}"] = bax
    ll = np.zeros((128, 2, 2), np.float32)
    for d in range(2):
        ll[:, :, d] = _fpart(inp["lru_lam"][l][d], 2)
    o[f"lrulam{l}"] = ll

    def tp_layout(a):
        a = np.asarray(a)
        rest = a.shape[3:]
        a = a.reshape((2, 8, 2, 64) + rest)
        a = np.moveaxis(a, (2, 3), (0, 1))
        return a.reshape((128, 16) + rest)

    s5lam = np.zeros((128, 3, 16), np.float32)
    s5lam[:, 0] = tp_layout(inp["s5_lam_re"][l])
    s5lam[:, 1] = tp_layout(inp["s5_lam_im"][l])
    s5lam[:, 2] = tp_layout(np.repeat(inp["s5_log_dt"][l][:, :, None], 64, axis=2))
    o[f"s5lam{l}"] = s5lam
    sb_ = np.zeros((128, 2, 16, 16), np.float32)
    sb_[:, 0] = tp_layout(inp["s5_b_re"][l])
    sb_[:, 1] = tp_layout(inp["s5_b_im"][l])
    o[f"s5b{l}"] = sb_
    sc_ = np.zeros((128, 2, 16, 16), np.float32)
    sc_[:, 0] = tp_layout(np.swapaxes(inp["s5_c_re"][l], 2, 3))
    sc_[:, 1] = tp_layout(np.swapaxes(inp["s5_c_im"][l], 2, 3))
    o[f"s5c{l}"] = sc_
    o[f"s5d{l}"] = _fp(inp["s5_d"][l][None, :])
    o[f"wglu{l}"] = _fp(inp["s5_w_glu"][l])
    o[f"bglu{l}"] = _fpart(inp["s5_b_glu"][l], 2)
    o[f"dalam{l}"] = _fp(inp["da_lam"][l].reshape(1, 256))
    o[f"dag{l}"] = _fp(inp["da_norm_g"][l][:, None])
    return o


def prep_inputs(inp):
    shared = dict(_const_tables())
    shared["finalg"] = _fp(inp["final_g"][None, :])
    for l in range(DEPTH):
        shared.update(_layer_inputs(inp, l))
    maps = []
    for b in range(8):
        m = dict(shared)
        m["hin"] = _fp(np.concatenate([inp["ctx"][b], inp["x"][b]], axis=0))
        cv = np.zeros((128, 8, 2), np.float32)
        cv[:, :, 0] = _fpart(inp["c"][b], 8)
        cv[:, :, 1] = _fpart(inp["c_ctx"], 8)
        m["cvec"] = cv
        maps.append(m)
    return maps


_NC_CACHE = {}


def kernel(**inputs):
    inp = {k_: np.asarray(v) for k_, v in inputs.items()}
    maps = prep_inputs(inp)
    if "nc" not in _NC_CACHE:
        _NC_CACHE["nc"] = Builder().build()
    res = run_bass_kernel_spmd(_NC_CACHE["nc"], maps, core_ids=list(range(8)))
    return np.stack([np.asarray(r["out"]) for r in res.results], axis=0).astype(np.float32)


def _phase_lru(self, l):
    nc, k, p = self.nc, self.k, self.L[l]
    with contextlib.ExitStack() as ph:
        N = TT
        convw = self.sb(ph, "l_convw", [128, 2, 4])
        convb = self.sb(ph, "l_convb", [128, 2])
        waxf = self.sb(ph, "l_waxf", [128, 2, 2, 2, 128])
        waxb = self.sb(ph, "l_waxb", [128, 2, 2, 2, 128], BF16)
        bax = self.sb(ph, "l_bax", [128, 2, 2, 2])
        lam = self.sb(ph, "l_lam", [128, 2, 2])
        cl = self.sb(ph, "l_cl", [128, 2, 2])
        cl2 = self.sb(ph, "l_cl2", [128, 2, 2])
        ua = self.sb(ph, "l_ua", [128, N])
        xc = self.sb(ph, "l_xc", [128, N])
        xcb = self.sb(ph, "l_xcb", [128, N], BF16)
        sg = self.sb(ph, "l_sg", [128, N])
        gr = [self.sb(ph, f"l_gr{d}", [128, N]) for d in range(2)]
        gi = [self.sb(ph, f"l_gi{d}", [128, N]) for d in range(2)]
        a2 = [self.sb(ph, f"l_a2{d}", [128, N]) for d in range(2)]
        hd = [self.sb(ph, "l_h0", [128, N]), ua]
        yb = xcb
        pg = [self.ps(ph, f"l_pg{i}", [128, 512]) for i in range(4)]
        k.dma("sync", convw[:], p["convw"], writes=["convw"])
        k.dma("sync", convb[:], p["convb"], writes=["convb"])
        k.dma("sync", waxf[:], p["wax"], writes=["waxf"])
        k.dma("sync", bax[:], p["bax"], writes=["bax"])
        k.dma("sync", lam[:], p["lrulam"], writes=["lam"])
        k.op(k.dve, lambda: nc.vector.tensor_copy(waxb[:], waxf[:]), reads=["waxf"], writes=["waxb"])
        k.op(k.act, lambda: nc.scalar.activation(cl[:], lam[:], AF.Exp, scale=-1.0), reads=["lam"], writes=["cl"])
        k.op(k.act, lambda: nc.scalar.activation(cl[:], cl[:], AF.Ln, scale=1.0, bias=1.0), reads=["cl"], writes=["cl"])
        k.op(k.dve, lambda: nc.vector.tensor_scalar_mul(cl2[:], cl[:], -16.0), reads=["cl"], writes=["cl2"])
        k.op(k.dve, lambda: nc.vector.tensor_scalar_mul(cl[:], cl[:], -8.0), reads=["cl", "cl2"], writes=["cl"])
        segs = [(0, TC), (TC, TT)]
        nblk = [(i * 512, min(512, N - i * 512)) for i in range((N + 511) // 512)]
        pgi = 0
        for ct in range(2):
            rows = slice(ct * 128, (ct + 1) * 128)
            k.dma("sync", ua[:], self.uaT[rows, :], writes=["ua"])
            k.dma("scalar", sg[:], self.gaT[rows, :], writes=["sg"])
            k.op(k.dve, lambda: nc.vector.tensor_scalar(xc[:], ua[:], convw[:, ct, 2:3], convb[:, ct:ct + 1], ALU.mult, ALU.add),
                 reads=["ua", "convw", "convb"], writes=["xc"])
            for (s0, s1) in segs:
                for tap, off in ((0, -2), (1, -1), (3, 1)):
                    lo = max(s0, s0 - off)
                    hi = min(s1, s1 - off)
                    k.op(k.dve, lambda: nc.vector.scalar_tensor_tensor(
                        xc[:, lo:hi], ua[:, lo + off:hi + off], convw[:, ct, tap:tap + 1], xc[:, lo:hi], ALU.mult, ALU.add),
                        reads=["ua", "xc", "convw"], writes=["xc"])
            k.op(k.act, lambda: nc.scalar.copy(xcb[:], xc[:]), reads=["xc"], writes=["xcb"])
            for d in range(2):
                for bi, (c0, w) in enumerate(nblk):
                    pr_, pi_ = pg[pgi % 4], pg[(pgi + 1) % 4]
                    kr, ki = f"pg{pgi % 4}", f"pg{(pgi + 1) % 4}"
                    pgi += 2
                    k.op(k.pe, lambda: nc.tensor.matmul(pr_[:, :w], waxb[:, ct, 0, d, :], xcb[:, c0:c0 + w], start=True, stop=True),
                         reads=["waxb", "xcb"], writes=[kr])
                    k.op(k.pe, lambda: nc.tensor.matmul(pi_[:, :w], waxb[:, ct, 1, d, :], xcb[:, c0:c0 + w], start=True, stop=True),
                         reads=["waxb", "xcb"], writes=[ki])
                    k.op(k.act, lambda: nc.scalar.activation(gr[d][:, c0:c0 + w], pr_[:, :w], AF.Sigmoid, bias=bax[:, ct, 0, d:d + 1]),
                         reads=[kr, "bax"], writes=[f"gr{d}"])
                    k.op(k.act, lambda: nc.scalar.activation(gi[d][:, c0:c0 + w], pi_[:, :w], AF.Sigmoid, bias=bax[:, ct, 1, d:d + 1]),
                         reads=[ki, "bax"], writes=[f"gi{d}"])
                k.op(k.dve, lambda: nc.vector.tensor_tensor(gi[d][:], gi[d][:], xc[:], ALU.mult), reads=[f"gi{d}", "xc"], writes=[f"gi{d}"])
            for d in range(2):
                k.op(k.act, lambda: nc.scalar.activation(a2[d][:], gr[d][:], AF.Exp, scale=cl2[:, ct, d:d + 1]), reads=[f"gr{d}", "cl2"], writes=[f"a2{d}"])
                k.op(k.act, lambda: nc.scalar.activation(gr[d][:], gr[d][:], AF.Exp, scale=cl[:, ct, d:d + 1]), reads=[f"gr{d}", "cl", f"a2{d}"], writes=[f"gr{d}"])
            for d in range(2):
                k.op(k.act, lambda: nc.scalar.activation(a2[d][:], a2[d][:], AF.Sqrt, scale=-1.0, bias=1.0), reads=[f"a2{d}"], writes=[f"a2{d}"])
                k.op(k.dve, lambda: nc.vector.tensor_tensor(gi[d][:], gi[d][:], a2[d][:], ALU.mult), reads=[f"gi{d}", f"a2{d}"], writes=[f"gi{d}"])
            k.op(k.act, lambda: nc.scalar.activation(sg[:], sg[:], AF.Silu), reads=["sg"], writes=["sg"])
            k.op(k.dve, lambda: nc.vector.tensor_tensor_scan(hd[0][:], gr[0][:], gi[0][:], 0.0, ALU.mult, ALU.add),
                 reads=["gr0", "gi0"], writes=["h0"])
            k.op(k.dve, lambda: nc.vector.tensor_tensor_scan(hd[1][:, 0:TC][:, ::-1], gr[1][:, 0:TC][:, ::-1], gi[1][:, 0:TC][:, ::-1],
                                                              0.0, ALU.mult, ALU.add), reads=["gr1", "gi1"], writes=["ua"])
            k.op(k.dve, lambda: nc.vector.tensor_tensor_scan(hd[1][:, TC:TT][:, ::-1], gr[1][:, TC:TT][:, ::-1], gi[1][:, TC:TT][:, ::-1],
                                                              hd[1][:, 0:1], ALU.mult, ALU.add), reads=["gr1", "gi1", "ua"], writes=["ua"])
            if self.debug:
                for d in range(2):
                    k.dma("gpsimd", self.dbg_hl[ct, d], hd[d][:], reads=[("h0" if d == 0 else "ua")])
            k.op(k.pool, lambda: nc.gpsimd.tensor_tensor(hd[0][:], hd[0][:], sg[:], ALU.mult), reads=["h0", "sg"], writes=["h0"])
            k.op(k.dve, lambda: nc.vector.tensor_tensor(hd[1][:], hd[1][:], sg[:], ALU.mult), reads=["ua", "sg"], writes=["ua"])
            k.op(k.dve, lambda: nc.vector.tensor_tensor(yb[:], hd[0][:], hd[1][:], ALU.add), reads=["h0", "ua"], writes=["xcb"])
            k.dma("gpsimd", self.yas[rows, :], yb[:], reads=["xcb"])


Builder.phase_lru = _phase_lru


def _att_prefetch(self, l, st_):
    nc, k, p = self.nc, self.k, self.L[l]
    kTs = self.sb(st_, "a_kT", [128, 4, TT], BF16)
    vs = self.sb(st_, "a_v", [128, NTT, 512], BF16)
    woutb = self.sb(st_, "a_wout", [128, 8, D], BF16)
    self._att_tmp = contextlib.ExitStack()
    stg = self.sb(self._att_tmp, "a_stg", [128, 8, 256])
    wsrc = p["w_out"].rearrange("(k p) n -> p k n", p=128)
    for q4 in range(4):
        k.dma("scalar", stg[:], wsrc[:, :, q4 * 256:(q4 + 1) * 256], writes=["astg"])
        k.op(k.pool, lambda: nc.gpsimd.tensor_copy(woutb[:, :, q4 * 256:(q4 + 1) * 256], stg[:]), reads=["astg"], writes=[f"wout{q4}"])
    if True:
        k.dma("scalar", kTs[:], self.kT.rearrange("(j p) t -> p j t", p=128), writes=["kTs"])
        vsrc = self.v_s.rearrange("(tt p) e -> p tt e", p=128)
        for q4 in range(0, NTT, 9):
            hi_ = min(NTT, q4 + 9)
            k.dma("scalar", vs[:, q4:hi_, :], vsrc[:, q4:hi_, :], writes=[f"vs{q4}"])
    return kTs, vs, woutb


def _phase_att(self, l, hsrc, last, pre):
    nc, k, p = self.nc, self.k, self.L[l]
    lam_init = 0.8 - 0.6 * float(np.exp(-0.3 * l))
    with contextlib.ExitStack() as ph:
        kTs, vs, woutb = pre
        dal = self.sb(ph, "a_dal", [1, 256])
        prod = self.sb(ph, "a_prod", [1, 2, 64])
        e2 = self.sb(ph, "a_e2", [1, 2])
        nl1 = self.sb(ph, "a_nl1", [1, 1])
        dagc = self.sb(ph, "a_dagc", [128, 1])
        onesf = self.sb(ph, "a_onesf", [128, 128])
        onesb = self.sb(ph, "a_onesb", [128, 128], BF16)
        fgb = self.sb(ph, "a_fgb", [128, D])
        qTb = [self.sb(ph, f"a_q{i}", [128, 4, 512], BF16) for i in range(2)]
        yasb = [self.sb(ph, f"a_yas{i}", [128, 4, 512], BF16) for i in range(2)]
        gdb = [self.sb(ph, f"a_gdb{i}", [128, 4, 512]) for i in range(2)]
        NPT = 6
        PT = [[self.sb(ph, f"a_pt{m}{i}", [128, 512], BF16) for i in range(NPT)] for m in range(2)]
        acc = [self.sb(ph, f"a_acc{m}", [128, 512]) for m in range(2)]
        rden = [self.sb(ph, f"a_rden{m}", [128, 512]) for m in range(2)]
        obT = [self.sb(ph, f"a_obT{i}", [128, 4, 512]) for i in range(2)]
        cO = [self.sb(ph, f"a_cO{i}", [128, 512]) for i in range(2)]
        t1 = self.sb(ph, "a_t1", [128, 512])
        sqb = self.sb(ph, "a_sqb", [128, 4, 512])
        rstd1 = self.sb(ph, "a_rstd", [128, 512])
        ydT = self.sb(ph, "a_ydT", [128, 4, 512], BF16)
        hx = [self.sb(ph, f"a_hx{i}", [128, D]) for i in range(2)]
        hn = [self.sb(ph, f"a_hn{i}", [128, D]) for i in range(2)]
        junk = self.sb(ph, "a_junk", [128, D], BF16)
        fs = self.sb(ph, "a_fs", [128, 2])
        psc = [self.ps(ph, f"a_psc{i}", [128, 512]) for i in range(4)]
        pO = [self.ps(ph, f"a_pO{i}", [128, 512]) for i in range(2)]
        pden = [self.ps(ph, f"a_pden{i}", [128, 512]) for i in range(2)]
        po = psc[0:2]
        pokeys = ["psc0", "psc1"]

        k.dma("sync", dal[:], p["dalam"], writes=["dal"])
        dv_ = dal[:].rearrange("p (m t e) -> p m t e", m=2, t=2)
        k.op(k.dve, lambda: nc.vector.tensor_tensor(prod[:], dv_[:, :, 0, :], dv_[:, :, 1, :], ALU.mult), reads=["dal"], writes=["prod"])
        k.op(k.dve, lambda: nc.vector.reduce_sum(e2[:], prod[:], axis=AX.X), reads=["prod"], writes=["e2"])
        k.op(k.act, lambda: nc.scalar.activation(e2[:], e2[:], AF.Exp), reads=["e2"], writes=["e2"])
        k.op(k.dve, lambda: nc.vector.tensor_tensor(nl1[:], e2[:, 1:2], e2[:, 0:1], ALU.subtract), reads=["e2"], writes=["nl1"])
        k.op(k.dve, lambda: nc.vector.tensor_scalar_add(nl1[:], nl1[:], -lam_init), reads=["nl1"], writes=["nl1"])
        k.op(k.pe, lambda: nc.tensor.matmul(pden[0][:, 0:1], self.ones1[:], nl1[:], start=True, stop=True), reads=["ones1", "nl1"], writes=["pden0"])
        k.op(k.dve, lambda: nc.vector.tensor_copy(self.nlam[:], pden[0][:, 0:1]), reads=["pden0"], writes=["nlam"])
        k.dma("sync", dagc[:], p["dag"], writes=["dagc"])
        k.op(k.dve, lambda: nc.vector.tensor_scalar_mul(dagc[:], dagc[:], 1.0 - lam_init), reads=["dagc"], writes=["dagc"])
        k.op(k.dve, lambda: nc.vector.memset(onesf[:], 1.0), writes=["onesf"])
        k.op(k.dve, lambda: nc.vector.memset(onesb[:], 1.0), writes=["onesb"])
        if last:
            k.dma("sync", fgb[:], self.finalg.partition_broadcast(128), writes=["fgb"])

        qblocks = [] if last else [(0, 256, [0, 1])]
        qblocks += [(TC + 512 * i, 512, list(range(NTT))) for i in range(8)]
        qTv = self.qT.rearrange("(j p) t -> p j t", p=128)
        yasv = self.yas.rearrange("(j p) t -> p j t", p=128)
        gdv = self.gdT.rearrange("(j p) t -> p j t", p=128)
        pt_i = 0
        pair_i = 0
        tile_i = [0]

        def epilogue1(bi):
            t0, nq, _ = qblocks[bi]
            gd_, ob_ = gdb[bi % 2], obT[bi % 2]
            banks = [(pden[0], "pden0"), (pden[1], "pden1"), (pO[0], "pO0"), (pO[1], "pO1")]
            tmpb = [(rstd1, "rstd"), (t1, "t1"), (cO[0], "cO0"), (cO[1], "cO1")]
            for h in range(4):
                pd, pdk = banks[h]
                k.op(k.pe, lambda: nc.tensor.matmul(pd[:, :nq], onesf[:], sqb[:, h, :nq], start=True, stop=True),
                     reads=["onesf", f"sqb{h}"], writes=[pdk])
            for h in range(4):
                pd, pdk = banks[h]
                rs_, rk = tmpb[h]
                k.op(k.act, lambda: nc.scalar.activation(rs_[:, :nq], pd[:, :nq], AF.Sqrt, scale=1.0 / 128, bias=EPS),
                     reads=[pdk], writes=[rk])
            for h in range(4):
                rs_, rk = tmpb[h]
                k.op(k.dve, lambda: nc.vector.reciprocal(rs_[:, :nq], rs_[:, :nq]), reads=[rk], writes=[rk])
                k.op(k.pool, lambda: nc.gpsimd.tensor_tensor(rs_[:, :nq], rs_[:, :nq], ob_[:, h, :nq], ALU.mult),
                     reads=[rk, f"obT{bi % 2}{h}"], writes=[rk])
                k.op(k.dve, lambda: nc.vector.scalar_tensor_tensor(ydT[:, h, :nq], rs_[:, :nq], dagc[:, 0:1], gd_[:, h, :nq], ALU.mult, ALU.mult),
                     reads=[rk, "dagc", f"gdb{bi % 2}"], writes=["ydT"])

        def epilogue2(bi):
            t0, nq, _ = qblocks[bi]
            v = 1 if t0 < TC else 0
            yb_ = yasb[bi % 2]
            for qs in range(nq // 128):
                r0 = t0 + qs * 128
                gi_ = tile_i[0] % 2
                tile_i[0] += 1
                hx_, hn_ = hx[gi_], hn[gi_]
                k.dma("sync", hx_[:], hsrc[r0:r0 + 128, :], writes=[f"hx{gi_}"])
                for hf in range(2):
                    for mt in range(8):
                        lhs = yb_[:, mt, qs * 128:(qs + 1) * 128] if mt < 4 else ydT[:, mt - 4, qs * 128:(qs + 1) * 128]
                        k.op(k.pe, lambda: nc.tensor.matmul(po[hf][:], lhs, woutb[:, mt, hf * 512:(hf + 1) * 512],
                                                            start=(mt == 0), stop=(mt == 7)),
                             reads=[f"yas{bi % 2}", "ydT"], writes=[pokeys[hf]])
                    cs = slice(hf * 512, (hf + 1) * 512)
                    k.op(k.dve, lambda: nc.vector.tensor_tensor(hn_[:, cs], po[hf][:], self.gateb[:, v, cs], ALU.mult),
                         reads=[pokeys[hf], f"gateb{v}{hf}"], writes=[f"hn{gi_}{hf}"])
                    k.op(k.pool, lambda: nc.gpsimd.tensor_tensor(hn_[:, cs], hn_[:, cs], hx_[:, cs], ALU.add),
                         reads=[f"hn{gi_}{hf}", f"hx{gi_}"], writes=[f"hn{gi_}{hf}"])
                hk = [f"hn{gi_}0", f"hn{gi_}1"]
                if not last:
                    k.dma("gpsimd", self.hbuf[r0:r0 + 128, :], hn_[:], reads=hk)
                else:
                    k.op(k.act, lambda: nc.scalar.activation(junk[:], hn_[:], AF.Square, accum_out=fs[:, gi_:gi_ + 1]),
                         reads=hk, writes=["junk", f"fs{gi_}"])
                    k.op(k.act, lambda: nc.scalar.activation(fs[:, gi_:gi_ + 1], fs[:, gi_:gi_ + 1], AF.Sqrt, scale=1.0 / D, bias=EPS),
                         reads=[f"fs{gi_}"], writes=[f"fs{gi_}"])
                    k.op(k.dve, lambda: nc.vector.reciprocal(fs[:, gi_:gi_ + 1], fs[:, gi_:gi_ + 1]), reads=[f"fs{gi_}"], writes=[f"fs{gi_}"])
                    k.op(k.dve, lambda: nc.vector.scalar_tensor_tensor(hn_[:], hn_[:], fs[:, gi_:gi_ + 1], fgb[:], ALU.mult, ALU.mult),
                         reads=hk + [f"fs{gi_}", "fgb"], writes=hk)
                    k.dma("gpsimd", self.out[r0 - TC:r0 - TC + 128, :], hn_[:], reads=hk)

        for bi, (t0, nq, ktl) in enumerate(qblocks):
            qb_, yb_, gd_, ob_ = qTb[bi % 2], yasb[bi % 2], gdb[bi % 2], obT[bi % 2]
            k.dma("sync", qb_[:, :, :nq], qTv[:, :, t0:t0 + nq], writes=[f"q{bi % 2}"])
            k.dma("sync", yb_[:, :, :nq], yasv[:, :, t0:t0 + nq], writes=[f"yas{bi % 2}"])
            k.dma("sync", gd_[:, :, :nq], gdv[:, :, t0:t0 + nq], writes=[f"gdb{bi % 2}"])
            k.op(k.act, lambda: nc.scalar.activation(gd_[:, :, :nq], gd_[:, :, :nq], AF.Silu), reads=[f"gdb{bi % 2}"], writes=[f"gdb{bi % 2}"])
            its = [(h, ki_, kt) for h in range(4) for ki_, kt in enumerate(ktl)]

            def emit_scores(i):
                h, ki_, kt = its[i]
                for m in range(2):
                    prt = slice(m * 64, (m + 1) * 64)
                    bnk = 2 * ((pair_i + i) % 2) + m
                    k.op(k.pe, lambda: nc.tensor.matmul(psc[bnk][:, :nq], kTs[prt, h, kt * 128:(kt + 1) * 128], qb_[prt, h, :nq],
                                                        start=True, stop=True), reads=["kTs", f"q{bi % 2}"], writes=[f"psc{bnk}"])

            emit_scores(0)
            for i, (h, ki_, kt) in enumerate(its):
                first, lastk = (ki_ == 0), (ki_ == len(ktl) - 1)
                defer_here = lastk and bi > 0 and h == 0
                if i + 1 < len(its) and not defer_here:
                    emit_scores(i + 1)
                for m in range(2):
                    bnk = 2 * ((pair_i + i) % 2) + m
                    pt_ = PT[m][pt_i % NPT]
                    pk = f"pt{m}{pt_i % NPT}"
                    k.op(k.act, lambda: nc.scalar.activation(pt_[:, :nq], psc[bnk][:, :nq], AF.Exp, scale=0.125), reads=[f"psc{bnk}"], writes=[pk])
                    k.op(k.pe, lambda: nc.tensor.matmul(pO[m][:, :nq], vs[:, kt, h * 128:(h + 1) * 128], pt_[:, :nq], start=first, stop=lastk),
                         reads=[pk], writes=[f"pO{m}"])
                    if m == 0:
                        k.op(k.pe, lambda: nc.tensor.matmul(pden[0][:, :nq], onesb[:], pt_[:, :nq], start=first, stop=lastk),
                             reads=[pk, "onesb"], writes=["pden0"])
                    else:
                        e_ = ki_ % 2
                        eng = k.dve if e_ == 0 else k.pool
                        if ki_ < 2:
                            k.op(eng, lambda: eng.h.tensor_copy(acc[e_][:, :nq], pt_[:, :nq]), reads=[pk], writes=[f"acc{e_}"])
                        else:
                            k.op(eng, lambda: eng.h.tensor_tensor(acc[e_][:, :nq], acc[e_][:, :nq], pt_[:, :nq], ALU.add),
                                 reads=[pk, f"acc{e_}"], writes=[f"acc{e_}"])
                pt_i += 1
                if not lastk:
                    continue
                k.op(k.act, lambda: nc.scalar.copy(cO[0][:, :nq], pO[0][:, :nq]), reads=["pO0"], writes=["cO0"])
                k.op(k.dve, lambda: nc.vector.tensor_copy(cO[1][:, :nq], pO[1][:, :nq]), reads=["pO1"], writes=["cO1"])
                k.op(k.dve, lambda: nc.vector.reciprocal(rden[0][:, :nq], pden[0][:, :nq]), reads=["pden0"], writes=["rden0"])
                k.op(k.pe, lambda: nc.tensor.matmul(pden[1][:, :nq], onesf[:], acc[0][:, :nq], start=True, stop=False),
                     reads=["onesf", "acc0"], writes=["pden1"])
                k.op(k.pe, lambda: nc.tensor.matmul(pden[1][:, :nq], onesf[:], acc[1][:, :nq], start=False, stop=True),
                     reads=["onesf", "acc1"], writes=["pden1"])
                k.op(k.dve, lambda: nc.vector.reciprocal(rden[1][:, :nq], pden[1][:, :nq]), reads=["pden1"], writes=["rden1"])
                k.op(k.dve, lambda: nc.vector.tensor_scalar_mul(rden[1][:, :nq], rden[1][:, :nq], self.nlam[:, 0:1]), reads=["rden1", "nlam"], writes=["rden1"])
                k.op(k.pool, lambda: nc.gpsimd.tensor_tensor(t1[:, :nq], cO[1][:, :nq], rden[1][:, :nq], ALU.mult), reads=["cO1", "rden1"], writes=["t1"])
                k.op(k.dve, lambda: nc.vector.tensor_tensor(ob_[:, h, :nq], cO[0][:, :nq], rden[0][:, :nq], ALU.mult), reads=["cO0", "rden0"], writes=[f"obT{bi % 2}{h}"])
                k.op(k.pool, lambda: nc.gpsimd.tensor_tensor(ob_[:, h, :nq], ob_[:, h, :nq], t1[:, :nq], ALU.add), reads=[f"obT{bi % 2}{h}", "t1"], writes=[f"obT{bi % 2}{h}"])
                k.op(k.pool, lambda: nc.gpsimd.tensor_tensor(sqb[:, h, :nq], ob_[:, h, :nq], ob_[:, h, :nq], ALU.mult),
                     reads=[f"obT{bi % 2}{h}"], writes=[f"sqb{h}"])
                if h == 3:
                    epilogue1(bi)
                if bi > 0 and h == 0:
                    epilogue2(bi - 1)
                    if i + 1 < len(its):
                        emit_scores(i + 1)
            pair_i += len(its)
        epilogue2(len(qblocks) - 1)


Builder.phase_att = _phase_att
Builder.att_prefetch = _att_prefetch
Builder.phase_s5 = lambda self, l: None


def _phase_s5(self, l):
    nc, k, p = self.nc, self.k, self.L[l]
    PI = float(np.pi)
    uid = [0]

    def nm(s_):
        uid[0] += 1
        return f"s_{s_}{uid[0]}"

    def dv(fn, reads, writes):
        return k.op(k.dve, fn, reads=reads, writes=writes)

    I32 = mybir.dt.int32
    PI_LO = 3.1415925

    tcache = {}

    def reduce_pi(st_, out_t, src, shape, rkeys, tmps=None):
        if tmps is None:
            u = self.sb(st_, nm("ru"), shape)
            qi = self.sb(st_, nm("rq"), shape, I32)
        else:
            u, qi = tmps
        dv(lambda: nc.vector.tensor_scalar_mul(u[:], src, 1.0 / TWO_PI), rkeys, [u.name])
        dv(lambda: nc.vector.tensor_copy(qi[:], u[:]), [u.name], [qi.name])
        dv(lambda: nc.vector.tensor_copy(u[:], qi[:]), [qi.name], [u.name])
        dv(lambda: nc.vector.scalar_tensor_tensor(out_t[:], u[:], -TWO_PI, src, ALU.mult, ALU.add), [u.name] + rkeys, [out_t.name])
        dv(lambda: nc.vector.tensor_scalar(out_t[:], out_t[:], -PI_LO, PI_LO, ALU.max, ALU.min), [out_t.name], [out_t.name])

    def sincos(st_, ang, shape, K_, key):
        sn = self.sb(st_, nm("sn"), shape)
        cs = self.sb(st_, nm("cs"), shape)
        ck = (id(st_), tuple(shape))
        if ck not in tcache:
            tcache[ck] = (self.sb(st_, nm("ah"), shape), self.sb(st_, nm("ru"), shape), self.sb(st_, nm("rq"), shape, I32))
        ah, u_, q_ = tcache[ck]
        reduce_pi(st_, sn, ang, shape, [key], (u_, q_))
        dv(lambda: nc.vector.tensor_scalar_add(ah[:], ang, PI / 2), [key], [ah.name])
        reduce_pi(st_, cs, ah[:], shape, [ah.name], (u_, q_))
        for t_ in (sn, cs):
            k.op(k.act, lambda: nc.scalar.activation(t_[:], t_[:], AF.Sin), reads=[t_.name], writes=[t_.name])
        return sn, cs

    with contextlib.ExitStack() as ph:
        BLt = self.sb(ph, "s_BLt", [128, 128, 64], BF16)
        DLt = self.sb(ph, "s_DLt", [128, 32, 256], BF16)
        CLR = self.sb(ph, "s_CLR", [128, 16, 256], BF16)
        CLI = self.sb(ph, "s_CLI", [128, 16, 256], BF16)
        lamp = self.sb(ph, "s_lamp", [128, 3, 16])
        lrd = self.sb(ph, "s_lrd", [128, 16])
        ang = self.sb(ph, "s_ang", [128, 16])
        k.dma("sync", lamp[:], p["s5lam"], writes=["lamp"])
        dt = self.sb(ph, "s_dt", [128, 16])
        k.op(k.act, lambda: nc.scalar.activation(dt[:], lamp[:, 2, :], AF.Exp), reads=["lamp"], writes=["dt"])
        dv(lambda: nc.vector.tensor_tensor(lrd[:], lamp[:, 0, :], dt[:], ALU.mult), ["lamp", "dt"], ["lrd"])
        dv(lambda: nc.vector.tensor_tensor(ang[:], lamp[:, 1, :], dt[:], ALU.mult), ["lamp", "dt"], ["ang"])

        with contextlib.ExitStack() as sa:
            S2 = [128, 16]
            bsrc = self.sb(sa, "s_bsrc", [128, 2, 16, 16])
            csrc = self.sb(sa, "s_csrc", [128, 2, 16, 16])
            expt = self.sb(sa, "s_expt", [128, 3, 16, 16])
            mask = self.sb(sa, "s_mask", [128, 2, 2, 256])
            k.dma("sync", bsrc[:], p["s5b"], writes=["bsrc"])
            k.dma("sync", csrc[:], p["s5c"], writes=["csrc"])
            k.dma("sync", expt[:], self.s5exp, writes=["expt"])
            k.dma("sync", mask[:], self.s5mask, writes=["mask"])
            mag = self.sb(sa, "s_mag", S2)
            k.op(k.act, lambda: nc.scalar.activation(mag[:], lrd[:], AF.Exp), reads=["lrd"], writes=["mag"])
            sn, cs = sincos(sa, ang[:], S2, 1, "ang")
            nr = self.sb(sa, "s_nr", S2); ni = self.sb(sa, "s_ni", S2); den = self.sb(sa, "s_den", S2)
            t1 = self.sb(sa, "s_t1", S2); t2 = self.sb(sa, "s_t2", S2)
            cfr = self.sb(sa, "s_cfr", S2); cfi = self.sb(sa, "s_cfi", S2)
            dv(lambda: nc.vector.tensor_tensor(nr[:], mag[:], cs[:], ALU.mult), ["mag", cs.name], ["nr"])
            dv(lambda: nc.vector.tensor_scalar_add(nr[:], nr[:], -1.0), ["nr"], ["nr"])
            dv(lambda: nc.vector.tensor_tensor(ni[:], mag[:], sn[:], ALU.mult), ["mag", sn.name], ["ni"])
            lre, lim = lamp[:, 0, :], lamp[:, 1, :]
            dv(lambda: nc.vector.tensor_tensor(den[:], lre, lre, ALU.mult), ["lamp"], ["den"])
            dv(lambda: nc.vector.tensor_tensor(t1[:], lim, lim, ALU.mult), ["lamp"], ["t1"])
            dv(lambda: nc.vector.tensor_tensor(den[:], den[:], t1[:], ALU.add), ["den", "t1"], ["den"])
            dv(lambda: nc.vector.reciprocal(den[:], den[:]), ["den"], ["den"])
            dv(lambda: nc.vector.tensor_tensor(t1[:], nr[:], lre, ALU.mult), ["nr", "lamp"], ["t1"])
            dv(lambda: nc.vector.tensor_tensor(t2[:], ni[:], lim, ALU.mult), ["ni", "lamp"], ["t2"])
            dv(lambda: nc.vector.tensor_tensor(cfr[:], t1[:], t2[:], ALU.add), ["t1", "t2"], ["cfr"])
            dv(lambda: nc.vector.tensor_tensor(cfr[:], cfr[:], den[:], ALU.mult), ["cfr", "den"], ["cfr"])
            dv(lambda: nc.vector.tensor_tensor(t1[:], ni[:], lre, ALU.mult), ["ni", "lamp", "cfr"], ["t1"])
            dv(lambda: nc.vector.tensor_tensor(t2[:], nr[:], lim, ALU.mult), ["nr", "lamp", "cfr"], ["t2"])
            dv(lambda: nc.vector.tensor_tensor(cfi[:], t1[:], t2[:], ALU.subtract), ["t1", "t2"], ["cfi"])
            dv(lambda: nc.vector.tensor_tensor(cfi[:], cfi[:], den[:], ALU.mult), ["cfi", "den"], ["cfi"])
            S3 = [128, 16, 16]
            S4 = [128, 16, 16, 16]
            bbr = self.sb(sa, "s_bbr", S3); bbi = self.sb(sa, "s_bbi", S3); u1 = self.sb(sa, "s_u1", S3)
            cfrb = cfr[:].unsqueeze(2).broadcast_to(S3)
            cfib = cfi[:].unsqueeze(2).broadcast_to(S3)
            dv(lambda: nc.vector.tensor_tensor(bbr[:], bsrc[:, 0], cfrb, ALU.mult), ["bsrc", "cfr"], ["bbr"])
            dv(lambda: nc.vector.tensor_tensor(u1[:], bsrc[:, 1], cfib, ALU.mult), ["bsrc", "cfi"], ["u1"])
            dv(lambda: nc.vector.tensor_tensor(bbr[:], bbr[:], u1[:], ALU.subtract), ["bbr", "u1"], ["bbr"])
            dv(lambda: nc.vector.tensor_tensor(bbi[:], bsrc[:, 1], cfrb, ALU.mult), ["bsrc", "cfr", "bbr"], ["bbi"])
            dv(lambda: nc.vector.tensor_tensor(u1[:], bsrc[:, 0], cfib, ALU.mult), ["bsrc", "cfi", "bbr"], ["u1"])
            dv(lambda: nc.vector.tensor_tensor(bbi[:], bbi[:], u1[:], ALU.add), ["bbi", "u1"], ["bbi"])

            def cpow(e):
                lr = self.sb(sa, nm("lr"), S3)
                an = self.sb(sa, nm("an"), S3)
                dv(lambda: nc.vector.tensor_tensor(lr[:], expt[:, e], lrd[:].unsqueeze(2).broadcast_to(S3), ALU.mult), ["expt", "lrd"], [lr.name])
                k.op(k.act, lambda: nc.scalar.activation(lr[:], lr[:], AF.Exp), reads=[lr.name], writes=[lr.name])
                dv(lambda: nc.vector.tensor_tensor(an[:], expt[:, e], ang[:].unsqueeze(2).broadcast_to(S3), ALU.mult), ["expt", "ang"], [an.name])
                s_, c_ = sincos(sa, an[:], S3, 60, an.name)
                dv(lambda: nc.vector.tensor_tensor(c_[:], c_[:], lr[:], ALU.mult), [c_.name, lr.name], [c_.name])
                dv(lambda: nc.vector.tensor_tensor(s_[:], s_[:], lr[:], ALU.mult), [s_.name, lr.name], [s_.name])
                return c_, s_

            w1 = self.sb(sa, "s_w1", S4)
            w2 = self.sb(sa, "s_w2", S4)

            def cmul(out_re, out_im_neg, out_im, pw, vr, vi, vkeys):
                pr_, pi_ = pw
                prb = pr_[:].unsqueeze(3).broadcast_to(S4)
                pib = pi_[:].unsqueeze(3).broadcast_to(S4)
                vrb = vr.unsqueeze(2).broadcast_to(S4)
                vib = vi.unsqueeze(2).broadcast_to(S4)
                rk = [pr_.name, pi_.name] + vkeys
                dv(lambda: nc.vector.tensor_tensor(w1[:], prb, vrb, ALU.mult), rk, ["w1"])
                dv(lambda: nc.vector.tensor_tensor(w2[:], pib, vib, ALU.mult), rk, ["w2"])
                dv(lambda: nc.vector.tensor_tensor(out_re, w1[:], w2[:], ALU.subtract), ["w1", "w2"], [nm("o")])
                dv(lambda: nc.vector.tensor_tensor(w1[:], prb, vib, ALU.mult), rk, ["w1"])
                dv(lambda: nc.vector.tensor_tensor(w2[:], pib, vrb, ALU.mult), rk, ["w2"])
                if out_im is not None:
                    dv(lambda: nc.vector.tensor_tensor(out_im, w1[:], w2[:], ALU.add), ["w1", "w2"], [nm("o")])
                else:
                    dv(lambda: nc.vector.scalar_tensor_tensor(out_im_neg, w1[:], -1.0, w2[:], ALU.mult, ALU.subtract), ["w1", "w2"], [nm("o")])

            PBr = self.sb(sa, "s_PBr", S4); PBi = self.sb(sa, "s_PBi", S4)
            QCr = self.sb(sa, "s_QCr", S4); QCn = self.sb(sa, "s_QCn", S4)
            cmul(PBr[:], None, PBi[:], cpow(0), bbr[:], bbi[:], ["bbr", "bbi"])
            cmul(CLR[:].rearrange("p t (j h) -> p t j h", h=16), CLI[:].rearrange("p t (j h) -> p t j h", h=16), None,
                 cpow(1), csrc[:, 0], csrc[:, 1], ["csrc"])
            cmul(QCr[:], QCn[:], None, cpow(2), csrc[:, 0], csrc[:, 1], ["csrc"])
            k.barrier()
            pst = [self.ps(sa, f"s_pst{i}", [128, 512]) for i in range(4)]
            psd = [self.ps(sa, f"s_psd{i}", [128, 512]) for i in range(4)]
            for tp in range(16):
                pts = [pst[(tp % 2) * 2 + gl] for gl in range(2)]
                pks = [f"pst{(tp % 2) * 2 + gl}" for gl in range(2)]
                for kt in range(2):
                    for pl_, PB in enumerate((PBr, PBi)):
                        q_ = kt * 2 + pl_
                        for gl in range(2):
                            prt = slice(gl * 64, (gl + 1) * 64)
                            k.op(k.pe, lambda: nc.tensor.transpose(pts[gl][:, q_ * 64:(q_ + 1) * 64], PB[prt, tp, kt * 8:(kt + 1) * 8, :],
                                                                   self.identf[prt, prt]), reads=["identf"], writes=[pks[gl]])
                for gl in range(2):
                    s0 = (tp * 2 + gl) * 4
                    k.op(k.act, lambda: nc.scalar.copy(BLt[:, s0:s0 + 4, :], pts[gl][:, 0:256].rearrange("p (a b) -> p a b", b=64)),
                         reads=[pks[gl]], writes=["BLt"])
            m1s = [self.sb(sa, nm("m"), [128, 256]) for _ in range(2)]
            for gp in range(8):
                for kt in range(2):
                    for dr, tp in ((0, gp), (1, 8 + gp)):
                        for PBx, QCx, st_, sp_ in ((PBr, QCr, True, False), (PBi, QCn, False, True)):
                            for gl in range(2):
                                prt = slice(gl * 64, (gl + 1) * 64)
                                ps_ = psd[gl * 2 + dr]
                                k.op(k.pe, lambda: nc.tensor.matmul(ps_[:, 0:256], PBx[prt, tp, kt * 8:(kt + 1) * 8, :], QCx[prt, tp, :, :],
                                                                    start=st_, stop=sp_), reads=[], writes=[f"psd{gl * 2 + dr}"])
                    for gl in range(2):
                        g = gp * 2 + gl
                        pf, pb = psd[gl * 2], psd[gl * 2 + 1]
                        kf_, kb_ = f"psd{gl * 2}", f"psd{gl * 2 + 1}"
                        m1 = m1s[gl]
                        dv(lambda: nc.vector.tensor_tensor(m1[:], pf[:, 0:256], mask[:, 0, kt, :], ALU.mult), [kf_, "mask"], [m1.name])
                        dv(lambda: nc.vector.tensor_tensor(pb[:, 256:512], pb[:, 0:256], mask[:, 1, kt, :], ALU.mult), [kb_, "mask"], [kb_])
                        dv(lambda: nc.vector.tensor_tensor(DLt[:, g * 2 + kt, :], pb[:, 256:512], m1[:], ALU.add), [kb_, m1.name], ["DLt"])
            k.barrier()

        if getattr(self, "s5_stop", 9) <= 1:
            return
        Ut = self.sb(ph, "s_Ut", [128, 32, NCH], BF16)
        SFR = self.sb(ph, "s_SFR", [128, 8, NCH + 1], BF16); SFI = self.sb(ph, "s_SFI", [128, 8, NCH + 1], BF16)
        SBR = self.sb(ph, "s_SBR", [128, 8, NCH + 1], BF16); SBI = self.sb(ph, "s_SBI", [128, 8, NCH + 1], BF16)
        P16 = self.sb(ph, "s_P16", [128, 16])
        phr = self.sb(ph, "s_phr", [128, 16])
        k.op(k.act, lambda: nc.scalar.activation(P16[:], lrd[:], AF.Exp, scale=16.0), reads=["lrd"], writes=["P16"])
        ph16 = self.sb(ph, "s_ph16", [128, 16])
        dv(lambda: nc.vector.tensor_scalar_mul(ph16[:], ang[:], 16.0), ["ang"], ["ph16"])
        reduce_pi(ph, phr, ph16[:], [128, 16], ["ph16"])
        phn = self.sb(ph, "s_phn", [128, 16])
        dv(lambda: nc.vector.tensor_scalar_mul(phn[:], phr[:], 1.0 / TWO_PI), [phr.name], ["phn"])
        CT = [(0, 16), (16, 128), (144, 128)]
        usv = self.us_s.rearrange("(c j) ch -> c j ch", j=16)
        with contextlib.ExitStack() as su:
            ucm = [self.sb(su, f"s_ucm{i}", [128, 16, 256]) for i in range(3)]
            psu = [self.ps(su, f"s_psu{i}", [128, 512]) for i in range(4)]
            n_ = 0
            for ci, (c0, n) in enumerate(CT):
                k.dma("sync", ucm[ci][0:n], usv[c0:c0 + n], writes=[f"ucm{ci}"])
                ucp_ = self.sb(su, f"s_ucp{ci}", [128, 16, 16, 16])
                k.op(k.pool if ci == 1 else k.dve,
                     lambda: (nc.gpsimd if ci == 1 else nc.vector).tensor_copy(
                         ucp_[0:n], ucm[ci][0:n].rearrange("p i (g h) -> p g i h", h=16)),
                     reads=[f"ucm{ci}"], writes=[f"ucp{ci}"])
                for q4 in range(8):
                    pu = psu[n_ % 4]
                    pk = f"psu{n_ % 4}"
                    n_ += 1
                    for a in range(4):
                        s_ = q4 * 4 + a
                        g, kt = s_ // 2, s_ % 2
                        k.op(k.pe, lambda: nc.tensor.transpose(pu[:, a * 128:a * 128 + n], ucp_[0:n, g, kt * 8:(kt + 1) * 8, :],
                                                               self.identf[0:n, 0:n]), reads=[f"ucp{ci}", "identf"], writes=[pk])
                    eng = k.act if n_ % 2 == 0 else k.dve
                    src = pu[:].rearrange("p (a b) -> p a b", b=128)[:, :, 0:n]
                    dst = Ut[:, q4 * 4:(q4 + 1) * 4, c0:c0 + n]
                    if eng is k.act:
                        k.op(eng, lambda: nc.scalar.copy(dst, src), reads=[pk], writes=["Ut"])
                    else:
                        k.op(eng, lambda: nc.vector.tensor_copy(dst, src), reads=[pk], writes=["Ut"])
        k.barrier()
        if getattr(self, "s5_stop", 9) <= 2:
            return
        for d in range(2):
            with contextlib.ExitStack() as sd:
                S8 = [128, 8, NCH]
                idx = self.sb(sd, "s_idx", S8)
                k.dma("sync", idx[:], self.s5idx[:, d * 8:(d + 1) * 8, 0:NCH], writes=["idx"])
                dv(lambda: nc.vector.tensor_tensor(idx[:], idx[:], phn[:, d * 8:(d + 1) * 8].unsqueeze(2).broadcast_to(S8), ALU.mult),
                   ["idx", "phn"], ["idx"])
                SN = self.sb(sd, nm("SN"), S8)
                CS = self.sb(sd, nm("CS"), S8)
                qi8 = self.sb(sd, nm("qi8"), S8, I32)
                dv(lambda: nc.vector.tensor_copy(qi8[:], idx[:]), ["idx"], [qi8.name])
                dv(lambda: nc.vector.tensor_copy(CS[:], qi8[:]), [qi8.name], [CS.name])
                dv(lambda: nc.vector.tensor_tensor(SN[:], idx[:], CS[:], ALU.subtract), ["idx", CS.name], [SN.name])
                dv(lambda: nc.vector.scalar_tensor_tensor(CS[:], SN[:], -1.0, SN[:], ALU.mult, ALU.max), [SN.name], [CS.name])
                k.op(k.act, lambda: nc.scalar.activation(SN[:], SN[:], AF.Sin, scale=6.283179), reads=[SN.name], writes=[SN.name])
                k.op(k.act, lambda: nc.scalar.activation(CS[:], CS[:], AF.Sin, scale=-6.283179, bias=1.5707960), reads=[CS.name], writes=[CS.name])
                PCO = self.sb(sd, "s_PCO", S8)
                XR = self.sb(sd, "s_XR", S8); XI = self.sb(sd, "s_XI", S8)
                VR = self.sb(sd, "s_VR", S8); VI = self.sb(sd, "s_VI", S8)
                WR = self.sb(sd, "s_WR", S8); WI = self.sb(sd, "s_WI", S8)
                psx = [self.ps(sd, f"s_psx{i}", [128, 512]) for i in range(4)]
                dv(lambda: nc.vector.memset(PCO[:], 1.0), [], ["PCO"])
                dv(lambda: nc.vector.tensor_tensor(PCO[:], PCO[:], P16[:, d * 8:(d + 1) * 8].unsqueeze(2).broadcast_to(S8), ALU.mult),
                   ["PCO", "P16"], ["PCO"])
                zc = 0 if d == 0 else NCH - 1
                dv(lambda: nc.vector.memset(PCO[:, :, zc:zc + 1], 0.0), ["PCO"], ["PCO"])
                n_ = 0
                for a in range(8):
                    tp = d * 8 + a
                    for pl_, X in enumerate((XR, XI)):
                        px = psx[n_ % 4]
                        pk = f"psx{n_ % 4}"
                        n_ += 1
                        for gl in range(2):
                            g = a * 2 + gl
                            prt = slice(gl * 64, (gl + 1) * 64)
                            segs = [(0, NCH, 0)] if d == 0 else [(16, NCH, 0), (0, 16, 256)]
                            for (u0, u1_, o0) in segs:
                                for kt in range(2):
                                    slot = ((tp * 2 + gl) * 2 + kt) * 2 + pl_
                                    k.op(k.pe, lambda: nc.tensor.matmul(px[prt, o0:o0 + (u1_ - u0)], BLt[:, slot, :], Ut[:, g * 2 + kt, u0:u1_],
                                                                        start=(kt == 0), stop=(kt == 1)), reads=["BLt", "Ut"], writes=[pk])
                        k.op(k.act, lambda: nc.scalar.copy(X[:, a, :], px[:, 0:NCH]), reads=[pk], writes=[X.name])
                dv(lambda: nc.vector.tensor_tensor(VR[:], XR[:], CS[:], ALU.mult), [XR.name, CS.name], ["VR"])
                k.op(k.pool, lambda: nc.gpsimd.tensor_tensor(WR[:], XI[:], SN[:], ALU.mult), reads=[XI.name, SN.name], writes=["WR"])
                dv(lambda: nc.vector.tensor_tensor(VR[:], VR[:], WR[:], ALU.add), ["VR", "WR"], ["VR"])
                dv(lambda: nc.vector.tensor_tensor(VI[:], XI[:], CS[:], ALU.mult), [XI.name, CS.name], ["VI"])
                k.op(k.pool, lambda: nc.gpsimd.tensor_tensor(WI[:], XR[:], SN[:], ALU.mult), reads=[XR.name, SN.name], writes=["WI"])
                dv(lambda: nc.vector.tensor_tensor(VI[:], VI[:], WI[:], ALU.subtract), ["VI", "WI"], ["VI"])
                fl = lambda t_: t_[:].rearrange("p a b -> p (a b)")
                rv = (lambda ap: ap) if d == 0 else (lambda ap: ap[:, ::-1])
                dv(lambda: nc.vector.tensor_tensor_scan(rv(fl(WR)), rv(fl(PCO)), rv(fl(VR)), 0.0, ALU.mult, ALU.add), ["PCO", "VR", "WR"], ["WR"])
                dv(lambda: nc.vector.tensor_tensor_scan(rv(fl(WI)), rv(fl(PCO)), rv(fl(VI)), 0.0, ALU.mult, ALU.add), ["PCO", "VI", "WI"], ["WI"])
                SR_, SI_ = (SFR, SFI) if d == 0 else (SBR, SBI)
                o_ = 1 if d == 0 else 0
                zcol = 0 if d == 0 else NCH
                dv(lambda: nc.vector.memset(SR_[:, :, zcol:zcol + 1], 0.0), [], [SR_.name])
                dv(lambda: nc.vector.memset(SI_[:, :, zcol:zcol + 1], 0.0), [], [SI_.name])
                dv(lambda: nc.vector.tensor_tensor(VR[:], WR[:], CS[:], ALU.mult), ["WR", CS.name, "VR"], ["VR"])
                k.op(k.pool, lambda: nc.gpsimd.tensor_tensor(VI[:], WI[:], SN[:], ALU.mult), reads=["WI", SN.name, "VI"], writes=["VI"])
                dv(lambda: nc.vector.tensor_tensor(SR_[:, :, o_:o_ + NCH], VR[:], VI[:], ALU.subtract), ["VR", "VI", SR_.name], [SR_.name])
                dv(lambda: nc.vector.tensor_tensor(VR[:], WI[:], CS[:], ALU.mult), ["WI", CS.name, "VR", SR_.name], ["VR"])
                k.op(k.pool, lambda: nc.gpsimd.tensor_tensor(VI[:], WR[:], SN[:], ALU.mult), reads=["WR", SN.name, "VI", SR_.name], writes=["VI"])
                dv(lambda: nc.vector.tensor_tensor(SI_[:, :, o_:o_ + NCH], VR[:], VI[:], ALU.add), ["VR", "VI", SI_.name], [SI_.name])
            k.barrier()
        if getattr(self, "s5_stop", 9) <= 3:
            return
        with contextlib.ExitStack() as sy:
            dsk = self.sb(sy, "s_dsk", [128, 256])
            k.dma("sync", dsk[:], p["s5d"].partition_broadcast(128), writes=["dsk"])
            ucm = [self.sb(sy, f"s_ucy{i}", [128, 16, 256]) for i in range(2)]
            ycm = [self.sb(sy, f"s_ycm{i}", [128, 16, 256]) for i in range(2)]
            tq = [self.sb(sy, f"s_tq{i}", [128, 16, 256]) for i in range(2)]
            psy = [self.ps(sy, f"s_psy{i}", [128, 512]) for i in range(4)]
            zsv = self.zs.rearrange("(c j) ch -> c j ch", j=16)
            n_ = 0
            for ci, (c0, n) in enumerate(CT):
                u_, y_, t_ = ucm[ci % 2], ycm[ci % 2], tq[ci % 2]
                uk, yk, tk = f"ucy{ci % 2}", f"ycm{ci % 2}", f"tq{ci % 2}"
                k.dma("sync", u_[0:n], usv[c0:c0 + n], writes=[uk])
                mb0 = (256 if c0 < 16 else c0 - 16) + 1
                for g2 in range(8):
                    pys = [psy[(2 * n_) % 4], psy[(2 * n_ + 1) % 4]]
                    pks = [f"psy{(2 * n_) % 4}", f"psy{(2 * n_ + 1) % 4}"]
                    n_ += 1
                    mms = []
                    for gg in range(2):
                        g = g2 * 2 + gg
                        gp, gl = g // 2, g % 2
                        prt = slice(gl * 64, (gl + 1) * 64)
                        mms.append([(Ut[:, g * 2 + 0, c0:c0 + n], DLt[:, g * 2 + 0, :]),
                                    (Ut[:, g * 2 + 1, c0:c0 + n], DLt[:, g * 2 + 1, :]),
                                    (SFR[prt, gp, c0:c0 + n], CLR[prt, gp, :]),
                                    (SFI[prt, gp, c0:c0 + n], CLI[prt, gp, :]),
                                    (SBR[prt, gp, mb0:mb0 + n], CLR[prt, 8 + gp, :]),
                                    (SBI[prt, gp, mb0:mb0 + n], CLI[prt, 8 + gp, :])])
                    for mi in range(6):
                        for gg in range(2):
                            lh, rh = mms[gg][mi]
                            k.op(k.pe, lambda: nc.tensor.matmul(pys[gg][0:n, 0:256], lh, rh, start=(mi == 0), stop=(mi == 5)),
                                 reads=["Ut", "DLt", "CLR", "CLI", SFR.name, SFI.name, SBR.name, SBI.name], writes=[pks[gg]])
                    for gg in range(2):
                        g = g2 * 2 + gg
                        src = pys[gg][0:n, 0:256].rearrange("p (j h) -> p j h", h=16)
                        dst = y_[0:n, :, g * 16:(g + 1) * 16]
                        if gg == 0:
                            k.op(k.act, lambda: nc.scalar.copy(dst, src), reads=[pks[gg]], writes=[yk])
                        else:
                            k.op(k.dve, lambda: nc.vector.tensor_copy(dst, src), reads=[pks[gg]], writes=[yk])
                dsb = dsk[0:n].unsqueeze(1).broadcast_to([n, 16, 256])
                k.op(k.pool, lambda: nc.gpsimd.tensor_tensor(u_[0:n], u_[0:n], dsb, ALU.mult), reads=[uk, "dsk"], writes=[uk])
                dv(lambda: nc.vector.tensor_tensor(y_[0:n], y_[0:n], u_[0:n], ALU.add), [yk, uk], [yk])
                k.op(k.pool, lambda: nc.gpsimd.tensor_tensor(t_[0:n], y_[0:n], y_[0:n], ALU.mult), reads=[yk], writes=[tk])
                dv(lambda: nc.vector.tensor_scalar(t_[0:n], t_[0:n], 0.044715, 1.0, ALU.mult, ALU.add), [tk], [tk])
                k.op(k.pool, lambda: nc.gpsimd.tensor_tensor(t_[0:n], t_[0:n], y_[0:n], ALU.mult), reads=[tk, yk], writes=[tk])
                k.op(k.act, lambda: nc.scalar.activation(t_[0:n], t_[0:n], AF.Sigmoid, scale=1.5957691216057308), reads=[tk], writes=[tk])
                dv(lambda: nc.vector.tensor_tensor(y_[0:n], y_[0:n], t_[0:n], ALU.mult), [yk, tk], [yk])
                k.dma("gpsimd", zsv[c0:c0 + n], y_[0:n], reads=[yk])


def _phase_s5b(self, l):
    nc, k, p = self.nc, self.k, self.L[l]

    def dv(fn, reads, writes):
        return k.op(k.dve, fn, reads=reads, writes=writes)

    with contextlib.ExitStack() as sb_:
        wgf = self.sb(sb_, "g_wgf", [128, 2, 256])
        wgb = self.sb(sb_, "g_wgb", [128, 2, 256], BF16)
        bgl = self.sb(sb_, "g_bgl", [128, 2])
        zT = self.sb(sb_, "g_zT", [128, 2, TT])
        zTb = self.sb(sb_, "g_zTb", [128, 2, TT], BF16)
        zt = [self.sb(sb_, f"g_zt{i}", [128, 256]) for i in range(2)]
        gsl = self.sb(sb_, "g_gs", [128, TT])
        sgl = self.sb(sb_, "g_sg", [128, TT])
        yb = self.sb(sb_, "g_yb", [128, TT], BF16)
        pz = [self.ps(sb_, f"g_pz{i}", [128, 512]) for i in range(2)]
        pg = [self.ps(sb_, f"g_pg{i}", [128, 512]) for i in range(2)]
        k.dma("sync", wgf[:], p["wglu"].rearrange("(k p) n -> p k n", p=128), writes=["wgf"])
        k.dma("sync", bgl[:], p["bglu"], writes=["bgl"])
        dv(lambda: nc.vector.tensor_copy(wgb[:], wgf[:]), ["wgf"], ["wgb"])
        for tt in range(NTT):
            z_ = zt[tt % 2]
            zk = f"zt{tt % 2}"
            pz_ = pz[tt % 2]
            k.dma("sync", z_[:], self.zs[tt * 128:(tt + 1) * 128, :], writes=[zk])
            for c_ in range(2):
                k.op(k.pe, lambda: nc.tensor.matmul(pz_[:, c_ * 128:(c_ + 1) * 128], z_[:, c_ * 128:(c_ + 1) * 128], self.identf[:],
                                                    start=True, stop=True),
                     reads=[zk, "identf"], writes=[f"pz{tt % 2}"])
            for c_ in range(2):
                k.op(k.dve, lambda: nc.vector.tensor_copy(zT[:, c_, tt * 128:(tt + 1) * 128], pz_[:, c_ * 128:(c_ + 1) * 128]),
                     reads=[f"pz{tt % 2}"], writes=["zT"])
                dv(lambda: nc.vector.tensor_copy(zTb[:, c_, tt * 128:(tt + 1) * 128], pz_[:, c_ * 128:(c_ + 1) * 128]),
                   [f"pz{tt % 2}"], ["zTb"])
        if getattr(self, "s5_stop", 9) <= 5:
            return
        nblk = [(i * 512, min(512, TT - i * 512)) for i in range((TT + 511) // 512)]
        for co in range(2):
            rows = slice(256 + co * 128, 256 + (co + 1) * 128)
            k.dma("sync", gsl[:], self.gsT[co * 128:(co + 1) * 128, :], writes=["gsl"])
            k.op(k.act, lambda: nc.scalar.activation(gsl[:], gsl[:], AF.Silu), reads=["gsl"], writes=["gsl"])
            for bi, (c0, w) in enumerate(nblk):
                pg_ = pg[bi % 2]
                for ci_ in range(2):
                    k.op(k.pe, lambda: nc.tensor.matmul(pg_[:, :w], wgb[:, ci_, co * 128:(co + 1) * 128], zTb[:, ci_, c0:c0 + w],
                                                        start=(ci_ == 0), stop=(ci_ == 1)), reads=["wgb", "zTb"], writes=[f"pg{bi % 2}"])
                k.op(k.act, lambda: nc.scalar.activation(sgl[:, c0:c0 + w], pg_[:, :w], AF.Sigmoid, bias=bgl[:, co:co + 1]),
                     reads=[f"pg{bi % 2}", "bgl"], writes=["sgl"])
            dv(lambda: nc.vector.tensor_tensor(sgl[:], sgl[:], zT[:, co, :], ALU.mult), ["sgl", "zT"], ["sgl"])
            k.op(k.pool, lambda: nc.gpsimd.tensor_tensor(yb[:], sgl[:], gsl[:], ALU.mult), reads=["sgl", "gsl"], writes=["yb"])
            k.dma("gpsimd", self.yas[rows, :], yb[:], reads=["yb"])


Builder.phase_s5 = _phase_s5
Builder.phase_s5b = _phase_s5b
```
